# Optimizing a Trainium2 kernel written in Bass

```python
import math
import jax, jax.numpy as jnp
from jax import lax
import numpy as np

D_MODEL = 1024
BATCH = 32
SEQ = 256
DEPTH = 4
DEC_BATCH = 8
DEC_SEQ = 4096
PAST_LEN = 512

GRID_W = 64
N_MIXERS = 2
N_SSM_LAYERS = (DEPTH + 1) // 2
N_ATTN_LAYERS = DEPTH // 2
SSM_GROUP = 16
SSM_GROUPS = D_MODEL // SSM_GROUP
SSM_STATE = 64
DT_MIN = 1e-3
DT_MAX = 1e-1
HEAD_DIM = 64
N_HEADS = D_MODEL // HEAD_DIM
N_KV_HEADS = 4
KV_REP = N_HEADS // N_KV_HEADS
WINDOW = 128
ATTN_BLOCK = 128
ROPE_BASE = 10000.0
ROPE_AXIS_DIM = HEAD_DIM // 2
QKV_DIM = (N_HEADS + 2 * N_KV_HEADS) * HEAD_DIM
PEER_HEADS = 8
PEER_NKEYS = 128
PEER_EXPERTS = PEER_NKEYS * PEER_NKEYS
PEER_TOPK = 16
PEER_KEY_DIM = 256
PEER_HALF = PEER_KEY_DIM // 2
PEER_CHUNK = 128
DN_ALPHA = (2 * DEPTH) ** 0.25
DN_BETA = (8 * DEPTH) ** -0.25
LN_EPS = 1e-5
NEG_INF = -1e30

kernel_name = 'hybrid_s5_swa_peer_diffusion_step'


def _layer_norm(x, g, b):
    xf = x.astype(jnp.float32)
    mu = jnp.mean(xf, axis=-1, keepdims=True)
    var = jnp.mean(jnp.square(xf - mu), axis=-1, keepdims=True)
    y = (xf - mu) * lax.rsqrt(var + LN_EPS) * g.astype(jnp.float32) + b.astype(jnp.float32)
    return y.astype(x.dtype)


def _modulation(cond, w_mod, b_mod):
    m = jax.nn.silu(cond.astype(jnp.float32)).astype(w_mod.dtype) @ w_mod + b_mod
    return [t[:, None, :] for t in jnp.split(m, 6, axis=-1)]


def _modulate(x, shift, scale):
    return x * (1.0 + scale) + shift


def _post_norm(x, out, gate, g, b):
    return _layer_norm(DN_ALPHA * x + (1.0 + gate) * out, g, b)


def _rope_2d(x):
    L = x.shape[1]
    rows = L // GRID_W
    row = jnp.repeat(jnp.arange(rows, dtype=jnp.float32), GRID_W)
    col = jnp.tile(jnp.arange(GRID_W, dtype=jnp.float32), rows)
    n_freq = ROPE_AXIS_DIM // 2
    inv = ROPE_BASE ** (-jnp.arange(n_freq, dtype=jnp.float32) / n_freq)

    def rot(seg, pos):
        ang = pos[:, None] * inv[None, :]
        cos = jnp.cos(ang)[None, :, None, :]
        sin = jnp.sin(ang)[None, :, None, :]
        s1, s2 = seg[..., :n_freq], seg[..., n_freq:]
        return jnp.concatenate([s1 * cos - s2 * sin, s1 * sin + s2 * cos], axis=-1)

    xf = x.astype(jnp.float32)
    y = jnp.concatenate([rot(xf[..., :ROPE_AXIS_DIM], row), rot(xf[..., ROPE_AXIS_DIM:], col)], axis=-1)
    return y.astype(x.dtype)


def _s5_scan_dir(u_c, a_re, a_im, log_dt, b_re, b_im, c_re, c_im, h0):
    A = lax.complex(a_re.astype(jnp.float32), a_im.astype(jnp.float32))
    dt = jnp.exp(log_dt.astype(jnp.float32))[:, None]
    A_bar = jnp.exp(A * dt)
    Bm = lax.complex(b_re.astype(jnp.float32), b_im.astype(jnp.float32))
    B_bar = ((A_bar - 1.0) / A)[..., None] * Bm
    Cm = lax.complex(c_re.astype(jnp.float32), c_im.astype(jnp.float32))
    bu = jnp.einsum('blgc,gpc->blgp', u_c, B_bar)
    bu = bu.at[:, 0].add(A_bar[None] * h0)
    a = jnp.broadcast_to(A_bar, (1,) + bu.shape[1:])

    def combine(e1, e2):
        a1, b1 = e1
        a2, b2 = e2
        return a1 * a2, a2 * b1 + b2

    _, hs = lax.associative_scan(combine, (a, bu), axis=1)
    y = jnp.real(jnp.einsum('blgp,gcp->blgc', hs, Cm))
    return y, hs[:, -1]


def _s5_mixer(h, h0_re, h0_im, w_in, a_re, a_im, log_dt, b_re, b_im, c_re, c_im, d_skip, w_glu, w_out):
    Bsz, L, _ = h.shape
    u = (h @ w_in).astype(jnp.float32)
    u_c = u.astype(jnp.complex64).reshape(Bsz, L, SSM_GROUPS, SSM_GROUP)
    h0 = lax.complex(h0_re.astype(jnp.float32), h0_im.astype(jnp.float32))
    y_f, s_f = _s5_scan_dir(u_c, a_re[0], a_im[0], log_dt[0], b_re[0], b_im[0], c_re[0], c_im[0], h0[:, 0])
    y_b, s_b = _s5_scan_dir(jnp.flip(u_c, axis=1), a_re[1], a_im[1], log_dt[1], b_re[1], b_im[1],
                            c_re[1], c_im[1], h0[:, 1])
    y = (y_f + jnp.flip(y_b, axis=1)).reshape(Bsz, L, D_MODEL) + d_skip.astype(jnp.float32) * u
    y = jax.nn.gelu(y).astype(h.dtype)
    val, gate = jnp.split(y @ w_glu, 2, axis=-1)
    out = (val * jax.nn.sigmoid(gate)) @ w_out
    s = jnp.stack([s_f, s_b], axis=1)
    return out, jnp.real(s), jnp.imag(s)


def _project_qkv(h, w_qkv):
    Bsz, L, _ = h.shape
    qkv = h @ w_qkv
    nq = N_HEADS * HEAD_DIM
    nk = N_KV_HEADS * HEAD_DIM
    q = qkv[..., :nq].reshape(Bsz, L, N_HEADS, HEAD_DIM)
    k = qkv[..., nq:nq + nk].reshape(Bsz, L, N_KV_HEADS, HEAD_DIM)
    v = qkv[..., nq + nk:].reshape(Bsz, L, N_KV_HEADS, HEAD_DIM)
    return q, k, v


def _block_attention(q, k_ctx, v_ctx, sink, k_lat=None, v_lat=None):
    Bsz, T = q.shape[0], q.shape[1]
    Lc = k_ctx.shape[1]
    nb = T // ATTN_BLOCK
    scale = HEAD_DIM ** -0.5
    sink_l = sink.astype(jnp.float32).reshape(1, N_KV_HEADS, KV_REP, 1, 1)
    span = ATTN_BLOCK + 2 * WINDOW
    if k_lat is not None:
        pad = ((0, 0), (WINDOW, WINDOW), (0, 0), (0, 0))
        k_pad = jnp.pad(k_lat, pad)
        v_pad = jnp.pad(v_lat, pad)
        qi = jnp.arange(ATTN_BLOCK)[:, None]
        kj = jnp.arange(span)[None, :]
        rel_ok = (kj - qi >= 0) & (kj - qi <= 2 * WINDOW)

    def one_block(b):
        q_b = lax.dynamic_slice_in_dim(q, b * ATTN_BLOCK, ATTN_BLOCK, axis=1)
        q_b = q_b.reshape(Bsz, ATTN_BLOCK, N_KV_HEADS, KV_REP, HEAD_DIM)
        logits = [jnp.broadcast_to(sink_l, (Bsz, N_KV_HEADS, KV_REP, ATTN_BLOCK, 1)),
                  jnp.einsum('bqgrd,bkgd->bgrqk', q_b, k_ctx).astype(jnp.float32) * scale]
        if k_lat is not None:
            k_b = lax.dynamic_slice_in_dim(k_pad, b * ATTN_BLOCK, span, axis=1)
            v_b = lax.dynamic_slice_in_dim(v_pad, b * ATTN_BLOCK, span, axis=1)
            kpos = b * ATTN_BLOCK - WINDOW + kj
            ok = rel_ok & (kpos >= 0) & (kpos < T)
            s_loc = jnp.einsum('bqgrd,bkgd->bgrqk', q_b, k_b).astype(jnp.float32) * scale
            logits.append(jnp.where(ok, s_loc, NEG_INF))
        p = jax.nn.softmax(jnp.concatenate(logits, axis=-1), axis=-1)
        out = jnp.einsum('bgrqk,bkgd->bqgrd', p[..., 1:1 + Lc].astype(v_ctx.dtype), v_ctx)
        if k_lat is not None:
            out = out + jnp.einsum('bgrqk,bkgd->bqgrd', p[..., 1 + Lc:].astype(v_b.dtype), v_b)
        return out.reshape(Bsz, ATTN_BLOCK, N_HEADS * HEAD_DIM)

    out = lax.map(one_block, jnp.arange(nb))
    return jnp.transpose(out, (1, 0, 2, 3)).reshape(Bsz, T, N_HEADS * HEAD_DIM)


def _attn_context(h, w_qkv, sink, w_out):
    q, k, v = _project_qkv(h, w_qkv)
    out = _block_attention(q, k, v, sink)
    return out.astype(h.dtype) @ w_out, k, v


def _attn_latent(h, k_ctx, v_ctx, w_qkv, sink, w_out):
    q, k, v = _project_qkv(h, w_qkv)
    q = _rope_2d(q)
    k = _rope_2d(k)
    out = _block_attention(q, k_ctx, v_ctx, sink, k, v)
    return out.astype(h.dtype) @ w_out


def _peer(h, w_q, keys, u_tab, v_tab):
    Bsz, L, _ = h.shape
    T = Bsz * L
    x = h.reshape(T, D_MODEL)
    q = (x @ w_q).reshape(T, PEER_HEADS, 2, PEER_HALF)
    s1 = jnp.einsum('thd,hnd->thn', q[:, :, 0], keys[0])
    s2 = jnp.einsum('thd,hnd->thn', q[:, :, 1], keys[1])
    v1, i1 = lax.top_k(s1, PEER_TOPK)
    v2, i2 = lax.top_k(s2, PEER_TOPK)
    cand = (v1[..., :, None] + v2[..., None, :]).reshape(T, PEER_HEADS, PEER_TOPK * PEER_TOPK)
    cand_idx = (i1[..., :, None] * PEER_NKEYS + i2[..., None, :]).reshape(T, PEER_HEADS, PEER_TOPK * PEER_TOPK)
    top, pos = lax.top_k(cand, PEER_TOPK)
    idx = jnp.take_along_axis(cand_idx, pos, axis=-1)
    g = jax.nn.softmax(top.astype(jnp.float32), axis=-1).astype(h.dtype)
    nc = T // PEER_CHUNK

    def chunk(args):
        xc, ic, gc = args
        u = u_tab[ic]
        act = jax.nn.gelu(jnp.einsum('chkd,cd->chk', u, xc)) * gc
        return jnp.einsum('chk,chkd->cd', act, v_tab[ic])

    out = lax.map(chunk, (x.reshape(nc, PEER_CHUNK, D_MODEL),
                          idx.reshape(nc, PEER_CHUNK, PEER_HEADS, PEER_TOPK),
                          g.reshape(nc, PEER_CHUNK, PEER_HEADS, PEER_TOPK)))
    return out.reshape(Bsz, L, D_MODEL)


def setup_inputs(seed: int = 0) -> dict:
    key = jax.random.key(seed)
    ks = jax.random.split(key, 32)
    f32 = jnp.float32
    inv_d = D_MODEL ** -0.5

    def nrm(k, shape, s=1.0):
        return jax.random.normal(k, shape, f32) * s

    ssm_shape = (N_SSM_LAYERS, 2, SSM_GROUPS, SSM_STATE)
    a_im_init = math.pi * jnp.arange(SSM_STATE, dtype=f32)
    return {
        'x_prompt': nrm(ks[0], (BATCH, SEQ, D_MODEL)),
        'x_sample': nrm(ks[1], (DEC_BATCH, DEC_SEQ, D_MODEL)),
        'state_ssm_re': nrm(ks[2], (DEC_BATCH, N_SSM_LAYERS, 2, SSM_GROUPS, SSM_STATE), 0.3),
        'state_ssm_im': nrm(ks[3], (DEC_BATCH, N_SSM_LAYERS, 2, SSM_GROUPS, SSM_STATE), 0.3),
        'cache_k': nrm(ks[4], (DEC_BATCH, N_ATTN_LAYERS, PAST_LEN, N_KV_HEADS, HEAD_DIM)),
        'cache_v': nrm(ks[5], (DEC_BATCH, N_ATTN_LAYERS, PAST_LEN, N_KV_HEADS, HEAD_DIM)),
        'c': nrm(ks[6], (DEC_BATCH, D_MODEL)),
        'c_ctx': nrm(ks[7], (D_MODEL,)),
        'w_mod': nrm(ks[8], (DEPTH, D_MODEL, 6 * D_MODEL), 0.5 * inv_d),
        'b_mod': nrm(ks[9], (DEPTH, 6 * D_MODEL), 0.01),
        'ln_g': 1.0 + nrm(ks[10], (DEPTH, 2, D_MODEL), 0.01),
        'ln_b': nrm(ks[11], (DEPTH, 2, D_MODEL), 0.01),
        'ssm_w_in': nrm(ks[12], (N_SSM_LAYERS, D_MODEL, D_MODEL), inv_d),
        'ssm_a_re': -0.5 + nrm(ks[13], ssm_shape, 0.01),
        'ssm_a_im': a_im_init + nrm(ks[14], ssm_shape, 0.01),
        'ssm_log_dt': jax.random.uniform(ks[15], (N_SSM_LAYERS, 2, SSM_GROUPS), f32,
                                         math.log(DT_MIN), math.log(DT_MAX)),
        'ssm_b_re': nrm(ks[16], ssm_shape + (SSM_GROUP,), (2 * SSM_GROUP) ** -0.5),
        'ssm_b_im': nrm(ks[17], ssm_shape + (SSM_GROUP,), (2 * SSM_GROUP) ** -0.5),
        'ssm_c_re': nrm(ks[18], (N_SSM_LAYERS, 2, SSM_GROUPS, SSM_GROUP, SSM_STATE), SSM_STATE ** -0.5),
        'ssm_c_im': nrm(ks[19], (N_SSM_LAYERS, 2, SSM_GROUPS, SSM_GROUP, SSM_STATE), SSM_STATE ** -0.5),
        'ssm_d': nrm(ks[20], (N_SSM_LAYERS, D_MODEL)),
        'ssm_w_glu': nrm(ks[21], (N_SSM_LAYERS, D_MODEL, 2 * D_MODEL), inv_d),
        'ssm_w_out': nrm(ks[22], (N_SSM_LAYERS, D_MODEL, D_MODEL), inv_d * DN_BETA),
        'attn_w_qkv': nrm(ks[23], (N_ATTN_LAYERS, D_MODEL, QKV_DIM), inv_d),
        'attn_sink': nrm(ks[24], (N_ATTN_LAYERS, N_HEADS)),
        'attn_w_out': nrm(ks[25], (N_ATTN_LAYERS, D_MODEL, D_MODEL), inv_d * DN_BETA),
        'peer_w_q': nrm(ks[26], (DEPTH, D_MODEL, PEER_HEADS * PEER_KEY_DIM), inv_d),
        'peer_keys': nrm(ks[27], (DEPTH, 2, PEER_HEADS, PEER_NKEYS, PEER_HALF), PEER_HALF ** -0.5),
        'peer_u': nrm(ks[28], (DEPTH, PEER_EXPERTS, D_MODEL), inv_d),
        'peer_v': nrm(ks[29], (DEPTH, PEER_EXPERTS, D_MODEL), DN_BETA * PEER_HEADS ** -0.5),
    }


def reference(x_prompt, x_sample, state_ssm_re, state_ssm_im, cache_k, cache_v, c, c_ctx,
              w_mod, b_mod, ln_g, ln_b,
              ssm_w_in, ssm_a_re, ssm_a_im, ssm_log_dt, ssm_b_re, ssm_b_im, ssm_c_re, ssm_c_im,
              ssm_d, ssm_w_glu, ssm_w_out,
              attn_w_qkv, attn_sink, attn_w_out,
              peer_w_q, peer_keys, peer_u, peer_v):
    xc = x_prompt
    xs = x_sample
    n_ctx_batch = x_prompt.shape[0]
    st_re, st_im, st_k, st_v = [], [], [], []
    for i in range(DEPTH):
        j = i // N_MIXERS
        mc = _modulation(c_ctx[None, :], w_mod[i], b_mod[i])
        ms = _modulation(c, w_mod[i], b_mod[i])
        hc = _modulate(xc, mc[0], mc[1])
        hs = _modulate(xs, ms[0], ms[1])
        if i % N_MIXERS == 0:
            p = (ssm_w_in[j], ssm_a_re[j], ssm_a_im[j], ssm_log_dt[j], ssm_b_re[j], ssm_b_im[j],
                 ssm_c_re[j], ssm_c_im[j], ssm_d[j], ssm_w_glu[j], ssm_w_out[j])
            zeros = jnp.zeros((n_ctx_batch, 2, SSM_GROUPS, SSM_STATE), jnp.float32)
            oc, s_re, s_im = _s5_mixer(hc, zeros, zeros, *p)
            os_, _, _ = _s5_mixer(hs, state_ssm_re[:, j], state_ssm_im[:, j], *p)
            st_re.append(s_re)
            st_im.append(s_im)
        else:
            oc, k_c, v_c = _attn_context(hc, attn_w_qkv[j], attn_sink[j], attn_w_out[j])
            os_ = _attn_latent(hs, cache_k[:, j], cache_v[:, j], attn_w_qkv[j], attn_sink[j], attn_w_out[j])
            st_k.append(k_c)
            st_v.append(v_c)
        xc = _post_norm(xc, oc, mc[2], ln_g[i, 0], ln_b[i, 0])
        xs = _post_norm(xs, os_, ms[2], ln_g[i, 0], ln_b[i, 0])
        hc = _modulate(xc, mc[3], mc[4])
        hs = _modulate(xs, ms[3], ms[4])
        oc = _peer(hc, peer_w_q[i], peer_keys[i], peer_u[i], peer_v[i])
        os_ = _peer(hs, peer_w_q[i], peer_keys[i], peer_u[i], peer_v[i])
        xc = _post_norm(xc, oc, mc[5], ln_g[i, 1], ln_b[i, 1])
        xs = _post_norm(xs, os_, ms[5], ln_g[i, 1], ln_b[i, 1])
    new_state_ssm_re = jnp.stack(st_re, axis=1)
    new_state_ssm_im = jnp.stack(st_im, axis=1)
    new_cache_k = jnp.stack(st_k, axis=1)
    new_cache_v = jnp.stack(st_v, axis=1)
    return (xc, xs, new_state_ssm_re, new_state_ssm_im, new_cache_k, new_cache_v)
```

```python
from contextlib import ExitStack
import numpy as np
import concourse.bass as bass
import concourse.mybir as mybir
from concourse.ap import AP
from concourse.bass_utils import run_bass_kernel_spmd

F32 = mybir.dt.float32
BF16 = mybir.dt.bfloat16
ALU = mybir.AluOpType
AF = mybir.ActivationFunctionType
AX = mybir.AxisListType

D = 1024
DEPTH = 4
DN_ALPHA = (2 * DEPTH) ** 0.25
LN_EPS = 1e-5
NEXP = 16384
NET = 128


class Dep:
    __slots__ = ("w", "r", "excl")

    def __init__(self):
        self.w = None
        self.r = {}
        self.excl = False


class TT:
    __slots__ = ("t", "d", "pstep", "base", "full")

    def __init__(self, t, base=0, shape=None, dep=None):
        self.t = t
        self.d = dep or Dep()
        self.pstep = t[:].ap[0][0]
        self.base = base
        if shape is None:
            self.full = t[:]
        else:
            dims = []
            st = 1
            for n in reversed(shape):
                dims.append([st, n])
                st *= n
            self.full = AP(t, base, [[self.pstep, 128]] + dims[::-1])

    def __getitem__(self, k):
        return self.full[k]

    def ap(self, off, dims, p0=0, np_=128):
        return AP(self.t, self.base + p0 * self.pstep + off, [[self.pstep, np_]] + [list(x) for x in dims])


def fence(src, dst):
    for a in src:
        a = a.d if isinstance(a, TT) else a
        for b in dst:
            b = b.d if isinstance(b, TT) else b
            if a.w is not None:
                b.r[a.w[0]] = max(b.r.get(a.w[0], 0), a.w[1])
            for k, c in a.r.items():
                b.r[k] = max(b.r.get(k, 0), c)


EMBED_WAIT = True


class Prog:
    def __init__(self, nc, n_dma_sems=(44, 30, 16)):
        self.nc = nc
        self.es = ExitStack()
        self.engs = {"pe": nc.tensor, "act": nc.scalar, "dve": nc.vector, "pool": nc.gpsimd, "sp": nc.sync}
        self.sem = {}
        self.cnt = {}
        for k in ("pe", "act", "dve", "pool"):
            self.sem[k] = self.es.enter_context(nc.semaphore("s_" + k))
            self.cnt[k] = 0
        self.dpool = {}
        for q, n in zip(("sp", "pool", "act"), n_dma_sems):
            lst = []
            for i in range(n):
                key = "d_%s_%d" % (q, i)
                self.sem[key] = self.es.enter_context(nc.semaphore(key))
                self.cnt[key] = 0
                lst.append(key)
            self.dpool[q] = [lst, 0]
        self.known = {e: {} for e in self.engs}
        self.ninstr = 0
        self.uid = 0

    def sbuf(self, shape, dt, name=None):
        self.uid += 1
        return TT(self.es.enter_context(self.nc.sbuf_tensor("%s_%d" % (name or "sb", self.uid), list(shape), dt)))

    def psum(self, shape, dt, name=None):
        self.uid += 1
        return TT(self.es.enter_context(self.nc.psum_tensor(name or ("ps%d" % self.uid), list(shape), dt)))

    def _need(self, eng, evs, defer_last=False):
        kn = self.known[eng]
        e = self.engs[eng]
        todo = {}
        for (k, c) in evs:
            if c <= 0 or (k == "pe" and eng == "pe") or kn.get(k, 0) >= c:
                continue
            if todo.get(k, 0) < c:
                todo[k] = c
        items = list(todo.items())
        last = None
        if defer_last and items:
            last = items.pop()
        for k, c in items:
            e.wait_ge(self.sem[k], c)
            kn[k] = c
            self.ninstr += 1
        if last is not None:
            kn[last[0]] = last[1]
        return last

    @staticmethod
    def _collect(reads, writes):
        evs = []
        for d in reads:
            if d.w is not None:
                evs.append(d.w)
        for d in writes:
            if d.w is not None:
                evs.append(d.w)
            evs.extend(d.r.items())
        return evs

    @staticmethod
    def _commit(ev, reads, writes):
        for d in reads:
            d.r[ev[0]] = ev[1]
        for d in writes:
            d.w = ev
            d.r = {}

    def op(self, eng, fn, reads=(), writes=()):
        reads = [x.d if isinstance(x, TT) else x for x in reads]
        writes = [x.d if isinstance(x, TT) else x for x in writes]
        ex = [d for d in reads if d.excl]
        if ex:
            reads = [d for d in reads if not d.excl]
            writes = writes + ex
        last = self._need(eng, self._collect(reads, writes), defer_last=EMBED_WAIT)
        ins = fn(self.engs[eng])
        if last is not None:
            ins._wait_ge(self.sem[last[0]], last[1])
        self.cnt[eng] += 1
        ins.then_inc(self.sem[eng], 1)
        self.ninstr += 1
        self._commit((eng, self.cnt[eng]), reads, writes)
        return ins

    def dma(self, q, out, in_, reads=(), writes=(), **kw):
        reads = [x.d if isinstance(x, TT) else x for x in reads]
        writes = [x.d if isinstance(x, TT) else x for x in writes]
        lst, idx = self.dpool[q]
        key = lst[idx % len(lst)]
        self.dpool[q][1] = idx + 1
        evs = self._collect(reads, writes)
        evs.append((key, self.cnt[key]))
        self._need(q, evs)
        ins = self.engs[q].dma_start(out=out, in_=in_, **kw)
        self.cnt[key] += 16
        ins.then_inc(self.sem[key], 16)
        self.ninstr += 1
        self._commit((key, self.cnt[key]), reads, writes)
        return ins

    def drain_all(self):
        self._need("sp", list(self.cnt.items()))


class Ctx:
    pass


def dram_rows(t, r0, nr, c0=0, ncol=None):
    if ncol is None:
        return t[r0:r0 + nr, :]
    return t[r0:r0 + nr, c0:c0 + ncol]


def setup_common(p, cx):
    nc = p.nc
    cx.banks = [p.psum([128, 512], F32, name="bank%d" % i) for i in range(8)]
    for b in cx.banks:
        b.d.excl = True
    cx.ident = p.sbuf([128, 128], F32, name="ident")
    p.dma("sp", cx.ident[:], cx.dram["c_ident"], writes=[cx.ident])
    cx.e3 = p.sbuf([128, 128], BF16, name="e3")
    p.dma("pool", cx.e3[:], cx.dram["c_e3"], writes=[cx.e3])
    cx.eps = p.sbuf([128, 1], F32, name="eps")
    p.op("dve", lambda e: e.memset(cx.eps[:], LN_EPS), writes=[cx.eps])


def layer_norm_rows(p, cx, y, out, g_bc, b_bc, scr):
    st = scr["st"]; mv = scr["mv"]; rstd = scr["rstd"]
    for c in range(2):
        p.op("dve", lambda e, c=c: e.bn_stats(st[:, c * 6:(c + 1) * 6], y[:, c * 512:(c + 1) * 512]), reads=[y], writes=[st])
    p.op("dve", lambda e: e.bn_aggr(mv[:], st[:]), reads=[st], writes=[mv])
    p.op("act", lambda e: e.activation(rstd[:], mv[:, 1:2], AF.Sqrt, bias=cx.eps[:], scale=1.0), reads=[mv, cx.eps], writes=[rstd])
    p.op("dve", lambda e: e.reciprocal(rstd[:], rstd[:]), reads=[rstd], writes=[rstd])
    p.op("dve", lambda e: e.tensor_scalar(y[:], y[:], mv[:, 0:1], rstd[:, 0:1], ALU.subtract, ALU.mult),
         reads=[y, mv, rstd], writes=[y])
    p.op("pool", lambda e: e.tensor_tensor(y[:], y[:], g_bc[:], ALU.mult), reads=[y, g_bc], writes=[y])
    p.op("pool", lambda e: e.tensor_tensor(out[:], y[:], b_bc[:], ALU.add), reads=[y, b_bc], writes=[out])


def peer_prep_weights(p, cx, l):
    ut_src = cx.dram["peer_ut"][l]
    v_src = cx.dram["peer_v"][l]
    CH = 8
    for c in range(NET // CH):
        p.dma("pool", cx.utb[c * CH:(c + 1) * CH], ut_src[c * CH:(c + 1) * CH], writes=[cx.d_utb[c]])
        p.dma("pool", cx.vb[c * CH * 128:(c + 1) * CH * 128, :], v_src[c * CH * 128:(c + 1) * CH * 128, :],
              writes=[cx.d_vb[c]])


def peer_alloc(p, cx):
    a = Ctx()
    a.big = p.sbuf([128, 16384], F32, name="p_big")
    hb = a.big.t[:].bitcast(BF16).tensor
    a.Wb = TT(hb, 0, [256, 128])
    a.wq = TT(hb, 0, [8, 2048])
    a.s_sb = TT(a.big.t, 8192, [16, 128])
    a.cand = TT(a.big.t, 10240, [8, 256])
    a.r1 = TT(a.big.t, 12288, [2, 8, 128])
    a.d1regs = [a.wq, a.s_sb, a.cand, a.r1]
    a.keys = p.sbuf([128, 16, 128], BF16, name="keysT")
    a.xt = [p.sbuf([128, 1024], F32, name="p_xt%d" % i) for i in range(2)]
    a.hm = p.sbuf([128, 1024], F32, name="p_hm")
    a.hT = p.sbuf([128, 8, 256], BF16, name="p_hT")
    a.qT = p.sbuf([128, 16, 128], BF16, name="p_qT")
    a.V = p.sbuf([128, 16, 16], F32, name="p_V")
    a.tmp128 = p.sbuf([128, 128], F32, name="p_tmp128")
    a.tmpc = [p.sbuf([128, 256], F32, name="p_tmpc%d" % i) for i in range(2)]
    a.C = p.sbuf([128, 8, 24], F32, name="p_C")
    a.sm = p.sbuf([128, 8, 8], F32, name="p_small")
    a.ex = p.sbuf([128, 8, 16], F32, name="p_ex")
    a.v1pp = p.sbuf([128, 128], F32, name="p_v1pp")
    a.thp = p.sbuf([128, 128], F32, name="p_thp")
    a.SPt = p.sbuf([128, 2, 3, 8, 128], BF16, name="p_SPt")
    a.ThT = p.sbuf([128, 256], F32, name="p_ThT")
    a.V1T = p.sbuf([128, 256], F32, name="p_V1T")
    a.SPh = [[p.sbuf([128, 16, 128], BF16, name="p_SPh%d_%d" % (i, s)) for s in range(2)] for i in range(2)]
    for i in range(2):
        for s_ in range(2):
            p.op("pool", lambda e, i=i, s_=s_: e.memset(a.SPh[i][s_][:], 0.0), writes=[a.SPh[i][s_]])
    a.EX = [[p.sbuf([128, 4, 128], BF16, name="p_EX%d_%d" % (i, s)) for s in range(2)] for i in range(3)]
    a.MK = [[p.sbuf([128, 4, 128], BF16, name="p_MK%d_%d" % (i, s)) for s in range(2)] for i in range(3)]
    a.LR = [[p.sbuf([128, 4, 128], BF16, name="p_LR%d_%d" % (i, s)) for s in range(2)] for i in range(3)]
    a.ut = [p.sbuf([128, 1024], BF16, name="p_ut%d" % i) for i in range(3)]
    a.vt = [p.sbuf([128, 1024], BF16, name="p_vt%d" % i) for i in range(3)]
    a.G = [p.sbuf([128, 256], BF16, name="p_G%d" % i) for i in range(2)]
    a.A = [p.sbuf([128, 256], BF16, name="p_A%d" % i) for i in range(2)]
    a.y = p.sbuf([128, 1024], F32, name="p_y")
    a.t1 = a.hm
    a.xo = a.hm
    a.scr = {"st": p.sbuf([128, 12], F32, name="p_st"), "mv": p.sbuf([128, 2], F32, name="p_mv"),
             "rstd": p.sbuf([128, 1], F32, name="p_rstd")}
    return a


def peer_load_layer(p, cx, a, l):
    p.dma("pool", a.keys[:], cx.dram["peer_keysT"][l], writes=[a.keys])


def peer_block(p, cx, a, l, blk, mod, lng, lnb):
    B = cx.banks
    tok0 = blk * 256
    NT = cx.NT
    spd = cx.spd
    fence([a.Wb], a.d1regs)
    wqd = cx.dram["peer_w_q"][l]
    for k in range(8):
        p.dma("pool", a.wq[:, k, :], wqd[k * 128:(k + 1) * 128, :], writes=[a.wq])
    for tl in range(2):
        r0 = tok0 + tl * 128
        xt = a.xt[tl]
        p.dma("sp", xt[:], cx.X[r0:r0 + 128, :], reads=[cx.dX[r0 // 128]], writes=[xt])
        p.op("dve", lambda e: e.tensor_tensor(a.hm[:], xt[:], mod["sc1"][:], ALU.mult), reads=[xt, mod["sc1"]], writes=[a.hm])
        p.op("dve", lambda e: e.tensor_tensor(a.hm[:], a.hm[:], mod["sh"][:], ALU.add), reads=[a.hm, mod["sh"]], writes=[a.hm])
        for half in range(2):
            bk = B[half]
            for k4 in range(4):
                k = half * 4 + k4
                p.op("pe", lambda e, k=k, k4=k4, bk=bk: e.transpose(bk[:, k4 * 128:(k4 + 1) * 128], a.hm[:, k * 128:(k + 1) * 128], cx.ident[:]),
                     reads=[a.hm, cx.ident], writes=[bk])
            p.op("act", lambda e, half=half, bk=bk: e.copy(
                a.hT.ap(half * 4 * 256 + tl * 128, [[256, 4], [1, 128]]), bk.ap(0, [[128, 4], [1, 128]])),
                reads=[bk], writes=[a.hT])
        for m4 in range(4):
            bk = B[2 + (m4 % 2)]
            for mi in range(4):
                m = m4 * 4 + mi
                for k in range(8):
                    p.op("pe", lambda e, m=m, mi=mi, k=k, bk=bk: e.matmul(
                        bk[:, mi * 128:(mi + 1) * 128], a.wq[:, k, m * 128:(m + 1) * 128],
                        a.hT[:, k, tl * 128:(tl + 1) * 128], start=(k == 0), stop=(k == 7)),
                        reads=[a.wq, a.hT], writes=[bk])
            eng = "act" if m4 % 2 == 0 else "dve"
            if eng == "act":
                p.op("act", lambda e, m4=m4, bk=bk: e.copy(a.qT[:, m4 * 4:(m4 + 1) * 4, :], bk.ap(0, [[128, 4], [1, 128]])),
                     reads=[bk], writes=[a.qT])
            else:
                p.op("dve", lambda e, m4=m4, bk=bk: e.tensor_copy(a.qT[:, m4 * 4:(m4 + 1) * 4, :], bk.ap(0, [[128, 4], [1, 128]])),
                     reads=[bk], writes=[a.qT])
        for m4 in range(4):
            bk = B[4 + m4]
            for mi in range(4):
                m = m4 * 4 + mi
                p.op("pe", lambda e, m=m, mi=mi, bk=bk: e.matmul(bk[:, mi * 128:(mi + 1) * 128], a.qT[:, m, :], a.keys[:, m, :],
                                                                  start=True, stop=True), reads=[a.qT, a.keys], writes=[bk])
            p.op("act", lambda e, m4=m4, bk=bk: e.copy(a.s_sb[:, m4 * 4:(m4 + 1) * 4, :], bk.ap(0, [[128, 4], [1, 128]])),
                 reads=[bk], writes=[a.s_sb])
        for m in range(16):
            p.op("dve", lambda e, m=m: e.max(out=a.V[:, m, 0:8], in_=a.s_sb[:, m, :]), reads=[a.s_sb], writes=[a.V])
            p.op("dve", lambda e, m=m: e.match_replace(out=a.tmp128[:], in_to_replace=a.V[:, m, 0:8], in_values=a.s_sb[:, m, :],
                                                        imm_value=-1e30), reads=[a.s_sb, a.V], writes=[a.tmp128])
            p.op("dve", lambda e, m=m: e.max(out=a.V[:, m, 8:16], in_=a.tmp128[:]), reads=[a.tmp128], writes=[a.V])
        in0 = a.V.ap(0, [[32, 8], [1, 16], [0, 16]])
        in1 = a.V.ap(16, [[32, 8], [0, 16], [1, 16]])
        outc = a.cand.ap(0, [[256, 8], [16, 16], [1, 16]])
        p.op("dve", lambda e: e.tensor_tensor(outc, in0, in1, ALU.add), reads=[a.V], writes=[a.cand])
        for h in range(8):
            p.op("dve", lambda e, h=h: e.max(out=a.C[:, h, 0:8], in_=a.cand[:, h, :]), reads=[a.cand], writes=[a.C])
            p.op("dve", lambda e, h=h: e.match_replace(out=a.tmpc[0][:], in_to_replace=a.C[:, h, 0:8], in_values=a.cand[:, h, :],
                                                        imm_value=-1e30), reads=[a.cand, a.C], writes=[a.tmpc[0]])
            p.op("dve", lambda e, h=h: e.max(out=a.C[:, h, 8:16], in_=a.tmpc[0][:]), reads=[a.tmpc[0]], writes=[a.C])
            p.op("dve", lambda e, h=h: e.match_replace(out=a.tmpc[1][:], in_to_replace=a.C[:, h, 8:16], in_values=a.tmpc[0][:],
                                                        imm_value=-1e30), reads=[a.tmpc[0], a.C], writes=[a.tmpc[1]])
            p.op("dve", lambda e, h=h: e.max(out=a.C[:, h, 16:24], in_=a.tmpc[1][:]), reads=[a.tmpc[1]], writes=[a.C])
        sm = a.sm
        TAU, LNZ, OFF1, C2, ZZ = 0, 1, 2, 3, 4
        c15 = a.C.ap(15, [[24, 8]]); c16 = a.C.ap(16, [[24, 8]]); c0 = a.C.ap(0, [[24, 8]])
        m1v = a.V.ap(0, [[32, 8]]); m2v = a.V.ap(16, [[32, 8]])
        p.op("dve", lambda e: e.tensor_tensor(sm[:, TAU, :], c15, c16, ALU.add), reads=[a.C], writes=[sm])
        p.op("dve", lambda e: e.tensor_scalar(sm[:, TAU, :], sm[:, TAU, :], 0.5, None, ALU.mult), reads=[sm], writes=[sm])
        c016 = a.C.ap(0, [[24, 8], [1, 16]]); c0b = a.C.ap(0, [[24, 8], [0, 16]])
        p.op("dve", lambda e: e.tensor_tensor(a.ex[:], c016, c0b, ALU.subtract), reads=[a.C], writes=[a.ex])
        p.op("act", lambda e: e.activation(a.ex[:], a.ex[:], AF.Exp), reads=[a.ex], writes=[a.ex])
        p.op("dve", lambda e: e.reduce_sum(sm[:, ZZ, :], a.ex[:], axis=AX.X), reads=[a.ex], writes=[sm])
        p.op("act", lambda e: e.activation(sm[:, LNZ, :], sm[:, ZZ, :], AF.Ln), reads=[sm], writes=[sm])
        p.op("dve", lambda e: e.tensor_tensor(sm[:, OFF1, :], m1v, sm[:, LNZ, :], ALU.add), reads=[a.V, sm], writes=[sm])
        p.op("dve", lambda e: e.tensor_tensor(sm[:, C2, :], sm[:, TAU, :], m2v, ALU.subtract), reads=[a.V, sm], writes=[sm])
        s1v = a.s_sb.ap(0, [[256, 8], [1, 128]]); s2v = a.s_sb.ap(128, [[256, 8], [1, 128]])
        off1b = sm.ap(OFF1 * 8, [[1, 8], [0, 128]]); m2b = a.V.ap(16, [[32, 8], [0, 128]])
        sfix = a.cand
        sf4 = sfix.ap(0, [[1024, 2], [128, 8], [1, 128]])
        p.op("dve", lambda e: e.tensor_tensor(sfix.ap(0, [[128, 8], [1, 128]]), s1v, off1b, ALU.subtract), reads=[a.s_sb, sm], writes=[sfix])
        p.op("pool", lambda e: e.tensor_tensor(sfix.ap(1024, [[128, 8], [1, 128]]), s2v, m2b, ALU.subtract), reads=[a.s_sb, a.V], writes=[sfix])
        v1v = a.V.ap(0, [[32, 8], [1, 16]]); off1b16 = sm.ap(OFF1 * 8, [[1, 8], [0, 16]]); c2b16 = sm.ap(C2 * 8, [[1, 8], [0, 16]])
        v1pp3 = a.v1pp.ap(0, [[16, 8], [1, 16]]); thp3 = a.thp.ap(0, [[16, 8], [1, 16]])
        p.op("dve", lambda e: e.tensor_tensor(v1pp3, v1v, off1b16, ALU.subtract), reads=[a.V, sm], writes=[a.v1pp])
        p.op("dve", lambda e: e.tensor_tensor(thp3, c2b16, v1v, ALU.subtract), reads=[a.V, sm], writes=[a.thp])
        SP = a.SPt
        p.op("act", lambda e: e.copy(SP[:, :, 0, :, :], sf4), reads=[sfix], writes=[SP])
        p.op("dve", lambda e: e.tensor_tensor(a.r1[:], sf4, SP[:, :, 0, :, :], ALU.subtract), reads=[sfix, SP], writes=[a.r1])
        p.op("act", lambda e: e.copy(SP[:, :, 1, :, :], a.r1[:]), reads=[a.r1], writes=[SP])
        p.op("dve", lambda e: e.tensor_tensor(a.r1[:], a.r1[:], SP[:, :, 1, :, :], ALU.subtract), reads=[a.r1, SP], writes=[a.r1])
        p.op("act", lambda e: e.copy(SP[:, :, 2, :, :], a.r1[:]), reads=[a.r1], writes=[SP])
        for side in range(2):
            dst = spd[side, :, :, r0:r0 + 128, :].rearrange("a h t i -> t a h i")
            p.dma("sp", dst, SP[:, side, :, :, :], reads=[SP], writes=[cx.d_spd[r0 // 128]])
        bk = B[0]
        p.op("pe", lambda e: e.transpose(bk[:, 0:128], a.thp[:], cx.ident[:]), reads=[a.thp, cx.ident], writes=[bk])
        p.op("pe", lambda e: e.transpose(bk[:, 128:256], a.v1pp[:], cx.ident[:]), reads=[a.v1pp, cx.ident], writes=[bk])
        p.op("act", lambda e: e.copy(a.ThT[:, tl * 128:(tl + 1) * 128], bk[:, 0:128]), reads=[bk], writes=[a.ThT])
        p.op("act", lambda e: e.copy(a.V1T[:, tl * 128:(tl + 1) * 128], bk[:, 128:256]), reads=[bk], writes=[a.V1T])

    if getattr(cx, "stage", 9) < 2:
        return
    fence(a.d1regs, [a.Wb])
    groups = [(sub, g4) for sub in range(16) for g4 in range(4)]
    NG = len(groups)

    def d2_load(sub):
        ta = tok0 + sub * 16
        for side in range(2):
            src = spd[side, :, :, ta:ta + 16, :].rearrange("a h t i -> (a h) t i")
            p.dma("sp", a.SPh[sub % 2][side][0:24, :, :], src, reads=[cx.d_spd[ta // 128]], writes=[a.SPh[sub % 2][side]])

    def stA(gi):
        sub, g4 = groups[gi]
        st = gi % 3
        sph = a.SPh[sub % 2]
        for side in range(2):
            bk = B[st * 2 + side]
            p.op("pe", lambda e, side=side, bk=bk: e.matmul(bk[:], cx.e3[0:24, :], sph[side][0:24, g4 * 4:(g4 + 1) * 4, :], start=True, stop=True),
                 reads=[cx.e3, sph[side]], writes=[bk])
        if g4 == 3 and sub + 2 < 16:
            d2_load(sub + 2)

    def stB(gi):
        sub, g4 = groups[gi]
        st = gi % 3
        tb = sub * 16 + g4 * 4
        EX = a.EX[st]; MK = a.MK[st]
        ops = (ALU.is_equal, ALU.is_ge)
        srcs = (a.V1T, a.ThT)
        for side in range(2):
            bk = B[st * 2 + side]
            p.op("act", lambda e, side=side, bk=bk: e.activation(EX[side][:], bk.ap(0, [[128, 4], [1, 128]]), AF.Exp), reads=[bk], writes=[EX[side]])
            cb = srcs[side].ap(tb, [[1, 4], [0, 128]])
            p.op("dve", lambda e, side=side, bk=bk, cb=cb: e.tensor_tensor(MK[side][:], bk.ap(0, [[128, 4], [1, 128]]), cb, ops[side]),
                 reads=[bk, srcs[side]], writes=[MK[side]])

    def stC(gi):
        st = gi % 3
        for side in range(2):
            p.op("pool", lambda e, side=side: e.tensor_tensor(a.LR[st][side][:], a.MK[st][side][:], a.EX[st][side][:], ALU.mult),
                 reads=[a.MK[st][side], a.EX[st][side]], writes=[a.LR[st][side]])

    def stD(gi):
        sub, g4 = groups[gi]
        st = gi % 3
        tb = sub * 16 + g4 * 4
        psW = B[6 + gi % 2]
        LR = a.LR[st]
        for tk in range(4):
            p.op("pe", lambda e, tk=tk: e.matmul(psW[:, tk * 128:(tk + 1) * 128], LR[1][:, tk, :], LR[0][:, tk, :], start=True, stop=True),
                 reads=[LR[0], LR[1]], writes=[psW])
        p.op("act", lambda e: e.copy(a.Wb[:, tb:tb + 4, :], psW.ap(0, [[128, 4], [1, 128]])), reads=[psW], writes=[a.Wb])

    d2_load(0); d2_load(1)
    for i in range(NG + 2):
        if i < NG:
            stA(i)
        if 0 <= i - 1 < NG:
            stB(i - 1); stC(i - 1)
        if 0 <= i - 2 < NG:
            stD(i - 2)

    if getattr(cx, "stage", 9) < 3:
        return
    psO = [[B[0], B[1]], [B[2], B[3]]]

    def d3_load(et):
        p.dma("sp", a.ut[et % 3][:], cx.utb[et], reads=[cx.d_utb[et // 8]], writes=[a.ut[et % 3]])
        p.dma("sp", a.vt[et % 3][:], cx.vb[et * 128:(et + 1) * 128, :], reads=[cx.d_vb[et // 8]], writes=[a.vt[et % 3]])

    def d3_ux(et):
        ut = a.ut[et % 3]
        psU = B[4 + (et % 2)]
        for k in range(8):
            p.op("pe", lambda e, k=k: e.matmul(psU[:, 0:256], ut[:, k * 128:(k + 1) * 128], a.hT[:, k, :], start=(k == 0), stop=(k == 7)),
                 reads=[ut, a.hT], writes=[psU])
        G = a.G[et % 2]; A = a.A[et % 2]
        p.op("act", lambda e: e.activation(G[:], psU[:, 0:256], AF.Gelu_apprx_tanh), reads=[psU], writes=[G])
        wsl = a.Wb.ap(et, [[128, 256]])
        p.op("dve" if et % 2 == 0 else "pool", lambda e: e.tensor_tensor(A[:], G[:], wsl, ALU.mult), reads=[G, a.Wb], writes=[A])

    def d3_v(et):
        A = a.A[et % 2]; vt = a.vt[et % 3]
        for tl in range(2):
            for dh in range(2):
                p.op("pe", lambda e, tl=tl, dh=dh: e.matmul(psO[tl][dh][:], A[:, tl * 128:(tl + 1) * 128], vt[:, dh * 512:(dh + 1) * 512],
                                                            start=(et == 0), stop=(et == NET - 1)), reads=[A, vt], writes=[psO[tl][dh]])

    d3_load(0); d3_load(1); d3_ux(0)
    for et in range(NET):
        if et + 2 < NET:
            d3_load(et + 2)
        if et + 1 < NET:
            d3_ux(et + 1)
        d3_v(et)
    if getattr(cx, "stage", 9) < 4:
        return
    for tl in range(2):
        r0 = tok0 + tl * 128
        xt = a.xt[tl]
        for dh in range(2):
            p.op("dve", lambda e, dh=dh: e.tensor_tensor(a.t1[:, dh * 512:(dh + 1) * 512], psO[tl][dh][:], mod["g1"][:, dh * 512:(dh + 1) * 512], ALU.mult),
                 reads=[psO[tl][dh], mod["g1"]], writes=[a.t1])
        p.op("dve", lambda e: e.scalar_tensor_tensor(a.y[:], xt[:], DN_ALPHA, a.t1[:], ALU.mult, ALU.add), reads=[xt, a.t1], writes=[a.y])
        layer_norm_rows(p, cx, a.y, a.xo, lng, lnb, a.scr)
        p.dma("act", cx.X[r0:r0 + 128, :], a.xo[:], reads=[a.xo], writes=[cx.dX[r0 // 128]])


def modulate_transpose(p, cx, xt, hm, hT_dst_fn, mod, banks):
    p.op("dve", lambda e: e.tensor_tensor(hm[:], xt[:], mod["sc1"][:], ALU.mult), reads=[xt, mod["sc1"]], writes=[hm])
    p.op("pool", lambda e: e.tensor_tensor(hm[:], hm[:], mod["sh"][:], ALU.add), reads=[hm, mod["sh"]], writes=[hm])
    for half in range(2):
        bk = banks[half]
        for k4 in range(4):
            k = half * 4 + k4
            p.op("pe", lambda e, k=k, k4=k4, bk=bk: e.transpose(bk[:, k4 * 128:(k4 + 1) * 128], hm[:, k * 128:(k + 1) * 128], cx.ident[:]),
                 reads=[hm, cx.ident], writes=[bk])
        dst, dep = hT_dst_fn(half)
        p.op("act", lambda e, bk=bk, dst=dst: e.copy(dst, bk.ap(0, [[128, 4], [1, 128]])), reads=[bk], writes=[dep])


def post_norm_store(p, cx, xt, psO, mod, lng, lnb, t1, y, scr, r0):
    for dh in range(2):
        p.op("dve", lambda e, dh=dh: e.tensor_tensor(t1[:, dh * 512:(dh + 1) * 512], psO[dh][:], mod["g1"][:, dh * 512:(dh + 1) * 512], ALU.mult),
             reads=[psO[dh], mod["g1"]], writes=[t1])
    p.op("dve", lambda e: e.scalar_tensor_tensor(y[:], xt[:], DN_ALPHA, t1[:], ALU.mult, ALU.add), reads=[xt, t1], writes=[y])
    layer_norm_rows(p, cx, y, t1, lng, lnb, scr)
    p.dma("act", cx.X[r0:r0 + 128, :], t1[:], reads=[t1], writes=[cx.dX[r0 // 128]])


def attn_alloc(p, cx):
    a = Ctx()
    a.wqkv = p.sbuf([128, 8, 1536], BF16, name="a_wqkv")
    a.wqkp = p.sbuf([128, 8, 1280], BF16, name="a_wqkp")
    a.wout = p.sbuf([128, 8, 1024], BF16, name="a_wout")
    a.esink = p.sbuf([128, 16], F32, name="a_esink")
    a.kcf = p.sbuf([128, 256], F32, name="a_kcf")
    a.KTc = p.sbuf([64, 4, 4, 128], BF16, name="a_KTc")
    a.Vc = p.sbuf([128, 4, 4, 65], BF16, name="a_Vc")
    a.xt = [p.sbuf([128, 1024], F32, name="a_xt%d" % i) for i in range(3)]
    a.hm = p.sbuf([128, 1024], F32, name="a_hm")
    a.hT = p.sbuf([128, 8, 128], BF16, name="a_hT")
    a.QT = [p.sbuf([64, 16, 128], BF16, name="a_QT%d" % i) for i in range(3)]
    a.KT = [p.sbuf([64, 4, 128], BF16, name="a_KT%d" % i) for i in range(3)]
    a.V = [p.sbuf([128, 4, 65], BF16, name="a_V%d" % i) for i in range(3)]
    a.cs = [p.sbuf([64, 2, 128], F32, name="a_cs%d" % i) for i in range(2)]
    a.r1 = p.sbuf([64, 4, 128], F32, name="a_r1")
    a.r2 = p.sbuf([64, 4, 128], F32, name="a_r2")
    a.kvtok = p.sbuf([128, 2, 256], F32, name="a_kvtok")
    a.PT = [p.sbuf([128, 4, 128], BF16, name="a_PT%d" % i) for i in range(7)]
    a.masks = p.sbuf([128, 2, 128], BF16, name="a_masks")
    a.den = p.sbuf([128, 4], F32, name="a_den")
    a.O = p.sbuf([128, 1024], F32, name="a_O")
    a.OT = p.sbuf([128, 8, 128], BF16, name="a_OT")
    a.y = p.sbuf([128, 1024], F32, name="a_y")
    a.scr = {"st": p.sbuf([128, 12], F32, name="a_st"), "mv": p.sbuf([128, 2], F32, name="a_mv"),
             "rstd": p.sbuf([128, 1], F32, name="a_rstd")}
    p.dma("pool", a.masks[:], cx.dram["c_masks"], writes=[a.masks])
    for i in range(3):
        p.op("pool", lambda e, i=i: e.memset(a.V[i][:], 1.0), writes=[a.V[i]])
    p.op("pool", lambda e: e.memset(a.Vc[:], 1.0), writes=[a.Vc])
    return a


def attn_load_layer(p, cx, a, j):
    B = cx.banks
    w = cx.dram["attn_w_qkv"][j]; wp = cx.dram["attn_w_qkp"][j]; wo = cx.dram["attn_w_out"][j]
    for k in range(8):
        p.dma("pool", a.wqkv[:, k, :], w[k * 128:(k + 1) * 128, :], writes=[a.wqkv])
        p.dma("pool", a.wqkp[:, k, :], wp[k * 128:(k + 1) * 128, :], writes=[a.wqkp])
        p.dma("pool", a.wout[:, k, :], wo[k * 128:(k + 1) * 128, :], writes=[a.wout])
    p.dma("sp", a.esink[:], cx.dram["attn_sink"][j].partition_broadcast(128), writes=[a.esink])
    p.op("act", lambda e: e.activation(a.esink[:], a.esink[:], AF.Exp), reads=[a.esink], writes=[a.esink])
    if cx.n_smp_tiles == 0:
        return
    ck = cx.dram["cache_k"][j]
    cv = cx.dram["cache_v"][j]
    for c in range(4):
        p.dma("sp", a.kcf[:], ck[c * 128:(c + 1) * 128, :], writes=[a.kcf])
        bk = B[0]
        for g in range(4):
            p.op("pe", lambda e, g=g: e.transpose(bk[0:64, g * 128:(g + 1) * 128], a.kcf[:, g * 64:(g + 1) * 64], cx.ident[:]),
                 reads=[a.kcf, cx.ident], writes=[bk])
        p.op("act", lambda e, c=c: e.copy(a.KTc[:, :, c, :], bk.ap(0, [[128, 4], [1, 128]], np_=64)), reads=[bk], writes=[a.KTc])
        p.dma("pool", a.Vc[:, c, :, 0:64], cv[c * 128:(c + 1) * 128, :].rearrange("t (g d) -> t g d", g=4), writes=[a.Vc])


def attn_project(p, cx, a, r0, slot, rope_pos, mod, want_kv_out=None):
    B = cx.banks
    xt = a.xt[slot]
    p.dma("sp", xt[:], cx.X[r0:r0 + 128, :], reads=[cx.dX[r0 // 128]], writes=[xt])
    modulate_transpose(p, cx, xt, a.hm, lambda half: (a.hT[:, half * 4:(half + 1) * 4, :], a.hT), mod, [B[0], B[1]])
    QT = a.QT[slot]; KT = a.KT[slot]; V = a.V[slot]
    if rope_pos is not None:
        cs = a.cs[rope_pos % 2]
        p.dma("sp", cs[:], cx.dram["c_rope"][:, :, rope_pos * 128:(rope_pos + 1) * 128], writes=[cs])
    for hg in range(5):
        colbase = hg * 256 if hg < 4 else 1024
        b0 = B[0]; b1 = B[1]
        for r in range(4):
            for k in range(8):
                p.op("pe", lambda e, r=r, k=k: e.matmul(b0[0:64, r * 128:(r + 1) * 128], a.wqkv[:, k, colbase + r * 64: colbase + (r + 1) * 64],
                                                        a.hT[:, k, :], start=(k == 0), stop=(k == 7)), reads=[a.wqkv, a.hT], writes=[b0])
        dst = QT[:, hg * 4:(hg + 1) * 4, :] if hg < 4 else KT[:]
        ddep = QT if hg < 4 else KT
        if rope_pos is None:
            p.op("act", lambda e, dst=dst: e.copy(dst, b0.ap(0, [[128, 4], [1, 128]], np_=64)), reads=[b0], writes=[ddep])
        else:
            for r in range(4):
                for k in range(8):
                    p.op("pe", lambda e, r=r, k=k: e.matmul(b1[0:64, r * 128:(r + 1) * 128], a.wqkp[:, k, colbase + r * 64: colbase + (r + 1) * 64],
                                                            a.hT[:, k, :], start=(k == 0), stop=(k == 7)), reads=[a.wqkp, a.hT], writes=[b1])
            cosb = cs.ap(0, [[0, 4], [1, 128]], np_=64); sinb = cs.ap(128, [[0, 4], [1, 128]], np_=64)
            p.op("dve", lambda e: e.tensor_tensor(a.r1[:], b0.ap(0, [[128, 4], [1, 128]], np_=64), cosb, ALU.mult), reads=[b0, cs], writes=[a.r1])
            p.op("dve", lambda e: e.tensor_tensor(a.r2[:], b1.ap(0, [[128, 4], [1, 128]], np_=64), sinb, ALU.mult), reads=[b1, cs], writes=[a.r2])
            p.op("pool", lambda e, dst=dst: e.tensor_tensor(dst, a.r1[:], a.r2[:], ALU.add), reads=[a.r1, a.r2], writes=[ddep])
    bv = B[2]
    for k in range(8):
        p.op("pe", lambda e, k=k: e.matmul(bv[:, 0:256], a.hT[:, k, :], a.wqkv[:, k, 1280:1536], start=(k == 0), stop=(k == 7)),
             reads=[a.hT, a.wqkv], writes=[bv])
    if want_kv_out is not None:
        for k in range(8):
            p.op("pe", lambda e, k=k: e.matmul(bv[:, 256:512], a.hT[:, k, :], a.wqkv[:, k, 1024:1280], start=(k == 0), stop=(k == 7)),
                 reads=[a.hT, a.wqkv], writes=[bv])
    p.op("act", lambda e: e.copy(V[:, :, 0:64], bv.ap(0, [[64, 4], [1, 64]])), reads=[bv], writes=[V])
    if want_kv_out is not None:
        ko, vo = want_kv_out
        p.op("dve", lambda e: e.tensor_copy(a.kvtok[:, 0, :], bv[:, 256:512]), reads=[bv], writes=[a.kvtok])
        p.op("dve", lambda e: e.tensor_copy(a.kvtok[:, 1, :], bv[:, 0:256]), reads=[bv], writes=[a.kvtok])
        p.dma("act", ko, a.kvtok[:, 0, :], reads=[a.kvtok], writes=[cx.d_out])
        p.dma("act", vo, a.kvtok[:, 1, :], reads=[a.kvtok], writes=[cx.d_out])


def attn_attend(p, cx, a, r0, slot, chunks, mod, lng, lnb):
    B = cx.banks
    QT = a.QT[slot]
    nch = len(chunks)
    for g in range(4):
        for ci, (kfn, vfn, deps, mk) in enumerate(chunks):
            bs = B[2 + (ci % 2)]
            p.op("pe", lambda e, kfn=kfn, bs=bs: e.matmul(bs[:], kfn(g), QT[:, g * 4:(g + 1) * 4, :], start=True, stop=True),
                 reads=[QT] + deps, writes=[bs])
            PT = a.PT[ci]
            p.op("act", lambda e, bs=bs, PT=PT: e.activation(PT[:], bs.ap(0, [[128, 4], [1, 128]]), AF.Exp, scale=0.125), reads=[bs], writes=[PT])
            if mk is not None:
                mb = a.masks.ap(mk * 128, [[0, 4], [1, 128]])
                p.op("pool", lambda e, PT=PT, mb=mb: e.tensor_tensor(PT[:], PT[:], mb, ALU.mult), reads=[PT, a.masks], writes=[PT])
        bo = B[4 + g]
        for r in range(4):
            for ci, (kfn, vfn, deps, mk) in enumerate(chunks):
                p.op("pe", lambda e, r=r, ci=ci, vfn=vfn: e.matmul(bo[:, r * 65:(r + 1) * 65], a.PT[ci][:, r, :], vfn(g),
                                                                   start=(ci == 0), stop=(ci == nch - 1)), reads=[a.PT[ci]] + deps, writes=[bo])
        p.op("dve", lambda e: e.tensor_tensor(a.den[:], bo.ap(64, [[65, 4]]), a.esink[:, g * 4:(g + 1) * 4], ALU.add), reads=[bo, a.esink], writes=[a.den])
        p.op("dve", lambda e: e.reciprocal(a.den[:], a.den[:]), reads=[a.den], writes=[a.den])
        p.op("dve", lambda e: e.tensor_tensor(a.O.ap(g * 256, [[64, 4], [1, 64]]), bo.ap(0, [[65, 4], [1, 64]]), a.den.ap(0, [[1, 4], [0, 64]]), ALU.mult),
             reads=[bo, a.den], writes=[a.O])
    for half in range(2):
        bk = B[half]
        for k4 in range(4):
            k = half * 4 + k4
            p.op("pe", lambda e, k=k, k4=k4, bk=bk: e.transpose(bk[:, k4 * 128:(k4 + 1) * 128], a.O[:, k * 128:(k + 1) * 128], cx.ident[:]),
                 reads=[a.O, cx.ident], writes=[bk])
        p.op("act", lambda e, half=half, bk=bk: e.copy(a.OT[:, half * 4:(half + 1) * 4, :], bk.ap(0, [[128, 4], [1, 128]])), reads=[bk], writes=[a.OT])
    psO = [B[2], B[3]]
    for dh in range(2):
        for k in range(8):
            p.op("pe", lambda e, dh=dh, k=k: e.matmul(psO[dh][:], a.OT[:, k, :], a.wout[:, k, dh * 512:(dh + 1) * 512], start=(k == 0), stop=(k == 7)),
                 reads=[a.OT, a.wout], writes=[psO[dh]])
    post_norm_store(p, cx, a.xt[slot], psO, mod, lng, lnb, a.hm, a.y, a.scr, r0)


def attn_layer(p, cx, a, j, mods, lng, lnb, kv_out):
    nS = cx.n_smp_tiles
    def loc(slot):
        return (lambda g: a.KT[slot][:, g, :]), (lambda g: a.V[slot][:, g, :]), [a.KT[slot], a.V[slot]]
    if nS:
        attn_project(p, cx, a, 0, 0, 0, mods["smp"])
    for i in range(nS):
        if i + 1 < nS:
            attn_project(p, cx, a, (i + 1) * 128, (i + 1) % 3, i + 1, mods["smp"])
        chunks = []
        for c in range(4):
            chunks.append(((lambda g, c=c: a.KTc[:, g, c, :]), (lambda g, c=c: a.Vc[:, c, g, :]), [a.KTc, a.Vc], None))
        if i > 0:
            k_, v_, d_ = loc((i - 1) % 3); chunks.append((k_, v_, d_, 0))
        k_, v_, d_ = loc(i % 3); chunks.append((k_, v_, d_, None))
        if i + 1 < nS:
            k_, v_, d_ = loc((i + 1) % 3); chunks.append((k_, v_, d_, 1))
        attn_attend(p, cx, a, i * 128, i % 3, chunks, mods["smp"], lng, lnb)
    for s in range(cx.n_prm_seq):
        base = nS * 128 + s * 256
        for t in range(2):
            attn_project(p, cx, a, base + t * 128, t, None, mods["ctx"], want_kv_out=kv_out(s, t))
        for t in range(2):
            chunks = []
            for t2 in range(2):
                k_, v_, d_ = loc(t2); chunks.append((k_, v_, d_, None))
            attn_attend(p, cx, a, base + t * 128, t, chunks, mods["ctx"], lng, lnb)


def host_consts():
    C = {}
    C["c_ident"] = np.eye(128, dtype=np.float32)
    e3 = np.zeros((128, 128), np.float32)
    for q in range(4):
        for part in range(3):
            for hh in range(8):
                e3[32 * q + part * 8 + hh, 16 * hh:16 * hh + 16] = 1
    C["c_e3"] = e3
    kk = np.arange(128)[:, None]; qq = np.arange(128)[None, :]
    C["c_masks"] = np.stack([(kk >= qq), (kk <= qq)], axis=1).astype(np.float32)
    pos = np.arange(4096)
    row = (pos // 64).astype(np.float32); col = (pos % 64).astype(np.float32)
    inv = (10000.0 ** (-np.arange(16, dtype=np.float32) / 16)).astype(np.float32)
    rope = np.zeros((64, 2, 4096), np.float32)
    for d in range(64):
        axis = row if d < 32 else col
        ang = (axis * inv[d % 16]).astype(np.float32)
        first = (d % 32) < 16
        rope[d, 0] = np.cos(ang)
        rope[d, 1] = -np.sin(ang) if first else np.sin(ang)
    C["c_rope"] = rope
    return C


def perm_qk(wqkv):
    idx = np.arange(1280)
    d = idx % 64
    partner = np.where((d % 32) < 16, idx + 16, idx - 16)
    return np.ascontiguousarray(wqkv[..., partner])


class Scope:
    def __init__(self, p):
        self.p = p

    def __enter__(self):
        self.saved = self.p.es
        self.p.es = ExitStack()
        return self

    def __exit__(self, *a):
        p = self.p
        allc = list(p.cnt.items())
        for e in ("pe", "act", "dve", "pool", "sp"):
            p._need(e, allc)
        p.es.close()
        p.es = self.saved
        return False


TWO_PI = 2.0 * np.pi
JB = 32


def _tt(p, eng, out, in0, in1, op, reads, writes):
    p.op(eng, lambda e: e.tensor_tensor(out, in0, in1, op), reads=reads, writes=writes)


def ssm_precompute(p, cx, R, jl, d):
    B = cx.banks
    G = 64
    with Scope(p):
        pg = p.sbuf([64, 3, 64], F32); Bt = p.sbuf([64, 2, 1024], F32); Ct = p.sbuf([64, 2, 1024], F32)
        PW = p.sbuf([64, 17, 2, 64], F32); sm = p.sbuf([64, 8, 64], F32); Bb = p.sbuf([64, 2, 1024], F32)
        WT = p.sbuf([64, 8, 2, 128], F32); XT = p.sbuf([64, 8, 2, 128], F32); Mt = p.sbuf([64, 8, 2, 128], F32)
        XTb = p.sbuf([64, 8, 2, 128], BF16); z = [p.sbuf([64, 8, 16], F32) for _ in range(4)]
        tmask = p.sbuf([128, 128], F32)
        qi = p.sbuf([64, 64], mybir.dt.int32)
        p.dma("sp", pg[:], cx.dram["ssm_pg"][jl, d].rearrange("k p g -> p k g"), writes=[pg])
        p.dma("sp", Bt[:], cx.dram["ssm_b"][jl, d].rearrange("k p x -> p k x"), writes=[Bt])
        p.dma("sp", Ct[:], cx.dram["ssm_c"][jl, d].rearrange("k p x -> p k x"), writes=[Ct])
        p.dma("sp", tmask[:], cx.dram["c_tmask"][d], writes=[tmask])
        DT, LR, LI, T0, T1, T2, CR, CI = range(8)
        p.op("act", lambda e: e.activation(sm[:, DT, :], pg[:, 2, :], AF.Exp), reads=[pg], writes=[sm])
        _tt(p, "dve", sm[:, LR, :], pg[:, 0, :], sm[:, DT, :], ALU.mult, [pg, sm], [sm])
        _tt(p, "dve", sm[:, LI, :], pg[:, 1, :], sm[:, DT, :], ALU.mult, [pg, sm], [sm])
        for k in range(-8, 9):
            kk = k + 8
            p.op("act", lambda e, k=k: e.activation(sm[:, T0, :], sm[:, LR, :], AF.Exp, scale=float(k)), reads=[sm], writes=[sm])
            for ri, ph in ((1, 0.0), (0, np.pi / 2)):
                p.op("dve", lambda e, k=k, ph=ph: e.tensor_scalar(sm[:, T1, :], sm[:, LI, :], float(k / TWO_PI), float(ph / TWO_PI + 64.0), ALU.mult, ALU.add),
                     reads=[sm], writes=[sm])
                p.op("dve", lambda e: e.tensor_copy(qi[:], sm[:, T1, :]), reads=[sm], writes=[qi])
                p.op("dve", lambda e: e.tensor_copy(sm[:, T2, :], qi[:]), reads=[qi], writes=[sm])
                _tt(p, "dve", sm[:, T1, :], sm[:, T1, :], sm[:, T2, :], ALU.subtract, [sm], [sm])
                p.op("dve", lambda e: e.tensor_single_scalar(sm[:, T2, :], sm[:, T1, :], 0.5, ALU.is_gt), reads=[sm], writes=[sm])
                _tt(p, "dve", sm[:, T1, :], sm[:, T1, :], sm[:, T2, :], ALU.subtract, [sm], [sm])
                p.op("act", lambda e: e.activation(sm[:, T2, :], sm[:, T1, :], AF.Sin, scale=float(TWO_PI)), reads=[sm], writes=[sm])
                _tt(p, "dve", PW[:, kk, ri, :], sm[:, T0, :], sm[:, T2, :], ALU.mult, [sm], [PW])
        p.op("act", lambda e: e.copy(R.A2[:, 0, :], PW[:, 16, 0, :]), reads=[PW], writes=[R.A2])
        p.op("act", lambda e: e.copy(R.A2[:, 1, :], PW[:, 16, 0, :]), reads=[PW], writes=[R.A2])
        p.op("act", lambda e: e.mul(R.AI2[:, 0, :], PW[:, 16, 1, :], -1.0), reads=[PW], writes=[R.AI2])
        p.op("act", lambda e: e.copy(R.AI2[:, 1, :], PW[:, 16, 1, :]), reads=[PW], writes=[R.AI2])
        are = pg[:, 0, :]; aim = pg[:, 1, :]
        p.op("dve", lambda e: e.tensor_scalar(sm[:, T0, :], PW[:, 9, 0, :], -1.0, None, ALU.add), reads=[PW], writes=[sm])
        _tt(p, "dve", sm[:, T1, :], are, are, ALU.mult, [pg], [sm])
        _tt(p, "dve", sm[:, T2, :], aim, aim, ALU.mult, [pg], [sm])
        _tt(p, "dve", sm[:, T1, :], sm[:, T1, :], sm[:, T2, :], ALU.add, [sm], [sm])
        p.op("dve", lambda e: e.reciprocal(sm[:, T1, :], sm[:, T1, :]), reads=[sm], writes=[sm])
        _tt(p, "dve", sm[:, CR, :], sm[:, T0, :], are, ALU.mult, [sm, pg], [sm])
        _tt(p, "dve", sm[:, T2, :], PW[:, 9, 1, :], aim, ALU.mult, [PW, pg], [sm])
        _tt(p, "dve", sm[:, CR, :], sm[:, CR, :], sm[:, T2, :], ALU.add, [sm], [sm])
        _tt(p, "dve", sm[:, CR, :], sm[:, CR, :], sm[:, T1, :], ALU.mult, [sm], [sm])
        _tt(p, "dve", sm[:, CI, :], PW[:, 9, 1, :], are, ALU.mult, [PW, pg], [sm])
        _tt(p, "dve", sm[:, T2, :], sm[:, T0, :], aim, ALU.mult, [sm, pg], [sm])
        _tt(p, "dve", sm[:, CI, :], sm[:, CI, :], sm[:, T2, :], ALU.subtract, [sm], [sm])
        _tt(p, "dve", sm[:, CI, :], sm[:, CI, :], sm[:, T1, :], ALU.mult, [sm], [sm])
        crb = sm.ap(CR * 64, [[1, 64], [0, 16]], np_=64); cib = sm.ap(CI * 64, [[1, 64], [0, 16]], np_=64)
        Br = Bt.ap(0, [[16, 64], [1, 16]], np_=64); Bi = Bt.ap(1024, [[16, 64], [1, 16]], np_=64)
        Bbr = Bb.ap(0, [[16, 64], [1, 16]], np_=64); Bbi = Bb.ap(1024, [[16, 64], [1, 16]], np_=64)
        big0 = Ct
        tA = XT.ap(0, [[16, 64], [1, 16]], np_=64); tB = XT.ap(1024, [[16, 64], [1, 16]], np_=64)
        _tt(p, "dve", tA, Br, crb, ALU.mult, [Bt, sm], [XT])
        _tt(p, "dve", tB, Bi, cib, ALU.mult, [Bt, sm], [XT])
        _tt(p, "dve", Bbr, tA, tB, ALU.subtract, [XT], [Bb])
        _tt(p, "dve", tA, Br, cib, ALU.mult, [Bt, sm], [XT])
        _tt(p, "dve", tB, Bi, crb, ALU.mult, [Bt, sm], [XT])
        _tt(p, "dve", Bbi, tA, tB, ALU.add, [XT], [Bb])
        for gb in range(8):
            g0 = gb * 8
            def pw(k, ri):
                return PW.ap(((k + 8) * 2 + ri) * 64 + g0, [[1, 8], [0, 16]], np_=64)
            def cplx(dst, ri_stride, slot, Xr, Xi, k, deps_x, neg_im=False):
                dr = dst.ap(slot * 16, [[256, 8], [1, 16]], np_=64); di = dst.ap(128 + slot * 16, [[256, 8], [1, 16]], np_=64)
                _tt(p, "dve", z[0][:], Xr, pw(k, 0), ALU.mult, deps_x + [PW], [z[0]])
                _tt(p, "pool", z[1][:], Xi, pw(k, 1), ALU.mult, deps_x + [PW], [z[1]])
                _tt(p, "dve", dr, z[0][:], z[1][:], ALU.subtract, [z[0], z[1]], [dst])
                _tt(p, "pool", z[2][:], Xr, pw(k, 1), ALU.mult, deps_x + [PW], [z[2]])
                _tt(p, "dve", z[3][:], Xi, pw(k, 0), ALU.mult, deps_x + [PW], [z[3]])
                if neg_im:
                    p.op("dve", lambda e: e.scalar_tensor_tensor(di, z[2][:], -1.0, z[3][:], ALU.mult, ALU.subtract), reads=[z[2], z[3]], writes=[dst])
                else:
                    _tt(p, "dve", di, z[2][:], z[3][:], ALU.add, [z[2], z[3]], [dst])
            bbr = Bb.ap(g0 * 16, [[16, 8], [1, 16]], np_=64); bbi = Bb.ap(1024 + g0 * 16, [[16, 8], [1, 16]], np_=64)
            ccr = Ct.ap(g0 * 16, [[16, 8], [1, 16]], np_=64); cci = Ct.ap(1024 + g0 * 16, [[16, 8], [1, 16]], np_=64)
            for s in range(8):
                kw = (7 - s) if d == 0 else s
                kx = (-s - 1) if d == 0 else (s - 8)
                km = (s + 1) if d == 0 else (8 - s)
                cplx(WT, 0, s, bbr, bbi, kw, [Bb])
                cplx(XT, 0, s, bbr, bbi, kx, [Bb])
                cplx(Mt, 0, s, ccr, cci, km, [Ct], neg_im=True)
            p.op("act", lambda e: e.copy(XTb[:], XT[:]), reads=[XT], writes=[XTb])
            p.op("act", lambda e, g0=g0: e.copy(R.M[:, g0:g0 + 8, :, :], Mt[:]), reads=[Mt], writes=[R.M])
            for gl in range(8):
                g = g0 + gl
                bk = B[gl % 2]
                for ri in range(2):
                    p.op("pe", lambda e, gl=gl, ri=ri, bk=bk: e.transpose(bk[:, ri * 64:(ri + 1) * 64], WT[:, gl, ri, :], cx.ident[0:64, 0:64]),
                         reads=[WT, cx.ident], writes=[bk])
                p.op("act", lambda e, g=g, bk=bk: e.copy(R.Win[:, g, :], bk[:, 0:128]), reads=[bk], writes=[R.Win])
                bt = B[2 + gl % 2]
                for ri in range(2):
                    p.op("pe", lambda e, gl=gl, g=g, ri=ri, bt=bt: e.matmul(bt[:, 0:128], XTb[:, gl, ri, :], R.M[:, g, ri, :], start=(ri == 0), stop=(ri == 1)),
                         reads=[XTb, R.M], writes=[bt])
                p.op("dve", lambda e, g=g, bt=bt: e.tensor_tensor(R.Toep[:, g, :], bt[:, 0:128], tmask[:], ALU.mult), reads=[bt, tmask], writes=[R.Toep])


def ssm_sweep(p, cx, R, jl, d, seqs):
    B = cx.banks
    with Scope(p):
        u8 = p.sbuf([128, JB, 64], BF16); S = p.sbuf([64, 2, 64, JB], F32); H = p.sbuf([64, 2, 64], F32)
        t1 = p.sbuf([64, 2, 64], F32); t2 = p.sbuf([64, 2, 64], F32); Hh = p.sbuf([64, 2, 64, JB + 1], BF16)
        Y8 = p.sbuf([128, JB, 64], F32); Yf = p.sbuf([128, JB, 64], F32); d8 = p.sbuf([128, 64], F32)
        hs = p.sbuf([64, 2, 64], F32)
        p.dma("sp", d8[:], cx.dram["ssm_d8"][jl], writes=[d8])
        udv = cx.Ud.rearrange("(j s) (c g) -> s c j g", s=8, c=16)
        ydv = cx.Yd.rearrange("(j s) (c g) -> s c j g", s=8, c=16)
        for sq in seqs:
            nb = sq["n"] // JB
            if sq["init"] is not None:
                p.dma("sp", H[:], sq["init"], writes=[H])
            else:
                p.op("dve", lambda e: e.memset(H[:], 0.0), writes=[H])
            for bi in (range(nb) if d == 0 else range(nb - 1, -1, -1)):
                j0 = sq["j0"] + bi * JB
                tl = (j0 * 8) // 128
                ddeps = [cx.dUd[tl], cx.dUd[tl + 1]]
                for s in range(8):
                    p.dma("pool", u8[16 * s:16 * s + 16, :, :], udv[s, :, j0:j0 + JB, :], reads=ddeps, writes=[u8])
                for gq in range(8):
                    bk = B[gq % 2]
                    for gl in range(8):
                        g = gq * 8 + gl
                        for ri in range(2):
                            p.op("pe", lambda e, g=g, gl=gl, ri=ri, bk=bk: e.matmul(bk[0:64, (gl * 2 + ri) * JB:(gl * 2 + ri + 1) * JB], R.Win[:, g, ri * 64:(ri + 1) * 64],
                                                                                   u8.ap(g, [[64, JB]]), start=True, stop=True), reads=[R.Win, u8], writes=[bk])
                    p.op("act", lambda e, gq=gq, bk=bk: e.copy(S.ap(gq * 8 * JB, [[JB, 8], [64 * JB, 2], [1, JB]], np_=64),
                                                               bk.ap(0, [[2 * JB, 8], [JB, 2], [1, JB]], np_=64)), reads=[bk], writes=[S])
                hidx0 = 0 if d == 0 else JB
                p.op("act", lambda e: e.copy(Hh.ap(hidx0, [[JB + 1, 128]], np_=64), H.ap(0, [[1, 128]], np_=64)), reads=[H], writes=[Hh])
                for jj in (range(JB) if d == 0 else range(JB - 1, -1, -1)):
                    _tt(p, "dve", t1[:], H[:], R.A2[:], ALU.mult, [H, R.A2], [t1])
                    _tt(p, "pool", t2[:, 0, :], H[:, 1, :], R.AI2[:, 0, :], ALU.mult, [H, R.AI2], [t2])
                    _tt(p, "pool", t2[:, 1, :], H[:, 0, :], R.AI2[:, 1, :], ALU.mult, [H, R.AI2], [t2])
                    _tt(p, "dve", t1[:], t1[:], t2[:], ALU.add, [t1, t2], [t1])
                    _tt(p, "dve", H[:], t1[:], S.ap(jj, [[64 * JB, 2], [JB, 64]], np_=64), ALU.add, [t1, S], [H])
                    hi = (jj + 1) if d == 0 else jj
                    p.op("act", lambda e, hi=hi: e.copy(Hh.ap(hi, [[JB + 1, 128]], np_=64), H.ap(0, [[1, 128]], np_=64)), reads=[H], writes=[Hh])
                hoff = 0 if d == 0 else 1
                for gq in range(4):
                    bk = B[2 + gq]
                    for gl in range(16):
                        g = gq * 16 + gl
                        o = bk[:, gl * JB:(gl + 1) * JB]
                        p.op("pe", lambda e, g=g, o=o: e.matmul(o, R.Toep[:, g, :], u8.ap(g, [[64, JB]]), start=True, stop=False), reads=[R.Toep, u8], writes=[bk])
                        for ri in range(2):
                            p.op("pe", lambda e, g=g, o=o, ri=ri: e.matmul(o, R.M[:, g, ri, :], Hh.ap((ri * 64 + g) * (JB + 1) + hoff, [[1, JB]], np_=64),
                                                                           start=False, stop=(ri == 1)), reads=[R.M, Hh], writes=[bk])
                    p.op("act", lambda e, gq=gq, bk=bk: e.copy(Y8.ap(gq * 16, [[1, 16], [64, JB]]), bk.ap(0, [[JB, 16], [1, JB]])), reads=[bk], writes=[Y8])
                if d == 0:
                    p.dma("sp", cx.Yf8[:, j0:j0 + JB, :], Y8[:], reads=[Y8], writes=[cx.dYf[j0 // JB]])
                else:
                    p.dma("sp", Yf[:], cx.Yf8[:, j0:j0 + JB, :], reads=[cx.dYf[j0 // JB]], writes=[Yf])
                    _tt(p, "dve", Y8[:], Y8[:], Yf[:], ALU.add, [Y8, Yf], [Y8])
                    _tt(p, "pool", Yf[:], u8[:], d8.ap(0, [[0, JB], [1, 64]]), ALU.mult, [u8, d8], [Yf])
                    _tt(p, "dve", Y8[:], Y8[:], Yf[:], ALU.add, [Y8, Yf], [Y8])
                    p.op("act", lambda e: e.activation(Y8[:], Y8[:], AF.Gelu_apprx_tanh), reads=[Y8], writes=[Y8])
                    for s in range(8):
                        p.dma("sp", ydv[s, :, j0:j0 + JB, :], Y8[16 * s:16 * s + 16, :, :], reads=[Y8], writes=[cx.dYd[tl], cx.dYd[tl + 1]])
            if sq["out"] is not None:
                bk = B[0]
                for ri in range(2):
                    p.op("pe", lambda e, ri=ri: e.transpose(bk[0:64, ri * 64:(ri + 1) * 64], H[:, ri, :], cx.ident[0:64, 0:64]), reads=[H, cx.ident], writes=[bk])
                p.op("act", lambda e: e.copy(hs[:], bk.ap(0, [[64, 2], [1, 64]], np_=64)), reads=[bk], writes=[hs])
                p.dma("act", sq["out"].rearrange("r g q -> g r q"), hs[:], reads=[hs], writes=[cx.d_out])


def ssm_layer(p, cx, jl, mods, lng, lnb, seqs_fn):
    B = cx.banks
    NTt = cx.NT // 128
    nS = cx.n_smp_tiles
    with Scope(p):
        win = p.sbuf([128, 8, 1024], BF16)
        xt = [p.sbuf([128, 1024], F32) for _ in range(2)]; hm = p.sbuf([128, 1024], F32); hT = p.sbuf([128, 8, 128], BF16)
        uo = [p.sbuf([128, 1024], F32) for _ in range(2)]
        w = cx.dram["ssm_w_in"][jl]
        for k in range(8):
            p.dma("pool", win[:, k, :], w[k * 128:(k + 1) * 128, :], writes=[win])
        for i in range(NTt):
            mod = mods["smp"] if i < nS else mods["ctx"]
            x_ = xt[i % 2]
            p.dma("sp", x_[:], cx.X[i * 128:(i + 1) * 128, :], reads=[cx.dX[i]], writes=[x_])
            modulate_transpose(p, cx, x_, hm, lambda half: (hT[:, half * 4:(half + 1) * 4, :], hT), mod, [B[0], B[1]])
            for dh in range(2):
                bk = B[2 + dh]
                for k in range(8):
                    p.op("pe", lambda e, dh=dh, k=k, bk=bk: e.matmul(bk[:], hT[:, k, :], win[:, k, dh * 512:(dh + 1) * 512], start=(k == 0), stop=(k == 7)),
                         reads=[hT, win], writes=[bk])
                p.op("act" if dh == 0 else "dve", (lambda e, dh=dh, bk=bk, i=i: e.copy(uo[i % 2][:, dh * 512:(dh + 1) * 512], bk[:])) if dh == 0 else
                     (lambda e, dh=dh, bk=bk, i=i: e.tensor_copy(uo[i % 2][:, dh * 512:(dh + 1) * 512], bk[:])), reads=[bk], writes=[uo[i % 2]])
            p.dma("act", cx.Ud[i * 128:(i + 1) * 128, :], uo[i % 2][:], reads=[uo[i % 2]], writes=[cx.dUd[i]])
    with Scope(p):
        R = Ctx()
        R.Win = p.sbuf([128, 64, 128], BF16); R.Toep = p.sbuf([128, 64, 128], BF16); R.M = p.sbuf([64, 64, 2, 128], BF16)
        R.A2 = p.sbuf([64, 2, 64], F32); R.AI2 = p.sbuf([64, 2, 64], F32)
        for d in range(2):
            ssm_precompute(p, cx, R, jl, d)
            ssm_sweep(p, cx, R, jl, d, seqs_fn(d))
    with Scope(p):
        wg = p.sbuf([128, 8, 2048], BF16); wo = p.sbuf([128, 8, 1024], BF16)
        xt = [p.sbuf([128, 1024], F32) for _ in range(2)]; yt = [p.sbuf([128, 1024], F32) for _ in range(2)]
        yT = p.sbuf([128, 8, 128], BF16); sg = p.sbuf([128, 1024], F32); vv = p.sbuf([128, 1024], F32); vT = p.sbuf([128, 8, 128], BF16)
        t1 = p.sbuf([128, 1024], F32); y = p.sbuf([128, 1024], F32)
        scr = {"st": p.sbuf([128, 12], F32), "mv": p.sbuf([128, 2], F32), "rstd": p.sbuf([128, 1], F32)}
        for k in range(8):
            p.dma("pool", wg[:, k, :], cx.dram["ssm_w_glu"][jl][k * 128:(k + 1) * 128, :], writes=[wg])
            p.dma("pool", wo[:, k, :], cx.dram["ssm_w_out"][jl][k * 128:(k + 1) * 128, :], writes=[wo])
        for i in range(NTt):
            mod = mods["smp"] if i < nS else mods["ctx"]
            x_ = xt[i % 2]; y_ = yt[i % 2]
            p.dma("sp", x_[:], cx.X[i * 128:(i + 1) * 128, :], reads=[cx.dX[i]], writes=[x_])
            p.dma("sp", y_[:], cx.Yd[i * 128:(i + 1) * 128, :], reads=[cx.dYd[i]], writes=[y_])
            for half in range(2):
                bk = B[half]
                for k4 in range(4):
                    k = half * 4 + k4
                    p.op("pe", lambda e, k=k, k4=k4, bk=bk: e.transpose(bk[:, k4 * 128:(k4 + 1) * 128], y_[:, k * 128:(k + 1) * 128], cx.ident[:]), reads=[y_, cx.ident], writes=[bk])
                p.op("act", lambda e, half=half, bk=bk: e.copy(yT[:, half * 4:(half + 1) * 4, :], bk.ap(0, [[128, 4], [1, 128]])), reads=[bk], writes=[yT])
            for q in range(4):
                bk = B[2 + q]
                for k in range(8):
                    p.op("pe", lambda e, q=q, k=k, bk=bk: e.matmul(bk[:], yT[:, k, :], wg[:, k, q * 512:(q + 1) * 512], start=(k == 0), stop=(k == 7)), reads=[yT, wg], writes=[bk])
            for q in range(2):
                p.op("act", lambda e, q=q: e.activation(sg[:, q * 512:(q + 1) * 512], B[4 + q][:], AF.Sigmoid), reads=[B[4 + q]], writes=[sg])
                p.op("dve", lambda e, q=q: e.tensor_tensor(vv[:, q * 512:(q + 1) * 512], B[2 + q][:], sg[:, q * 512:(q + 1) * 512], ALU.mult), reads=[B[2 + q], sg], writes=[vv])
            for half in range(2):
                bk = B[half]
                for k4 in range(4):
                    k = half * 4 + k4
                    p.op("pe", lambda e, k=k, k4=k4, bk=bk: e.transpose(bk[:, k4 * 128:(k4 + 1) * 128], vv[:, k * 128:(k + 1) * 128], cx.ident[:]), reads=[vv, cx.ident], writes=[bk])
                p.op("act", lambda e, half=half, bk=bk: e.copy(vT[:, half * 4:(half + 1) * 4, :], bk.ap(0, [[128, 4], [1, 128]])), reads=[bk], writes=[vT])
            psO = [B[6], B[7]]
            for dh in range(2):
                for k in range(8):
                    p.op("pe", lambda e, dh=dh, k=k: e.matmul(psO[dh][:], vT[:, k, :], wo[:, k, dh * 512:(dh + 1) * 512], start=(k == 0), stop=(k == 7)), reads=[vT, wo], writes=[psO[dh]])
            post_norm_store(p, cx, x_, psO, mod, lng, lnb, t1, y, scr, i * 128)


def host_ssm_layouts(inp, jl):
    o = {}
    pg = np.stack([np.stack([inp["ssm_a_re"][jl, d].T, inp["ssm_a_im"][jl, d].T,
                             np.broadcast_to(inp["ssm_log_dt"][jl, d][None, :], (64, 64))]) for d in range(2)])
    o["ssm_pg"] = np.ascontiguousarray(pg, dtype=np.float32)
    o["ssm_b"] = np.ascontiguousarray(np.stack([np.stack([inp["ssm_b_re"][jl, d].transpose(1, 0, 2).reshape(64, 1024),
                                                          inp["ssm_b_im"][jl, d].transpose(1, 0, 2).reshape(64, 1024)]) for d in range(2)]))
    o["ssm_c"] = np.ascontiguousarray(np.stack([np.stack([inp["ssm_c_re"][jl, d].transpose(2, 0, 1).reshape(64, 1024),
                                                          inp["ssm_c_im"][jl, d].transpose(2, 0, 1).reshape(64, 1024)]) for d in range(2)]))
    dd = inp["ssm_d"][jl].reshape(64, 16).T
    o["ssm_d8"] = np.ascontiguousarray(np.tile(dd, (8, 1)))
    perm = (np.arange(64)[None, :] * 16 + np.arange(16)[:, None]).reshape(-1)
    o["ssm_w_in"] = np.ascontiguousarray(inp["ssm_w_in"][jl][:, perm])
    o["ssm_w_glu"] = np.ascontiguousarray(inp["ssm_w_glu"][jl][perm, :])
    o["ssm_w_out"] = np.ascontiguousarray(inp["ssm_w_out"][jl])
    return o


def host_tmask():
    s = np.arange(128)[:, None] // 16; t = np.arange(128)[None, :] // 16
    return np.stack([(s <= t), (s >= t)]).astype(np.float32)


def compute_mods(p, cx, l, half):
    B = cx.banks
    names = ["sh", "sc1", "g1"]
    with Scope(p):
        wm = [p.sbuf([128, 8, 512], BF16) for _ in range(2)]
        bm = [p.sbuf([128, 512], F32) for _ in range(2)]
        it = 0
        for v in range(3):
            for dh in range(2):
                col = half * 3072 + v * 1024 + dh * 512
                w_ = wm[it % 2]; b_ = bm[it % 2]
                p.dma("pool", w_[:], cx.dram["w_mod"][l][:, col:col + 512].rearrange("(k q) c -> q k c", q=128), writes=[w_])
                p.dma("sp", b_[:], cx.dram["b_mod"][l][col:col + 512].partition_broadcast(128), writes=[b_])
                for ci, cn in enumerate(("smp", "ctx")):
                    bk = B[(it * 2 + ci) % 8]
                    for k in range(8):
                        p.op("pe", lambda e, k=k, ci=ci, bk=bk, w_=w_: e.matmul(bk[:], cx.SCb[ci][:, k, :], w_[:, k, :], start=(k == 0), stop=(k == 7)),
                             reads=[cx.SCb[ci], w_], writes=[bk])
                    dst = cx.mods[cn][names[v]]
                    p.op("dve", lambda e, bk=bk, dst=dst, b_=b_, v=v, dh=dh: e.scalar_tensor_tensor(
                        dst[:, dh * 512:(dh + 1) * 512], bk[:], 1.0 if v >= 1 else 0.0, b_[:], ALU.add, ALU.add), reads=[bk, b_], writes=[dst])
                it += 1


def load_ln(p, cx, l, which):
    p.dma("sp", cx.lng[:], cx.dram["ln_g"][l, which].partition_broadcast(128), writes=[cx.lng])
    p.dma("sp", cx.lnb[:], cx.dram["ln_b"][l, which].partition_broadcast(128), writes=[cx.lnb])


NS_TILES = 32
NP_SEQ = 4
NTOK = NS_TILES * 128 + NP_SEQ * 256


def build_program(nS=NS_TILES, nP=NP_SEQ, depth=DEPTH):
    nc = bass.Bass("TRN2", target_bir_lowering=False)
    p = Prog(nc)
    cx = Ctx()
    NT = nS * 128 + nP * 256
    NTt = NT // 128
    NCH = NT // 8
    cx.NT = NT; cx.n_smp_tiles = nS; cx.n_prm_seq = nP

    def din(name, shape, dt=F32):
        return nc.dram_tensor(name, list(shape), dt, kind="ExternalInput").ap()

    def dout(name, shape):
        return nc.dram_tensor(name, list(shape), F32, kind="ExternalOutput").ap()

    cx.dram = {
        "c_ident": din("c_ident", [128, 128]), "c_e3": din("c_e3", [128, 128]), "c_masks": din("c_masks", [128, 2, 128]),
        "c_rope": din("c_rope", [64, 2, 4096]), "c_tmask": din("c_tmask", [2, 128, 128]),
        "w_mod": din("w_mod", [4, 1024, 6144]), "b_mod": din("b_mod", [4, 6144]), "ln_g": din("ln_g", [4, 2, 1024]), "ln_b": din("ln_b", [4, 2, 1024]),
        "ssm_pg": din("ssm_pg", [2, 2, 3, 64, 64]), "ssm_b": din("ssm_b", [2, 2, 2, 64, 1024]), "ssm_c": din("ssm_c", [2, 2, 2, 64, 1024]),
        "ssm_d8": din("ssm_d8", [2, 128, 64]), "ssm_w_in": din("ssm_w_in", [2, 1024, 1024]), "ssm_w_glu": din("ssm_w_glu", [2, 1024, 2048]),
        "ssm_w_out": din("ssm_w_out", [2, 1024, 1024]),
        "attn_w_qkv": din("attn_w_qkv", [2, 1024, 1536]), "attn_w_qkp": din("attn_w_qkp", [2, 1024, 1280]),
        "attn_w_out": din("attn_w_out", [2, 1024, 1024]), "attn_sink": din("attn_sink", [2, 16]),
        "cache_k": din("cache_k", [2, 512, 256]), "cache_v": din("cache_v", [2, 512, 256]),
        "peer_w_q": din("peer_w_q", [4, 1024, 2048]), "peer_keysT": din("peer_keysT", [4, 128, 16, 128]),
        "peer_ut": din("peer_ut", [4, 128, 128, 1024]), "peer_v": din("peer_v", [4, NEXP, 1024]),
    }
    x_in = din("x_in", [NT, 1024]); cond2 = din("cond2", [2, 128, 8]); st0 = din("st0", [2, 2, 64, 2, 64])
    y_out = dout("y_out", [NT, 1024]); o_state = dout("o_state", [nP, 2, 2, 2, 64, 64])
    o_k = dout("o_k", [nP, 2, 256, 256]); o_v = dout("o_v", [nP, 2, 256, 256])
    cx.X = nc.dram_tensor("Xs", [NT, 1024], F32).ap()
    cx.Ud = nc.dram_tensor("Ud", [NT, 1024], F32).ap(); cx.Yd = nc.dram_tensor("Yd", [NT, 1024], F32).ap()
    cx.Yf8 = nc.dram_tensor("Yf8", [128, NCH, 64], F32).ap()
    cx.spd = nc.dram_tensor("spd", [2, 3, 8, NT, 128], BF16).ap()
    cx.utb = nc.dram_tensor("utb", [128, 128, 1024], BF16).ap(); cx.vb = nc.dram_tensor("vb", [NEXP, 1024], BF16).ap()
    cx.dX = [Dep() for _ in range(NTt)]; cx.dUd = [Dep() for _ in range(NTt)]; cx.dYd = [Dep() for _ in range(NTt)]
    cx.dYf = [Dep() for _ in range(NCH // JB)]; cx.d_spd = [Dep() for _ in range(NTt)]
    cx.d_utb = [Dep() for _ in range(16)]; cx.d_vb = [Dep() for _ in range(16)]
    cx.d_out = Dep()
    setup_common(p, cx)
    for i in range(NTt):
        p.dma("sp", cx.X[i * 128:(i + 1) * 128, :], x_in[i * 128:(i + 1) * 128, :], writes=[cx.dX[i]])
    cx.mods = {cn: {n: p.sbuf([128, 1024], F32, name="mod_%s_%s" % (cn, n)) for n in ("sh", "sc1", "g1")} for cn in ("smp", "ctx")}
    cx.lng = p.sbuf([128, 1024], F32, name="lng"); cx.lnb = p.sbuf([128, 1024], F32, name="lnb")
    cx.SCb = [p.sbuf([128, 8, 128], BF16, name="SCb%d" % i) for i in range(2)]
    ctile = p.sbuf([128, 2, 8], F32, name="condt")
    p.dma("sp", ctile[:], cond2.rearrange("a q k -> q a k"), writes=[ctile])
    p.op("act", lambda e: e.activation(ctile[:], ctile[:], AF.Silu), reads=[ctile], writes=[ctile])
    for ci in range(2):
        p.op("dve", lambda e, ci=ci: e.tensor_copy(cx.SCb[ci][:], ctile.ap(ci * 8, [[1, 8], [0, 128]])), reads=[ctile], writes=[cx.SCb[ci]])

    for l in range(depth):
        peer_prep_weights(p, cx, l)
        compute_mods(p, cx, l, 0)
        load_ln(p, cx, l, 0)
        if l % 2 == 0:
            jl = l // 2

            def seqs_fn(d, jl=jl):
                L = []
                if nS:
                    L.append(dict(j0=0, n=nS * 16, init=st0[jl, d], out=None))
                for s in range(nP):
                    L.append(dict(j0=nS * 16 + s * 32, n=32, init=None, out=o_state[s, jl, d]))
                return L
            ssm_layer(p, cx, jl, cx.mods, cx.lng, cx.lnb, seqs_fn)
        else:
            j = l // 2
            with Scope(p):
                a = attn_alloc(p, cx)
                attn_load_layer(p, cx, a, j)
                attn_layer(p, cx, a, j, cx.mods, cx.lng, cx.lnb,
                           lambda s, t, j=j: (o_k[s, j, t * 128:(t + 1) * 128, :], o_v[s, j, t * 128:(t + 1) * 128, :]))
        compute_mods(p, cx, l, 1)
        load_ln(p, cx, l, 1)
        with Scope(p):
            a = peer_alloc(p, cx)
            peer_load_layer(p, cx, a, l)
            for blk in range(NT // 256):
                mod = cx.mods["smp"] if blk * 2 < nS else cx.mods["ctx"]
                peer_block(p, cx, a, l, blk, mod, cx.lng, cx.lnb)
    for i in range(NTt):
        p.dma("sp", y_out[i * 128:(i + 1) * 128, :], cx.X[i * 128:(i + 1) * 128, :], reads=[cx.dX[i]], writes=[cx.d_out])
    p.drain_all()
    p.es.close()
    return nc, p


def host_inputs(inp, core, nS=NS_TILES, nP=NP_SEQ):
    f = lambda a: np.ascontiguousarray(a, dtype=np.float32)
    m = dict(_SHARED)
    xs = inp["x_sample"][core][:nS * 128]
    xp = inp["x_prompt"][core * NP_SEQ: core * NP_SEQ + nP].reshape(nP * 256, 1024)
    m["x_in"] = f(np.concatenate([xs, xp], axis=0))
    m["cond2"] = f(np.stack([inp["c"][core].reshape(8, 128).T, inp["c_ctx"].reshape(8, 128).T]))
    st = np.zeros((2, 2, 64, 2, 64), np.float32)
    for jl in range(2):
        for d in range(2):
            st[jl, d, :, 0, :] = inp["state_ssm_re"][core, jl, d].T
            st[jl, d, :, 1, :] = inp["state_ssm_im"][core, jl, d].T
    m["st0"] = st
    m["cache_k"] = f(inp["cache_k"][core].reshape(2, 512, 256))
    m["cache_v"] = f(inp["cache_v"][core].reshape(2, 512, 256))
    return m


_SHARED = {}


def prepare_shared(inp):
    f = lambda a: np.ascontiguousarray(a, dtype=np.float32)
    S = {}
    S.update(host_consts())
    S["c_tmask"] = host_tmask()
    for k in ("w_mod", "b_mod", "ln_g", "ln_b", "attn_w_qkv", "attn_w_out", "attn_sink", "peer_w_q", "peer_v"):
        S[k] = f(inp[k])
    S["attn_w_qkp"] = f(perm_qk(inp["attn_w_qkv"]))
    L = [host_ssm_layouts(inp, jl) for jl in range(2)]
    for k in L[0]:
        S[k] = f(np.stack([L[0][k], L[1][k]]))
    keys = inp["peer_keys"]
    S["peer_keysT"] = f(keys.transpose(0, 4, 2, 1, 3).reshape(4, 128, 16, 128))
    u = inp["peer_u"]
    S["peer_ut"] = f(u.reshape(4, 128, 128, 8, 128).transpose(0, 1, 4, 3, 2).reshape(4, 128, 128, 1024))
    _SHARED.clear()
    _SHARED.update(S)


_PROG = {}


def kernel(**inputs):
    inp = {k: np.asarray(v) for k, v in inputs.items()}
    if "nc" not in _PROG:
        _PROG["nc"], _ = build_program()
    nc = _PROG["nc"]
    prepare_shared(inp)
    in_maps = [host_inputs(inp, c) for c in range(8)]
    res = run_bass_kernel_spmd(nc, in_maps, core_ids=list(range(8)))
    R = res.results
    nS, nP = NS_TILES, NP_SEQ
    y_sample = np.stack([R[c]["y_out"][:nS * 128] for c in range(8)]).astype(np.float32)
    y_prompt = np.concatenate([R[c]["y_out"][nS * 128:].reshape(nP, 256, 1024) for c in range(8)], axis=0).astype(np.float32)
    ost = np.concatenate([R[c]["o_state"] for c in range(8)], axis=0)
    ns_re = np.ascontiguousarray(ost[:, :, :, 0]).astype(np.float32)
    ns_im = np.ascontiguousarray(ost[:, :, :, 1]).astype(np.float32)
    nk = np.concatenate([R[c]["o_k"] for c in range(8)], axis=0).reshape(32, 2, 256, 4, 64).astype(np.float32)
    nv = np.concatenate([R[c]["o_v"] for c in range(8)], axis=0).reshape(32, 2, 256, 4, 64).astype(np.float32)
    return (y_prompt, y_sample, ns_re, ns_im, nk, nv)
```

```python
from contextlib import ExitStack
import numpy as np
import concourse.bass as bass
import concourse.mybir as mybir
from concourse.ap import AP
from concourse.bass_utils import run_bass_kernel_spmd

F32 = mybir.dt.float32
BF16 = mybir.dt.bfloat16
ALU = mybir.AluOpType
AF = mybir.ActivationFunctionType
AX = mybir.AxisListType

D = 1024
DEPTH = 4
DN_ALPHA = (2 * DEPTH) ** 0.25
LN_EPS = 1e-5
NEXP = 16384
NET = 128


class Dep:
    __slots__ = ("w", "r", "excl")

    def __init__(self):
        self.w = None
        self.r = {}
        self.excl = False


class TT:
    __slots__ = ("t", "d", "pstep", "base", "full")

    def __init__(self, t, base=0, shape=None, dep=None):
        self.t = t
        self.d = dep or Dep()
        self.pstep = t[:].ap[0][0]
        self.base = base
        if shape is None:
            self.full = t[:]
        else:
            dims = []
            st = 1
            for n in reversed(shape):
                dims.append([st, n])
                st *= n
            self.full = AP(t, base, [[self.pstep, 128]] + dims[::-1])

    def __getitem__(self, k):
        return self.full[k]

    def ap(self, off, dims, p0=0, np_=128):
        return AP(self.t, self.base + p0 * self.pstep + off, [[self.pstep, np_]] + [list(x) for x in dims])


def fence(src, dst):
    for a in src:
        a = a.d if isinstance(a, TT) else a
        for b in dst:
            b = b.d if isinstance(b, TT) else b
            if a.w is not None:
                b.r[a.w[0]] = max(b.r.get(a.w[0], 0), a.w[1])
            for k, c in a.r.items():
                b.r[k] = max(b.r.get(k, 0), c)


EMBED_WAIT = True


class Prog:
    def __init__(self, nc, n_dma_sems=(44, 30, 16)):
        self.nc = nc
        self.es = ExitStack()
        self.engs = {"pe": nc.tensor, "act": nc.scalar, "dve": nc.vector, "pool": nc.gpsimd, "sp": nc.sync}
        self.sem = {}
        self.cnt = {}
        for k in ("pe", "act", "dve", "pool"):
            self.sem[k] = self.es.enter_context(nc.semaphore("s_" + k))
            self.cnt[k] = 0
        self.dpool = {}
        for q, n in zip(("sp", "pool", "act"), n_dma_sems):
            lst = []
            for i in range(n):
                key = "d_%s_%d" % (q, i)
                self.sem[key] = self.es.enter_context(nc.semaphore(key))
                self.cnt[key] = 0
                lst.append(key)
            self.dpool[q] = [lst, 0]
        self.known = {e: {} for e in self.engs}
        self.ninstr = 0
        self.uid = 0

    def sbuf(self, shape, dt, name=None):
        self.uid += 1
        return TT(self.es.enter_context(self.nc.sbuf_tensor("%s_%d" % (name or "sb", self.uid), list(shape), dt)))

    def psum(self, shape, dt, name=None):
        self.uid += 1
        return TT(self.es.enter_context(self.nc.psum_tensor(name or ("ps%d" % self.uid), list(shape), dt)))

    def _need(self, eng, evs, defer_last=False):
        kn = self.known[eng]
        e = self.engs[eng]
        todo = {}
        for (k, c) in evs:
            if c <= 0 or (k == "pe" and eng == "pe") or kn.get(k, 0) >= c:
                continue
            if todo.get(k, 0) < c:
                todo[k] = c
        items = list(todo.items())
        last = None
        if defer_last and items:
            last = items.pop()
        for k, c in items:
            e.wait_ge(self.sem[k], c)
            kn[k] = c
            self.ninstr += 1
        if last is not None:
            kn[last[0]] = last[1]
        return last

    @staticmethod
    def _collect(reads, writes):
        evs = []
        for d in reads:
            if d.w is not None:
                evs.append(d.w)
        for d in writes:
            if d.w is not None:
                evs.append(d.w)
            evs.extend(d.r.items())
        return evs

    @staticmethod
    def _commit(ev, reads, writes):
        for d in reads:
            d.r[ev[0]] = ev[1]
        for d in writes:
            d.w = ev
            d.r = {}

    def op(self, eng, fn, reads=(), writes=()):
        reads = [x.d if isinstance(x, TT) else x for x in reads]
        writes = [x.d if isinstance(x, TT) else x for x in writes]
        ex = [d for d in reads if d.excl]
        if ex:
            reads = [d for d in reads if not d.excl]
            writes = writes + ex
        last = self._need(eng, self._collect(reads, writes), defer_last=EMBED_WAIT)
        ins = fn(self.engs[eng])
        if last is not None:
            ins._wait_ge(self.sem[last[0]], last[1])
        self.cnt[eng] += 1
        ins.then_inc(self.sem[eng], 1)
        self.ninstr += 1
        self._commit((eng, self.cnt[eng]), reads, writes)
        return ins

    def dma(self, q, out, in_, reads=(), writes=(), **kw):
        reads = [x.d if isinstance(x, TT) else x for x in reads]
        writes = [x.d if isinstance(x, TT) else x for x in writes]
        lst, idx = self.dpool[q]
        key = lst[idx % len(lst)]
        self.dpool[q][1] = idx + 1
        evs = self._collect(reads, writes)
        evs.append((key, self.cnt[key]))
        self._need(q, evs)
        ins = self.engs[q].dma_start(out=out, in_=in_, **kw)
        self.cnt[key] += 16
        ins.then_inc(self.sem[key], 16)
        self.ninstr += 1
        self._commit((key, self.cnt[key]), reads, writes)
        return ins

    def drain_all(self):
        self._need("sp", list(self.cnt.items()))


class Ctx:
    pass


def dram_rows(t, r0, nr, c0=0, ncol=None):
    if ncol is None:
        return t[r0:r0 + nr, :]
    return t[r0:r0 + nr, c0:c0 + ncol]


def setup_common(p, cx):
    nc = p.nc
    cx.banks = [p.psum([128, 512], F32, name="bank%d" % i) for i in range(8)]
    for b in cx.banks:
        b.d.excl = True
    cx.ident = p.sbuf([128, 128], F32, name="ident")
    p.dma("sp", cx.ident[:], cx.dram["c_ident"], writes=[cx.ident])
    cx.e3 = p.sbuf([128, 128], BF16, name="e3")
    p.dma("pool", cx.e3[:], cx.dram["c_e3"], writes=[cx.e3])
    cx.eps = p.sbuf([128, 1], F32, name="eps")
    p.op("dve", lambda e: e.memset(cx.eps[:], LN_EPS), writes=[cx.eps])


def layer_norm_rows(p, cx, y, out, g_bc, b_bc, scr):
    st = scr["st"]; mv = scr["mv"]; rstd = scr["rstd"]
    for c in range(2):
        p.op("dve", lambda e, c=c: e.bn_stats(st[:, c * 6:(c + 1) * 6], y[:, c * 512:(c + 1) * 512]), reads=[y], writes=[st])
    p.op("dve", lambda e: e.bn_aggr(mv[:], st[:]), reads=[st], writes=[mv])
    p.op("act", lambda e: e.activation(rstd[:], mv[:, 1:2], AF.Sqrt, bias=cx.eps[:], scale=1.0), reads=[mv, cx.eps], writes=[rstd])
    p.op("dve", lambda e: e.reciprocal(rstd[:], rstd[:]), reads=[rstd], writes=[rstd])
    p.op("dve", lambda e: e.tensor_scalar(y[:], y[:], mv[:, 0:1], rstd[:, 0:1], ALU.subtract, ALU.mult),
         reads=[y, mv, rstd], writes=[y])
    p.op("pool", lambda e: e.tensor_tensor(y[:], y[:], g_bc[:], ALU.mult), reads=[y, g_bc], writes=[y])
    p.op("pool", lambda e: e.tensor_tensor(out[:], y[:], b_bc[:], ALU.add), reads=[y, b_bc], writes=[out])


def peer_prep_weights(p, cx, l):
    ut_src = cx.dram["peer_ut"][l]
    v_src = cx.dram["peer_v"][l]
    CH = 8
    for c in range(NET // CH):
        p.dma("pool", cx.utb[c * CH:(c + 1) * CH], ut_src[c * CH:(c + 1) * CH], writes=[cx.d_utb[c]])
        p.dma("pool", cx.vb[c * CH * 128:(c + 1) * CH * 128, :], v_src[c * CH * 128:(c + 1) * CH * 128, :],
              writes=[cx.d_vb[c]])


def peer_alloc(p, cx):
    a = Ctx()
    a.big = p.sbuf([128, 16384], F32, name="p_big")
    hb = a.big.t[:].bitcast(BF16).tensor
    a.Wb = TT(hb, 0, [256, 128])
    a.wq = TT(hb, 0, [8, 2048])
    a.s_sb = TT(a.big.t, 8192, [16, 128])
    a.cand = TT(a.big.t, 10240, [8, 256])
    a.r1 = TT(a.big.t, 12288, [2, 8, 128])
    a.d1regs = [a.wq, a.s_sb, a.cand, a.r1]
    a.keys = p.sbuf([128, 16, 128], BF16, name="keysT")
    a.xt = [p.sbuf([128, 1024], F32, name="p_xt%d" % i) for i in range(2)]
    a.hm = p.sbuf([128, 1024], F32, name="p_hm")
    a.hT = p.sbuf([128, 8, 256], BF16, name="p_hT")
    a.qT = p.sbuf([128, 16, 128], BF16, name="p_qT")
    a.V = p.sbuf([128, 16, 16], F32, name="p_V")
    a.tmp128 = p.sbuf([128, 128], F32, name="p_tmp128")
    a.tmpc = [p.sbuf([128, 256], F32, name="p_tmpc%d" % i) for i in range(2)]
    a.C = p.sbuf([128, 8, 24], F32, name="p_C")
    a.sm = p.sbuf([128, 8, 8], F32, name="p_small")
    a.ex = p.sbuf([128, 8, 16], F32, name="p_ex")
    a.v1pp = p.sbuf([128, 128], F32, name="p_v1pp")
    a.thp = p.sbuf([128, 128], F32, name="p_thp")
    a.SPt = p.sbuf([128, 2, 3, 8, 128], BF16, name="p_SPt")
    a.ThT = p.sbuf([128, 256], F32, name="p_ThT")
    a.V1T = p.sbuf([128, 256], F32, name="p_V1T")
    a.SPh = [[p.sbuf([128, 16, 128], BF16, name="p_SPh%d_%d" % (i, s)) for s in range(2)] for i in range(2)]
    for i in range(2):
        for s_ in range(2):
            p.op("pool", lambda e, i=i, s_=s_: e.memset(a.SPh[i][s_][:], 0.0), writes=[a.SPh[i][s_]])
    a.EX = [[p.sbuf([128, 4, 128], BF16, name="p_EX%d_%d" % (i, s)) for s in range(2)] for i in range(3)]
    a.MK = [[p.sbuf([128, 4, 128], BF16, name="p_MK%d_%d" % (i, s)) for s in range(2)] for i in range(3)]
    a.LR = [[p.sbuf([128, 4, 128], BF16, name="p_LR%d_%d" % (i, s)) for s in range(2)] for i in range(3)]
    a.ut = [p.sbuf([128, 1024], BF16, name="p_ut%d" % i) for i in range(4)]
    a.vt = [p.sbuf([128, 1024], BF16, name="p_vt%d" % i) for i in range(4)]
    a.G = [p.sbuf([128, 256], BF16, name="p_G%d" % i) for i in range(3)]
    a.A = [p.sbuf([128, 256], BF16, name="p_A%d" % i) for i in range(3)]
    a.y = p.sbuf([128, 1024], F32, name="p_y")
    a.t1 = a.hm
    a.xo = a.hm
    a.scr = {"st": p.sbuf([128, 12], F32, name="p_st"), "mv": p.sbuf([128, 2], F32, name="p_mv"),
             "rstd": p.sbuf([128, 1], F32, name="p_rstd")}
    return a


def peer_load_layer(p, cx, a, l):
    p.dma("pool", a.keys[:], cx.dram["peer_keysT"][l], writes=[a.keys])


def peer_block(p, cx, a, l, blk, mod, lng, lnb):
    B = cx.banks
    tok0 = blk * 256
    NT = cx.NT
    spd = cx.spd
    fence([a.Wb], a.d1regs)
    wqd = cx.dram["peer_w_q"][l]
    for k in range(8):
        p.dma("pool", a.wq[:, k, :], wqd[k * 128:(k + 1) * 128, :], writes=[a.wq])
    for tl in range(2):
        r0 = tok0 + tl * 128
        xt = a.xt[tl]
        p.dma("sp", xt[:], cx.X[r0:r0 + 128, :], reads=[cx.dX[r0 // 128]], writes=[xt])
        p.op("dve", lambda e: e.tensor_tensor(a.hm[:], xt[:], mod["sc1"][:], ALU.mult), reads=[xt, mod["sc1"]], writes=[a.hm])
        p.op("dve", lambda e: e.tensor_tensor(a.hm[:], a.hm[:], mod["sh"][:], ALU.add), reads=[a.hm, mod["sh"]], writes=[a.hm])
        for half in range(2):
            bk = B[half]
            for k4 in range(4):
                k = half * 4 + k4
                p.op("pe", lambda e, k=k, k4=k4, bk=bk: e.transpose(bk[:, k4 * 128:(k4 + 1) * 128], a.hm[:, k * 128:(k + 1) * 128], cx.ident[:]),
                     reads=[a.hm, cx.ident], writes=[bk])
            p.op("act", lambda e, half=half, bk=bk: e.copy(
                a.hT.ap(half * 4 * 256 + tl * 128, [[256, 4], [1, 128]]), bk.ap(0, [[128, 4], [1, 128]])),
                reads=[bk], writes=[a.hT])
        for m4 in range(4):
            bk = B[2 + (m4 % 2)]
            for mi in range(4):
                m = m4 * 4 + mi
                for k in range(8):
                    p.op("pe", lambda e, m=m, mi=mi, k=k, bk=bk: e.matmul(
                        bk[:, mi * 128:(mi + 1) * 128], a.wq[:, k, m * 128:(m + 1) * 128],
                        a.hT[:, k, tl * 128:(tl + 1) * 128], start=(k == 0), stop=(k == 7)),
                        reads=[a.wq, a.hT], writes=[bk])
            eng = "act" if m4 % 2 == 0 else "dve"
            if eng == "act":
                p.op("act", lambda e, m4=m4, bk=bk: e.copy(a.qT[:, m4 * 4:(m4 + 1) * 4, :], bk.ap(0, [[128, 4], [1, 128]])),
                     reads=[bk], writes=[a.qT])
            else:
                p.op("dve", lambda e, m4=m4, bk=bk: e.tensor_copy(a.qT[:, m4 * 4:(m4 + 1) * 4, :], bk.ap(0, [[128, 4], [1, 128]])),
                     reads=[bk], writes=[a.qT])
        for m4 in range(4):
            bk = B[4 + m4]
            for mi in range(4):
                m = m4 * 4 + mi
                p.op("pe", lambda e, m=m, mi=mi, bk=bk: e.matmul(bk[:, mi * 128:(mi + 1) * 128], a.qT[:, m, :], a.keys[:, m, :],
                                                                  start=True, stop=True), reads=[a.qT, a.keys], writes=[bk])
            p.op("act", lambda e, m4=m4, bk=bk: e.copy(a.s_sb[:, m4 * 4:(m4 + 1) * 4, :], bk.ap(0, [[128, 4], [1, 128]])),
                 reads=[bk], writes=[a.s_sb])
        for m in range(16):
            p.op("dve", lambda e, m=m: e.max(out=a.V[:, m, 0:8], in_=a.s_sb[:, m, :]), reads=[a.s_sb], writes=[a.V])
            p.op("dve", lambda e, m=m: e.match_replace(out=a.tmp128[:], in_to_replace=a.V[:, m, 0:8], in_values=a.s_sb[:, m, :],
                                                        imm_value=-1e30), reads=[a.s_sb, a.V], writes=[a.tmp128])
            p.op("dve", lambda e, m=m: e.max(out=a.V[:, m, 8:16], in_=a.tmp128[:]), reads=[a.tmp128], writes=[a.V])
        in0 = a.V.ap(0, [[32, 8], [1, 16], [0, 16]])
        in1 = a.V.ap(16, [[32, 8], [0, 16], [1, 16]])
        outc = a.cand.ap(0, [[256, 8], [16, 16], [1, 16]])
        p.op("dve", lambda e: e.tensor_tensor(outc, in0, in1, ALU.add), reads=[a.V], writes=[a.cand])
        for h in range(8):
            p.op("dve", lambda e, h=h: e.max(out=a.C[:, h, 0:8], in_=a.cand[:, h, :]), reads=[a.cand], writes=[a.C])
            p.op("dve", lambda e, h=h: e.match_replace(out=a.tmpc[0][:], in_to_replace=a.C[:, h, 0:8], in_values=a.cand[:, h, :],
                                                        imm_value=-1e30), reads=[a.cand, a.C], writes=[a.tmpc[0]])
            p.op("dve", lambda e, h=h: e.max(out=a.C[:, h, 8:16], in_=a.tmpc[0][:]), reads=[a.tmpc[0]], writes=[a.C])
            p.op("dve", lambda e, h=h: e.match_replace(out=a.tmpc[1][:], in_to_replace=a.C[:, h, 8:16], in_values=a.tmpc[0][:],
                                                        imm_value=-1e30), reads=[a.tmpc[0], a.C], writes=[a.tmpc[1]])
            p.op("dve", lambda e, h=h: e.max(out=a.C[:, h, 16:24], in_=a.tmpc[1][:]), reads=[a.tmpc[1]], writes=[a.C])
        sm = a.sm
        TAU, LNZ, OFF1, C2, ZZ = 0, 1, 2, 3, 4
        c15 = a.C.ap(15, [[24, 8]]); c16 = a.C.ap(16, [[24, 8]]); c0 = a.C.ap(0, [[24, 8]])
        m1v = a.V.ap(0, [[32, 8]]); m2v = a.V.ap(16, [[32, 8]])
        p.op("dve", lambda e: e.tensor_tensor(sm[:, TAU, :], c15, c16, ALU.add), reads=[a.C], writes=[sm])
        p.op("dve", lambda e: e.tensor_scalar(sm[:, TAU, :], sm[:, TAU, :], 0.5, None, ALU.mult), reads=[sm], writes=[sm])
        c016 = a.C.ap(0, [[24, 8], [1, 16]]); c0b = a.C.ap(0, [[24, 8], [0, 16]])
        p.op("dve", lambda e: e.tensor_tensor(a.ex[:], c016, c0b, ALU.subtract), reads=[a.C], writes=[a.ex])
        p.op("act", lambda e: e.activation(a.ex[:], a.ex[:], AF.Exp), reads=[a.ex], writes=[a.ex])
        p.op("dve", lambda e: e.reduce_sum(sm[:, ZZ, :], a.ex[:], axis=AX.X), reads=[a.ex], writes=[sm])
        p.op("act", lambda e: e.activation(sm[:, LNZ, :], sm[:, ZZ, :], AF.Ln), reads=[sm], writes=[sm])
        p.op("dve", lambda e: e.tensor_tensor(sm[:, OFF1, :], m1v, sm[:, LNZ, :], ALU.add), reads=[a.V, sm], writes=[sm])
        p.op("dve", lambda e: e.tensor_tensor(sm[:, C2, :], sm[:, TAU, :], m2v, ALU.subtract), reads=[a.V, sm], writes=[sm])
        s1v = a.s_sb.ap(0, [[256, 8], [1, 128]]); s2v = a.s_sb.ap(128, [[256, 8], [1, 128]])
        off1b = sm.ap(OFF1 * 8, [[1, 8], [0, 128]]); m2b = a.V.ap(16, [[32, 8], [0, 128]])
        sfix = a.cand
        sf4 = sfix.ap(0, [[1024, 2], [128, 8], [1, 128]])
        p.op("dve", lambda e: e.tensor_tensor(sfix.ap(0, [[128, 8], [1, 128]]), s1v, off1b, ALU.subtract), reads=[a.s_sb, sm], writes=[sfix])
        p.op("pool", lambda e: e.tensor_tensor(sfix.ap(1024, [[128, 8], [1, 128]]), s2v, m2b, ALU.subtract), reads=[a.s_sb, a.V], writes=[sfix])
        v1v = a.V.ap(0, [[32, 8], [1, 16]]); off1b16 = sm.ap(OFF1 * 8, [[1, 8], [0, 16]]); c2b16 = sm.ap(C2 * 8, [[1, 8], [0, 16]])
        v1pp3 = a.v1pp.ap(0, [[16, 8], [1, 16]]); thp3 = a.thp.ap(0, [[16, 8], [1, 16]])
        p.op("dve", lambda e: e.tensor_tensor(v1pp3, v1v, off1b16, ALU.subtract), reads=[a.V, sm], writes=[a.v1pp])
        p.op("dve", lambda e: e.tensor_tensor(thp3, c2b16, v1v, ALU.subtract), reads=[a.V, sm], writes=[a.thp])
        SP = a.SPt
        p.op("act", lambda e: e.copy(SP[:, :, 0, :, :], sf4), reads=[sfix], writes=[SP])
        p.op("dve", lambda e: e.tensor_tensor(a.r1[:], sf4, SP[:, :, 0, :, :], ALU.subtract), reads=[sfix, SP], writes=[a.r1])
        p.op("act", lambda e: e.copy(SP[:, :, 1, :, :], a.r1[:]), reads=[a.r1], writes=[SP])
        p.op("dve", lambda e: e.tensor_tensor(a.r1[:], a.r1[:], SP[:, :, 1, :, :], ALU.subtract), reads=[a.r1, SP], writes=[a.r1])
        p.op("act", lambda e: e.copy(SP[:, :, 2, :, :], a.r1[:]), reads=[a.r1], writes=[SP])
        for side in range(2):
            dst = spd[side, :, :, r0:r0 + 128, :].rearrange("a h t i -> t a h i")
            p.dma("sp", dst, SP[:, side, :, :, :], reads=[SP], writes=[cx.d_spd[r0 // 128]])
        bk = B[0]
        p.op("pe", lambda e: e.transpose(bk[:, 0:128], a.thp[:], cx.ident[:]), reads=[a.thp, cx.ident], writes=[bk])
        p.op("pe", lambda e: e.transpose(bk[:, 128:256], a.v1pp[:], cx.ident[:]), reads=[a.v1pp, cx.ident], writes=[bk])
        p.op("act", lambda e: e.copy(a.ThT[:, tl * 128:(tl + 1) * 128], bk[:, 0:128]), reads=[bk], writes=[a.ThT])
        p.op("act", lambda e: e.copy(a.V1T[:, tl * 128:(tl + 1) * 128], bk[:, 128:256]), reads=[bk], writes=[a.V1T])

    if getattr(cx, "stage", 9) < 2:
        return
    fence(a.d1regs, [a.Wb])
    groups = [(sub, g4) for sub in range(16) for g4 in range(4)]
    NG = len(groups)

    def d2_load(sub):
        ta = tok0 + sub * 16
        for side in range(2):
            src = spd[side, :, :, ta:ta + 16, :].rearrange("a h t i -> (a h) t i")
            p.dma("sp", a.SPh[sub % 2][side][0:24, :, :], src, reads=[cx.d_spd[ta // 128]], writes=[a.SPh[sub % 2][side]])

    def stA(gi):
        sub, g4 = groups[gi]
        st = gi % 3
        sph = a.SPh[sub % 2]
        for side in range(2):
            bk = B[st * 2 + side]
            p.op("pe", lambda e, side=side, bk=bk: e.matmul(bk[:], cx.e3[0:24, :], sph[side][0:24, g4 * 4:(g4 + 1) * 4, :], start=True, stop=True),
                 reads=[cx.e3, sph[side]], writes=[bk])
        if g4 == 3 and sub + 2 < 16:
            d2_load(sub + 2)

    def stB(gi):
        sub, g4 = groups[gi]
        st = gi % 3
        tb = sub * 16 + g4 * 4
        EX = a.EX[st]; MK = a.MK[st]
        ops = (ALU.is_equal, ALU.is_ge)
        srcs = (a.V1T, a.ThT)
        for side in range(2):
            bk = B[st * 2 + side]
            p.op("act", lambda e, side=side, bk=bk: e.activation(EX[side][:], bk.ap(0, [[128, 4], [1, 128]]), AF.Exp), reads=[bk], writes=[EX[side]])
            cb = srcs[side].ap(tb, [[1, 4], [0, 128]])
            p.op("dve", lambda e, side=side, bk=bk, cb=cb: e.tensor_tensor(MK[side][:], bk.ap(0, [[128, 4], [1, 128]]), cb, ops[side]),
                 reads=[bk, srcs[side]], writes=[MK[side]])

    def stC(gi):
        st = gi % 3
        for side in range(2):
            p.op("pool", lambda e, side=side: e.tensor_tensor(a.LR[st][side][:], a.MK[st][side][:], a.EX[st][side][:], ALU.mult),
                 reads=[a.MK[st][side], a.EX[st][side]], writes=[a.LR[st][side]])

    def stD(gi):
        sub, g4 = groups[gi]
        st = gi % 3
        tb = sub * 16 + g4 * 4
        psW = B[6 + gi % 2]
        LR = a.LR[st]
        for tk in range(4):
            p.op("pe", lambda e, tk=tk: e.matmul(psW[:, tk * 128:(tk + 1) * 128], LR[1][:, tk, :], LR[0][:, tk, :], start=True, stop=True),
                 reads=[LR[0], LR[1]], writes=[psW])
        p.op("act", lambda e: e.copy(a.Wb[:, tb:tb + 4, :], psW.ap(0, [[128, 4], [1, 128]])), reads=[psW], writes=[a.Wb])

    d2_load(0); d2_load(1)
    for i in range(NG + 3):
        if i < NG:
            stA(i)
        if 0 <= i - 1 < NG:
            stB(i - 1)
        if 0 <= i - 2 < NG:
            stC(i - 2)
        if 0 <= i - 3 < NG:
            stD(i - 3)

    if getattr(cx, "stage", 9) < 3:
        return
    psO = [[B[0], B[1]], [B[2], B[3]]]

    def d3_load(et):
        p.dma("sp", a.ut[et % 4][:], cx.utb[et], reads=[cx.d_utb[et // 8]], writes=[a.ut[et % 4]])
        p.dma("sp", a.vt[et % 4][:], cx.vb[et * 128:(et + 1) * 128, :], reads=[cx.d_vb[et // 8]], writes=[a.vt[et % 4]])

    def d3_ux(et):
        ut = a.ut[et % 4]
        psU = B[4 + (et % 3)]
        for k in range(8):
            p.op("pe", lambda e, k=k: e.matmul(psU[:, 0:256], ut[:, k * 128:(k + 1) * 128], a.hT[:, k, :], start=(k == 0), stop=(k == 7)),
                 reads=[ut, a.hT], writes=[psU])
        G = a.G[et % 3]; A = a.A[et % 3]
        p.op("act", lambda e: e.activation(G[:], psU[:, 0:256], AF.Gelu_apprx_tanh), reads=[psU], writes=[G])
        wsl = a.Wb.ap(et, [[128, 256]])
        p.op("dve" if et % 2 == 0 else "pool", lambda e: e.tensor_tensor(A[:], G[:], wsl, ALU.mult), reads=[G, a.Wb], writes=[A])

    def d3_v(et):
        A = a.A[et % 3]; vt = a.vt[et % 4]
        for tl in range(2):
            for dh in range(2):
                p.op("pe", lambda e, tl=tl, dh=dh: e.matmul(psO[tl][dh][:], A[:, tl * 128:(tl + 1) * 128], vt[:, dh * 512:(dh + 1) * 512],
                                                            start=(et == 0), stop=(et == NET - 1)), reads=[A, vt], writes=[psO[tl][dh]])

    d3_load(0); d3_load(1); d3_load(2); d3_ux(0); d3_ux(1)
    for et in range(NET):
        if et + 3 < NET:
            d3_load(et + 3)
        if et + 2 < NET:
            d3_ux(et + 2)
        d3_v(et)
    if getattr(cx, "stage", 9) < 4:
        return
    for tl in range(2):
        r0 = tok0 + tl * 128
        xt = a.xt[tl]
        for dh in range(2):
            p.op("dve", lambda e, dh=dh: e.tensor_tensor(a.t1[:, dh * 512:(dh + 1) * 512], psO[tl][dh][:], mod["g1"][:, dh * 512:(dh + 1) * 512], ALU.mult),
                 reads=[psO[tl][dh], mod["g1"]], writes=[a.t1])
        p.op("dve", lambda e: e.scalar_tensor_tensor(a.y[:], xt[:], DN_ALPHA, a.t1[:], ALU.mult, ALU.add), reads=[xt, a.t1], writes=[a.y])
        layer_norm_rows(p, cx, a.y, a.xo, lng, lnb, a.scr)
        p.dma("act", cx.X[r0:r0 + 128, :], a.xo[:], reads=[a.xo], writes=[cx.dX[r0 // 128]])


def modulate_transpose(p, cx, xt, hm, hT_dst_fn, mod, banks):
    p.op("dve", lambda e: e.tensor_tensor(hm[:], xt[:], mod["sc1"][:], ALU.mult), reads=[xt, mod["sc1"]], writes=[hm])
    p.op("pool", lambda e: e.tensor_tensor(hm[:], hm[:], mod["sh"][:], ALU.add), reads=[hm, mod["sh"]], writes=[hm])
    for half in range(2):
        bk = banks[half]
        for k4 in range(4):
            k = half * 4 + k4
            p.op("pe", lambda e, k=k, k4=k4, bk=bk: e.transpose(bk[:, k4 * 128:(k4 + 1) * 128], hm[:, k * 128:(k + 1) * 128], cx.ident[:]),
                 reads=[hm, cx.ident], writes=[bk])
        dst, dep = hT_dst_fn(half)
        p.op("act", lambda e, bk=bk, dst=dst: e.copy(dst, bk.ap(0, [[128, 4], [1, 128]])), reads=[bk], writes=[dep])


def post_norm_store(p, cx, xt, psO, mod, lng, lnb, t1, y, scr, r0):
    for dh in range(2):
        p.op("dve", lambda e, dh=dh: e.tensor_tensor(t1[:, dh * 512:(dh + 1) * 512], psO[dh][:], mod["g1"][:, dh * 512:(dh + 1) * 512], ALU.mult),
             reads=[psO[dh], mod["g1"]], writes=[t1])
    p.op("dve", lambda e: e.scalar_tensor_tensor(y[:], xt[:], DN_ALPHA, t1[:], ALU.mult, ALU.add), reads=[xt, t1], writes=[y])
    layer_norm_rows(p, cx, y, t1, lng, lnb, scr)
    p.dma("act", cx.X[r0:r0 + 128, :], t1[:], reads=[t1], writes=[cx.dX[r0 // 128]])


def attn_alloc(p, cx):
    a = Ctx()
    a.wqkv = p.sbuf([128, 8, 1536], BF16, name="a_wqkv")
    a.wqkp = p.sbuf([128, 8, 1280], BF16, name="a_wqkp")
    a.wout = p.sbuf([128, 8, 1024], BF16, name="a_wout")
    a.esink = p.sbuf([128, 16], F32, name="a_esink")
    a.kcf = p.sbuf([128, 256], F32, name="a_kcf")
    a.KTc = p.sbuf([64, 4, 4, 128], BF16, name="a_KTc")
    a.Vc = p.sbuf([128, 4, 4, 65], BF16, name="a_Vc")
    a.xt = [p.sbuf([128, 1024], F32, name="a_xt%d" % i) for i in range(3)]
    a.hm = p.sbuf([128, 1024], F32, name="a_hm")
    a.hT = p.sbuf([128, 8, 128], BF16, name="a_hT")
    a.QT = [p.sbuf([64, 16, 128], BF16, name="a_QT%d" % i) for i in range(3)]
    a.KT = [p.sbuf([64, 4, 128], BF16, name="a_KT%d" % i) for i in range(3)]
    a.V = [p.sbuf([128, 4, 65], BF16, name="a_V%d" % i) for i in range(3)]
    a.cs = [p.sbuf([64, 2, 128], F32, name="a_cs%d" % i) for i in range(2)]
    a.r1 = p.sbuf([64, 4, 128], F32, name="a_r1")
    a.r2 = p.sbuf([64, 4, 128], F32, name="a_r2")
    a.kvtok = p.sbuf([128, 2, 256], F32, name="a_kvtok")
    a.PT = [p.sbuf([128, 4, 128], BF16, name="a_PT%d" % i) for i in range(7)]
    a.masks = p.sbuf([128, 2, 128], BF16, name="a_masks")
    a.den = p.sbuf([128, 4], F32, name="a_den")
    a.O = p.sbuf([128, 1024], F32, name="a_O")
    a.OT = p.sbuf([128, 8, 128], BF16, name="a_OT")
    a.y = p.sbuf([128, 1024], F32, name="a_y")
    a.scr = {"st": p.sbuf([128, 12], F32, name="a_st"), "mv": p.sbuf([128, 2], F32, name="a_mv"),
             "rstd": p.sbuf([128, 1], F32, name="a_rstd")}
    p.dma("pool", a.masks[:], cx.dram["c_masks"], writes=[a.masks])
    for i in range(3):
        p.op("pool", lambda e, i=i: e.memset(a.V[i][:], 1.0), writes=[a.V[i]])
    p.op("pool", lambda e: e.memset(a.Vc[:], 1.0), writes=[a.Vc])
    return a


def attn_load_layer(p, cx, a, j):
    B = cx.banks
    w = cx.dram["attn_w_qkv"][j]; wp = cx.dram["attn_w_qkp"][j]; wo = cx.dram["attn_w_out"][j]
    for k in range(8):
        p.dma("pool", a.wqkv[:, k, :], w[k * 128:(k + 1) * 128, :], writes=[a.wqkv])
        p.dma("pool", a.wqkp[:, k, :], wp[k * 128:(k + 1) * 128, :], writes=[a.wqkp])
        p.dma("pool", a.wout[:, k, :], wo[k * 128:(k + 1) * 128, :], writes=[a.wout])
    p.dma("sp", a.esink[:], cx.dram["attn_sink"][j].partition_broadcast(128), writes=[a.esink])
    p.op("act", lambda e: e.activation(a.esink[:], a.esink[:], AF.Exp), reads=[a.esink], writes=[a.esink])
    if cx.n_smp_tiles == 0:
        return
    ck = cx.dram["cache_k"][j]
    cv = cx.dram["cache_v"][j]
    for c in range(4):
        p.dma("sp", a.kcf[:], ck[c * 128:(c + 1) * 128, :], writes=[a.kcf])
        bk = B[0]
        for g in range(4):
            p.op("pe", lambda e, g=g: e.transpose(bk[0:64, g * 128:(g + 1) * 128], a.kcf[:, g * 64:(g + 1) * 64], cx.ident[:]),
                 reads=[a.kcf, cx.ident], writes=[bk])
        p.op("act", lambda e, c=c: e.copy(a.KTc[:, :, c, :], bk.ap(0, [[128, 4], [1, 128]], np_=64)), reads=[bk], writes=[a.KTc])
        p.dma("pool", a.Vc[:, c, :, 0:64], cv[c * 128:(c + 1) * 128, :].rearrange("t (g d) -> t g d", g=4), writes=[a.Vc])


def attn_project(p, cx, a, r0, slot, rope_pos, mod, want_kv_out=None):
    B = cx.banks
    xt = a.xt[slot]
    p.dma("sp", xt[:], cx.X[r0:r0 + 128, :], reads=[cx.dX[r0 // 128]], writes=[xt])
    modulate_transpose(p, cx, xt, a.hm, lambda half: (a.hT[:, half * 4:(half + 1) * 4, :], a.hT), mod, [B[0], B[1]])
    QT = a.QT[slot]; KT = a.KT[slot]; V = a.V[slot]
    if rope_pos is not None:
        cs = a.cs[rope_pos % 2]
        p.dma("sp", cs[:], cx.dram["c_rope"][:, :, rope_pos * 128:(rope_pos + 1) * 128], writes=[cs])
    for hg in range(5):
        colbase = hg * 256 if hg < 4 else 1024
        b0 = B[0]; b1 = B[1]
        for r in range(4):
            for k in range(8):
                p.op("pe", lambda e, r=r, k=k: e.matmul(b0[0:64, r * 128:(r + 1) * 128], a.wqkv[:, k, colbase + r * 64: colbase + (r + 1) * 64],
                                                        a.hT[:, k, :], start=(k == 0), stop=(k == 7)), reads=[a.wqkv, a.hT], writes=[b0])
        dst = QT[:, hg * 4:(hg + 1) * 4, :] if hg < 4 else KT[:]
        ddep = QT if hg < 4 else KT
        if rope_pos is None:
            p.op("act", lambda e, dst=dst: e.copy(dst, b0.ap(0, [[128, 4], [1, 128]], np_=64)), reads=[b0], writes=[ddep])
        else:
            for r in range(4):
                for k in range(8):
                    p.op("pe", lambda e, r=r, k=k: e.matmul(b1[0:64, r * 128:(r + 1) * 128], a.wqkp[:, k, colbase + r * 64: colbase + (r + 1) * 64],
                                                            a.hT[:, k, :], start=(k == 0), stop=(k == 7)), reads=[a.wqkp, a.hT], writes=[b1])
            cosb = cs.ap(0, [[0, 4], [1, 128]], np_=64); sinb = cs.ap(128, [[0, 4], [1, 128]], np_=64)
            p.op("dve", lambda e: e.tensor_tensor(a.r1[:], b0.ap(0, [[128, 4], [1, 128]], np_=64), cosb, ALU.mult), reads=[b0, cs], writes=[a.r1])
            p.op("dve", lambda e: e.tensor_tensor(a.r2[:], b1.ap(0, [[128, 4], [1, 128]], np_=64), sinb, ALU.mult), reads=[b1, cs], writes=[a.r2])
            p.op("pool", lambda e, dst=dst: e.tensor_tensor(dst, a.r1[:], a.r2[:], ALU.add), reads=[a.r1, a.r2], writes=[ddep])
    bv = B[2]
    for k in range(8):
        p.op("pe", lambda e, k=k: e.matmul(bv[:, 0:256], a.hT[:, k, :], a.wqkv[:, k, 1280:1536], start=(k == 0), stop=(k == 7)),
             reads=[a.hT, a.wqkv], writes=[bv])
    if want_kv_out is not None:
        for k in range(8):
            p.op("pe", lambda e, k=k: e.matmul(bv[:, 256:512], a.hT[:, k, :], a.wqkv[:, k, 1024:1280], start=(k == 0), stop=(k == 7)),
                 reads=[a.hT, a.wqkv], writes=[bv])
    p.op("act", lambda e: e.copy(V[:, :, 0:64], bv.ap(0, [[64, 4], [1, 64]])), reads=[bv], writes=[V])
    if want_kv_out is not None:
        ko, vo = want_kv_out
        p.op("dve", lambda e: e.tensor_copy(a.kvtok[:, 0, :], bv[:, 256:512]), reads=[bv], writes=[a.kvtok])
        p.op("dve", lambda e: e.tensor_copy(a.kvtok[:, 1, :], bv[:, 0:256]), reads=[bv], writes=[a.kvtok])
        p.dma("act", ko, a.kvtok[:, 0, :], reads=[a.kvtok], writes=[cx.d_out])
        p.dma("act", vo, a.kvtok[:, 1, :], reads=[a.kvtok], writes=[cx.d_out])


def attn_attend(p, cx, a, r0, slot, chunks, mod, lng, lnb):
    B = cx.banks
    QT = a.QT[slot]
    nch = len(chunks)
    for g in range(4):
        for ci, (kfn, vfn, deps, mk) in enumerate(chunks):
            bs = B[2 + (ci % 2)]
            p.op("pe", lambda e, kfn=kfn, bs=bs: e.matmul(bs[:], kfn(g), QT[:, g * 4:(g + 1) * 4, :], start=True, stop=True),
                 reads=[QT] + deps, writes=[bs])
            PT = a.PT[ci]
            p.op("act", lambda e, bs=bs, PT=PT: e.activation(PT[:], bs.ap(0, [[128, 4], [1, 128]]), AF.Exp, scale=0.125), reads=[bs], writes=[PT])
            if mk is not None:
                mb = a.masks.ap(mk * 128, [[0, 4], [1, 128]])
                p.op("pool", lambda e, PT=PT, mb=mb: e.tensor_tensor(PT[:], PT[:], mb, ALU.mult), reads=[PT, a.masks], writes=[PT])
        bo = B[4 + g]
        for r in range(4):
            for ci, (kfn, vfn, deps, mk) in enumerate(chunks):
                p.op("pe", lambda e, r=r, ci=ci, vfn=vfn: e.matmul(bo[:, r * 65:(r + 1) * 65], a.PT[ci][:, r, :], vfn(g),
                                                                   start=(ci == 0), stop=(ci == nch - 1)), reads=[a.PT[ci]] + deps, writes=[bo])
        p.op("dve", lambda e: e.tensor_tensor(a.den[:], bo.ap(64, [[65, 4]]), a.esink[:, g * 4:(g + 1) * 4], ALU.add), reads=[bo, a.esink], writes=[a.den])
        p.op("dve", lambda e: e.reciprocal(a.den[:], a.den[:]), reads=[a.den], writes=[a.den])
        p.op("dve", lambda e: e.tensor_tensor(a.O.ap(g * 256, [[64, 4], [1, 64]]), bo.ap(0, [[65, 4], [1, 64]]), a.den.ap(0, [[1, 4], [0, 64]]), ALU.mult),
             reads=[bo, a.den], writes=[a.O])
    for half in range(2):
        bk = B[half]
        for k4 in range(4):
            k = half * 4 + k4
            p.op("pe", lambda e, k=k, k4=k4, bk=bk: e.transpose(bk[:, k4 * 128:(k4 + 1) * 128], a.O[:, k * 128:(k + 1) * 128], cx.ident[:]),
                 reads=[a.O, cx.ident], writes=[bk])
        p.op("act", lambda e, half=half, bk=bk: e.copy(a.OT[:, half * 4:(half + 1) * 4, :], bk.ap(0, [[128, 4], [1, 128]])), reads=[bk], writes=[a.OT])
    psO = [B[2], B[3]]
    for dh in range(2):
        for k in range(8):
            p.op("pe", lambda e, dh=dh, k=k: e.matmul(psO[dh][:], a.OT[:, k, :], a.wout[:, k, dh * 512:(dh + 1) * 512], start=(k == 0), stop=(k == 7)),
                 reads=[a.OT, a.wout], writes=[psO[dh]])
    post_norm_store(p, cx, a.xt[slot], psO, mod, lng, lnb, a.hm, a.y, a.scr, r0)


def attn_layer(p, cx, a, j, mods, lng, lnb, kv_out):
    nS = cx.n_smp_tiles
    def loc(slot):
        return (lambda g: a.KT[slot][:, g, :]), (lambda g: a.V[slot][:, g, :]), [a.KT[slot], a.V[slot]]
    if nS:
        attn_project(p, cx, a, 0, 0, 0, mods["smp"])
    for i in range(nS):
        if i + 1 < nS:
            attn_project(p, cx, a, (i + 1) * 128, (i + 1) % 3, i + 1, mods["smp"])
        chunks = []
        for c in range(4):
            chunks.append(((lambda g, c=c: a.KTc[:, g, c, :]), (lambda g, c=c: a.Vc[:, c, g, :]), [a.KTc, a.Vc], None))
        if i > 0:
            k_, v_, d_ = loc((i - 1) % 3); chunks.append((k_, v_, d_, 0))
        k_, v_, d_ = loc(i % 3); chunks.append((k_, v_, d_, None))
        if i + 1 < nS:
            k_, v_, d_ = loc((i + 1) % 3); chunks.append((k_, v_, d_, 1))
        attn_attend(p, cx, a, i * 128, i % 3, chunks, mods["smp"], lng, lnb)
    for s in range(cx.n_prm_seq):
        base = nS * 128 + s * 256
        for t in range(2):
            attn_project(p, cx, a, base + t * 128, t, None, mods["ctx"], want_kv_out=kv_out(s, t))
        for t in range(2):
            chunks = []
            for t2 in range(2):
                k_, v_, d_ = loc(t2); chunks.append((k_, v_, d_, None))
            attn_attend(p, cx, a, base + t * 128, t, chunks, mods["ctx"], lng, lnb)


def host_consts():
    C = {}
    C["c_ident"] = np.eye(128, dtype=np.float32)
    e3 = np.zeros((128, 128), np.float32)
    for q in range(4):
        for part in range(3):
            for hh in range(8):
                e3[32 * q + part * 8 + hh, 16 * hh:16 * hh + 16] = 1
    C["c_e3"] = e3
    kk = np.arange(128)[:, None]; qq = np.arange(128)[None, :]
    C["c_masks"] = np.stack([(kk >= qq), (kk <= qq)], axis=1).astype(np.float32)
    pos = np.arange(4096)
    row = (pos // 64).astype(np.float32); col = (pos % 64).astype(np.float32)
    inv = (10000.0 ** (-np.arange(16, dtype=np.float32) / 16)).astype(np.float32)
    rope = np.zeros((64, 2, 4096), np.float32)
    for d in range(64):
        axis = row if d < 32 else col
        ang = (axis * inv[d % 16]).astype(np.float32)
        first = (d % 32) < 16
        rope[d, 0] = np.cos(ang)
        rope[d, 1] = -np.sin(ang) if first else np.sin(ang)
    C["c_rope"] = rope
    return C


def perm_qk(wqkv):
    idx = np.arange(1280)
    d = idx % 64
    partner = np.where((d % 32) < 16, idx + 16, idx - 16)
    return np.ascontiguousarray(wqkv[..., partner])


class Scope:
    def __init__(self, p):
        self.p = p

    def __enter__(self):
        self.saved = self.p.es
        self.p.es = ExitStack()
        return self

    def __exit__(self, *a):
        p = self.p
        allc = list(p.cnt.items())
        for e in ("pe", "act", "dve", "pool", "sp"):
            p._need(e, allc)
        p.es.close()
        p.es = self.saved
        return False


TWO_PI = 2.0 * np.pi
JB = 32


def _tt(p, eng, out, in0, in1, op, reads, writes):
    p.op(eng, lambda e: e.tensor_tensor(out, in0, in1, op), reads=reads, writes=writes)


def ssm_precompute(p, cx, R, jl, d):
    B = cx.banks
    G = 64
    with Scope(p):
        pg = p.sbuf([64, 3, 64], F32); Bt = p.sbuf([64, 2, 1024], F32); Ct = p.sbuf([64, 2, 1024], F32)
        PW = p.sbuf([64, 17, 2, 64], F32); sm = p.sbuf([64, 8, 64], F32); Bb = p.sbuf([64, 2, 1024], F32)
        WT = p.sbuf([64, 8, 2, 128], F32); XT = p.sbuf([64, 8, 2, 128], F32); Mt = p.sbuf([64, 8, 2, 128], F32)
        XTb = p.sbuf([64, 8, 2, 128], BF16); z = [p.sbuf([64, 8, 16], F32) for _ in range(4)]
        tmask = p.sbuf([128, 128], F32)
        qi = p.sbuf([64, 64], mybir.dt.int32)
        p.dma("sp", pg[:], cx.dram["ssm_pg"][jl, d].rearrange("k p g -> p k g"), writes=[pg])
        p.dma("sp", Bt[:], cx.dram["ssm_b"][jl, d].rearrange("k p x -> p k x"), writes=[Bt])
        p.dma("sp", Ct[:], cx.dram["ssm_c"][jl, d].rearrange("k p x -> p k x"), writes=[Ct])
        p.dma("sp", tmask[:], cx.dram["c_tmask"][d], writes=[tmask])
        DT, LR, LI, T0, T1, T2, CR, CI = range(8)
        p.op("act", lambda e: e.activation(sm[:, DT, :], pg[:, 2, :], AF.Exp), reads=[pg], writes=[sm])
        _tt(p, "dve", sm[:, LR, :], pg[:, 0, :], sm[:, DT, :], ALU.mult, [pg, sm], [sm])
        _tt(p, "dve", sm[:, LI, :], pg[:, 1, :], sm[:, DT, :], ALU.mult, [pg, sm], [sm])
        for k in range(-8, 9):
            kk = k + 8
            p.op("act", lambda e, k=k: e.activation(sm[:, T0, :], sm[:, LR, :], AF.Exp, scale=float(k)), reads=[sm], writes=[sm])
            for ri, ph in ((1, 0.0), (0, np.pi / 2)):
                p.op("dve", lambda e, k=k, ph=ph: e.tensor_scalar(sm[:, T1, :], sm[:, LI, :], float(k / TWO_PI), float(ph / TWO_PI + 64.0), ALU.mult, ALU.add),
                     reads=[sm], writes=[sm])
                p.op("dve", lambda e: e.tensor_copy(qi[:], sm[:, T1, :]), reads=[sm], writes=[qi])
                p.op("dve", lambda e: e.tensor_copy(sm[:, T2, :], qi[:]), reads=[qi], writes=[sm])
                _tt(p, "dve", sm[:, T1, :], sm[:, T1, :], sm[:, T2, :], ALU.subtract, [sm], [sm])
                p.op("dve", lambda e: e.tensor_single_scalar(sm[:, T2, :], sm[:, T1, :], 0.5, ALU.is_gt), reads=[sm], writes=[sm])
                _tt(p, "dve", sm[:, T1, :], sm[:, T1, :], sm[:, T2, :], ALU.subtract, [sm], [sm])
                p.op("act", lambda e: e.activation(sm[:, T2, :], sm[:, T1, :], AF.Sin, scale=float(TWO_PI)), reads=[sm], writes=[sm])
                _tt(p, "dve", PW[:, kk, ri, :], sm[:, T0, :], sm[:, T2, :], ALU.mult, [sm], [PW])
        p.op("act", lambda e: e.copy(R.A2[:, 0, :], PW[:, 16, 0, :]), reads=[PW], writes=[R.A2])
        p.op("act", lambda e: e.copy(R.A2[:, 1, :], PW[:, 16, 0, :]), reads=[PW], writes=[R.A2])
        p.op("act", lambda e: e.mul(R.AI2[:, 0, :], PW[:, 16, 1, :], -1.0), reads=[PW], writes=[R.AI2])
        p.op("act", lambda e: e.copy(R.AI2[:, 1, :], PW[:, 16, 1, :]), reads=[PW], writes=[R.AI2])
        are = pg[:, 0, :]; aim = pg[:, 1, :]
        p.op("dve", lambda e: e.tensor_scalar(sm[:, T0, :], PW[:, 9, 0, :], -1.0, None, ALU.add), reads=[PW], writes=[sm])
        _tt(p, "dve", sm[:, T1, :], are, are, ALU.mult, [pg], [sm])
        _tt(p, "dve", sm[:, T2, :], aim, aim, ALU.mult, [pg], [sm])
        _tt(p, "dve", sm[:, T1, :], sm[:, T1, :], sm[:, T2, :], ALU.add, [sm], [sm])
        p.op("dve", lambda e: e.reciprocal(sm[:, T1, :], sm[:, T1, :]), reads=[sm], writes=[sm])
        _tt(p, "dve", sm[:, CR, :], sm[:, T0, :], are, ALU.mult, [sm, pg], [sm])
        _tt(p, "dve", sm[:, T2, :], PW[:, 9, 1, :], aim, ALU.mult, [PW, pg], [sm])
        _tt(p, "dve", sm[:, CR, :], sm[:, CR, :], sm[:, T2, :], ALU.add, [sm], [sm])
        _tt(p, "dve", sm[:, CR, :], sm[:, CR, :], sm[:, T1, :], ALU.mult, [sm], [sm])
        _tt(p, "dve", sm[:, CI, :], PW[:, 9, 1, :], are, ALU.mult, [PW, pg], [sm])
        _tt(p, "dve", sm[:, T2, :], sm[:, T0, :], aim, ALU.mult, [sm, pg], [sm])
        _tt(p, "dve", sm[:, CI, :], sm[:, CI, :], sm[:, T2, :], ALU.subtract, [sm], [sm])
        _tt(p, "dve", sm[:, CI, :], sm[:, CI, :], sm[:, T1, :], ALU.mult, [sm], [sm])
        crb = sm.ap(CR * 64, [[1, 64], [0, 16]], np_=64); cib = sm.ap(CI * 64, [[1, 64], [0, 16]], np_=64)
        Br = Bt.ap(0, [[16, 64], [1, 16]], np_=64); Bi = Bt.ap(1024, [[16, 64], [1, 16]], np_=64)
        Bbr = Bb.ap(0, [[16, 64], [1, 16]], np_=64); Bbi = Bb.ap(1024, [[16, 64], [1, 16]], np_=64)
        big0 = Ct
        tA = XT.ap(0, [[16, 64], [1, 16]], np_=64); tB = XT.ap(1024, [[16, 64], [1, 16]], np_=64)
        _tt(p, "dve", tA, Br, crb, ALU.mult, [Bt, sm], [XT])
        _tt(p, "dve", tB, Bi, cib, ALU.mult, [Bt, sm], [XT])
        _tt(p, "dve", Bbr, tA, tB, ALU.subtract, [XT], [Bb])
        _tt(p, "dve", tA, Br, cib, ALU.mult, [Bt, sm], [XT])
        _tt(p, "dve", tB, Bi, crb, ALU.mult, [Bt, sm], [XT])
        _tt(p, "dve", Bbi, tA, tB, ALU.add, [XT], [Bb])
        for gb in range(8):
            g0 = gb * 8
            def pw(k, ri):
                return PW.ap(((k + 8) * 2 + ri) * 64 + g0, [[1, 8], [0, 16]], np_=64)
            def cplx(dst, ri_stride, slot, Xr, Xi, k, deps_x, neg_im=False):
                dr = dst.ap(slot * 16, [[256, 8], [1, 16]], np_=64); di = dst.ap(128 + slot * 16, [[256, 8], [1, 16]], np_=64)
                _tt(p, "dve", z[0][:], Xr, pw(k, 0), ALU.mult, deps_x + [PW], [z[0]])
                _tt(p, "pool", z[1][:], Xi, pw(k, 1), ALU.mult, deps_x + [PW], [z[1]])
                _tt(p, "dve", dr, z[0][:], z[1][:], ALU.subtract, [z[0], z[1]], [dst])
                _tt(p, "pool", z[2][:], Xr, pw(k, 1), ALU.mult, deps_x + [PW], [z[2]])
                _tt(p, "dve", z[3][:], Xi, pw(k, 0), ALU.mult, deps_x + [PW], [z[3]])
                if neg_im:
                    p.op("dve", lambda e: e.scalar_tensor_tensor(di, z[2][:], -1.0, z[3][:], ALU.mult, ALU.subtract), reads=[z[2], z[3]], writes=[dst])
                else:
                    _tt(p, "dve", di, z[2][:], z[3][:], ALU.add, [z[2], z[3]], [dst])
            bbr = Bb.ap(g0 * 16, [[16, 8], [1, 16]], np_=64); bbi = Bb.ap(1024 + g0 * 16, [[16, 8], [1, 16]], np_=64)
            ccr = Ct.ap(g0 * 16, [[16, 8], [1, 16]], np_=64); cci = Ct.ap(1024 + g0 * 16, [[16, 8], [1, 16]], np_=64)
            for s in range(8):
                kw = (7 - s) if d == 0 else s
                kx = (-s - 1) if d == 0 else (s - 8)
                km = (s + 1) if d == 0 else (8 - s)
                cplx(WT, 0, s, bbr, bbi, kw, [Bb])
                cplx(XT, 0, s, bbr, bbi, kx, [Bb])
                cplx(Mt, 0, s, ccr, cci, km, [Ct], neg_im=True)
            p.op("act", lambda e: e.copy(XTb[:], XT[:]), reads=[XT], writes=[XTb])
            p.op("act", lambda e, g0=g0: e.copy(R.M[:, g0:g0 + 8, :, :], Mt[:]), reads=[Mt], writes=[R.M])
            for gl in range(8):
                g = g0 + gl
                bk = B[gl % 2]
                for ri in range(2):
                    p.op("pe", lambda e, gl=gl, ri=ri, bk=bk: e.transpose(bk[:, ri * 64:(ri + 1) * 64], WT[:, gl, ri, :], cx.ident[0:64, 0:64]),
                         reads=[WT, cx.ident], writes=[bk])
                p.op("act", lambda e, g=g, bk=bk: e.copy(R.Win[:, g, :], bk[:, 0:128]), reads=[bk], writes=[R.Win])
                bt = B[2 + gl % 2]
                for ri in range(2):
                    p.op("pe", lambda e, gl=gl, g=g, ri=ri, bt=bt: e.matmul(bt[:, 0:128], XTb[:, gl, ri, :], R.M[:, g, ri, :], start=(ri == 0), stop=(ri == 1)),
                         reads=[XTb, R.M], writes=[bt])
                p.op("dve", lambda e, g=g, bt=bt: e.tensor_tensor(R.Toep[:, g, :], bt[:, 0:128], tmask[:], ALU.mult), reads=[bt, tmask], writes=[R.Toep])


def ssm_sweep(p, cx, R, jl, d, seqs):
    B = cx.banks
    with Scope(p):
        u8 = p.sbuf([128, JB, 64], BF16); S = p.sbuf([64, 2, 64, JB], F32); Hs = [p.sbuf([64, 2, 64], F32) for _ in range(2)]; cur = [0]
        t1 = p.sbuf([64, 2, 64], F32); t2 = p.sbuf([64, 2, 64], F32); Hh = p.sbuf([64, 2, 64, JB + 1], BF16)
        Y8 = p.sbuf([128, JB, 64], F32); Yf = p.sbuf([128, JB, 64], F32); d8 = p.sbuf([128, 64], F32)
        hs = p.sbuf([64, 2, 64], F32)
        p.dma("sp", d8[:], cx.dram["ssm_d8"][jl], writes=[d8])
        udv = cx.Ud.rearrange("(j s) (c g) -> s c j g", s=8, c=16)
        ydv = cx.Yd.rearrange("(j s) (c g) -> s c j g", s=8, c=16)
        for sq in seqs:
            nb = sq["n"] // JB
            H = Hs[cur[0]]
            if sq["init"] is not None:
                p.dma("sp", H[:], sq["init"], writes=[H])
            else:
                p.op("dve", lambda e, H=H: e.memset(H[:], 0.0), writes=[H])
            for bi in (range(nb) if d == 0 else range(nb - 1, -1, -1)):
                j0 = sq["j0"] + bi * JB
                tl = (j0 * 8) // 128
                ddeps = [cx.dUd[tl], cx.dUd[tl + 1]]
                for s in range(8):
                    p.dma("pool", u8[16 * s:16 * s + 16, :, :], udv[s, :, j0:j0 + JB, :], reads=ddeps, writes=[u8])
                for gq in range(8):
                    bk = B[gq % 2]
                    for gl in range(8):
                        g = gq * 8 + gl
                        for ri in range(2):
                            p.op("pe", lambda e, g=g, gl=gl, ri=ri, bk=bk: e.matmul(bk[0:64, (gl * 2 + ri) * JB:(gl * 2 + ri + 1) * JB], R.Win[:, g, ri * 64:(ri + 1) * 64],
                                                                                   u8.ap(g, [[64, JB]]), start=True, stop=True), reads=[R.Win, u8], writes=[bk])
                    p.op("act", lambda e, gq=gq, bk=bk: e.copy(S.ap(gq * 8 * JB, [[JB, 8], [64 * JB, 2], [1, JB]], np_=64),
                                                               bk.ap(0, [[2 * JB, 8], [JB, 2], [1, JB]], np_=64)), reads=[bk], writes=[S])
                hidx0 = 0 if d == 0 else JB
                p.op("act", lambda e: e.copy(Hh.ap(hidx0, [[JB + 1, 128]], np_=64), Hs[cur[0]].ap(0, [[1, 128]], np_=64)), reads=[Hs[cur[0]]], writes=[Hh])
                for jj in (range(JB) if d == 0 else range(JB - 1, -1, -1)):
                    H = Hs[cur[0]]; Hn = Hs[1 - cur[0]]
                    _tt(p, "dve", t1[:], H[:], R.A2[:], ALU.mult, [H, R.A2], [t1])
                    _tt(p, "dve", t2[:, 0, :], H[:, 1, :], R.AI2[:, 0, :], ALU.mult, [H, R.AI2], [t2])
                    _tt(p, "dve", t2[:, 1, :], H[:, 0, :], R.AI2[:, 1, :], ALU.mult, [H, R.AI2], [t2])
                    _tt(p, "dve", t1[:], t1[:], t2[:], ALU.add, [t1, t2], [t1])
                    _tt(p, "dve", Hn[:], t1[:], S.ap(jj, [[64 * JB, 2], [JB, 64]], np_=64), ALU.add, [t1, S], [Hn])
                    hi = (jj + 1) if d == 0 else jj
                    p.op("act", lambda e, hi=hi, Hn=Hn: e.copy(Hh.ap(hi, [[JB + 1, 128]], np_=64), Hn.ap(0, [[1, 128]], np_=64)), reads=[Hn], writes=[Hh])
                    cur[0] = 1 - cur[0]
                hoff = 0 if d == 0 else 1
                for gq in range(4):
                    bk = B[2 + gq]
                    for gl in range(16):
                        g = gq * 16 + gl
                        o = bk[:, gl * JB:(gl + 1) * JB]
                        p.op("pe", lambda e, g=g, o=o: e.matmul(o, R.Toep[:, g, :], u8.ap(g, [[64, JB]]), start=True, stop=False), reads=[R.Toep, u8], writes=[bk])
                        for ri in range(2):
                            p.op("pe", lambda e, g=g, o=o, ri=ri: e.matmul(o, R.M[:, g, ri, :], Hh.ap((ri * 64 + g) * (JB + 1) + hoff, [[1, JB]], np_=64),
                                                                           start=False, stop=(ri == 1)), reads=[R.M, Hh], writes=[bk])
                    p.op("act", lambda e, gq=gq, bk=bk: e.copy(Y8.ap(gq * 16, [[1, 16], [64, JB]]), bk.ap(0, [[JB, 16], [1, JB]])), reads=[bk], writes=[Y8])
                if d == 0:
                    p.dma("sp", cx.Yf8[:, j0:j0 + JB, :], Y8[:], reads=[Y8], writes=[cx.dYf[j0 // JB]])
                else:
                    p.dma("sp", Yf[:], cx.Yf8[:, j0:j0 + JB, :], reads=[cx.dYf[j0 // JB]], writes=[Yf])
                    _tt(p, "dve", Y8[:], Y8[:], Yf[:], ALU.add, [Y8, Yf], [Y8])
                    _tt(p, "pool", Yf[:], u8[:], d8.ap(0, [[0, JB], [1, 64]]), ALU.mult, [u8, d8], [Yf])
                    _tt(p, "dve", Y8[:], Y8[:], Yf[:], ALU.add, [Y8, Yf], [Y8])
                    p.op("act", lambda e: e.activation(Y8[:], Y8[:], AF.Gelu_apprx_tanh), reads=[Y8], writes=[Y8])
                    for s in range(8):
                        p.dma("sp", ydv[s, :, j0:j0 + JB, :], Y8[16 * s:16 * s + 16, :, :], reads=[Y8], writes=[cx.dYd[tl], cx.dYd[tl + 1]])
            if sq["out"] is not None:
                bk = B[0]
                for ri in range(2):
                    p.op("pe", lambda e, ri=ri: e.transpose(bk[0:64, ri * 64:(ri + 1) * 64], Hs[cur[0]][:, ri, :], cx.ident[0:64, 0:64]), reads=[Hs[cur[0]], cx.ident], writes=[bk])
                p.op("act", lambda e: e.copy(hs[:], bk.ap(0, [[64, 2], [1, 64]], np_=64)), reads=[bk], writes=[hs])
                p.dma("act", sq["out"].rearrange("r g q -> g r q"), hs[:], reads=[hs], writes=[cx.d_out])


def ssm_layer(p, cx, jl, mods, lng, lnb, seqs_fn):
    B = cx.banks
    NTt = cx.NT // 128
    nS = cx.n_smp_tiles
    with Scope(p):
        win = p.sbuf([128, 8, 1024], BF16)
        xt = [p.sbuf([128, 1024], F32) for _ in range(2)]; hm = p.sbuf([128, 1024], F32); hT = p.sbuf([128, 8, 128], BF16)
        uo = [p.sbuf([128, 1024], F32) for _ in range(2)]
        w = cx.dram["ssm_w_in"][jl]
        for k in range(8):
            p.dma("pool", win[:, k, :], w[k * 128:(k + 1) * 128, :], writes=[win])
        for i in range(NTt):
            mod = mods["smp"] if i < nS else mods["ctx"]
            x_ = xt[i % 2]
            p.dma("sp", x_[:], cx.X[i * 128:(i + 1) * 128, :], reads=[cx.dX[i]], writes=[x_])
            modulate_transpose(p, cx, x_, hm, lambda half: (hT[:, half * 4:(half + 1) * 4, :], hT), mod, [B[0], B[1]])
            for dh in range(2):
                bk = B[2 + dh]
                for k in range(8):
                    p.op("pe", lambda e, dh=dh, k=k, bk=bk: e.matmul(bk[:], hT[:, k, :], win[:, k, dh * 512:(dh + 1) * 512], start=(k == 0), stop=(k == 7)),
                         reads=[hT, win], writes=[bk])
                p.op("act" if dh == 0 else "dve", (lambda e, dh=dh, bk=bk, i=i: e.copy(uo[i % 2][:, dh * 512:(dh + 1) * 512], bk[:])) if dh == 0 else
                     (lambda e, dh=dh, bk=bk, i=i: e.tensor_copy(uo[i % 2][:, dh * 512:(dh + 1) * 512], bk[:])), reads=[bk], writes=[uo[i % 2]])
            p.dma("act", cx.Ud[i * 128:(i + 1) * 128, :], uo[i % 2][:], reads=[uo[i % 2]], writes=[cx.dUd[i]])
    with Scope(p):
        R = Ctx()
        R.Win = p.sbuf([128, 64, 128], BF16); R.Toep = p.sbuf([128, 64, 128], BF16); R.M = p.sbuf([64, 64, 2, 128], BF16)
        R.A2 = p.sbuf([64, 2, 64], F32); R.AI2 = p.sbuf([64, 2, 64], F32)
        for d in range(2):
            ssm_precompute(p, cx, R, jl, d)
            ssm_sweep(p, cx, R, jl, d, seqs_fn(d))
    with Scope(p):
        wg = p.sbuf([128, 8, 2048], BF16); wo = p.sbuf([128, 8, 1024], BF16)
        xt = [p.sbuf([128, 1024], F32) for _ in range(2)]; yt = [p.sbuf([128, 1024], F32) for _ in range(2)]
        yT = p.sbuf([128, 8, 128], BF16); sg = p.sbuf([128, 1024], F32); vv = p.sbuf([128, 1024], F32); vT = p.sbuf([128, 8, 128], BF16)
        t1 = p.sbuf([128, 1024], F32); y = p.sbuf([128, 1024], F32)
        scr = {"st": p.sbuf([128, 12], F32), "mv": p.sbuf([128, 2], F32), "rstd": p.sbuf([128, 1], F32)}
        for k in range(8):
            p.dma("pool", wg[:, k, :], cx.dram["ssm_w_glu"][jl][k * 128:(k + 1) * 128, :], writes=[wg])
            p.dma("pool", wo[:, k, :], cx.dram["ssm_w_out"][jl][k * 128:(k + 1) * 128, :], writes=[wo])
        for i in range(NTt):
            mod = mods["smp"] if i < nS else mods["ctx"]
            x_ = xt[i % 2]; y_ = yt[i % 2]
            p.dma("sp", x_[:], cx.X[i * 128:(i + 1) * 128, :], reads=[cx.dX[i]], writes=[x_])
            p.dma("sp", y_[:], cx.Yd[i * 128:(i + 1) * 128, :], reads=[cx.dYd[i]], writes=[y_])
            for half in range(2):
                bk = B[half]
                for k4 in range(4):
                    k = half * 4 + k4
                    p.op("pe", lambda e, k=k, k4=k4, bk=bk: e.transpose(bk[:, k4 * 128:(k4 + 1) * 128], y_[:, k * 128:(k + 1) * 128], cx.ident[:]), reads=[y_, cx.ident], writes=[bk])
                p.op("act", lambda e, half=half, bk=bk: e.copy(yT[:, half * 4:(half + 1) * 4, :], bk.ap(0, [[128, 4], [1, 128]])), reads=[bk], writes=[yT])
            for q in range(4):
                bk = B[2 + q]
                for k in range(8):
                    p.op("pe", lambda e, q=q, k=k, bk=bk: e.matmul(bk[:], yT[:, k, :], wg[:, k, q * 512:(q + 1) * 512], start=(k == 0), stop=(k == 7)), reads=[yT, wg], writes=[bk])
            for q in range(2):
                p.op("act", lambda e, q=q: e.activation(sg[:, q * 512:(q + 1) * 512], B[4 + q][:], AF.Sigmoid), reads=[B[4 + q]], writes=[sg])
                p.op("dve", lambda e, q=q: e.tensor_tensor(vv[:, q * 512:(q + 1) * 512], B[2 + q][:], sg[:, q * 512:(q + 1) * 512], ALU.mult), reads=[B[2 + q], sg], writes=[vv])
            for half in range(2):
                bk = B[half]
                for k4 in range(4):
                    k = half * 4 + k4
                    p.op("pe", lambda e, k=k, k4=k4, bk=bk: e.transpose(bk[:, k4 * 128:(k4 + 1) * 128], vv[:, k * 128:(k + 1) * 128], cx.ident[:]), reads=[vv, cx.ident], writes=[bk])
                p.op("act", lambda e, half=half, bk=bk: e.copy(vT[:, half * 4:(half + 1) * 4, :], bk.ap(0, [[128, 4], [1, 128]])), reads=[bk], writes=[vT])
            psO = [B[6], B[7]]
            for dh in range(2):
                for k in range(8):
                    p.op("pe", lambda e, dh=dh, k=k: e.matmul(psO[dh][:], vT[:, k, :], wo[:, k, dh * 512:(dh + 1) * 512], start=(k == 0), stop=(k == 7)), reads=[vT, wo], writes=[psO[dh]])
            post_norm_store(p, cx, x_, psO, mod, lng, lnb, t1, y, scr, i * 128)


def host_ssm_layouts(inp, jl):
    o = {}
    pg = np.stack([np.stack([inp["ssm_a_re"][jl, d].T, inp["ssm_a_im"][jl, d].T,
                             np.broadcast_to(inp["ssm_log_dt"][jl, d][None, :], (64, 64))]) for d in range(2)])
    o["ssm_pg"] = np.ascontiguousarray(pg, dtype=np.float32)
    o["ssm_b"] = np.ascontiguousarray(np.stack([np.stack([inp["ssm_b_re"][jl, d].transpose(1, 0, 2).reshape(64, 1024),
                                                          inp["ssm_b_im"][jl, d].transpose(1, 0, 2).reshape(64, 1024)]) for d in range(2)]))
    o["ssm_c"] = np.ascontiguousarray(np.stack([np.stack([inp["ssm_c_re"][jl, d].transpose(2, 0, 1).reshape(64, 1024),
                                                          inp["ssm_c_im"][jl, d].transpose(2, 0, 1).reshape(64, 1024)]) for d in range(2)]))
    dd = inp["ssm_d"][jl].reshape(64, 16).T
    o["ssm_d8"] = np.ascontiguousarray(np.tile(dd, (8, 1)))
    perm = (np.arange(64)[None, :] * 16 + np.arange(16)[:, None]).reshape(-1)
    o["ssm_w_in"] = np.ascontiguousarray(inp["ssm_w_in"][jl][:, perm])
    o["ssm_w_glu"] = np.ascontiguousarray(inp["ssm_w_glu"][jl][perm, :])
    o["ssm_w_out"] = np.ascontiguousarray(inp["ssm_w_out"][jl])
    return o


def host_tmask():
    s = np.arange(128)[:, None] // 16; t = np.arange(128)[None, :] // 16
    return np.stack([(s <= t), (s >= t)]).astype(np.float32)


def compute_mods(p, cx, l, half):
    B = cx.banks
    names = ["sh", "sc1", "g1"]
    with Scope(p):
        wm = [p.sbuf([128, 8, 512], BF16) for _ in range(2)]
        bm = [p.sbuf([128, 512], F32) for _ in range(2)]
        it = 0
        for v in range(3):
            for dh in range(2):
                col = half * 3072 + v * 1024 + dh * 512
                w_ = wm[it % 2]; b_ = bm[it % 2]
                p.dma("pool", w_[:], cx.dram["w_mod"][l][:, col:col + 512].rearrange("(k q) c -> q k c", q=128), writes=[w_])
                p.dma("sp", b_[:], cx.dram["b_mod"][l][col:col + 512].partition_broadcast(128), writes=[b_])
                for ci, cn in enumerate(("smp", "ctx")):
                    bk = B[(it * 2 + ci) % 8]
                    for k in range(8):
                        p.op("pe", lambda e, k=k, ci=ci, bk=bk, w_=w_: e.matmul(bk[:], cx.SCb[ci][:, k, :], w_[:, k, :], start=(k == 0), stop=(k == 7)),
                             reads=[cx.SCb[ci], w_], writes=[bk])
                    dst = cx.mods[cn][names[v]]
                    p.op("dve", lambda e, bk=bk, dst=dst, b_=b_, v=v, dh=dh: e.scalar_tensor_tensor(
                        dst[:, dh * 512:(dh + 1) * 512], bk[:], 1.0 if v >= 1 else 0.0, b_[:], ALU.add, ALU.add), reads=[bk, b_], writes=[dst])
                it += 1


def load_ln(p, cx, l, which):
    p.dma("sp", cx.lng[:], cx.dram["ln_g"][l, which].partition_broadcast(128), writes=[cx.lng])
    p.dma("sp", cx.lnb[:], cx.dram["ln_b"][l, which].partition_broadcast(128), writes=[cx.lnb])


NS_TILES = 32
NP_SEQ = 4
NTOK = NS_TILES * 128 + NP_SEQ * 256


def build_program(nS=NS_TILES, nP=NP_SEQ, depth=DEPTH):
    nc = bass.Bass("TRN2", target_bir_lowering=False)
    p = Prog(nc)
    cx = Ctx()
    NT = nS * 128 + nP * 256
    NTt = NT // 128
    NCH = NT // 8
    cx.NT = NT; cx.n_smp_tiles = nS; cx.n_prm_seq = nP

    def din(name, shape, dt=F32):
        return nc.dram_tensor(name, list(shape), dt, kind="ExternalInput").ap()

    def dout(name, shape):
        return nc.dram_tensor(name, list(shape), F32, kind="ExternalOutput").ap()

    cx.dram = {
        "c_ident": din("c_ident", [128, 128]), "c_e3": din("c_e3", [128, 128]), "c_masks": din("c_masks", [128, 2, 128]),
        "c_rope": din("c_rope", [64, 2, 4096]), "c_tmask": din("c_tmask", [2, 128, 128]),
        "w_mod": din("w_mod", [4, 1024, 6144]), "b_mod": din("b_mod", [4, 6144]), "ln_g": din("ln_g", [4, 2, 1024]), "ln_b": din("ln_b", [4, 2, 1024]),
        "ssm_pg": din("ssm_pg", [2, 2, 3, 64, 64]), "ssm_b": din("ssm_b", [2, 2, 2, 64, 1024]), "ssm_c": din("ssm_c", [2, 2, 2, 64, 1024]),
        "ssm_d8": din("ssm_d8", [2, 128, 64]), "ssm_w_in": din("ssm_w_in", [2, 1024, 1024]), "ssm_w_glu": din("ssm_w_glu", [2, 1024, 2048]),
        "ssm_w_out": din("ssm_w_out", [2, 1024, 1024]),
        "attn_w_qkv": din("attn_w_qkv", [2, 1024, 1536]), "attn_w_qkp": din("attn_w_qkp", [2, 1024, 1280]),
        "attn_w_out": din("attn_w_out", [2, 1024, 1024]), "attn_sink": din("attn_sink", [2, 16]),
        "cache_k": din("cache_k", [2, 512, 256]), "cache_v": din("cache_v", [2, 512, 256]),
        "peer_w_q": din("peer_w_q", [4, 1024, 2048]), "peer_keysT": din("peer_keysT", [4, 128, 16, 128]),
        "peer_ut": din("peer_ut", [4, 128, 128, 1024]), "peer_v": din("peer_v", [4, NEXP, 1024]),
    }
    x_in = din("x_in", [NT, 1024]); cond2 = din("cond2", [2, 128, 8]); st0 = din("st0", [2, 2, 64, 2, 64])
    y_out = dout("y_out", [NT, 1024]); o_state = dout("o_state", [nP, 2, 2, 2, 64, 64])
    o_k = dout("o_k", [nP, 2, 256, 256]); o_v = dout("o_v", [nP, 2, 256, 256])
    cx.X = nc.dram_tensor("Xs", [NT, 1024], F32).ap()
    cx.Ud = nc.dram_tensor("Ud", [NT, 1024], F32).ap(); cx.Yd = nc.dram_tensor("Yd", [NT, 1024], F32).ap()
    cx.Yf8 = nc.dram_tensor("Yf8", [128, NCH, 64], F32).ap()
    cx.spd = nc.dram_tensor("spd", [2, 3, 8, NT, 128], BF16).ap()
    cx.utb = nc.dram_tensor("utb", [128, 128, 1024], BF16).ap(); cx.vb = nc.dram_tensor("vb", [NEXP, 1024], BF16).ap()
    cx.dX = [Dep() for _ in range(NTt)]; cx.dUd = [Dep() for _ in range(NTt)]; cx.dYd = [Dep() for _ in range(NTt)]
    cx.dYf = [Dep() for _ in range(NCH // JB)]; cx.d_spd = [Dep() for _ in range(NTt)]
    cx.d_utb = [Dep() for _ in range(16)]; cx.d_vb = [Dep() for _ in range(16)]
    cx.d_out = Dep()
    setup_common(p, cx)
    for i in range(NTt):
        p.dma("sp", cx.X[i * 128:(i + 1) * 128, :], x_in[i * 128:(i + 1) * 128, :], writes=[cx.dX[i]])
    cx.mods = {cn: {n: p.sbuf([128, 1024], F32, name="mod_%s_%s" % (cn, n)) for n in ("sh", "sc1", "g1")} for cn in ("smp", "ctx")}
    cx.lng = p.sbuf([128, 1024], F32, name="lng"); cx.lnb = p.sbuf([128, 1024], F32, name="lnb")
    cx.SCb = [p.sbuf([128, 8, 128], BF16, name="SCb%d" % i) for i in range(2)]
    ctile = p.sbuf([128, 2, 8], F32, name="condt")
    p.dma("sp", ctile[:], cond2.rearrange("a q k -> q a k"), writes=[ctile])
    p.op("act", lambda e: e.activation(ctile[:], ctile[:], AF.Silu), reads=[ctile], writes=[ctile])
    for ci in range(2):
        p.op("dve", lambda e, ci=ci: e.tensor_copy(cx.SCb[ci][:], ctile.ap(ci * 8, [[1, 8], [0, 128]])), reads=[ctile], writes=[cx.SCb[ci]])

    for l in range(depth):
        peer_prep_weights(p, cx, l)
        compute_mods(p, cx, l, 0)
        load_ln(p, cx, l, 0)
        if l % 2 == 0:
            jl = l // 2

            def seqs_fn(d, jl=jl):
                L = []
                if nS:
                    L.append(dict(j0=0, n=nS * 16, init=st0[jl, d], out=None))
                for s in range(nP):
                    L.append(dict(j0=nS * 16 + s * 32, n=32, init=None, out=o_state[s, jl, d]))
                return L
            ssm_layer(p, cx, jl, cx.mods, cx.lng, cx.lnb, seqs_fn)
        else:
            j = l // 2
            with Scope(p):
                a = attn_alloc(p, cx)
                attn_load_layer(p, cx, a, j)
                attn_layer(p, cx, a, j, cx.mods, cx.lng, cx.lnb,
                           lambda s, t, j=j: (o_k[s, j, t * 128:(t + 1) * 128, :], o_v[s, j, t * 128:(t + 1) * 128, :]))
        compute_mods(p, cx, l, 1)
        load_ln(p, cx, l, 1)
        with Scope(p):
            a = peer_alloc(p, cx)
            peer_load_layer(p, cx, a, l)
            for blk in range(NT // 256):
                mod = cx.mods["smp"] if blk * 2 < nS else cx.mods["ctx"]
                peer_block(p, cx, a, l, blk, mod, cx.lng, cx.lnb)
    for i in range(NTt):
        p.dma("sp", y_out[i * 128:(i + 1) * 128, :], cx.X[i * 128:(i + 1) * 128, :], reads=[cx.dX[i]], writes=[cx.d_out])
    p.drain_all()
    p.es.close()
    return nc, p


def host_inputs(inp, core, nS=NS_TILES, nP=NP_SEQ):
    f = lambda a: np.ascontiguousarray(a, dtype=np.float32)
    m = dict(_SHARED)
    xs = inp["x_sample"][core][:nS * 128]
    xp = inp["x_prompt"][core * NP_SEQ: core * NP_SEQ + nP].reshape(nP * 256, 1024)
    m["x_in"] = f(np.concatenate([xs, xp], axis=0))
    m["cond2"] = f(np.stack([inp["c"][core].reshape(8, 128).T, inp["c_ctx"].reshape(8, 128).T]))
    st = np.zeros((2, 2, 64, 2, 64), np.float32)
    for jl in range(2):
        for d in range(2):
            st[jl, d, :, 0, :] = inp["state_ssm_re"][core, jl, d].T
            st[jl, d, :, 1, :] = inp["state_ssm_im"][core, jl, d].T
    m["st0"] = st
    m["cache_k"] = f(inp["cache_k"][core].reshape(2, 512, 256))
    m["cache_v"] = f(inp["cache_v"][core].reshape(2, 512, 256))
    return m


_SHARED = {}


def prepare_shared(inp):
    f = lambda a: np.ascontiguousarray(a, dtype=np.float32)
    S = {}
    S.update(host_consts())
    S["c_tmask"] = host_tmask()
    for k in ("w_mod", "b_mod", "ln_g", "ln_b", "attn_w_qkv", "attn_w_out", "attn_sink", "peer_w_q", "peer_v"):
        S[k] = f(inp[k])
    S["attn_w_qkp"] = f(perm_qk(inp["attn_w_qkv"]))
    L = [host_ssm_layouts(inp, jl) for jl in range(2)]
    for k in L[0]:
        S[k] = f(np.stack([L[0][k], L[1][k]]))
    keys = inp["peer_keys"]
    S["peer_keysT"] = f(keys.transpose(0, 4, 2, 1, 3).reshape(4, 128, 16, 128))
    u = inp["peer_u"]
    S["peer_ut"] = f(u.reshape(4, 128, 128, 8, 128).transpose(0, 1, 4, 3, 2).reshape(4, 128, 128, 1024))
    _SHARED.clear()
    _SHARED.update(S)


_PROG = {}


def kernel(**inputs):
    inp = {k: np.asarray(v) for k, v in inputs.items()}
    if "nc" not in _PROG:
        _PROG["nc"], _ = build_program()
    nc = _PROG["nc"]
    prepare_shared(inp)
    in_maps = [host_inputs(inp, c) for c in range(8)]
    res = run_bass_kernel_spmd(nc, in_maps, core_ids=list(range(8)))
    R = res.results
    nS, nP = NS_TILES, NP_SEQ
    y_sample = np.stack([R[c]["y_out"][:nS * 128] for c in range(8)]).astype(np.float32)
    y_prompt = np.concatenate([R[c]["y_out"][nS * 128:].reshape(nP, 256, 1024) for c in range(8)], axis=0).astype(np.float32)
    ost = np.concatenate([R[c]["o_state"] for c in range(8)], axis=0)
    ns_re = np.ascontiguousarray(ost[:, :, :, 0]).astype(np.float32)
    ns_im = np.ascontiguousarray(ost[:, :, :, 1]).astype(np.float32)
    nk = np.concatenate([R[c]["o_k"] for c in range(8)], axis=0).reshape(32, 2, 256, 4, 64).astype(np.float32)
    nv = np.concatenate([R[c]["o_v"] for c in range(8)], axis=0).reshape(32, 2, 256, 4, 64).astype(np.float32)
    return (y_prompt, y_sample, ns_re, ns_im, nk, nv)
```

```python
from contextlib import ExitStack
import numpy as np
import concourse.bass as bass
import concourse.mybir as mybir
from concourse.ap import AP
from concourse.bass_utils import run_bass_kernel_spmd

F32 = mybir.dt.float32
BF16 = mybir.dt.bfloat16
ALU = mybir.AluOpType
AF = mybir.ActivationFunctionType
AX = mybir.AxisListType

D = 1024
DEPTH = 4
DN_ALPHA = (2 * DEPTH) ** 0.25
LN_EPS = 1e-5
NEXP = 16384
NET = 128


class Dep:
    __slots__ = ("w", "r", "excl")

    def __init__(self):
        self.w = None
        self.r = {}
        self.excl = False


class TT:
    __slots__ = ("t", "d", "pstep", "base", "full")

    def __init__(self, t, base=0, shape=None, dep=None):
        self.t = t
        self.d = dep or Dep()
        self.pstep = t[:].ap[0][0]
        self.base = base
        if shape is None:
            self.full = t[:]
        else:
            dims = []
            st = 1
            for n in reversed(shape):
                dims.append([st, n])
                st *= n
            self.full = AP(t, base, [[self.pstep, 128]] + dims[::-1])

    def __getitem__(self, k):
        return self.full[k]

    def ap(self, off, dims, p0=0, np_=128):
        return AP(self.t, self.base + p0 * self.pstep + off, [[self.pstep, np_]] + [list(x) for x in dims])


def fence(src, dst):
    for a in src:
        a = a.d if isinstance(a, TT) else a
        for b in dst:
            b = b.d if isinstance(b, TT) else b
            if a.w is not None:
                b.r[a.w[0]] = max(b.r.get(a.w[0], 0), a.w[1])
            for k, c in a.r.items():
                b.r[k] = max(b.r.get(k, 0), c)


EMBED_WAIT = True
LN_POW = False


class Prog:
    def __init__(self, nc, n_dma_sems=(44, 30, 16)):
        self.nc = nc
        self.es = ExitStack()
        self.engs = {"pe": nc.tensor, "act": nc.scalar, "dve": nc.vector, "pool": nc.gpsimd, "sp": nc.sync}
        self.sem = {}
        self.cnt = {}
        for k in ("pe", "act", "dve", "pool"):
            self.sem[k] = self.es.enter_context(nc.semaphore("s_" + k))
            self.cnt[k] = 0
        self.dpool = {}
        for q, n in zip(("sp", "pool", "act"), n_dma_sems):
            lst = []
            for i in range(n):
                key = "d_%s_%d" % (q, i)
                self.sem[key] = self.es.enter_context(nc.semaphore(key))
                self.cnt[key] = 0
                lst.append(key)
            self.dpool[q] = [lst, 0]
        self.known = {e: {} for e in self.engs}
        self.ninstr = 0
        self.uid = 0

    def sbuf(self, shape, dt, name=None):
        self.uid += 1
        return TT(self.es.enter_context(self.nc.sbuf_tensor("%s_%d" % (name or "sb", self.uid), list(shape), dt)))

    def psum(self, shape, dt, name=None):
        self.uid += 1
        return TT(self.es.enter_context(self.nc.psum_tensor(name or ("ps%d" % self.uid), list(shape), dt)))

    def _need(self, eng, evs, defer_last=False):
        kn = self.known[eng]
        e = self.engs[eng]
        todo = {}
        for (k, c) in evs:
            if c <= 0 or (k == "pe" and eng == "pe") or kn.get(k, 0) >= c:
                continue
            if todo.get(k, 0) < c:
                todo[k] = c
        items = list(todo.items())
        last = None
        if defer_last and items:
            last = items.pop()
        for k, c in items:
            e.wait_ge(self.sem[k], c)
            kn[k] = c
            self.ninstr += 1
        if last is not None:
            kn[last[0]] = last[1]
        return last

    @staticmethod
    def _collect(reads, writes):
        evs = []
        for d in reads:
            if d.w is not None:
                evs.append(d.w)
        for d in writes:
            if d.w is not None:
                evs.append(d.w)
            evs.extend(d.r.items())
        return evs

    @staticmethod
    def _commit(ev, reads, writes):
        for d in reads:
            d.r[ev[0]] = ev[1]
        for d in writes:
            d.w = ev
            d.r = {}

    def op(self, eng, fn, reads=(), writes=()):
        reads = [x.d if isinstance(x, TT) else x for x in reads]
        writes = [x.d if isinstance(x, TT) else x for x in writes]
        ex = [d for d in reads if d.excl]
        if ex:
            reads = [d for d in reads if not d.excl]
            writes = writes + ex
        last = self._need(eng, self._collect(reads, writes), defer_last=EMBED_WAIT)
        ins = fn(self.engs[eng])
        if last is not None:
            ins._wait_ge(self.sem[last[0]], last[1])
        self.cnt[eng] += 1
        ins.then_inc(self.sem[eng], 1)
        self.ninstr += 1
        self._commit((eng, self.cnt[eng]), reads, writes)
        return ins

    def dma(self, q, out, in_, reads=(), writes=(), **kw):
        reads = [x.d if isinstance(x, TT) else x for x in reads]
        writes = [x.d if isinstance(x, TT) else x for x in writes]
        lst, idx = self.dpool[q]
        key = lst[idx % len(lst)]
        self.dpool[q][1] = idx + 1
        evs = self._collect(reads, writes)
        evs.append((key, self.cnt[key]))
        self._need(q, evs)
        ins = self.engs[q].dma_start(out=out, in_=in_, **kw)
        self.cnt[key] += 16
        ins.then_inc(self.sem[key], 16)
        self.ninstr += 1
        self._commit((key, self.cnt[key]), reads, writes)
        return ins

    def drain_all(self):
        self._need("sp", list(self.cnt.items()))


class Ctx:
    pass


def dram_rows(t, r0, nr, c0=0, ncol=None):
    if ncol is None:
        return t[r0:r0 + nr, :]
    return t[r0:r0 + nr, c0:c0 + ncol]


def setup_common(p, cx):
    nc = p.nc
    cx.banks = [p.psum([128, 512], F32, name="bank%d" % i) for i in range(8)]
    for b in cx.banks:
        b.d.excl = True
    cx.ident = p.sbuf([128, 128], F32, name="ident")
    p.dma("sp", cx.ident[:], cx.dram["c_ident"], writes=[cx.ident])
    cx.e3 = p.sbuf([128, 128], BF16, name="e3")
    p.dma("pool", cx.e3[:], cx.dram["c_e3"], writes=[cx.e3])
    cx.eps = p.sbuf([128, 1], F32, name="eps")
    p.op("dve", lambda e: e.memset(cx.eps[:], LN_EPS), writes=[cx.eps])


def layer_norm_rows(p, cx, y, out, g_bc, b_bc, scr):
    st = scr["st"]; mv = scr["mv"]; rstd = scr["rstd"]
    for c in range(2):
        p.op("dve", lambda e, c=c: e.bn_stats(st[:, c * 6:(c + 1) * 6], y[:, c * 512:(c + 1) * 512]), reads=[y], writes=[st])
    p.op("dve", lambda e: e.bn_aggr(mv[:], st[:]), reads=[st], writes=[mv])
    if LN_POW:
        p.op("dve", lambda e: e.tensor_scalar(rstd[:], mv[:, 1:2], LN_EPS, None, ALU.add), reads=[mv], writes=[rstd])
        p.op("dve", lambda e: e.tensor_single_scalar(rstd[:], rstd[:], -0.5, ALU.pow), reads=[rstd], writes=[rstd])
    else:
        p.op("act", lambda e: e.activation(rstd[:], mv[:, 1:2], AF.Sqrt, bias=cx.eps[:], scale=1.0), reads=[mv, cx.eps], writes=[rstd])
        p.op("dve", lambda e: e.reciprocal(rstd[:], rstd[:]), reads=[rstd], writes=[rstd])
    p.op("dve", lambda e: e.tensor_scalar(y[:], y[:], mv[:, 0:1], rstd[:, 0:1], ALU.subtract, ALU.mult),
         reads=[y, mv, rstd], writes=[y])
    p.op("pool", lambda e: e.tensor_tensor(y[:], y[:], g_bc[:], ALU.mult), reads=[y, g_bc], writes=[y])
    p.op("pool", lambda e: e.tensor_tensor(out[:], y[:], b_bc[:], ALU.add), reads=[y, b_bc], writes=[out])


def peer_prep_weights(p, cx, l):
    ut_src = cx.dram["peer_ut"][l]
    v_src = cx.dram["peer_v"][l]
    CH = 8
    for c in range(NET // CH):
        p.dma("pool", cx.utb[c * CH:(c + 1) * CH], ut_src[c * CH:(c + 1) * CH], writes=[cx.d_utb[c]])
        p.dma("pool", cx.vb[c * CH * 128:(c + 1) * CH * 128, :], v_src[c * CH * 128:(c + 1) * CH * 128, :],
              writes=[cx.d_vb[c]])


def peer_alloc(p, cx):
    a = Ctx()
    a.big = p.sbuf([128, 16384], F32, name="p_big")
    hb = a.big.t[:].bitcast(BF16).tensor
    a.Wb = TT(hb, 0, [256, 128])
    a.wq = TT(hb, 0, [8, 2048])
    a.s_sb = TT(a.big.t, 8192, [16, 128])
    a.cand = TT(a.big.t, 10240, [8, 256])
    a.r1 = TT(a.big.t, 12288, [2, 8, 128])
    a.d1regs = [a.wq, a.s_sb, a.cand, a.r1]
    a.keys = p.sbuf([128, 16, 128], BF16, name="keysT")
    a.xt = [p.sbuf([128, 1024], F32, name="p_xt%d" % i) for i in range(2)]
    a.hm = p.sbuf([128, 1024], F32, name="p_hm")
    a.hT = p.sbuf([128, 8, 256], BF16, name="p_hT")
    a.qT = p.sbuf([128, 16, 128], BF16, name="p_qT")
    a.V = p.sbuf([128, 16, 16], F32, name="p_V")
    a.tmp128 = p.sbuf([128, 128], F32, name="p_tmp128")
    a.tmpc = [p.sbuf([128, 256], F32, name="p_tmpc%d" % i) for i in range(2)]
    a.C = p.sbuf([128, 8, 24], F32, name="p_C")
    a.sm = p.sbuf([128, 8, 8], F32, name="p_small")
    a.ex = p.sbuf([128, 8, 16], F32, name="p_ex")
    a.v1pp = p.sbuf([128, 128], F32, name="p_v1pp")
    a.thp = p.sbuf([128, 128], F32, name="p_thp")
    a.SPt = p.sbuf([128, 2, 3, 8, 128], BF16, name="p_SPt")
    a.ThT = p.sbuf([128, 256], F32, name="p_ThT")
    a.V1T = p.sbuf([128, 256], F32, name="p_V1T")
    a.E1T = p.sbuf([128, 256], F32, name="p_E1T")
    a.ev1 = p.sbuf([128, 128], F32, name="p_ev1")
    a.SPh = [[p.sbuf([128, 16, 128], BF16, name="p_SPh%d_%d" % (i, s)) for s in range(2)] for i in range(2)]
    for i in range(2):
        for s_ in range(2):
            p.op("pool", lambda e, i=i, s_=s_: e.memset(a.SPh[i][s_][:], 0.0), writes=[a.SPh[i][s_]])
    a.EX = [[p.sbuf([128, 4, 128], BF16, name="p_EX%d_%d" % (i, s)) for s in range(2)] for i in range(3)]
    a.MK = [[p.sbuf([128, 4, 128], BF16, name="p_MK%d_%d" % (i, s)) for s in range(2)] for i in range(3)]
    a.LR = [[p.sbuf([128, 4, 128], BF16, name="p_LR%d_%d" % (i, s)) for s in range(2)] for i in range(3)]
    a.ut = [p.sbuf([128, 1024], BF16, name="p_ut%d" % i) for i in range(4)]
    a.vt = [p.sbuf([128, 1024], BF16, name="p_vt%d" % i) for i in range(4)]
    a.G = [p.sbuf([128, 256], BF16, name="p_G%d" % i) for i in range(3)]
    a.A = [p.sbuf([128, 256], BF16, name="p_A%d" % i) for i in range(3)]
    a.y = p.sbuf([128, 1024], F32, name="p_y")
    a.t1 = a.hm
    a.xo = a.hm
    a.scr = {"st": p.sbuf([128, 12], F32, name="p_st"), "mv": p.sbuf([128, 2], F32, name="p_mv"),
             "rstd": p.sbuf([128, 1], F32, name="p_rstd")}
    return a


def peer_load_layer(p, cx, a, l):
    p.dma("pool", a.keys[:], cx.dram["peer_keysT"][l], writes=[a.keys])


def peer_block(p, cx, a, l, blk, mod, lng, lnb):
    B = cx.banks
    tok0 = blk * 256
    NT = cx.NT
    spd = cx.spd
    fence([a.Wb], a.d1regs)
    wqd = cx.dram["peer_w_q"][l]
    for k in range(8):
        p.dma("pool", a.wq[:, k, :], wqd[k * 128:(k + 1) * 128, :], writes=[a.wq])
    for tl in range(2):
        r0 = tok0 + tl * 128
        xt = a.xt[tl]
        p.dma("sp", xt[:], cx.X[r0:r0 + 128, :], reads=[cx.dX[r0 // 128]], writes=[xt])
        p.op("dve", lambda e: e.tensor_tensor(a.hm[:], xt[:], mod["sc1"][:], ALU.mult), reads=[xt, mod["sc1"]], writes=[a.hm])
        p.op("dve", lambda e: e.tensor_tensor(a.hm[:], a.hm[:], mod["sh"][:], ALU.add), reads=[a.hm, mod["sh"]], writes=[a.hm])
        for half in range(2):
            bk = B[half]
            for k4 in range(4):
                k = half * 4 + k4
                p.op("pe", lambda e, k=k, k4=k4, bk=bk: e.transpose(bk[:, k4 * 128:(k4 + 1) * 128], a.hm[:, k * 128:(k + 1) * 128], cx.ident[:]),
                     reads=[a.hm, cx.ident], writes=[bk])
            p.op("act", lambda e, half=half, bk=bk: e.copy(
                a.hT.ap(half * 4 * 256 + tl * 128, [[256, 4], [1, 128]]), bk.ap(0, [[128, 4], [1, 128]])),
                reads=[bk], writes=[a.hT])
        for m4 in range(4):
            bk = B[2 + (m4 % 2)]
            for mi in range(4):
                m = m4 * 4 + mi
                for k in range(8):
                    p.op("pe", lambda e, m=m, mi=mi, k=k, bk=bk: e.matmul(
                        bk[:, mi * 128:(mi + 1) * 128], a.wq[:, k, m * 128:(m + 1) * 128],
                        a.hT[:, k, tl * 128:(tl + 1) * 128], start=(k == 0), stop=(k == 7)),
                        reads=[a.wq, a.hT], writes=[bk])
            eng = "act" if m4 % 2 == 0 else "dve"
            if eng == "act":
                p.op("act", lambda e, m4=m4, bk=bk: e.copy(a.qT[:, m4 * 4:(m4 + 1) * 4, :], bk.ap(0, [[128, 4], [1, 128]])),
                     reads=[bk], writes=[a.qT])
            else:
                p.op("dve", lambda e, m4=m4, bk=bk: e.tensor_copy(a.qT[:, m4 * 4:(m4 + 1) * 4, :], bk.ap(0, [[128, 4], [1, 128]])),
                     reads=[bk], writes=[a.qT])
        for m4 in range(4):
            bk = B[4 + m4]
            for mi in range(4):
                m = m4 * 4 + mi
                p.op("pe", lambda e, m=m, mi=mi, bk=bk: e.matmul(bk[:, mi * 128:(mi + 1) * 128], a.qT[:, m, :], a.keys[:, m, :],
                                                                  start=True, stop=True), reads=[a.qT, a.keys], writes=[bk])
            p.op("act", lambda e, m4=m4, bk=bk: e.copy(a.s_sb[:, m4 * 4:(m4 + 1) * 4, :], bk.ap(0, [[128, 4], [1, 128]])),
                 reads=[bk], writes=[a.s_sb])
        for m in range(16):
            p.op("dve", lambda e, m=m: e.max(out=a.V[:, m, 0:8], in_=a.s_sb[:, m, :]), reads=[a.s_sb], writes=[a.V])
            p.op("dve", lambda e, m=m: e.match_replace(out=a.tmp128[:], in_to_replace=a.V[:, m, 0:8], in_values=a.s_sb[:, m, :],
                                                        imm_value=-1e30), reads=[a.s_sb, a.V], writes=[a.tmp128])
            p.op("dve", lambda e, m=m: e.max(out=a.V[:, m, 8:16], in_=a.tmp128[:]), reads=[a.tmp128], writes=[a.V])
        in0 = a.V.ap(0, [[32, 8], [1, 16], [0, 16]])
        in1 = a.V.ap(16, [[32, 8], [0, 16], [1, 16]])
        outc = a.cand.ap(0, [[256, 8], [16, 16], [1, 16]])
        p.op("dve", lambda e: e.tensor_tensor(outc, in0, in1, ALU.add), reads=[a.V], writes=[a.cand])
        for h in range(8):
            p.op("dve", lambda e, h=h: e.max(out=a.C[:, h, 0:8], in_=a.cand[:, h, :]), reads=[a.cand], writes=[a.C])
            p.op("dve", lambda e, h=h: e.match_replace(out=a.tmpc[0][:], in_to_replace=a.C[:, h, 0:8], in_values=a.cand[:, h, :],
                                                        imm_value=-1e30), reads=[a.cand, a.C], writes=[a.tmpc[0]])
            p.op("dve", lambda e, h=h: e.max(out=a.C[:, h, 8:16], in_=a.tmpc[0][:]), reads=[a.tmpc[0]], writes=[a.C])
            p.op("dve", lambda e, h=h: e.match_replace(out=a.tmpc[1][:], in_to_replace=a.C[:, h, 8:16], in_values=a.tmpc[0][:],
                                                        imm_value=-1e30), reads=[a.tmpc[0], a.C], writes=[a.tmpc[1]])
            p.op("dve", lambda e, h=h: e.max(out=a.C[:, h, 16:24], in_=a.tmpc[1][:]), reads=[a.tmpc[1]], writes=[a.C])
        sm = a.sm
        TAU, LNZ, OFF1, C2, ZZ = 0, 1, 2, 3, 4
        c15 = a.C.ap(15, [[24, 8]]); c16 = a.C.ap(16, [[24, 8]]); c0 = a.C.ap(0, [[24, 8]])
        m1v = a.V.ap(0, [[32, 8]]); m2v = a.V.ap(16, [[32, 8]])
        p.op("dve", lambda e: e.tensor_tensor(sm[:, TAU, :], c15, c16, ALU.add), reads=[a.C], writes=[sm])
        p.op("dve", lambda e: e.tensor_scalar(sm[:, TAU, :], sm[:, TAU, :], 0.5, None, ALU.mult), reads=[sm], writes=[sm])
        c016 = a.C.ap(0, [[24, 8], [1, 16]]); c0b = a.C.ap(0, [[24, 8], [0, 16]])
        p.op("dve", lambda e: e.tensor_tensor(a.ex[:], c016, c0b, ALU.subtract), reads=[a.C], writes=[a.ex])
        p.op("act", lambda e: e.activation(a.ex[:], a.ex[:], AF.Exp), reads=[a.ex], writes=[a.ex])
        p.op("dve", lambda e: e.reduce_sum(sm[:, ZZ, :], a.ex[:], axis=AX.X), reads=[a.ex], writes=[sm])
        p.op("act", lambda e: e.activation(sm[:, LNZ, :], sm[:, ZZ, :], AF.Ln), reads=[sm], writes=[sm])
        p.op("dve", lambda e: e.tensor_tensor(sm[:, OFF1, :], m1v, sm[:, LNZ, :], ALU.add), reads=[a.V, sm], writes=[sm])
        p.op("dve", lambda e: e.tensor_tensor(sm[:, C2, :], sm[:, TAU, :], m2v, ALU.subtract), reads=[a.V, sm], writes=[sm])
        s1v = a.s_sb.ap(0, [[256, 8], [1, 128]]); s2v = a.s_sb.ap(128, [[256, 8], [1, 128]])
        off1b = sm.ap(OFF1 * 8, [[1, 8], [0, 128]]); m2b = a.V.ap(16, [[32, 8], [0, 128]])
        sfix = a.cand
        sf4 = sfix.ap(0, [[1024, 2], [128, 8], [1, 128]])
        p.op("dve", lambda e: e.tensor_tensor(sfix.ap(0, [[128, 8], [1, 128]]), s1v, off1b, ALU.subtract), reads=[a.s_sb, sm], writes=[sfix])
        p.op("pool", lambda e: e.tensor_tensor(sfix.ap(1024, [[128, 8], [1, 128]]), s2v, m2b, ALU.subtract), reads=[a.s_sb, a.V], writes=[sfix])
        v1v = a.V.ap(0, [[32, 8], [1, 16]]); off1b16 = sm.ap(OFF1 * 8, [[1, 8], [0, 16]]); c2b16 = sm.ap(C2 * 8, [[1, 8], [0, 16]])
        v1pp3 = a.v1pp.ap(0, [[16, 8], [1, 16]]); thp3 = a.thp.ap(0, [[16, 8], [1, 16]])
        p.op("dve", lambda e: e.tensor_tensor(v1pp3, v1v, off1b16, ALU.subtract), reads=[a.V, sm], writes=[a.v1pp])
        p.op("dve", lambda e: e.tensor_tensor(thp3, c2b16, v1v, ALU.subtract), reads=[a.V, sm], writes=[a.thp])
        SP = a.SPt
        p.op("act", lambda e: e.copy(SP[:, :, 0, :, :], sf4), reads=[sfix], writes=[SP])
        p.op("dve", lambda e: e.tensor_tensor(a.r1[:], sf4, SP[:, :, 0, :, :], ALU.subtract), reads=[sfix, SP], writes=[a.r1])
        p.op("act", lambda e: e.copy(SP[:, :, 1, :, :], a.r1[:]), reads=[a.r1], writes=[SP])
        p.op("dve", lambda e: e.tensor_tensor(a.r1[:], a.r1[:], SP[:, :, 1, :, :], ALU.subtract), reads=[a.r1, SP], writes=[a.r1])
        p.op("act", lambda e: e.copy(SP[:, :, 2, :, :], a.r1[:]), reads=[a.r1], writes=[SP])
        for side in range(2):
            dst = spd[side, :, :, r0:r0 + 128, :].rearrange("a h t i -> t a h i")
            p.dma("sp", dst, SP[:, side, :, :, :], reads=[SP], writes=[cx.d_spd[r0 // 128]])
        bk = B[0]
        p.op("act", lambda e: e.activation(a.ev1[:], a.v1pp[:], AF.Exp), reads=[a.v1pp], writes=[a.ev1])
        p.op("pe", lambda e: e.transpose(bk[:, 0:128], a.thp[:], cx.ident[:]), reads=[a.thp, cx.ident], writes=[bk])
        p.op("pe", lambda e: e.transpose(bk[:, 128:256], a.v1pp[:], cx.ident[:]), reads=[a.v1pp, cx.ident], writes=[bk])
        p.op("pe", lambda e: e.transpose(bk[:, 256:384], a.ev1[:], cx.ident[:]), reads=[a.ev1, cx.ident], writes=[bk])
        p.op("act", lambda e: e.copy(a.ThT[:, tl * 128:(tl + 1) * 128], bk[:, 0:128]), reads=[bk], writes=[a.ThT])
        p.op("act", lambda e: e.copy(a.V1T[:, tl * 128:(tl + 1) * 128], bk[:, 128:256]), reads=[bk], writes=[a.V1T])
        p.op("act", lambda e: e.copy(a.E1T[:, tl * 128:(tl + 1) * 128], bk[:, 256:384]), reads=[bk], writes=[a.E1T])

    if getattr(cx, "stage", 9) < 2:
        return
    fence(a.d1regs, [a.Wb])
    groups = [(sub, g4) for sub in range(16) for g4 in range(4)]
    NG = len(groups)

    def d2_load(sub):
        ta = tok0 + sub * 16
        for side in range(2):
            src = spd[side, :, :, ta:ta + 16, :].rearrange("a h t i -> (a h) t i")
            p.dma("sp", a.SPh[sub % 2][side][0:24, :, :], src, reads=[cx.d_spd[ta // 128]], writes=[a.SPh[sub % 2][side]])

    def stA(gi):
        sub, g4 = groups[gi]
        st = gi % 3
        sph = a.SPh[sub % 2]
        for side in range(2):
            bk = B[st * 2 + side]
            p.op("pe", lambda e, side=side, bk=bk: e.matmul(bk[:], cx.e3[0:24, :], sph[side][0:24, g4 * 4:(g4 + 1) * 4, :], start=True, stop=True),
                 reads=[cx.e3, sph[side]], writes=[bk])
        if g4 == 3 and sub + 2 < 16:
            d2_load(sub + 2)

    def stB(gi):
        sub, g4 = groups[gi]
        st = gi % 3
        tb = sub * 16 + g4 * 4
        EX = a.EX[st]; MK = a.MK[st]
        ops = (ALU.is_equal, ALU.is_ge)
        srcs = (a.V1T, a.ThT)
        for side in range(2):
            bk = B[st * 2 + side]
            if side == 1:
                p.op("act", lambda e, side=side, bk=bk: e.activation(EX[side][:], bk.ap(0, [[128, 4], [1, 128]]), AF.Exp), reads=[bk], writes=[EX[side]])
            cb = srcs[side].ap(tb, [[1, 4], [0, 128]])
            p.op("dve", lambda e, side=side, bk=bk, cb=cb: e.tensor_tensor(MK[side][:], bk.ap(0, [[128, 4], [1, 128]]), cb, ops[side]),
                 reads=[bk, srcs[side]], writes=[MK[side]])

    def stC(gi):
        sub, g4 = groups[gi]
        st = gi % 3
        tb = sub * 16 + g4 * 4
        e1b = a.E1T.ap(tb, [[1, 4], [0, 128]])
        p.op("pool", lambda e: e.tensor_tensor(a.LR[st][0][:], a.MK[st][0][:], e1b, ALU.mult), reads=[a.MK[st][0], a.E1T], writes=[a.LR[st][0]])
        p.op("pool", lambda e: e.tensor_tensor(a.LR[st][1][:], a.MK[st][1][:], a.EX[st][1][:], ALU.mult),
             reads=[a.MK[st][1], a.EX[st][1]], writes=[a.LR[st][1]])

    def stD(gi):
        sub, g4 = groups[gi]
        st = gi % 3
        tb = sub * 16 + g4 * 4
        psW = B[6 + gi % 2]
        LR = a.LR[st]
        for tk in range(4):
            p.op("pe", lambda e, tk=tk: e.matmul(psW[:, tk * 128:(tk + 1) * 128], LR[1][:, tk, :], LR[0][:, tk, :], start=True, stop=True),
                 reads=[LR[0], LR[1]], writes=[psW])
        p.op("act", lambda e: e.copy(a.Wb[:, tb:tb + 4, :], psW.ap(0, [[128, 4], [1, 128]])), reads=[psW], writes=[a.Wb])

    d2_load(0); d2_load(1)
    for i in range(NG + 3):
        if i < NG:
            stA(i)
        if 0 <= i - 1 < NG:
            stB(i - 1)
        if 0 <= i - 2 < NG:
            stC(i - 2)
        if 0 <= i - 3 < NG:
            stD(i - 3)

    if getattr(cx, "stage", 9) < 3:
        return
    psO = [[B[0], B[1]], [B[2], B[3]]]

    def d3_load(et):
        p.dma("sp", a.ut[et % 4][:], cx.utb[et], reads=[cx.d_utb[et // 8]], writes=[a.ut[et % 4]])
        p.dma("sp", a.vt[et % 4][:], cx.vb[et * 128:(et + 1) * 128, :], reads=[cx.d_vb[et // 8]], writes=[a.vt[et % 4]])

    def d3_ux(et):
        ut = a.ut[et % 4]
        psU = B[4 + (et % 3)]
        for k in range(8):
            p.op("pe", lambda e, k=k: e.matmul(psU[:, 0:256], ut[:, k * 128:(k + 1) * 128], a.hT[:, k, :], start=(k == 0), stop=(k == 7)),
                 reads=[ut, a.hT], writes=[psU])
        G = a.G[et % 3]; A = a.A[et % 3]
        p.op("act", lambda e: e.activation(G[:], psU[:, 0:256], AF.Gelu_apprx_tanh), reads=[psU], writes=[G])
        wsl = a.Wb.ap(et, [[128, 256]])
        p.op("dve" if et % 2 == 0 else "pool", lambda e: e.tensor_tensor(A[:], G[:], wsl, ALU.mult), reads=[G, a.Wb], writes=[A])

    def d3_v(et):
        A = a.A[et % 3]; vt = a.vt[et % 4]
        for tl in range(2):
            for dh in range(2):
                p.op("pe", lambda e, tl=tl, dh=dh: e.matmul(psO[tl][dh][:], A[:, tl * 128:(tl + 1) * 128], vt[:, dh * 512:(dh + 1) * 512],
                                                            start=(et == 0), stop=(et == NET - 1)), reads=[A, vt], writes=[psO[tl][dh]])

    d3_load(0); d3_load(1); d3_load(2); d3_ux(0); d3_ux(1)
    for et in range(NET):
        if et + 3 < NET:
            d3_load(et + 3)
        if et + 2 < NET:
            d3_ux(et + 2)
        d3_v(et)
    if getattr(cx, "stage", 9) < 4:
        return
    for tl in range(2):
        r0 = tok0 + tl * 128
        xt = a.xt[tl]
        for dh in range(2):
            p.op("dve", lambda e, dh=dh: e.tensor_tensor(a.t1[:, dh * 512:(dh + 1) * 512], psO[tl][dh][:], mod["g1"][:, dh * 512:(dh + 1) * 512], ALU.mult),
                 reads=[psO[tl][dh], mod["g1"]], writes=[a.t1])
        p.op("dve", lambda e: e.scalar_tensor_tensor(a.y[:], xt[:], DN_ALPHA, a.t1[:], ALU.mult, ALU.add), reads=[xt, a.t1], writes=[a.y])
        layer_norm_rows(p, cx, a.y, a.xo, lng, lnb, a.scr)
        p.dma("act", cx.X[r0:r0 + 128, :], a.xo[:], reads=[a.xo], writes=[cx.dX[r0 // 128]])


def modulate_transpose(p, cx, xt, hm, hT_dst_fn, mod, banks):
    p.op("dve", lambda e: e.tensor_tensor(hm[:], xt[:], mod["sc1"][:], ALU.mult), reads=[xt, mod["sc1"]], writes=[hm])
    p.op("pool", lambda e: e.tensor_tensor(hm[:], hm[:], mod["sh"][:], ALU.add), reads=[hm, mod["sh"]], writes=[hm])
    for half in range(2):
        bk = banks[half]
        for k4 in range(4):
            k = half * 4 + k4
            p.op("pe", lambda e, k=k, k4=k4, bk=bk: e.transpose(bk[:, k4 * 128:(k4 + 1) * 128], hm[:, k * 128:(k + 1) * 128], cx.ident[:]),
                 reads=[hm, cx.ident], writes=[bk])
        dst, dep = hT_dst_fn(half)
        p.op("act", lambda e, bk=bk, dst=dst: e.copy(dst, bk.ap(0, [[128, 4], [1, 128]])), reads=[bk], writes=[dep])


def post_norm_store(p, cx, xt, psO, mod, lng, lnb, t1, y, scr, r0, q="act"):
    for dh in range(2):
        p.op("dve", lambda e, dh=dh: e.tensor_tensor(t1[:, dh * 512:(dh + 1) * 512], psO[dh][:], mod["g1"][:, dh * 512:(dh + 1) * 512], ALU.mult),
             reads=[psO[dh], mod["g1"]], writes=[t1])
    p.op("dve", lambda e: e.scalar_tensor_tensor(y[:], xt[:], DN_ALPHA, t1[:], ALU.mult, ALU.add), reads=[xt, t1], writes=[y])
    layer_norm_rows(p, cx, y, t1, lng, lnb, scr)
    p.dma(q, cx.X[r0:r0 + 128, :], t1[:], reads=[t1], writes=[cx.dX[r0 // 128]])


def attn_alloc(p, cx):
    a = Ctx()
    a.wqkv = p.sbuf([128, 8, 1536], BF16, name="a_wqkv")
    a.wqkp = p.sbuf([128, 8, 1280], BF16, name="a_wqkp")
    a.wout = p.sbuf([128, 8, 1024], BF16, name="a_wout")
    a.esink = p.sbuf([128, 16], F32, name="a_esink")
    a.kcf = p.sbuf([128, 256], F32, name="a_kcf")
    a.KTc = p.sbuf([64, 4, 4, 128], BF16, name="a_KTc")
    a.Vc = p.sbuf([128, 4, 4, 65], BF16, name="a_Vc")
    a.xt = [p.sbuf([128, 1024], F32, name="a_xt%d" % i) for i in range(3)]
    a.hm = p.sbuf([128, 1024], F32, name="a_hm")
    a.hT = p.sbuf([128, 8, 128], BF16, name="a_hT")
    a.QT = [p.sbuf([64, 16, 128], BF16, name="a_QT%d" % i) for i in range(3)]
    a.KT = [p.sbuf([64, 4, 128], BF16, name="a_KT%d" % i) for i in range(3)]
    a.V = [p.sbuf([128, 4, 65], BF16, name="a_V%d" % i) for i in range(3)]
    a.cs = [p.sbuf([64, 2, 128], F32, name="a_cs%d" % i) for i in range(2)]
    a.r1 = p.sbuf([64, 4, 128], F32, name="a_r1")
    a.r2 = p.sbuf([64, 4, 128], F32, name="a_r2")
    a.kvtok = p.sbuf([128, 2, 256], F32, name="a_kvtok")
    a.PT = [p.sbuf([128, 4, 128], BF16, name="a_PT%d" % i) for i in range(7)]
    a.masks = p.sbuf([128, 2, 128], BF16, name="a_masks")
    a.den = p.sbuf([128, 4], F32, name="a_den")
    a.O = p.sbuf([128, 1024], F32, name="a_O")
    a.OT = p.sbuf([128, 8, 128], BF16, name="a_OT")
    a.y = p.sbuf([128, 1024], F32, name="a_y")
    a.scr = {"st": p.sbuf([128, 12], F32, name="a_st"), "mv": p.sbuf([128, 2], F32, name="a_mv"),
             "rstd": p.sbuf([128, 1], F32, name="a_rstd")}
    p.dma("pool", a.masks[:], cx.dram["c_masks"], writes=[a.masks])
    for i in range(3):
        p.op("pool", lambda e, i=i: e.memset(a.V[i][:], 1.0), writes=[a.V[i]])
    p.op("pool", lambda e: e.memset(a.Vc[:], 1.0), writes=[a.Vc])
    return a


def attn_load_layer(p, cx, a, j):
    B = cx.banks
    w = cx.dram["attn_w_qkv"][j]; wp = cx.dram["attn_w_qkp"][j]; wo = cx.dram["attn_w_out"][j]
    for k in range(8):
        p.dma("pool", a.wqkv[:, k, :], w[k * 128:(k + 1) * 128, :], writes=[a.wqkv])
        p.dma("pool", a.wqkp[:, k, :], wp[k * 128:(k + 1) * 128, :], writes=[a.wqkp])
        p.dma("pool", a.wout[:, k, :], wo[k * 128:(k + 1) * 128, :], writes=[a.wout])
    p.dma("sp", a.esink[:], cx.dram["attn_sink"][j].partition_broadcast(128), writes=[a.esink])
    p.op("act", lambda e: e.activation(a.esink[:], a.esink[:], AF.Exp), reads=[a.esink], writes=[a.esink])
    if cx.n_smp_tiles == 0:
        return
    ck = cx.dram["cache_k"][j]
    cv = cx.dram["cache_v"][j]
    for c in range(4):
        p.dma("sp", a.kcf[:], ck[c * 128:(c + 1) * 128, :], writes=[a.kcf])
        bk = B[0]
        for g in range(4):
            p.op("pe", lambda e, g=g: e.transpose(bk[0:64, g * 128:(g + 1) * 128], a.kcf[:, g * 64:(g + 1) * 64], cx.ident[:]),
                 reads=[a.kcf, cx.ident], writes=[bk])
        p.op("act", lambda e, c=c: e.copy(a.KTc[:, :, c, :], bk.ap(0, [[128, 4], [1, 128]], np_=64)), reads=[bk], writes=[a.KTc])
        p.dma("pool", a.Vc[:, c, :, 0:64], cv[c * 128:(c + 1) * 128, :].rearrange("t (g d) -> t g d", g=4), writes=[a.Vc])


def attn_project(p, cx, a, r0, slot, rope_pos, mod, want_kv_out=None):
    B = cx.banks
    xt = a.xt[slot]
    p.dma("sp", xt[:], cx.X[r0:r0 + 128, :], reads=[cx.dX[r0 // 128]], writes=[xt])
    modulate_transpose(p, cx, xt, a.hm, lambda half: (a.hT[:, half * 4:(half + 1) * 4, :], a.hT), mod, [B[0], B[1]])
    QT = a.QT[slot]; KT = a.KT[slot]; V = a.V[slot]
    if rope_pos is not None:
        cs = a.cs[rope_pos % 2]
        p.dma("sp", cs[:], cx.dram["c_rope"][:, :, rope_pos * 128:(rope_pos + 1) * 128], writes=[cs])
    for hg in range(5):
        colbase = hg * 256 if hg < 4 else 1024
        b0 = B[0]; b1 = B[1]
        for r in range(4):
            for k in range(8):
                p.op("pe", lambda e, r=r, k=k: e.matmul(b0[0:64, r * 128:(r + 1) * 128], a.wqkv[:, k, colbase + r * 64: colbase + (r + 1) * 64],
                                                        a.hT[:, k, :], start=(k == 0), stop=(k == 7)), reads=[a.wqkv, a.hT], writes=[b0])
        dst = QT[:, hg * 4:(hg + 1) * 4, :] if hg < 4 else KT[:]
        ddep = QT if hg < 4 else KT
        if rope_pos is None:
            p.op("act", lambda e, dst=dst: e.copy(dst, b0.ap(0, [[128, 4], [1, 128]], np_=64)), reads=[b0], writes=[ddep])
        else:
            for r in range(4):
                for k in range(8):
                    p.op("pe", lambda e, r=r, k=k: e.matmul(b1[0:64, r * 128:(r + 1) * 128], a.wqkp[:, k, colbase + r * 64: colbase + (r + 1) * 64],
                                                            a.hT[:, k, :], start=(k == 0), stop=(k == 7)), reads=[a.wqkp, a.hT], writes=[b1])
            cosb = cs.ap(0, [[0, 4], [1, 128]], np_=64); sinb = cs.ap(128, [[0, 4], [1, 128]], np_=64)
            p.op("dve", lambda e: e.tensor_tensor(a.r1[:], b0.ap(0, [[128, 4], [1, 128]], np_=64), cosb, ALU.mult), reads=[b0, cs], writes=[a.r1])
            p.op("dve", lambda e: e.tensor_tensor(a.r2[:], b1.ap(0, [[128, 4], [1, 128]], np_=64), sinb, ALU.mult), reads=[b1, cs], writes=[a.r2])
            p.op("pool", lambda e, dst=dst: e.tensor_tensor(dst, a.r1[:], a.r2[:], ALU.add), reads=[a.r1, a.r2], writes=[ddep])
    bv = B[2]
    for k in range(8):
        p.op("pe", lambda e, k=k: e.matmul(bv[:, 0:256], a.hT[:, k, :], a.wqkv[:, k, 1280:1536], start=(k == 0), stop=(k == 7)),
             reads=[a.hT, a.wqkv], writes=[bv])
    if want_kv_out is not None:
        for k in range(8):
            p.op("pe", lambda e, k=k: e.matmul(bv[:, 256:512], a.hT[:, k, :], a.wqkv[:, k, 1024:1280], start=(k == 0), stop=(k == 7)),
                 reads=[a.hT, a.wqkv], writes=[bv])
    p.op("act", lambda e: e.copy(V[:, :, 0:64], bv.ap(0, [[64, 4], [1, 64]])), reads=[bv], writes=[V])
    if want_kv_out is not None:
        ko, vo = want_kv_out
        p.op("dve", lambda e: e.tensor_copy(a.kvtok[:, 0, :], bv[:, 256:512]), reads=[bv], writes=[a.kvtok])
        p.op("dve", lambda e: e.tensor_copy(a.kvtok[:, 1, :], bv[:, 0:256]), reads=[bv], writes=[a.kvtok])
        p.dma("act", ko, a.kvtok[:, 0, :], reads=[a.kvtok], writes=[cx.d_out])
        p.dma("act", vo, a.kvtok[:, 1, :], reads=[a.kvtok], writes=[cx.d_out])


def attn_attend(p, cx, a, r0, slot, chunks, mod, lng, lnb):
    B = cx.banks
    QT = a.QT[slot]
    nch = len(chunks)
    for g in range(4):
        for ci, (kfn, vfn, deps, mk) in enumerate(chunks):
            bs = B[2 + (ci % 2)]
            p.op("pe", lambda e, kfn=kfn, bs=bs: e.matmul(bs[:], kfn(g), QT[:, g * 4:(g + 1) * 4, :], start=True, stop=True),
                 reads=[QT] + deps, writes=[bs])
            PT = a.PT[ci]
            p.op("act", lambda e, bs=bs, PT=PT: e.activation(PT[:], bs.ap(0, [[128, 4], [1, 128]]), AF.Exp, scale=0.125), reads=[bs], writes=[PT])
            if mk is not None:
                mb = a.masks.ap(mk * 128, [[0, 4], [1, 128]])
                p.op("pool", lambda e, PT=PT, mb=mb: e.tensor_tensor(PT[:], PT[:], mb, ALU.mult), reads=[PT, a.masks], writes=[PT])
        bo = B[4 + g]
        for r in range(4):
            for ci, (kfn, vfn, deps, mk) in enumerate(chunks):
                p.op("pe", lambda e, r=r, ci=ci, vfn=vfn: e.matmul(bo[:, r * 65:(r + 1) * 65], a.PT[ci][:, r, :], vfn(g),
                                                                   start=(ci == 0), stop=(ci == nch - 1)), reads=[a.PT[ci]] + deps, writes=[bo])
        p.op("dve", lambda e: e.tensor_tensor(a.den[:], bo.ap(64, [[65, 4]]), a.esink[:, g * 4:(g + 1) * 4], ALU.add), reads=[bo, a.esink], writes=[a.den])
        p.op("dve", lambda e: e.reciprocal(a.den[:], a.den[:]), reads=[a.den], writes=[a.den])
        p.op("dve", lambda e: e.tensor_tensor(a.O.ap(g * 256, [[64, 4], [1, 64]]), bo.ap(0, [[65, 4], [1, 64]]), a.den.ap(0, [[1, 4], [0, 64]]), ALU.mult),
             reads=[bo, a.den], writes=[a.O])
    for half in range(2):
        bk = B[half]
        for k4 in range(4):
            k = half * 4 + k4
            p.op("pe", lambda e, k=k, k4=k4, bk=bk: e.transpose(bk[:, k4 * 128:(k4 + 1) * 128], a.O[:, k * 128:(k + 1) * 128], cx.ident[:]),
                 reads=[a.O, cx.ident], writes=[bk])
        p.op("act", lambda e, half=half, bk=bk: e.copy(a.OT[:, half * 4:(half + 1) * 4, :], bk.ap(0, [[128, 4], [1, 128]])), reads=[bk], writes=[a.OT])
    psO = [B[2], B[3]]
    for dh in range(2):
        for k in range(8):
            p.op("pe", lambda e, dh=dh, k=k: e.matmul(psO[dh][:], a.OT[:, k, :], a.wout[:, k, dh * 512:(dh + 1) * 512], start=(k == 0), stop=(k == 7)),
                 reads=[a.OT, a.wout], writes=[psO[dh]])
    post_norm_store(p, cx, a.xt[slot], psO, mod, lng, lnb, a.hm, a.y, a.scr, r0)


def attn_layer(p, cx, a, j, mods, lng, lnb, kv_out):
    nS = cx.n_smp_tiles
    def loc(slot):
        return (lambda g: a.KT[slot][:, g, :]), (lambda g: a.V[slot][:, g, :]), [a.KT[slot], a.V[slot]]
    if nS:
        attn_project(p, cx, a, 0, 0, 0, mods["smp"])
    for i in range(nS):
        if i + 1 < nS:
            attn_project(p, cx, a, (i + 1) * 128, (i + 1) % 3, i + 1, mods["smp"])
        chunks = []
        for c in range(4):
            chunks.append(((lambda g, c=c: a.KTc[:, g, c, :]), (lambda g, c=c: a.Vc[:, c, g, :]), [a.KTc, a.Vc], None))
        if i > 0:
            k_, v_, d_ = loc((i - 1) % 3); chunks.append((k_, v_, d_, 0))
        k_, v_, d_ = loc(i % 3); chunks.append((k_, v_, d_, None))
        if i + 1 < nS:
            k_, v_, d_ = loc((i + 1) % 3); chunks.append((k_, v_, d_, 1))
        attn_attend(p, cx, a, i * 128, i % 3, chunks, mods["smp"], lng, lnb)
    for s in range(cx.n_prm_seq):
        base = nS * 128 + s * 256
        for t in range(2):
            attn_project(p, cx, a, base + t * 128, t, None, mods["ctx"], want_kv_out=kv_out(s, t))
        for t in range(2):
            chunks = []
            for t2 in range(2):
                k_, v_, d_ = loc(t2); chunks.append((k_, v_, d_, None))
            attn_attend(p, cx, a, base + t * 128, t, chunks, mods["ctx"], lng, lnb)


def host_consts():
    C = {}
    C["c_ident"] = np.eye(128, dtype=np.float32)
    e3 = np.zeros((128, 128), np.float32)
    for q in range(4):
        for part in range(3):
            for hh in range(8):
                e3[32 * q + part * 8 + hh, 16 * hh:16 * hh + 16] = 1
    C["c_e3"] = e3
    kk = np.arange(128)[:, None]; qq = np.arange(128)[None, :]
    C["c_masks"] = np.stack([(kk >= qq), (kk <= qq)], axis=1).astype(np.float32)
    pos = np.arange(4096)
    row = (pos // 64).astype(np.float32); col = (pos % 64).astype(np.float32)
    inv = (10000.0 ** (-np.arange(16, dtype=np.float32) / 16)).astype(np.float32)
    rope = np.zeros((64, 2, 4096), np.float32)
    for d in range(64):
        axis = row if d < 32 else col
        ang = (axis * inv[d % 16]).astype(np.float32)
        first = (d % 32) < 16
        rope[d, 0] = np.cos(ang)
        rope[d, 1] = -np.sin(ang) if first else np.sin(ang)
    C["c_rope"] = rope
    return C


def perm_qk(wqkv):
    idx = np.arange(1280)
    d = idx % 64
    partner = np.where((d % 32) < 16, idx + 16, idx - 16)
    return np.ascontiguousarray(wqkv[..., partner])


class Scope:
    def __init__(self, p):
        self.p = p

    def __enter__(self):
        self.saved = self.p.es
        self.p.es = ExitStack()
        return self

    def __exit__(self, *a):
        p = self.p
        allc = list(p.cnt.items())
        for e in ("pe", "act", "dve", "pool", "sp"):
            p._need(e, allc)
        p.es.close()
        p.es = self.saved
        return False


TWO_PI = 2.0 * np.pi
JB = 32


def _tt(p, eng, out, in0, in1, op, reads, writes):
    p.op(eng, lambda e: e.tensor_tensor(out, in0, in1, op), reads=reads, writes=writes)


def ssm_precompute(p, cx, R, jl, d):
    B = cx.banks
    G = 64
    with Scope(p):
        pg = p.sbuf([64, 3, 64], F32); Bt = p.sbuf([64, 2, 1024], F32); Ct = p.sbuf([64, 2, 1024], F32)
        PW = p.sbuf([64, 17, 2, 64], F32); sm = p.sbuf([64, 8, 64], F32); Bb = p.sbuf([64, 2, 1024], F32)
        WT = p.sbuf([64, 8, 2, 128], F32); XT = p.sbuf([64, 8, 2, 128], F32); Mt = p.sbuf([64, 8, 2, 128], F32)
        XTb = p.sbuf([64, 8, 2, 128], BF16); z = [p.sbuf([64, 8, 16], F32) for _ in range(4)]
        tmask = p.sbuf([128, 128], F32)
        qi = p.sbuf([64, 64], mybir.dt.int32)
        p.dma("sp", pg[:], cx.dram["ssm_pg"][jl, d].rearrange("k p g -> p k g"), writes=[pg])
        p.dma("sp", Bt[:], cx.dram["ssm_b"][jl, d].rearrange("k p x -> p k x"), writes=[Bt])
        p.dma("sp", Ct[:], cx.dram["ssm_c"][jl, d].rearrange("k p x -> p k x"), writes=[Ct])
        p.dma("sp", tmask[:], cx.dram["c_tmask"][d], writes=[tmask])
        DT, LR, LI, T0, T1, T2, CR, CI = range(8)
        p.op("act", lambda e: e.activation(sm[:, DT, :], pg[:, 2, :], AF.Exp), reads=[pg], writes=[sm])
        _tt(p, "dve", sm[:, LR, :], pg[:, 0, :], sm[:, DT, :], ALU.mult, [pg, sm], [sm])
        _tt(p, "dve", sm[:, LI, :], pg[:, 1, :], sm[:, DT, :], ALU.mult, [pg, sm], [sm])
        for k in range(-8, 9):
            kk = k + 8
            p.op("act", lambda e, k=k: e.activation(sm[:, T0, :], sm[:, LR, :], AF.Exp, scale=float(k)), reads=[sm], writes=[sm])
            for ri, ph in ((1, 0.0), (0, np.pi / 2)):
                p.op("dve", lambda e, k=k, ph=ph: e.tensor_scalar(sm[:, T1, :], sm[:, LI, :], float(k / TWO_PI), float(ph / TWO_PI + 64.0), ALU.mult, ALU.add),
                     reads=[sm], writes=[sm])
                p.op("dve", lambda e: e.tensor_copy(qi[:], sm[:, T1, :]), reads=[sm], writes=[qi])
                p.op("dve", lambda e: e.tensor_copy(sm[:, T2, :], qi[:]), reads=[qi], writes=[sm])
                _tt(p, "dve", sm[:, T1, :], sm[:, T1, :], sm[:, T2, :], ALU.subtract, [sm], [sm])
                p.op("dve", lambda e: e.tensor_single_scalar(sm[:, T2, :], sm[:, T1, :], 0.5, ALU.is_gt), reads=[sm], writes=[sm])
                _tt(p, "dve", sm[:, T1, :], sm[:, T1, :], sm[:, T2, :], ALU.subtract, [sm], [sm])
                p.op("act", lambda e: e.activation(sm[:, T2, :], sm[:, T1, :], AF.Sin, scale=float(TWO_PI)), reads=[sm], writes=[sm])
                _tt(p, "dve", PW[:, kk, ri, :], sm[:, T0, :], sm[:, T2, :], ALU.mult, [sm], [PW])
        p.op("act", lambda e: e.copy(R.A2[:, 0, :], PW[:, 16, 0, :]), reads=[PW], writes=[R.A2])
        p.op("act", lambda e: e.copy(R.A2[:, 1, :], PW[:, 16, 0, :]), reads=[PW], writes=[R.A2])
        p.op("act", lambda e: e.mul(R.AI2[:, 0, :], PW[:, 16, 1, :], -1.0), reads=[PW], writes=[R.AI2])
        p.op("act", lambda e: e.copy(R.AI2[:, 1, :], PW[:, 16, 1, :]), reads=[PW], writes=[R.AI2])
        are = pg[:, 0, :]; aim = pg[:, 1, :]
        p.op("dve", lambda e: e.tensor_scalar(sm[:, T0, :], PW[:, 9, 0, :], -1.0, None, ALU.add), reads=[PW], writes=[sm])
        _tt(p, "dve", sm[:, T1, :], are, are, ALU.mult, [pg], [sm])
        _tt(p, "dve", sm[:, T2, :], aim, aim, ALU.mult, [pg], [sm])
        _tt(p, "dve", sm[:, T1, :], sm[:, T1, :], sm[:, T2, :], ALU.add, [sm], [sm])
        p.op("dve", lambda e: e.reciprocal(sm[:, T1, :], sm[:, T1, :]), reads=[sm], writes=[sm])
        _tt(p, "dve", sm[:, CR, :], sm[:, T0, :], are, ALU.mult, [sm, pg], [sm])
        _tt(p, "dve", sm[:, T2, :], PW[:, 9, 1, :], aim, ALU.mult, [PW, pg], [sm])
        _tt(p, "dve", sm[:, CR, :], sm[:, CR, :], sm[:, T2, :], ALU.add, [sm], [sm])
        _tt(p, "dve", sm[:, CR, :], sm[:, CR, :], sm[:, T1, :], ALU.mult, [sm], [sm])
        _tt(p, "dve", sm[:, CI, :], PW[:, 9, 1, :], are, ALU.mult, [PW, pg], [sm])
        _tt(p, "dve", sm[:, T2, :], sm[:, T0, :], aim, ALU.mult, [sm, pg], [sm])
        _tt(p, "dve", sm[:, CI, :], sm[:, CI, :], sm[:, T2, :], ALU.subtract, [sm], [sm])
        _tt(p, "dve", sm[:, CI, :], sm[:, CI, :], sm[:, T1, :], ALU.mult, [sm], [sm])
        crb = sm.ap(CR * 64, [[1, 64], [0, 16]], np_=64); cib = sm.ap(CI * 64, [[1, 64], [0, 16]], np_=64)
        Br = Bt.ap(0, [[16, 64], [1, 16]], np_=64); Bi = Bt.ap(1024, [[16, 64], [1, 16]], np_=64)
        Bbr = Bb.ap(0, [[16, 64], [1, 16]], np_=64); Bbi = Bb.ap(1024, [[16, 64], [1, 16]], np_=64)
        big0 = Ct
        tA = XT.ap(0, [[16, 64], [1, 16]], np_=64); tB = XT.ap(1024, [[16, 64], [1, 16]], np_=64)
        _tt(p, "dve", tA, Br, crb, ALU.mult, [Bt, sm], [XT])
        _tt(p, "dve", tB, Bi, cib, ALU.mult, [Bt, sm], [XT])
        _tt(p, "dve", Bbr, tA, tB, ALU.subtract, [XT], [Bb])
        _tt(p, "dve", tA, Br, cib, ALU.mult, [Bt, sm], [XT])
        _tt(p, "dve", tB, Bi, crb, ALU.mult, [Bt, sm], [XT])
        _tt(p, "dve", Bbi, tA, tB, ALU.add, [XT], [Bb])
        for gb in range(8):
            g0 = gb * 8
            def pw(k, ri):
                return PW.ap(((k + 8) * 2 + ri) * 64 + g0, [[1, 8], [0, 16]], np_=64)
            def cplx(dst, ri_stride, slot, Xr, Xi, k, deps_x, neg_im=False):
                dr = dst.ap(slot * 16, [[256, 8], [1, 16]], np_=64); di = dst.ap(128 + slot * 16, [[256, 8], [1, 16]], np_=64)
                _tt(p, "dve", z[0][:], Xr, pw(k, 0), ALU.mult, deps_x + [PW], [z[0]])
                _tt(p, "pool", z[1][:], Xi, pw(k, 1), ALU.mult, deps_x + [PW], [z[1]])
                _tt(p, "dve", dr, z[0][:], z[1][:], ALU.subtract, [z[0], z[1]], [dst])
                _tt(p, "pool", z[2][:], Xr, pw(k, 1), ALU.mult, deps_x + [PW], [z[2]])
                _tt(p, "dve", z[3][:], Xi, pw(k, 0), ALU.mult, deps_x + [PW], [z[3]])
                if neg_im:
                    p.op("dve", lambda e: e.scalar_tensor_tensor(di, z[2][:], -1.0, z[3][:], ALU.mult, ALU.subtract), reads=[z[2], z[3]], writes=[dst])
                else:
                    _tt(p, "dve", di, z[2][:], z[3][:], ALU.add, [z[2], z[3]], [dst])
            bbr = Bb.ap(g0 * 16, [[16, 8], [1, 16]], np_=64); bbi = Bb.ap(1024 + g0 * 16, [[16, 8], [1, 16]], np_=64)
            ccr = Ct.ap(g0 * 16, [[16, 8], [1, 16]], np_=64); cci = Ct.ap(1024 + g0 * 16, [[16, 8], [1, 16]], np_=64)
            for s in range(8):
                kw = (7 - s) if d == 0 else s
                kx = (-s - 1) if d == 0 else (s - 8)
                km = (s + 1) if d == 0 else (8 - s)
                cplx(WT, 0, s, bbr, bbi, kw, [Bb])
                cplx(XT, 0, s, bbr, bbi, kx, [Bb])
                cplx(Mt, 0, s, ccr, cci, km, [Ct], neg_im=True)
            p.op("act", lambda e: e.copy(XTb[:], XT[:]), reads=[XT], writes=[XTb])
            p.op("act", lambda e, g0=g0: e.copy(R.M[:, g0:g0 + 8, :, :], Mt[:]), reads=[Mt], writes=[R.M])
            for gl in range(8):
                g = g0 + gl
                bk = B[gl % 2]
                for ri in range(2):
                    p.op("pe", lambda e, gl=gl, ri=ri, bk=bk: e.transpose(bk[:, ri * 64:(ri + 1) * 64], WT[:, gl, ri, :], cx.ident[0:64, 0:64]),
                         reads=[WT, cx.ident], writes=[bk])
                p.op("act", lambda e, g=g, bk=bk: e.copy(R.Win[:, g, :], bk[:, 0:128]), reads=[bk], writes=[R.Win])
                bt = B[2 + gl % 2]
                for ri in range(2):
                    p.op("pe", lambda e, gl=gl, g=g, ri=ri, bt=bt: e.matmul(bt[:, 0:128], XTb[:, gl, ri, :], R.M[:, g, ri, :], start=(ri == 0), stop=(ri == 1)),
                         reads=[XTb, R.M], writes=[bt])
                p.op("dve", lambda e, g=g, bt=bt: e.tensor_tensor(R.Toep[:, g, :], bt[:, 0:128], tmask[:], ALU.mult), reads=[bt, tmask], writes=[R.Toep])


def ssm_sweep(p, cx, R, jl, d, seqs):
    B = cx.banks
    with Scope(p):
        u8s = [p.sbuf([128, JB, 64], BF16) for _ in range(2)]; Ss = [p.sbuf([64, 2, 64, JB], F32) for _ in range(2)]
        Hs = [p.sbuf([64, 2, 64], F32) for _ in range(2)]; cur = [0]
        t1 = p.sbuf([64, 2, 64], F32); t2 = p.sbuf([64, 2, 64], F32); Hhs = [p.sbuf([64, 2, 64, JB + 1], BF16) for _ in range(2)]
        Y8s = [p.sbuf([128, JB, 64], F32) for _ in range(2)]; Yf = p.sbuf([128, JB, 64], F32); d8 = p.sbuf([128, 64], F32)
        hs = p.sbuf([64, 2, 64], F32)
        p.dma("sp", d8[:], cx.dram["ssm_d8"][jl], writes=[d8])
        udv = cx.Ud.rearrange("(j s) (c g) -> s c j g", s=8, c=16)
        ydv = cx.Yd.rearrange("(j s) (c g) -> s c j g", s=8, c=16)
        items = []
        for sq in seqs:
            nb = sq["n"] // JB
            order = list(range(nb)) if d == 0 else list(range(nb - 1, -1, -1))
            for n_, bi in enumerate(order):
                items.append((sq, sq["j0"] + bi * JB, n_ == 0, n_ == nb - 1))

        def prep(k):
            sq, j0, first, last = items[k]
            u8 = u8s[k % 2]; S = Ss[k % 2]
            tl = (j0 * 8) // 128
            ddeps = [cx.dUd[tl], cx.dUd[tl + 1]]
            for s_ in range(8):
                p.dma("sp", u8[16 * s_:16 * s_ + 16, :, :], udv[s_, :, j0:j0 + JB, :], reads=ddeps, writes=[u8])
            for gq in range(8):
                bk = B[gq % 2]
                for gl in range(8):
                    g = gq * 8 + gl
                    for ri in range(2):
                        p.op("pe", lambda e, g=g, gl=gl, ri=ri, bk=bk: e.matmul(bk[0:64, (gl * 2 + ri) * JB:(gl * 2 + ri + 1) * JB], R.Win[:, g, ri * 64:(ri + 1) * 64],
                                                                               u8.ap(g, [[64, JB]]), start=True, stop=True), reads=[R.Win, u8], writes=[bk])
                p.op("act", lambda e, gq=gq, bk=bk: e.copy(S.ap(gq * 8 * JB, [[JB, 8], [64 * JB, 2], [1, JB]], np_=64),
                                                           bk.ap(0, [[2 * JB, 8], [JB, 2], [1, JB]], np_=64)), reads=[bk], writes=[S])

        def scan(k):
            sq, j0, first, last = items[k]
            S = Ss[k % 2]; Hh = Hhs[k % 2]
            if first:
                H = Hs[cur[0]]
                if sq["init"] is not None:
                    p.dma("sp", H[:], sq["init"], writes=[H])
                else:
                    p.op("dve", lambda e, H=H: e.memset(H[:], 0.0), writes=[H])
            hidx0 = 0 if d == 0 else JB
            p.op("pool", lambda e: e.tensor_copy(Hh.ap(hidx0, [[JB + 1, 128]], np_=64), Hs[cur[0]].ap(0, [[1, 128]], np_=64)), reads=[Hs[cur[0]]], writes=[Hh])
            for jj in (range(JB) if d == 0 else range(JB - 1, -1, -1)):
                H = Hs[cur[0]]; Hn = Hs[1 - cur[0]]
                _tt(p, "dve", t1[:], H[:], R.A2[:], ALU.mult, [H, R.A2], [t1])
                _tt(p, "dve", t2[:, 0, :], H[:, 1, :], R.AI2[:, 0, :], ALU.mult, [H, R.AI2], [t2])
                _tt(p, "dve", t2[:, 1, :], H[:, 0, :], R.AI2[:, 1, :], ALU.mult, [H, R.AI2], [t2])
                _tt(p, "dve", t1[:], t1[:], t2[:], ALU.add, [t1, t2], [t1])
                _tt(p, "dve", Hn[:], t1[:], S.ap(jj, [[64 * JB, 2], [JB, 64]], np_=64), ALU.add, [t1, S], [Hn])
                hi = (jj + 1) if d == 0 else jj
                p.op("pool", lambda e, hi=hi, Hn=Hn: e.tensor_copy(Hh.ap(hi, [[JB + 1, 128]], np_=64), Hn.ap(0, [[1, 128]], np_=64)), reads=[Hn], writes=[Hh])
                cur[0] = 1 - cur[0]
            if last and sq["out"] is not None:
                bk = B[0]
                for ri in range(2):
                    p.op("pe", lambda e, ri=ri: e.transpose(bk[0:64, ri * 64:(ri + 1) * 64], Hs[cur[0]][:, ri, :], cx.ident[0:64, 0:64]), reads=[Hs[cur[0]], cx.ident], writes=[bk])
                p.op("act", lambda e: e.copy(hs[:], bk.ap(0, [[64, 2], [1, 64]], np_=64)), reads=[bk], writes=[hs])
                p.dma("act", sq["out"].rearrange("r g q -> g r q"), hs[:], reads=[hs], writes=[cx.d_out])

        def outp(k):
            sq, j0, first, last = items[k]
            u8 = u8s[k % 2]; Hh = Hhs[k % 2]; Y8 = Y8s[k % 2]
            tl = (j0 * 8) // 128
            hoff = 0 if d == 0 else 1
            for gq in range(4):
                bk = B[2 + gq]
                for gl in range(16):
                    g = gq * 16 + gl
                    o = bk[:, gl * JB:(gl + 1) * JB]
                    p.op("pe", lambda e, g=g, o=o: e.matmul(o, R.Toep[:, g, :], u8.ap(g, [[64, JB]]), start=True, stop=False), reads=[R.Toep, u8], writes=[bk])
                    for ri in range(2):
                        p.op("pe", lambda e, g=g, o=o, ri=ri: e.matmul(o, R.M[:, g, ri, :], Hh.ap((ri * 64 + g) * (JB + 1) + hoff, [[1, JB]], np_=64),
                                                                       start=False, stop=(ri == 1)), reads=[R.M, Hh], writes=[bk])
                p.op("act", lambda e, gq=gq, bk=bk: e.copy(Y8.ap(gq * 16, [[1, 16], [64, JB]]), bk.ap(0, [[JB, 16], [1, JB]])), reads=[bk], writes=[Y8])
            if d == 0:
                p.dma("sp", cx.Yf8[:, j0:j0 + JB, :], Y8[:], reads=[Y8], writes=[cx.dYf[j0 // JB]])
            else:
                p.dma("sp", Yf[:], cx.Yf8[:, j0:j0 + JB, :], reads=[cx.dYf[j0 // JB]], writes=[Yf])
                _tt(p, "pool", Y8[:], Y8[:], Yf[:], ALU.add, [Y8, Yf], [Y8])
                _tt(p, "pool", Yf[:], u8[:], d8.ap(0, [[0, JB], [1, 64]]), ALU.mult, [u8, d8], [Yf])
                _tt(p, "pool", Y8[:], Y8[:], Yf[:], ALU.add, [Y8, Yf], [Y8])
                p.op("act", lambda e: e.activation(Y8[:], Y8[:], AF.Gelu_apprx_tanh), reads=[Y8], writes=[Y8])
                for s_ in range(8):
                    p.dma("sp", ydv[s_, :, j0:j0 + JB, :], Y8[16 * s_:16 * s_ + 16, :, :], reads=[Y8], writes=[cx.dYd[tl], cx.dYd[tl + 1]])

        n = len(items)
        if n:
            prep(0)
        for k in range(n):
            if k + 1 < n:
                prep(k + 1)
            scan(k)
            outp(k)


def ssm_layer(p, cx, jl, mods, lng, lnb, seqs_fn):
    B = cx.banks
    NTt = cx.NT // 128
    nS = cx.n_smp_tiles
    with Scope(p):
        win = p.sbuf([128, 8, 1024], BF16)
        xt = [p.sbuf([128, 1024], F32) for _ in range(2)]; hm = p.sbuf([128, 1024], F32); hT = p.sbuf([128, 8, 128], BF16)
        uo = [p.sbuf([128, 1024], BF16) for _ in range(2)]
        w = cx.dram["ssm_w_in"][jl]
        for k in range(8):
            p.dma("pool", win[:, k, :], w[k * 128:(k + 1) * 128, :], writes=[win])
        p.dma("sp", xt[0][:], cx.X[0:128, :], reads=[cx.dX[0]], writes=[xt[0]])
        for i in range(NTt):
            mod = mods["smp"] if i < nS else mods["ctx"]
            x_ = xt[i % 2]
            if i + 1 < NTt:
                p.dma("sp", xt[(i + 1) % 2][:], cx.X[(i + 1) * 128:(i + 2) * 128, :], reads=[cx.dX[i + 1]], writes=[xt[(i + 1) % 2]])
            modulate_transpose(p, cx, x_, hm, lambda half: (hT[:, half * 4:(half + 1) * 4, :], hT), mod, [B[0], B[1]])
            for dh in range(2):
                bk = B[2 + dh]
                for k in range(8):
                    p.op("pe", lambda e, dh=dh, k=k, bk=bk: e.matmul(bk[:], hT[:, k, :], win[:, k, dh * 512:(dh + 1) * 512], start=(k == 0), stop=(k == 7)),
                         reads=[hT, win], writes=[bk])
                p.op("act" if dh == 0 else "dve", (lambda e, dh=dh, bk=bk, i=i: e.copy(uo[i % 2][:, dh * 512:(dh + 1) * 512], bk[:])) if dh == 0 else
                     (lambda e, dh=dh, bk=bk, i=i: e.tensor_copy(uo[i % 2][:, dh * 512:(dh + 1) * 512], bk[:])), reads=[bk], writes=[uo[i % 2]])
            p.dma("act", cx.Ud[i * 128:(i + 1) * 128, :], uo[i % 2][:], reads=[uo[i % 2]], writes=[cx.dUd[i]])
    with Scope(p):
        R = Ctx()
        R.Win = p.sbuf([128, 64, 128], BF16); R.Toep = p.sbuf([128, 64, 128], BF16); R.M = p.sbuf([64, 64, 2, 128], BF16)
        R.A2 = p.sbuf([64, 2, 64], F32); R.AI2 = p.sbuf([64, 2, 64], F32)
        for d in range(2):
            ssm_precompute(p, cx, R, jl, d)
            ssm_sweep(p, cx, R, jl, d, seqs_fn(d))
    with Scope(p):
        wg = p.sbuf([128, 8, 2048], BF16); wo = p.sbuf([128, 8, 1024], BF16)
        xt = [p.sbuf([128, 1024], F32) for _ in range(3)]; yt = [p.sbuf([128, 1024], F32) for _ in range(3)]
        yT = p.sbuf([128, 8, 128], BF16); sg = p.sbuf([128, 1024], F32); vvs = [p.sbuf([128, 1024], F32) for _ in range(2)]; vT = p.sbuf([128, 8, 128], BF16)
        t1 = p.sbuf([128, 1024], F32); y = p.sbuf([128, 1024], F32)
        scr = {"st": p.sbuf([128, 12], F32), "mv": p.sbuf([128, 2], F32), "rstd": p.sbuf([128, 1], F32)}
        for k in range(8):
            p.dma("pool", wg[:, k, :], cx.dram["ssm_w_glu"][jl][k * 128:(k + 1) * 128, :], writes=[wg])
            p.dma("pool", wo[:, k, :], cx.dram["ssm_w_out"][jl][k * 128:(k + 1) * 128, :], writes=[wo])

        def loads(i):
            p.dma("sp", xt[i % 3][:], cx.X[i * 128:(i + 1) * 128, :], reads=[cx.dX[i]], writes=[xt[i % 3]])
            p.dma("sp", yt[i % 3][:], cx.Yd[i * 128:(i + 1) * 128, :], reads=[cx.dYd[i]], writes=[yt[i % 3]])

        def S1(i):
            y_ = yt[i % 3]; vv = vvs[i % 2]
            for half in range(2):
                bk = B[half]
                for k4 in range(4):
                    k = half * 4 + k4
                    p.op("pe", lambda e, k=k, k4=k4, bk=bk: e.transpose(bk[:, k4 * 128:(k4 + 1) * 128], y_[:, k * 128:(k + 1) * 128], cx.ident[:]), reads=[y_, cx.ident], writes=[bk])
                p.op("act", lambda e, half=half, bk=bk: e.copy(yT[:, half * 4:(half + 1) * 4, :], bk.ap(0, [[128, 4], [1, 128]])), reads=[bk], writes=[yT])
            for q in range(4):
                bk = B[2 + q]
                for k in range(8):
                    p.op("pe", lambda e, q=q, k=k, bk=bk: e.matmul(bk[:], yT[:, k, :], wg[:, k, q * 512:(q + 1) * 512], start=(k == 0), stop=(k == 7)), reads=[yT, wg], writes=[bk])
            for q in range(2):
                p.op("act", lambda e, q=q: e.activation(sg[:, q * 512:(q + 1) * 512], B[4 + q][:], AF.Sigmoid), reads=[B[4 + q]], writes=[sg])
                p.op("dve", lambda e, q=q: e.tensor_tensor(vv[:, q * 512:(q + 1) * 512], B[2 + q][:], sg[:, q * 512:(q + 1) * 512], ALU.mult), reads=[B[2 + q], sg], writes=[vv])

        def S2(i):
            mod = mods["smp"] if i < nS else mods["ctx"]
            vv = vvs[i % 2]
            for half in range(2):
                bk = B[6 + half]
                for k4 in range(4):
                    k = half * 4 + k4
                    p.op("pe", lambda e, k=k, k4=k4, bk=bk: e.transpose(bk[:, k4 * 128:(k4 + 1) * 128], vv[:, k * 128:(k + 1) * 128], cx.ident[:]), reads=[vv, cx.ident], writes=[bk])
                p.op("act", lambda e, half=half, bk=bk: e.copy(vT[:, half * 4:(half + 1) * 4, :], bk.ap(0, [[128, 4], [1, 128]])), reads=[bk], writes=[vT])
            psO = [B[6], B[7]]
            for dh in range(2):
                for k in range(8):
                    p.op("pe", lambda e, dh=dh, k=k: e.matmul(psO[dh][:], vT[:, k, :], wo[:, k, dh * 512:(dh + 1) * 512], start=(k == 0), stop=(k == 7)), reads=[vT, wo], writes=[psO[dh]])
            post_norm_store(p, cx, xt[i % 3], psO, mod, lng, lnb, t1, y, scr, i * 128, q="sp")

        loads(0)
        if NTt > 1:
            loads(1)
        S1(0)
        for i in range(NTt):
            if i + 2 < NTt:
                loads(i + 2)
            if i + 1 < NTt:
                S1(i + 1)
            S2(i)


def host_ssm_layouts(inp, jl):
    o = {}
    pg = np.stack([np.stack([inp["ssm_a_re"][jl, d].T, inp["ssm_a_im"][jl, d].T,
                             np.broadcast_to(inp["ssm_log_dt"][jl, d][None, :], (64, 64))]) for d in range(2)])
    o["ssm_pg"] = np.ascontiguousarray(pg, dtype=np.float32)
    o["ssm_b"] = np.ascontiguousarray(np.stack([np.stack([inp["ssm_b_re"][jl, d].transpose(1, 0, 2).reshape(64, 1024),
                                                          inp["ssm_b_im"][jl, d].transpose(1, 0, 2).reshape(64, 1024)]) for d in range(2)]))
    o["ssm_c"] = np.ascontiguousarray(np.stack([np.stack([inp["ssm_c_re"][jl, d].transpose(2, 0, 1).reshape(64, 1024),
                                                          inp["ssm_c_im"][jl, d].transpose(2, 0, 1).reshape(64, 1024)]) for d in range(2)]))
    dd = inp["ssm_d"][jl].reshape(64, 16).T
    o["ssm_d8"] = np.ascontiguousarray(np.tile(dd, (8, 1)))
    perm = (np.arange(64)[None, :] * 16 + np.arange(16)[:, None]).reshape(-1)
    o["ssm_w_in"] = np.ascontiguousarray(inp["ssm_w_in"][jl][:, perm])
    o["ssm_w_glu"] = np.ascontiguousarray(inp["ssm_w_glu"][jl][perm, :])
    o["ssm_w_out"] = np.ascontiguousarray(inp["ssm_w_out"][jl])
    return o


def host_tmask():
    s = np.arange(128)[:, None] // 16; t = np.arange(128)[None, :] // 16
    return np.stack([(s <= t), (s >= t)]).astype(np.float32)


def compute_mods(p, cx, l, half):
    B = cx.banks
    names = ["sh", "sc1", "g1"]
    with Scope(p):
        wm = [p.sbuf([128, 8, 512], BF16) for _ in range(2)]
        bm = [p.sbuf([128, 512], F32) for _ in range(2)]
        it = 0
        for v in range(3):
            for dh in range(2):
                col = half * 3072 + v * 1024 + dh * 512
                w_ = wm[it % 2]; b_ = bm[it % 2]
                p.dma("pool", w_[:], cx.dram["w_mod"][l][:, col:col + 512].rearrange("(k q) c -> q k c", q=128), writes=[w_])
                p.dma("sp", b_[:], cx.dram["b_mod"][l][col:col + 512].partition_broadcast(128), writes=[b_])
                for ci, cn in enumerate(("smp", "ctx")):
                    bk = B[(it * 2 + ci) % 8]
                    for k in range(8):
                        p.op("pe", lambda e, k=k, ci=ci, bk=bk, w_=w_: e.matmul(bk[:], cx.SCb[ci][:, k, :], w_[:, k, :], start=(k == 0), stop=(k == 7)),
                             reads=[cx.SCb[ci], w_], writes=[bk])
                    dst = cx.mods[cn][names[v]]
                    p.op("dve", lambda e, bk=bk, dst=dst, b_=b_, v=v, dh=dh: e.scalar_tensor_tensor(
                        dst[:, dh * 512:(dh + 1) * 512], bk[:], 1.0 if v >= 1 else 0.0, b_[:], ALU.add, ALU.add), reads=[bk, b_], writes=[dst])
                it += 1


def load_ln(p, cx, l, which):
    p.dma("sp", cx.lng[:], cx.dram["ln_g"][l, which].partition_broadcast(128), writes=[cx.lng])
    p.dma("sp", cx.lnb[:], cx.dram["ln_b"][l, which].partition_broadcast(128), writes=[cx.lnb])


NS_TILES = 32
NP_SEQ = 4
NTOK = NS_TILES * 128 + NP_SEQ * 256


def build_program(nS=NS_TILES, nP=NP_SEQ, depth=DEPTH):
    nc = bass.Bass("TRN2", target_bir_lowering=False)
    p = Prog(nc)
    cx = Ctx()
    NT = nS * 128 + nP * 256
    NTt = NT // 128
    NCH = NT // 8
    cx.NT = NT; cx.n_smp_tiles = nS; cx.n_prm_seq = nP

    def din(name, shape, dt=F32):
        return nc.dram_tensor(name, list(shape), dt, kind="ExternalInput").ap()

    def dout(name, shape):
        return nc.dram_tensor(name, list(shape), F32, kind="ExternalOutput").ap()

    cx.dram = {
        "c_ident": din("c_ident", [128, 128]), "c_e3": din("c_e3", [128, 128]), "c_masks": din("c_masks", [128, 2, 128]),
        "c_rope": din("c_rope", [64, 2, 4096]), "c_tmask": din("c_tmask", [2, 128, 128]),
        "w_mod": din("w_mod", [4, 1024, 6144]), "b_mod": din("b_mod", [4, 6144]), "ln_g": din("ln_g", [4, 2, 1024]), "ln_b": din("ln_b", [4, 2, 1024]),
        "ssm_pg": din("ssm_pg", [2, 2, 3, 64, 64]), "ssm_b": din("ssm_b", [2, 2, 2, 64, 1024]), "ssm_c": din("ssm_c", [2, 2, 2, 64, 1024]),
        "ssm_d8": din("ssm_d8", [2, 128, 64]), "ssm_w_in": din("ssm_w_in", [2, 1024, 1024]), "ssm_w_glu": din("ssm_w_glu", [2, 1024, 2048]),
        "ssm_w_out": din("ssm_w_out", [2, 1024, 1024]),
        "attn_w_qkv": din("attn_w_qkv", [2, 1024, 1536]), "attn_w_qkp": din("attn_w_qkp", [2, 1024, 1280]),
        "attn_w_out": din("attn_w_out", [2, 1024, 1024]), "attn_sink": din("attn_sink", [2, 16]),
        "cache_k": din("cache_k", [2, 512, 256]), "cache_v": din("cache_v", [2, 512, 256]),
        "peer_w_q": din("peer_w_q", [4, 1024, 2048]), "peer_keysT": din("peer_keysT", [4, 128, 16, 128]),
        "peer_ut": din("peer_ut", [4, 128, 128, 1024]), "peer_v": din("peer_v", [4, NEXP, 1024]),
    }
    x_in = din("x_in", [NT, 1024]); cond2 = din("cond2", [2, 128, 8]); st0 = din("st0", [2, 2, 64, 2, 64])
    y_out = dout("y_out", [NT, 1024]); o_state = dout("o_state", [nP, 2, 2, 2, 64, 64])
    o_k = dout("o_k", [nP, 2, 256, 256]); o_v = dout("o_v", [nP, 2, 256, 256])
    cx.X = nc.dram_tensor("Xs", [NT, 1024], F32).ap()
    cx.Ud = nc.dram_tensor("Ud", [NT, 1024], BF16).ap(); cx.Yd = nc.dram_tensor("Yd", [NT, 1024], F32).ap()
    cx.Yf8 = nc.dram_tensor("Yf8", [128, NCH, 64], F32).ap()
    cx.spd = nc.dram_tensor("spd", [2, 3, 8, NT, 128], BF16).ap()
    cx.utb = nc.dram_tensor("utb", [128, 128, 1024], BF16).ap(); cx.vb = nc.dram_tensor("vb", [NEXP, 1024], BF16).ap()
    cx.dX = [Dep() for _ in range(NTt)]; cx.dUd = [Dep() for _ in range(NTt)]; cx.dYd = [Dep() for _ in range(NTt)]
    cx.dYf = [Dep() for _ in range(NCH // JB)]; cx.d_spd = [Dep() for _ in range(NTt)]
    cx.d_utb = [Dep() for _ in range(16)]; cx.d_vb = [Dep() for _ in range(16)]
    cx.d_out = Dep()
    setup_common(p, cx)
    for i in range(NTt):
        p.dma("sp", cx.X[i * 128:(i + 1) * 128, :], x_in[i * 128:(i + 1) * 128, :], writes=[cx.dX[i]])
    cx.mods = {cn: {n: p.sbuf([128, 1024], F32, name="mod_%s_%s" % (cn, n)) for n in ("sh", "sc1", "g1")} for cn in ("smp", "ctx")}
    cx.lng = p.sbuf([128, 1024], F32, name="lng"); cx.lnb = p.sbuf([128, 1024], F32, name="lnb")
    cx.SCb = [p.sbuf([128, 8, 128], BF16, name="SCb%d" % i) for i in range(2)]
    ctile = p.sbuf([128, 2, 8], F32, name="condt")
    p.dma("sp", ctile[:], cond2.rearrange("a q k -> q a k"), writes=[ctile])
    p.op("act", lambda e: e.activation(ctile[:], ctile[:], AF.Silu), reads=[ctile], writes=[ctile])
    for ci in range(2):
        p.op("dve", lambda e, ci=ci: e.tensor_copy(cx.SCb[ci][:], ctile.ap(ci * 8, [[1, 8], [0, 128]])), reads=[ctile], writes=[cx.SCb[ci]])

    for l in range(depth):
        peer_prep_weights(p, cx, l)
        compute_mods(p, cx, l, 0)
        load_ln(p, cx, l, 0)
        if l % 2 == 0:
            jl = l // 2

            def seqs_fn(d, jl=jl):
                L = []
                if nS:
                    L.append(dict(j0=0, n=nS * 16, init=st0[jl, d], out=None))
                for s in range(nP):
                    L.append(dict(j0=nS * 16 + s * 32, n=32, init=None, out=o_state[s, jl, d]))
                return L
            ssm_layer(p, cx, jl, cx.mods, cx.lng, cx.lnb, seqs_fn)
        else:
            j = l // 2
            with Scope(p):
                a = attn_alloc(p, cx)
                attn_load_layer(p, cx, a, j)
                attn_layer(p, cx, a, j, cx.mods, cx.lng, cx.lnb,
                           lambda s, t, j=j: (o_k[s, j, t * 128:(t + 1) * 128, :], o_v[s, j, t * 128:(t + 1) * 128, :]))
        compute_mods(p, cx, l, 1)
        load_ln(p, cx, l, 1)
        with Scope(p):
            a = peer_alloc(p, cx)
            peer_load_layer(p, cx, a, l)
            for blk in range(NT // 256):
                mod = cx.mods["smp"] if blk * 2 < nS else cx.mods["ctx"]
                peer_block(p, cx, a, l, blk, mod, cx.lng, cx.lnb)
    for i in range(NTt):
        p.dma("sp", y_out[i * 128:(i + 1) * 128, :], cx.X[i * 128:(i + 1) * 128, :], reads=[cx.dX[i]], writes=[cx.d_out])
    p.drain_all()
    p.es.close()
    return nc, p


def host_inputs(inp, core, nS=NS_TILES, nP=NP_SEQ):
    f = lambda a: np.ascontiguousarray(a, dtype=np.float32)
    m = dict(_SHARED)
    xs = inp["x_sample"][core][:nS * 128]
    xp = inp["x_prompt"][core * NP_SEQ: core * NP_SEQ + nP].reshape(nP * 256, 1024)
    m["x_in"] = f(np.concatenate([xs, xp], axis=0))
    m["cond2"] = f(np.stack([inp["c"][core].reshape(8, 128).T, inp["c_ctx"].reshape(8, 128).T]))
    st = np.zeros((2, 2, 64, 2, 64), np.float32)
    for jl in range(2):
        for d in range(2):
            st[jl, d, :, 0, :] = inp["state_ssm_re"][core, jl, d].T
            st[jl, d, :, 1, :] = inp["state_ssm_im"][core, jl, d].T
    m["st0"] = st
    m["cache_k"] = f(inp["cache_k"][core].reshape(2, 512, 256))
    m["cache_v"] = f(inp["cache_v"][core].reshape(2, 512, 256))
    return m


_SHARED = {}


def prepare_shared(inp):
    f = lambda a: np.ascontiguousarray(a, dtype=np.float32)
    S = {}
    S.update(host_consts())
    S["c_tmask"] = host_tmask()
    for k in ("w_mod", "b_mod", "ln_g", "ln_b", "attn_w_qkv", "attn_w_out", "attn_sink", "peer_w_q", "peer_v"):
        S[k] = f(inp[k])
    S["attn_w_qkp"] = f(perm_qk(inp["attn_w_qkv"]))
    L = [host_ssm_layouts(inp, jl) for jl in range(2)]
    for k in L[0]:
        S[k] = f(np.stack([L[0][k], L[1][k]]))
    keys = inp["peer_keys"]
    S["peer_keysT"] = f(keys.transpose(0, 4, 2, 1, 3).reshape(4, 128, 16, 128))
    u = inp["peer_u"]
    S["peer_ut"] = f(u.reshape(4, 128, 128, 8, 128).transpose(0, 1, 4, 3, 2).reshape(4, 128, 128, 1024))
    _SHARED.clear()
    _SHARED.update(S)


_PROG = {}


def kernel(**inputs):
    inp = {k: np.asarray(v) for k, v in inputs.items()}
    if "nc" not in _PROG:
        _PROG["nc"], _ = build_program()
    nc = _PROG["nc"]
    prepare_shared(inp)
    in_maps = [host_inputs(inp, c) for c in range(8)]
    res = run_bass_kernel_spmd(nc, in_maps, core_ids=list(range(8)))
    R = res.results
    nS, nP = NS_TILES, NP_SEQ
    y_sample = np.stack([R[c]["y_out"][:nS * 128] for c in range(8)]).astype(np.float32)
    y_prompt = np.concatenate([R[c]["y_out"][nS * 128:].reshape(nP, 256, 1024) for c in range(8)], axis=0).astype(np.float32)
    ost = np.concatenate([R[c]["o_state"] for c in range(8)], axis=0)
    ns_re = np.ascontiguousarray(ost[:, :, :, 0]).astype(np.float32)
    ns_im = np.ascontiguousarray(ost[:, :, :, 1]).astype(np.float32)
    nk = np.concatenate([R[c]["o_k"] for c in range(8)], axis=0).reshape(32, 2, 256, 4, 64).astype(np.float32)
    nv = np.concatenate([R[c]["o_v"] for c in range(8)], axis=0).reshape(32, 2, 256, 4, 64).astype(np.float32)
    return (y_prompt, y_sample, ns_re, ns_im, nk, nv)
```

```python
from contextlib import ExitStack
import numpy as np
import concourse.bass as bass
import concourse.mybir as mybir
from concourse.ap import AP
from concourse.bass_utils import run_bass_kernel_spmd

F32 = mybir.dt.float32
BF16 = mybir.dt.bfloat16
ALU = mybir.AluOpType
AF = mybir.ActivationFunctionType
AX = mybir.AxisListType

D = 1024
DEPTH = 4
DN_ALPHA = (2 * DEPTH) ** 0.25
LN_EPS = 1e-5
NEXP = 16384
NET = 128


class Dep:
    __slots__ = ("w", "r", "excl")

    def __init__(self):
        self.w = None
        self.r = {}
        self.excl = False


class TT:
    __slots__ = ("t", "d", "pstep", "base", "full")

    def __init__(self, t, base=0, shape=None, dep=None):
        self.t = t
        self.d = dep or Dep()
        self.pstep = t[:].ap[0][0]
        self.base = base
        if shape is None:
            self.full = t[:]
        else:
            dims = []
            st = 1
            for n in reversed(shape):
                dims.append([st, n])
                st *= n
            self.full = AP(t, base, [[self.pstep, 128]] + dims[::-1])

    def __getitem__(self, k):
        return self.full[k]

    def ap(self, off, dims, p0=0, np_=128):
        return AP(self.t, self.base + p0 * self.pstep + off, [[self.pstep, np_]] + [list(x) for x in dims])


def fence(src, dst):
    for a in src:
        a = a.d if isinstance(a, TT) else a
        for b in dst:
            b = b.d if isinstance(b, TT) else b
            if a.w is not None:
                b.r[a.w[0]] = max(b.r.get(a.w[0], 0), a.w[1])
            for k, c in a.r.items():
                b.r[k] = max(b.r.get(k, 0), c)


EMBED_WAIT = True
LN_POW = False


class Prog:
    def __init__(self, nc, n_dma_sems=(44, 30, 16)):
        self.nc = nc
        self.es = ExitStack()
        self.engs = {"pe": nc.tensor, "act": nc.scalar, "dve": nc.vector, "pool": nc.gpsimd, "sp": nc.sync}
        self.sem = {}
        self.cnt = {}
        for k in ("pe", "act", "dve", "pool"):
            self.sem[k] = self.es.enter_context(nc.semaphore("s_" + k))
            self.cnt[k] = 0
        self.dpool = {}
        for q, n in zip(("sp", "pool", "act"), n_dma_sems):
            lst = []
            for i in range(n):
                key = "d_%s_%d" % (q, i)
                self.sem[key] = self.es.enter_context(nc.semaphore(key))
                self.cnt[key] = 0
                lst.append(key)
            self.dpool[q] = [lst, 0]
        self.known = {e: {} for e in self.engs}
        self.ninstr = 0
        self.uid = 0

    def sbuf(self, shape, dt, name=None):
        self.uid += 1
        return TT(self.es.enter_context(self.nc.sbuf_tensor("%s_%d" % (name or "sb", self.uid), list(shape), dt)))

    def psum(self, shape, dt, name=None):
        self.uid += 1
        return TT(self.es.enter_context(self.nc.psum_tensor(name or ("ps%d" % self.uid), list(shape), dt)))

    def _need(self, eng, evs, defer_last=False):
        kn = self.known[eng]
        e = self.engs[eng]
        todo = {}
        for (k, c) in evs:
            if c <= 0 or (k == "pe" and eng == "pe") or kn.get(k, 0) >= c:
                continue
            if todo.get(k, 0) < c:
                todo[k] = c
        items = list(todo.items())
        last = None
        if defer_last and items:
            last = items.pop()
        for k, c in items:
            e.wait_ge(self.sem[k], c)
            kn[k] = c
            self.ninstr += 1
        if last is not None:
            kn[last[0]] = last[1]
        return last

    @staticmethod
    def _collect(reads, writes):
        evs = []
        for d in reads:
            if d.w is not None:
                evs.append(d.w)
        for d in writes:
            if d.w is not None:
                evs.append(d.w)
            evs.extend(d.r.items())
        return evs

    @staticmethod
    def _commit(ev, reads, writes):
        for d in reads:
            d.r[ev[0]] = ev[1]
        for d in writes:
            d.w = ev
            d.r = {}

    def op(self, eng, fn, reads=(), writes=()):
        reads = [x.d if isinstance(x, TT) else x for x in reads]
        writes = [x.d if isinstance(x, TT) else x for x in writes]
        ex = [d for d in reads if d.excl]
        if ex:
            reads = [d for d in reads if not d.excl]
            writes = writes + ex
        last = self._need(eng, self._collect(reads, writes), defer_last=EMBED_WAIT)
        ins = fn(self.engs[eng])
        if last is not None:
            ins._wait_ge(self.sem[last[0]], last[1])
        self.cnt[eng] += 1
        ins.then_inc(self.sem[eng], 1)
        self.ninstr += 1
        self._commit((eng, self.cnt[eng]), reads, writes)
        return ins

    def dma(self, q, out, in_, reads=(), writes=(), **kw):
        reads = [x.d if isinstance(x, TT) else x for x in reads]
        writes = [x.d if isinstance(x, TT) else x for x in writes]
        lst, idx = self.dpool[q]
        key = lst[idx % len(lst)]
        self.dpool[q][1] = idx + 1
        evs = self._collect(reads, writes)
        evs.append((key, self.cnt[key]))
        self._need(q, evs)
        ins = self.engs[q].dma_start(out=out, in_=in_, **kw)
        self.cnt[key] += 16
        ins.then_inc(self.sem[key], 16)
        self.ninstr += 1
        self._commit((key, self.cnt[key]), reads, writes)
        return ins

    def drain_all(self):
        self._need("sp", list(self.cnt.items()))


class Ctx:
    pass


def dram_rows(t, r0, nr, c0=0, ncol=None):
    if ncol is None:
        return t[r0:r0 + nr, :]
    return t[r0:r0 + nr, c0:c0 + ncol]


def setup_common(p, cx):
    nc = p.nc
    cx.banks = [p.psum([128, 512], F32, name="bank%d" % i) for i in range(8)]
    for b in cx.banks:
        b.d.excl = True
    cx.ident = p.sbuf([128, 128], F32, name="ident")
    p.dma("sp", cx.ident[:], cx.dram["c_ident"], writes=[cx.ident])
    cx.e3 = p.sbuf([128, 128], BF16, name="e3")
    p.dma("pool", cx.e3[:], cx.dram["c_e3"], writes=[cx.e3])
    cx.eps = p.sbuf([128, 1], F32, name="eps")
    p.op("dve", lambda e: e.memset(cx.eps[:], LN_EPS), writes=[cx.eps])


def layer_norm_rows(p, cx, y, out, g_bc, b_bc, scr):
    st = scr["st"]; mv = scr["mv"]; rstd = scr["rstd"]
    for c in range(2):
        p.op("dve", lambda e, c=c: e.bn_stats(st[:, c * 6:(c + 1) * 6], y[:, c * 512:(c + 1) * 512]), reads=[y], writes=[st])
    p.op("dve", lambda e: e.bn_aggr(mv[:], st[:]), reads=[st], writes=[mv])
    if LN_POW:
        p.op("dve", lambda e: e.tensor_scalar(rstd[:], mv[:, 1:2], LN_EPS, None, ALU.add), reads=[mv], writes=[rstd])
        p.op("dve", lambda e: e.tensor_single_scalar(rstd[:], rstd[:], -0.5, ALU.pow), reads=[rstd], writes=[rstd])
    else:
        p.op("act", lambda e: e.activation(rstd[:], mv[:, 1:2], AF.Sqrt, bias=cx.eps[:], scale=1.0), reads=[mv, cx.eps], writes=[rstd])
        p.op("dve", lambda e: e.reciprocal(rstd[:], rstd[:]), reads=[rstd], writes=[rstd])
    p.op("dve", lambda e: e.tensor_scalar(y[:], y[:], mv[:, 0:1], rstd[:, 0:1], ALU.subtract, ALU.mult),
         reads=[y, mv, rstd], writes=[y])
    p.op("pool", lambda e: e.tensor_tensor(y[:], y[:], g_bc[:], ALU.mult), reads=[y, g_bc], writes=[y])
    p.op("pool", lambda e: e.tensor_tensor(out[:], y[:], b_bc[:], ALU.add), reads=[y, b_bc], writes=[out])


def peer_prep_weights(p, cx, l):
    ut_src = cx.dram["peer_ut"][l]
    v_src = cx.dram["peer_v"][l]
    CH = 8
    for c in range(NET // CH):
        p.dma("pool", cx.utb[c * CH:(c + 1) * CH], ut_src[c * CH:(c + 1) * CH], writes=[cx.d_utb[c]])
        p.dma("pool", cx.vb[c * CH * 128:(c + 1) * CH * 128, :], v_src[c * CH * 128:(c + 1) * CH * 128, :],
              writes=[cx.d_vb[c]])


def peer_alloc(p, cx):
    a = Ctx()
    a.big = p.sbuf([128, 16384], F32, name="p_big")
    hb = a.big.t[:].bitcast(BF16).tensor
    a.Wb = TT(hb, 0, [256, 128])
    a.wq = TT(hb, 0, [8, 2048])
    a.wqk = [TT(hb, k * 2048, [2048]) for k in range(8)]
    a.s_sb = TT(a.big.t, 8192, [16, 128])
    a.cand = TT(a.big.t, 10240, [8, 256])
    a.r1 = TT(a.big.t, 12288, [2, 8, 128])
    a.d1regs = a.wqk + [a.s_sb, a.cand, a.r1]
    a.keys = p.sbuf([128, 16, 128], BF16, name="keysT")
    a.xt = [p.sbuf([128, 1024], F32, name="p_xt%d" % i) for i in range(2)]
    a.hm = p.sbuf([128, 1024], F32, name="p_hm")
    a.hT = p.sbuf([128, 8, 256], BF16, name="p_hT")
    a.qT = p.sbuf([128, 16, 128], BF16, name="p_qT")
    a.V = p.sbuf([128, 16, 16], F32, name="p_V")
    a.tmp128 = p.sbuf([128, 128], F32, name="p_tmp128")
    a.tmpc = [p.sbuf([128, 256], F32, name="p_tmpc%d" % i) for i in range(2)]
    a.C = p.sbuf([128, 8, 24], F32, name="p_C")
    a.sm = p.sbuf([128, 8, 8], F32, name="p_small")
    a.ex = p.sbuf([128, 8, 16], F32, name="p_ex")
    a.v1pp = p.sbuf([128, 128], F32, name="p_v1pp")
    a.thp = p.sbuf([128, 128], F32, name="p_thp")
    a.SPt = p.sbuf([128, 2, 3, 8, 128], BF16, name="p_SPt")
    a.ThT = p.sbuf([128, 256], F32, name="p_ThT")
    a.V1T = p.sbuf([128, 256], F32, name="p_V1T")
    a.E1T = p.sbuf([128, 256], F32, name="p_E1T")
    a.ev1 = p.sbuf([128, 128], F32, name="p_ev1")
    a.SPh = [[p.sbuf([128, 16, 128], BF16, name="p_SPh%d_%d" % (i, s)) for s in range(2)] for i in range(2)]
    for i in range(2):
        for s_ in range(2):
            p.op("pool", lambda e, i=i, s_=s_: e.memset(a.SPh[i][s_][:], 0.0), writes=[a.SPh[i][s_]])
    a.EX = [[p.sbuf([128, 4, 128], BF16, name="p_EX%d_%d" % (i, s)) for s in range(2)] for i in range(3)]
    a.MK = [[p.sbuf([128, 4, 128], BF16, name="p_MK%d_%d" % (i, s)) for s in range(2)] for i in range(3)]
    a.LR = [[p.sbuf([128, 4, 128], BF16, name="p_LR%d_%d" % (i, s)) for s in range(2)] for i in range(3)]
    a.ut = [p.sbuf([128, 1024], BF16, name="p_ut%d" % i) for i in range(4)]
    a.vt = [p.sbuf([128, 1024], BF16, name="p_vt%d" % i) for i in range(4)]
    a.G = [p.sbuf([128, 256], BF16, name="p_G%d" % i) for i in range(3)]
    a.A = [p.sbuf([128, 256], BF16, name="p_A%d" % i) for i in range(3)]
    a.y = p.sbuf([128, 1024], F32, name="p_y")
    a.t1 = a.hm
    a.xo = a.hm
    a.scr = {"st": p.sbuf([128, 12], F32, name="p_st"), "mv": p.sbuf([128, 2], F32, name="p_mv"),
             "rstd": p.sbuf([128, 1], F32, name="p_rstd")}
    return a


def peer_load_layer(p, cx, a, l):
    p.dma("pool", a.keys[:], cx.dram["peer_keysT"][l], writes=[a.keys])


def peer_block(p, cx, a, l, blk, mod, lng, lnb):
    B = cx.banks
    tok0 = blk * 256
    NT = cx.NT
    spd = cx.spd
    fence([a.Wb], a.d1regs)
    wqd = cx.dram["peer_w_q"][l]
    for k in range(8):
        p.dma("pool", a.wqk[k][:], wqd[k * 128:(k + 1) * 128, :], writes=[a.wqk[k]])
    for tl in range(2):
        r0 = tok0 + tl * 128
        xt = a.xt[tl]
        p.dma("sp", xt[:], cx.X[r0:r0 + 128, :], reads=[cx.dX[r0 // 128]], writes=[xt])
        p.op("dve", lambda e: e.tensor_tensor(a.hm[:], xt[:], mod["sc1"][:], ALU.mult), reads=[xt, mod["sc1"]], writes=[a.hm])
        p.op("dve", lambda e: e.tensor_tensor(a.hm[:], a.hm[:], mod["sh"][:], ALU.add), reads=[a.hm, mod["sh"]], writes=[a.hm])
        for half in range(2):
            bk = B[half]
            for k4 in range(4):
                k = half * 4 + k4
                p.op("pe", lambda e, k=k, k4=k4, bk=bk: e.transpose(bk[:, k4 * 128:(k4 + 1) * 128], a.hm[:, k * 128:(k + 1) * 128], cx.ident[:]),
                     reads=[a.hm, cx.ident], writes=[bk])
            p.op("act", lambda e, half=half, bk=bk: e.copy(
                a.hT.ap(half * 4 * 256 + tl * 128, [[256, 4], [1, 128]]), bk.ap(0, [[128, 4], [1, 128]])),
                reads=[bk], writes=[a.hT])
        for m4 in range(4):
            bk = B[2 + (m4 % 2)]
            for mi in range(4):
                m = m4 * 4 + mi
                for k in range(8):
                    p.op("pe", lambda e, m=m, mi=mi, k=k, bk=bk: e.matmul(
                        bk[:, mi * 128:(mi + 1) * 128], a.wqk[k][:, m * 128:(m + 1) * 128],
                        a.hT[:, k, tl * 128:(tl + 1) * 128], start=(k == 0), stop=(k == 7)),
                        reads=[a.wqk[k], a.hT], writes=[bk])
            eng = "act" if m4 % 2 == 0 else "dve"
            if eng == "act":
                p.op("act", lambda e, m4=m4, bk=bk: e.copy(a.qT[:, m4 * 4:(m4 + 1) * 4, :], bk.ap(0, [[128, 4], [1, 128]])),
                     reads=[bk], writes=[a.qT])
            else:
                p.op("dve", lambda e, m4=m4, bk=bk: e.tensor_copy(a.qT[:, m4 * 4:(m4 + 1) * 4, :], bk.ap(0, [[128, 4], [1, 128]])),
                     reads=[bk], writes=[a.qT])
        for m4 in range(4):
            bk = B[4 + m4]
            for mi in range(4):
                m = m4 * 4 + mi
                p.op("pe", lambda e, m=m, mi=mi, bk=bk: e.matmul(bk[:, mi * 128:(mi + 1) * 128], a.qT[:, m, :], a.keys[:, m, :],
                                                                  start=True, stop=True), reads=[a.qT, a.keys], writes=[bk])
            p.op("act", lambda e, m4=m4, bk=bk: e.copy(a.s_sb[:, m4 * 4:(m4 + 1) * 4, :], bk.ap(0, [[128, 4], [1, 128]])),
                 reads=[bk], writes=[a.s_sb])
        for m in range(16):
            p.op("dve", lambda e, m=m: e.max(out=a.V[:, m, 0:8], in_=a.s_sb[:, m, :]), reads=[a.s_sb], writes=[a.V])
            p.op("dve", lambda e, m=m: e.match_replace(out=a.tmp128[:], in_to_replace=a.V[:, m, 0:8], in_values=a.s_sb[:, m, :],
                                                        imm_value=-1e30), reads=[a.s_sb, a.V], writes=[a.tmp128])
            p.op("dve", lambda e, m=m: e.max(out=a.V[:, m, 8:16], in_=a.tmp128[:]), reads=[a.tmp128], writes=[a.V])
        in0 = a.V.ap(0, [[32, 8], [1, 16], [0, 16]])
        in1 = a.V.ap(16, [[32, 8], [0, 16], [1, 16]])
        outc = a.cand.ap(0, [[256, 8], [16, 16], [1, 16]])
        p.op("dve", lambda e: e.tensor_tensor(outc, in0, in1, ALU.add), reads=[a.V], writes=[a.cand])
        for h in range(8):
            p.op("dve", lambda e, h=h: e.max(out=a.C[:, h, 0:8], in_=a.cand[:, h, :]), reads=[a.cand], writes=[a.C])
            p.op("dve", lambda e, h=h: e.match_replace(out=a.tmpc[0][:], in_to_replace=a.C[:, h, 0:8], in_values=a.cand[:, h, :],
                                                        imm_value=-1e30), reads=[a.cand, a.C], writes=[a.tmpc[0]])
            p.op("dve", lambda e, h=h: e.max(out=a.C[:, h, 8:16], in_=a.tmpc[0][:]), reads=[a.tmpc[0]], writes=[a.C])
            p.op("dve", lambda e, h=h: e.match_replace(out=a.tmpc[1][:], in_to_replace=a.C[:, h, 8:16], in_values=a.tmpc[0][:],
                                                        imm_value=-1e30), reads=[a.tmpc[0], a.C], writes=[a.tmpc[1]])
            p.op("dve", lambda e, h=h: e.max(out=a.C[:, h, 16:24], in_=a.tmpc[1][:]), reads=[a.tmpc[1]], writes=[a.C])
        sm = a.sm
        TAU, LNZ, OFF1, C2, ZZ = 0, 1, 2, 3, 4
        c15 = a.C.ap(15, [[24, 8]]); c16 = a.C.ap(16, [[24, 8]]); c0 = a.C.ap(0, [[24, 8]])
        m1v = a.V.ap(0, [[32, 8]]); m2v = a.V.ap(16, [[32, 8]])
        p.op("dve", lambda e: e.tensor_tensor(sm[:, TAU, :], c15, c16, ALU.add), reads=[a.C], writes=[sm])
        p.op("dve", lambda e: e.tensor_scalar(sm[:, TAU, :], sm[:, TAU, :], 0.5, None, ALU.mult), reads=[sm], writes=[sm])
        c016 = a.C.ap(0, [[24, 8], [1, 16]]); c0b = a.C.ap(0, [[24, 8], [0, 16]])
        p.op("dve", lambda e: e.tensor_tensor(a.ex[:], c016, c0b, ALU.subtract), reads=[a.C], writes=[a.ex])
        p.op("act", lambda e: e.activation(a.ex[:], a.ex[:], AF.Exp), reads=[a.ex], writes=[a.ex])
        p.op("dve", lambda e: e.reduce_sum(sm[:, ZZ, :], a.ex[:], axis=AX.X), reads=[a.ex], writes=[sm])
        p.op("act", lambda e: e.activation(sm[:, LNZ, :], sm[:, ZZ, :], AF.Ln), reads=[sm], writes=[sm])
        p.op("dve", lambda e: e.tensor_tensor(sm[:, OFF1, :], m1v, sm[:, LNZ, :], ALU.add), reads=[a.V, sm], writes=[sm])
        p.op("dve", lambda e: e.tensor_tensor(sm[:, C2, :], sm[:, TAU, :], m2v, ALU.subtract), reads=[a.V, sm], writes=[sm])
        s1v = a.s_sb.ap(0, [[256, 8], [1, 128]]); s2v = a.s_sb.ap(128, [[256, 8], [1, 128]])
        off1b = sm.ap(OFF1 * 8, [[1, 8], [0, 128]]); m2b = a.V.ap(16, [[32, 8], [0, 128]])
        sfix = a.cand
        sf4 = sfix.ap(0, [[1024, 2], [128, 8], [1, 128]])
        p.op("dve", lambda e: e.tensor_tensor(sfix.ap(0, [[128, 8], [1, 128]]), s1v, off1b, ALU.subtract), reads=[a.s_sb, sm], writes=[sfix])
        p.op("pool", lambda e: e.tensor_tensor(sfix.ap(1024, [[128, 8], [1, 128]]), s2v, m2b, ALU.subtract), reads=[a.s_sb, a.V], writes=[sfix])
        v1v = a.V.ap(0, [[32, 8], [1, 16]]); off1b16 = sm.ap(OFF1 * 8, [[1, 8], [0, 16]]); c2b16 = sm.ap(C2 * 8, [[1, 8], [0, 16]])
        v1pp3 = a.v1pp.ap(0, [[16, 8], [1, 16]]); thp3 = a.thp.ap(0, [[16, 8], [1, 16]])
        p.op("dve", lambda e: e.tensor_tensor(v1pp3, v1v, off1b16, ALU.subtract), reads=[a.V, sm], writes=[a.v1pp])
        p.op("dve", lambda e: e.tensor_tensor(thp3, c2b16, v1v, ALU.subtract), reads=[a.V, sm], writes=[a.thp])
        SP = a.SPt
        p.op("act", lambda e: e.copy(SP[:, :, 0, :, :], sf4), reads=[sfix], writes=[SP])
        p.op("dve", lambda e: e.tensor_tensor(a.r1[:], sf4, SP[:, :, 0, :, :], ALU.subtract), reads=[sfix, SP], writes=[a.r1])
        p.op("act", lambda e: e.copy(SP[:, :, 1, :, :], a.r1[:]), reads=[a.r1], writes=[SP])
        p.op("dve", lambda e: e.tensor_tensor(a.r1[:], a.r1[:], SP[:, :, 1, :, :], ALU.subtract), reads=[a.r1, SP], writes=[a.r1])
        p.op("act", lambda e: e.copy(SP[:, :, 2, :, :], a.r1[:]), reads=[a.r1], writes=[SP])
        for side in range(2):
            dst = spd[side, :, :, r0:r0 + 128, :].rearrange("a h t i -> t a h i")
            p.dma("sp", dst, SP[:, side, :, :, :], reads=[SP], writes=[cx.d_spd[r0 // 128]])
        bk = B[0]
        p.op("act", lambda e: e.activation(a.ev1[:], a.v1pp[:], AF.Exp), reads=[a.v1pp], writes=[a.ev1])
        p.op("pe", lambda e: e.transpose(bk[:, 0:128], a.thp[:], cx.ident[:]), reads=[a.thp, cx.ident], writes=[bk])
        p.op("pe", lambda e: e.transpose(bk[:, 128:256], a.v1pp[:], cx.ident[:]), reads=[a.v1pp, cx.ident], writes=[bk])
        p.op("pe", lambda e: e.transpose(bk[:, 256:384], a.ev1[:], cx.ident[:]), reads=[a.ev1, cx.ident], writes=[bk])
        p.op("act", lambda e: e.copy(a.ThT[:, tl * 128:(tl + 1) * 128], bk[:, 0:128]), reads=[bk], writes=[a.ThT])
        p.op("act", lambda e: e.copy(a.V1T[:, tl * 128:(tl + 1) * 128], bk[:, 128:256]), reads=[bk], writes=[a.V1T])
        p.op("act", lambda e: e.copy(a.E1T[:, tl * 128:(tl + 1) * 128], bk[:, 256:384]), reads=[bk], writes=[a.E1T])

    if getattr(cx, "stage", 9) < 2:
        return
    fence(a.d1regs, [a.Wb])
    groups = [(sub, g4) for sub in range(16) for g4 in range(4)]
    NG = len(groups)

    def d2_load(sub):
        ta = tok0 + sub * 16
        for side in range(2):
            src = spd[side, :, :, ta:ta + 16, :].rearrange("a h t i -> (a h) t i")
            p.dma("sp", a.SPh[sub % 2][side][0:24, :, :], src, reads=[cx.d_spd[ta // 128]], writes=[a.SPh[sub % 2][side]])

    def stA(gi):
        sub, g4 = groups[gi]
        st = gi % 3
        sph = a.SPh[sub % 2]
        for side in range(2):
            bk = B[st * 2 + side]
            p.op("pe", lambda e, side=side, bk=bk: e.matmul(bk[:], cx.e3[0:24, :], sph[side][0:24, g4 * 4:(g4 + 1) * 4, :], start=True, stop=True),
                 reads=[cx.e3, sph[side]], writes=[bk])
        if g4 == 3 and sub + 2 < 16:
            d2_load(sub + 2)

    def stB(gi):
        sub, g4 = groups[gi]
        st = gi % 3
        tb = sub * 16 + g4 * 4
        EX = a.EX[st]; MK = a.MK[st]
        ops = (ALU.is_equal, ALU.is_ge)
        srcs = (a.V1T, a.ThT)
        for side in range(2):
            bk = B[st * 2 + side]
            if side == 1:
                p.op("act", lambda e, side=side, bk=bk: e.activation(EX[side][:], bk.ap(0, [[128, 4], [1, 128]]), AF.Exp), reads=[bk], writes=[EX[side]])
            cb = srcs[side].ap(tb, [[1, 4], [0, 128]])
            p.op("dve", lambda e, side=side, bk=bk, cb=cb: e.tensor_tensor(MK[side][:], bk.ap(0, [[128, 4], [1, 128]]), cb, ops[side]),
                 reads=[bk, srcs[side]], writes=[MK[side]])

    def stC(gi):
        sub, g4 = groups[gi]
        st = gi % 3
        tb = sub * 16 + g4 * 4
        e1b = a.E1T.ap(tb, [[1, 4], [0, 128]])
        p.op("pool", lambda e: e.tensor_tensor(a.LR[st][0][:], a.MK[st][0][:], e1b, ALU.mult), reads=[a.MK[st][0], a.E1T], writes=[a.LR[st][0]])
        p.op("pool", lambda e: e.tensor_tensor(a.LR[st][1][:], a.MK[st][1][:], a.EX[st][1][:], ALU.mult),
             reads=[a.MK[st][1], a.EX[st][1]], writes=[a.LR[st][1]])

    def stD(gi):
        sub, g4 = groups[gi]
        st = gi % 3
        tb = sub * 16 + g4 * 4
        psW = B[6 + gi % 2]
        LR = a.LR[st]
        for tk in range(4):
            p.op("pe", lambda e, tk=tk: e.matmul(psW[:, tk * 128:(tk + 1) * 128], LR[1][:, tk, :], LR[0][:, tk, :], start=True, stop=True),
                 reads=[LR[0], LR[1]], writes=[psW])
        p.op("act", lambda e: e.copy(a.Wb[:, tb:tb + 4, :], psW.ap(0, [[128, 4], [1, 128]])), reads=[psW], writes=[a.Wb])

    d2_load(0); d2_load(1)
    for i in range(NG + 3):
        if i < NG:
            stA(i)
        if 0 <= i - 1 < NG:
            stB(i - 1)
        if 0 <= i - 2 < NG:
            stC(i - 2)
        if 0 <= i - 3 < NG:
            stD(i - 3)

    if getattr(cx, "stage", 9) < 3:
        return
    psO = [[B[0], B[1]], [B[2], B[3]]]

    def d3_load(et):
        p.dma("sp", a.ut[et % 4][:], cx.utb[et], reads=[cx.d_utb[et // 8]], writes=[a.ut[et % 4]])
        p.dma("sp", a.vt[et % 4][:], cx.vb[et * 128:(et + 1) * 128, :], reads=[cx.d_vb[et // 8]], writes=[a.vt[et % 4]])

    def d3_ux(et):
        ut = a.ut[et % 4]
        psU = B[4 + (et % 3)]
        for k in range(8):
            p.op("pe", lambda e, k=k: e.matmul(psU[:, 0:256], ut[:, k * 128:(k + 1) * 128], a.hT[:, k, :], start=(k == 0), stop=(k == 7)),
                 reads=[ut, a.hT], writes=[psU])
        G = a.G[et % 3]; A = a.A[et % 3]
        p.op("act", lambda e: e.activation(G[:], psU[:, 0:256], AF.Gelu_apprx_tanh), reads=[psU], writes=[G])
        wsl = a.Wb.ap(et, [[128, 256]])
        p.op("dve" if et % 2 == 0 else "pool", lambda e: e.tensor_tensor(A[:], G[:], wsl, ALU.mult), reads=[G, a.Wb], writes=[A])

    def d3_v(et):
        A = a.A[et % 3]; vt = a.vt[et % 4]
        for tl in range(2):
            for dh in range(2):
                p.op("pe", lambda e, tl=tl, dh=dh: e.matmul(psO[tl][dh][:], A[:, tl * 128:(tl + 1) * 128], vt[:, dh * 512:(dh + 1) * 512],
                                                            start=(et == 0), stop=(et == NET - 1)), reads=[A, vt], writes=[psO[tl][dh]])

    d3_load(0); d3_load(1); d3_load(2); d3_ux(0); d3_ux(1)
    for et in range(NET):
        if et + 3 < NET:
            d3_load(et + 3)
        if et + 2 < NET:
            d3_ux(et + 2)
        d3_v(et)
    if getattr(cx, "stage", 9) < 4:
        return
    for tl in range(2):
        r0 = tok0 + tl * 128
        xt = a.xt[tl]
        for dh in range(2):
            p.op("dve", lambda e, dh=dh: e.tensor_tensor(a.t1[:, dh * 512:(dh + 1) * 512], psO[tl][dh][:], mod["g1"][:, dh * 512:(dh + 1) * 512], ALU.mult),
                 reads=[psO[tl][dh], mod["g1"]], writes=[a.t1])
        p.op("dve", lambda e: e.scalar_tensor_tensor(a.y[:], xt[:], DN_ALPHA, a.t1[:], ALU.mult, ALU.add), reads=[xt, a.t1], writes=[a.y])
        layer_norm_rows(p, cx, a.y, a.xo, lng, lnb, a.scr)
        p.dma("act", cx.X[r0:r0 + 128, :], a.xo[:], reads=[a.xo], writes=[cx.dX[r0 // 128]])


def modulate_transpose(p, cx, xt, hm, hT_dst_fn, mod, banks):
    p.op("dve", lambda e: e.tensor_tensor(hm[:], xt[:], mod["sc1"][:], ALU.mult), reads=[xt, mod["sc1"]], writes=[hm])
    p.op("pool", lambda e: e.tensor_tensor(hm[:], hm[:], mod["sh"][:], ALU.add), reads=[hm, mod["sh"]], writes=[hm])
    for half in range(2):
        bk = banks[half]
        for k4 in range(4):
            k = half * 4 + k4
            p.op("pe", lambda e, k=k, k4=k4, bk=bk: e.transpose(bk[:, k4 * 128:(k4 + 1) * 128], hm[:, k * 128:(k + 1) * 128], cx.ident[:]),
                 reads=[hm, cx.ident], writes=[bk])
        dst, dep = hT_dst_fn(half)
        p.op("act", lambda e, bk=bk, dst=dst: e.copy(dst, bk.ap(0, [[128, 4], [1, 128]])), reads=[bk], writes=[dep])


def post_norm_store(p, cx, xt, psO, mod, lng, lnb, t1, y, scr, r0, q="act"):
    for dh in range(2):
        p.op("dve", lambda e, dh=dh: e.tensor_tensor(t1[:, dh * 512:(dh + 1) * 512], psO[dh][:], mod["g1"][:, dh * 512:(dh + 1) * 512], ALU.mult),
             reads=[psO[dh], mod["g1"]], writes=[t1])
    p.op("dve", lambda e: e.scalar_tensor_tensor(y[:], xt[:], DN_ALPHA, t1[:], ALU.mult, ALU.add), reads=[xt, t1], writes=[y])
    layer_norm_rows(p, cx, y, t1, lng, lnb, scr)
    p.dma(q, cx.X[r0:r0 + 128, :], t1[:], reads=[t1], writes=[cx.dX[r0 // 128]])


def attn_alloc(p, cx):
    a = Ctx()
    a.wqkv = p.sbuf([128, 8, 1536], BF16, name="a_wqkv")
    a.wqkp = p.sbuf([128, 8, 1280], BF16, name="a_wqkp")
    a.wout = p.sbuf([128, 8, 1024], BF16, name="a_wout")
    a.esink = p.sbuf([128, 16], F32, name="a_esink")
    a.kcf = p.sbuf([128, 256], F32, name="a_kcf")
    a.KTc = p.sbuf([64, 4, 4, 128], BF16, name="a_KTc")
    a.Vc = p.sbuf([128, 4, 4, 65], BF16, name="a_Vc")
    a.xt = [p.sbuf([128, 1024], F32, name="a_xt%d" % i) for i in range(3)]
    a.hm = p.sbuf([128, 1024], F32, name="a_hm")
    a.hT = p.sbuf([128, 8, 128], BF16, name="a_hT")
    a.QT = [p.sbuf([64, 16, 128], BF16, name="a_QT%d" % i) for i in range(3)]
    a.KT = [p.sbuf([64, 4, 128], BF16, name="a_KT%d" % i) for i in range(3)]
    a.V = [p.sbuf([128, 4, 65], BF16, name="a_V%d" % i) for i in range(3)]
    a.cs = [p.sbuf([64, 2, 128], F32, name="a_cs%d" % i) for i in range(2)]
    a.r1 = p.sbuf([64, 4, 128], F32, name="a_r1")
    a.r2 = p.sbuf([64, 4, 128], F32, name="a_r2")
    a.kvtok = p.sbuf([128, 2, 256], F32, name="a_kvtok")
    a.PT = [p.sbuf([128, 4, 128], BF16, name="a_PT%d" % i) for i in range(7)]
    a.masks = p.sbuf([128, 2, 128], BF16, name="a_masks")
    a.den = p.sbuf([128, 4], F32, name="a_den")
    a.O = p.sbuf([128, 1024], F32, name="a_O")
    a.OT = p.sbuf([128, 8, 128], BF16, name="a_OT")
    a.y = p.sbuf([128, 1024], F32, name="a_y")
    a.scr = {"st": p.sbuf([128, 12], F32, name="a_st"), "mv": p.sbuf([128, 2], F32, name="a_mv"),
             "rstd": p.sbuf([128, 1], F32, name="a_rstd")}
    p.dma("pool", a.masks[:], cx.dram["c_masks"], writes=[a.masks])
    for i in range(3):
        p.op("pool", lambda e, i=i: e.memset(a.V[i][:], 1.0), writes=[a.V[i]])
    p.op("pool", lambda e: e.memset(a.Vc[:], 1.0), writes=[a.Vc])
    return a


def attn_load_layer(p, cx, a, j):
    B = cx.banks
    w = cx.dram["attn_w_qkv"][j]; wp = cx.dram["attn_w_qkp"][j]; wo = cx.dram["attn_w_out"][j]
    p.dma("pool", a.wqkv[:], w.rearrange("(k q) c -> q k c", q=128), writes=[a.wqkv])
    p.dma("pool", a.wqkp[:], wp.rearrange("(k q) c -> q k c", q=128), writes=[a.wqkp])
    p.dma("pool", a.wout[:], wo.rearrange("(k q) c -> q k c", q=128), writes=[a.wout])
    p.dma("sp", a.esink[:], cx.dram["attn_sink"][j].partition_broadcast(128), writes=[a.esink])
    p.op("act", lambda e: e.activation(a.esink[:], a.esink[:], AF.Exp), reads=[a.esink], writes=[a.esink])
    if cx.n_smp_tiles == 0:
        return
    ck = cx.dram["cache_k"][j]
    cv = cx.dram["cache_v"][j]
    for c in range(4):
        p.dma("sp", a.kcf[:], ck[c * 128:(c + 1) * 128, :], writes=[a.kcf])
        bk = B[0]
        for g in range(4):
            p.op("pe", lambda e, g=g: e.transpose(bk[0:64, g * 128:(g + 1) * 128], a.kcf[:, g * 64:(g + 1) * 64], cx.ident[:]),
                 reads=[a.kcf, cx.ident], writes=[bk])
        p.op("act", lambda e, c=c: e.copy(a.KTc[:, :, c, :], bk.ap(0, [[128, 4], [1, 128]], np_=64)), reads=[bk], writes=[a.KTc])
        p.dma("pool", a.Vc[:, c, :, 0:64], cv[c * 128:(c + 1) * 128, :].rearrange("t (g d) -> t g d", g=4), writes=[a.Vc])


def attn_project(p, cx, a, r0, slot, rope_pos, mod, want_kv_out=None):
    B = cx.banks
    xt = a.xt[slot]
    p.dma("sp", xt[:], cx.X[r0:r0 + 128, :], reads=[cx.dX[r0 // 128]], writes=[xt])
    modulate_transpose(p, cx, xt, a.hm, lambda half: (a.hT[:, half * 4:(half + 1) * 4, :], a.hT), mod, [B[0], B[1]])
    QT = a.QT[slot]; KT = a.KT[slot]; V = a.V[slot]
    if rope_pos is not None:
        cs = a.cs[rope_pos % 2]
        p.dma("sp", cs[:], cx.dram["c_rope"][:, :, rope_pos * 128:(rope_pos + 1) * 128], writes=[cs])
    for hg in range(5):
        colbase = hg * 256 if hg < 4 else 1024
        b0 = B[0]; b1 = B[1]
        for r in range(4):
            for k in range(8):
                p.op("pe", lambda e, r=r, k=k: e.matmul(b0[0:64, r * 128:(r + 1) * 128], a.wqkv[:, k, colbase + r * 64: colbase + (r + 1) * 64],
                                                        a.hT[:, k, :], start=(k == 0), stop=(k == 7)), reads=[a.wqkv, a.hT], writes=[b0])
        dst = QT[:, hg * 4:(hg + 1) * 4, :] if hg < 4 else KT[:]
        ddep = QT if hg < 4 else KT
        if rope_pos is None:
            p.op("act", lambda e, dst=dst: e.copy(dst, b0.ap(0, [[128, 4], [1, 128]], np_=64)), reads=[b0], writes=[ddep])
        else:
            for r in range(4):
                for k in range(8):
                    p.op("pe", lambda e, r=r, k=k: e.matmul(b1[0:64, r * 128:(r + 1) * 128], a.wqkp[:, k, colbase + r * 64: colbase + (r + 1) * 64],
                                                            a.hT[:, k, :], start=(k == 0), stop=(k == 7)), reads=[a.wqkp, a.hT], writes=[b1])
            cosb = cs.ap(0, [[0, 4], [1, 128]], np_=64); sinb = cs.ap(128, [[0, 4], [1, 128]], np_=64)
            p.op("dve", lambda e: e.tensor_tensor(a.r1[:], b0.ap(0, [[128, 4], [1, 128]], np_=64), cosb, ALU.mult), reads=[b0, cs], writes=[a.r1])
            p.op("dve", lambda e: e.tensor_tensor(a.r2[:], b1.ap(0, [[128, 4], [1, 128]], np_=64), sinb, ALU.mult), reads=[b1, cs], writes=[a.r2])
            p.op("pool", lambda e, dst=dst: e.tensor_tensor(dst, a.r1[:], a.r2[:], ALU.add), reads=[a.r1, a.r2], writes=[ddep])
    bv = B[2]
    for k in range(8):
        p.op("pe", lambda e, k=k: e.matmul(bv[:, 0:256], a.hT[:, k, :], a.wqkv[:, k, 1280:1536], start=(k == 0), stop=(k == 7)),
             reads=[a.hT, a.wqkv], writes=[bv])
    if want_kv_out is not None:
        for k in range(8):
            p.op("pe", lambda e, k=k: e.matmul(bv[:, 256:512], a.hT[:, k, :], a.wqkv[:, k, 1024:1280], start=(k == 0), stop=(k == 7)),
                 reads=[a.hT, a.wqkv], writes=[bv])
    p.op("act", lambda e: e.copy(V[:, :, 0:64], bv.ap(0, [[64, 4], [1, 64]])), reads=[bv], writes=[V])
    if want_kv_out is not None:
        ko, vo = want_kv_out
        p.op("dve", lambda e: e.tensor_copy(a.kvtok[:, 0, :], bv[:, 256:512]), reads=[bv], writes=[a.kvtok])
        p.op("dve", lambda e: e.tensor_copy(a.kvtok[:, 1, :], bv[:, 0:256]), reads=[bv], writes=[a.kvtok])
        p.dma("act", ko, a.kvtok[:, 0, :], reads=[a.kvtok], writes=[cx.d_out])
        p.dma("act", vo, a.kvtok[:, 1, :], reads=[a.kvtok], writes=[cx.d_out])


def attn_attend(p, cx, a, r0, slot, chunks, mod, lng, lnb):
    B = cx.banks
    QT = a.QT[slot]
    nch = len(chunks)
    for g in range(4):
        for ci, (kfn, vfn, deps, mk) in enumerate(chunks):
            bs = B[2 + (ci % 2)]
            p.op("pe", lambda e, kfn=kfn, bs=bs: e.matmul(bs[:], kfn(g), QT[:, g * 4:(g + 1) * 4, :], start=True, stop=True),
                 reads=[QT] + deps, writes=[bs])
            PT = a.PT[ci]
            p.op("act", lambda e, bs=bs, PT=PT: e.activation(PT[:], bs.ap(0, [[128, 4], [1, 128]]), AF.Exp, scale=0.125), reads=[bs], writes=[PT])
            if mk is not None:
                mb = a.masks.ap(mk * 128, [[0, 4], [1, 128]])
                p.op("pool", lambda e, PT=PT, mb=mb: e.tensor_tensor(PT[:], PT[:], mb, ALU.mult), reads=[PT, a.masks], writes=[PT])
        bo = B[4 + g]
        for r in range(4):
            for ci, (kfn, vfn, deps, mk) in enumerate(chunks):
                p.op("pe", lambda e, r=r, ci=ci, vfn=vfn: e.matmul(bo[:, r * 65:(r + 1) * 65], a.PT[ci][:, r, :], vfn(g),
                                                                   start=(ci == 0), stop=(ci == nch - 1)), reads=[a.PT[ci]] + deps, writes=[bo])
        p.op("dve", lambda e: e.tensor_tensor(a.den[:], bo.ap(64, [[65, 4]]), a.esink[:, g * 4:(g + 1) * 4], ALU.add), reads=[bo, a.esink], writes=[a.den])
        p.op("dve", lambda e: e.reciprocal(a.den[:], a.den[:]), reads=[a.den], writes=[a.den])
        p.op("dve", lambda e: e.tensor_tensor(a.O.ap(g * 256, [[64, 4], [1, 64]]), bo.ap(0, [[65, 4], [1, 64]]), a.den.ap(0, [[1, 4], [0, 64]]), ALU.mult),
             reads=[bo, a.den], writes=[a.O])
    for half in range(2):
        bk = B[half]
        for k4 in range(4):
            k = half * 4 + k4
            p.op("pe", lambda e, k=k, k4=k4, bk=bk: e.transpose(bk[:, k4 * 128:(k4 + 1) * 128], a.O[:, k * 128:(k + 1) * 128], cx.ident[:]),
                 reads=[a.O, cx.ident], writes=[bk])
        p.op("act", lambda e, half=half, bk=bk: e.copy(a.OT[:, half * 4:(half + 1) * 4, :], bk.ap(0, [[128, 4], [1, 128]])), reads=[bk], writes=[a.OT])
    psO = [B[2], B[3]]
    for dh in range(2):
        for k in range(8):
            p.op("pe", lambda e, dh=dh, k=k: e.matmul(psO[dh][:], a.OT[:, k, :], a.wout[:, k, dh * 512:(dh + 1) * 512], start=(k == 0), stop=(k == 7)),
                 reads=[a.OT, a.wout], writes=[psO[dh]])
    post_norm_store(p, cx, a.xt[slot], psO, mod, lng, lnb, a.hm, a.y, a.scr, r0)


def attn_layer(p, cx, a, j, mods, lng, lnb, kv_out):
    nS = cx.n_smp_tiles
    def loc(slot):
        return (lambda g: a.KT[slot][:, g, :]), (lambda g: a.V[slot][:, g, :]), [a.KT[slot], a.V[slot]]
    if nS:
        attn_project(p, cx, a, 0, 0, 0, mods["smp"])
    for i in range(nS):
        if i + 1 < nS:
            attn_project(p, cx, a, (i + 1) * 128, (i + 1) % 3, i + 1, mods["smp"])
        chunks = []
        for c in range(4):
            chunks.append(((lambda g, c=c: a.KTc[:, g, c, :]), (lambda g, c=c: a.Vc[:, c, g, :]), [a.KTc, a.Vc], None))
        if i > 0:
            k_, v_, d_ = loc((i - 1) % 3); chunks.append((k_, v_, d_, 0))
        k_, v_, d_ = loc(i % 3); chunks.append((k_, v_, d_, None))
        if i + 1 < nS:
            k_, v_, d_ = loc((i + 1) % 3); chunks.append((k_, v_, d_, 1))
        attn_attend(p, cx, a, i * 128, i % 3, chunks, mods["smp"], lng, lnb)
    for s in range(cx.n_prm_seq):
        base = nS * 128 + s * 256
        for t in range(2):
            attn_project(p, cx, a, base + t * 128, t, None, mods["ctx"], want_kv_out=kv_out(s, t))
        for t in range(2):
            chunks = []
            for t2 in range(2):
                k_, v_, d_ = loc(t2); chunks.append((k_, v_, d_, None))
            attn_attend(p, cx, a, base + t * 128, t, chunks, mods["ctx"], lng, lnb)


def host_consts():
    C = {}
    C["c_ident"] = np.eye(128, dtype=np.float32)
    e3 = np.zeros((128, 128), np.float32)
    for q in range(4):
        for part in range(3):
            for hh in range(8):
                e3[32 * q + part * 8 + hh, 16 * hh:16 * hh + 16] = 1
    C["c_e3"] = e3
    kk = np.arange(128)[:, None]; qq = np.arange(128)[None, :]
    C["c_masks"] = np.stack([(kk >= qq), (kk <= qq)], axis=1).astype(np.float32)
    pos = np.arange(4096)
    row = (pos // 64).astype(np.float32); col = (pos % 64).astype(np.float32)
    inv = (10000.0 ** (-np.arange(16, dtype=np.float32) / 16)).astype(np.float32)
    rope = np.zeros((64, 2, 4096), np.float32)
    for d in range(64):
        axis = row if d < 32 else col
        ang = (axis * inv[d % 16]).astype(np.float32)
        first = (d % 32) < 16
        rope[d, 0] = np.cos(ang)
        rope[d, 1] = -np.sin(ang) if first else np.sin(ang)
    C["c_rope"] = rope
    return C


def perm_qk(wqkv):
    idx = np.arange(1280)
    d = idx % 64
    partner = np.where((d % 32) < 16, idx + 16, idx - 16)
    return np.ascontiguousarray(wqkv[..., partner])


class Scope:
    def __init__(self, p):
        self.p = p

    def __enter__(self):
        self.saved = self.p.es
        self.p.es = ExitStack()
        return self

    def __exit__(self, *a):
        p = self.p
        allc = list(p.cnt.items())
        for e in ("pe", "act", "dve", "pool", "sp"):
            p._need(e, allc)
        p.es.close()
        p.es = self.saved
        return False


TWO_PI = 2.0 * np.pi
JB = 32


def _tt(p, eng, out, in0, in1, op, reads, writes):
    p.op(eng, lambda e: e.tensor_tensor(out, in0, in1, op), reads=reads, writes=writes)


def ssm_precompute(p, cx, R, jl, d):
    B = cx.banks
    G = 64
    with Scope(p):
        pg = p.sbuf([64, 3, 64], F32); Bt = p.sbuf([64, 2, 1024], F32); Ct = p.sbuf([64, 2, 1024], F32)
        PW = p.sbuf([64, 17, 2, 64], F32); sm = p.sbuf([64, 8, 64], F32); Bb = p.sbuf([64, 2, 1024], F32)
        WT = p.sbuf([64, 8, 2, 128], F32); XT = p.sbuf([64, 8, 2, 128], F32); Mt = p.sbuf([64, 8, 2, 128], F32)
        XTb = p.sbuf([64, 8, 2, 128], BF16); z = [p.sbuf([64, 8, 16], F32) for _ in range(4)]
        tmask = p.sbuf([128, 128], F32)
        qi = p.sbuf([64, 64], mybir.dt.int32)
        p.dma("sp", pg[:], cx.dram["ssm_pg"][jl, d].rearrange("k p g -> p k g"), writes=[pg])
        p.dma("sp", Bt[:], cx.dram["ssm_b"][jl, d].rearrange("k p x -> p k x"), writes=[Bt])
        p.dma("sp", Ct[:], cx.dram["ssm_c"][jl, d].rearrange("k p x -> p k x"), writes=[Ct])
        p.dma("sp", tmask[:], cx.dram["c_tmask"][d], writes=[tmask])
        DT, LR, LI, T0, T1, T2, CR, CI = range(8)
        p.op("act", lambda e: e.activation(sm[:, DT, :], pg[:, 2, :], AF.Exp), reads=[pg], writes=[sm])
        _tt(p, "dve", sm[:, LR, :], pg[:, 0, :], sm[:, DT, :], ALU.mult, [pg, sm], [sm])
        _tt(p, "dve", sm[:, LI, :], pg[:, 1, :], sm[:, DT, :], ALU.mult, [pg, sm], [sm])
        for k in range(-8, 9):
            kk = k + 8
            p.op("act", lambda e, k=k: e.activation(sm[:, T0, :], sm[:, LR, :], AF.Exp, scale=float(k)), reads=[sm], writes=[sm])
            for ri, ph in ((1, 0.0), (0, np.pi / 2)):
                p.op("dve", lambda e, k=k, ph=ph: e.tensor_scalar(sm[:, T1, :], sm[:, LI, :], float(k / TWO_PI), float(ph / TWO_PI + 64.0), ALU.mult, ALU.add),
                     reads=[sm], writes=[sm])
                p.op("dve", lambda e: e.tensor_copy(qi[:], sm[:, T1, :]), reads=[sm], writes=[qi])
                p.op("dve", lambda e: e.tensor_copy(sm[:, T2, :], qi[:]), reads=[qi], writes=[sm])
                _tt(p, "dve", sm[:, T1, :], sm[:, T1, :], sm[:, T2, :], ALU.subtract, [sm], [sm])
                p.op("dve", lambda e: e.tensor_single_scalar(sm[:, T2, :], sm[:, T1, :], 0.5, ALU.is_gt), reads=[sm], writes=[sm])
                _tt(p, "dve", sm[:, T1, :], sm[:, T1, :], sm[:, T2, :], ALU.subtract, [sm], [sm])
                p.op("act", lambda e: e.activation(sm[:, T2, :], sm[:, T1, :], AF.Sin, scale=float(TWO_PI)), reads=[sm], writes=[sm])
                _tt(p, "dve", PW[:, kk, ri, :], sm[:, T0, :], sm[:, T2, :], ALU.mult, [sm], [PW])
        p.op("act", lambda e: e.copy(R.A2[:, 0, :], PW[:, 16, 0, :]), reads=[PW], writes=[R.A2])
        p.op("act", lambda e: e.copy(R.A2[:, 1, :], PW[:, 16, 0, :]), reads=[PW], writes=[R.A2])
        p.op("act", lambda e: e.mul(R.AI2[:, 0, :], PW[:, 16, 1, :], -1.0), reads=[PW], writes=[R.AI2])
        p.op("act", lambda e: e.copy(R.AI2[:, 1, :], PW[:, 16, 1, :]), reads=[PW], writes=[R.AI2])
        are = pg[:, 0, :]; aim = pg[:, 1, :]
        p.op("dve", lambda e: e.tensor_scalar(sm[:, T0, :], PW[:, 9, 0, :], -1.0, None, ALU.add), reads=[PW], writes=[sm])
        _tt(p, "dve", sm[:, T1, :], are, are, ALU.mult, [pg], [sm])
        _tt(p, "dve", sm[:, T2, :], aim, aim, ALU.mult, [pg], [sm])
        _tt(p, "dve", sm[:, T1, :], sm[:, T1, :], sm[:, T2, :], ALU.add, [sm], [sm])
        p.op("dve", lambda e: e.reciprocal(sm[:, T1, :], sm[:, T1, :]), reads=[sm], writes=[sm])
        _tt(p, "dve", sm[:, CR, :], sm[:, T0, :], are, ALU.mult, [sm, pg], [sm])
        _tt(p, "dve", sm[:, T2, :], PW[:, 9, 1, :], aim, ALU.mult, [PW, pg], [sm])
        _tt(p, "dve", sm[:, CR, :], sm[:, CR, :], sm[:, T2, :], ALU.add, [sm], [sm])
        _tt(p, "dve", sm[:, CR, :], sm[:, CR, :], sm[:, T1, :], ALU.mult, [sm], [sm])
        _tt(p, "dve", sm[:, CI, :], PW[:, 9, 1, :], are, ALU.mult, [PW, pg], [sm])
        _tt(p, "dve", sm[:, T2, :], sm[:, T0, :], aim, ALU.mult, [sm, pg], [sm])
        _tt(p, "dve", sm[:, CI, :], sm[:, CI, :], sm[:, T2, :], ALU.subtract, [sm], [sm])
        _tt(p, "dve", sm[:, CI, :], sm[:, CI, :], sm[:, T1, :], ALU.mult, [sm], [sm])
        crb = sm.ap(CR * 64, [[1, 64], [0, 16]], np_=64); cib = sm.ap(CI * 64, [[1, 64], [0, 16]], np_=64)
        Br = Bt.ap(0, [[16, 64], [1, 16]], np_=64); Bi = Bt.ap(1024, [[16, 64], [1, 16]], np_=64)
        Bbr = Bb.ap(0, [[16, 64], [1, 16]], np_=64); Bbi = Bb.ap(1024, [[16, 64], [1, 16]], np_=64)
        big0 = Ct
        tA = XT.ap(0, [[16, 64], [1, 16]], np_=64); tB = XT.ap(1024, [[16, 64], [1, 16]], np_=64)
        _tt(p, "dve", tA, Br, crb, ALU.mult, [Bt, sm], [XT])
        _tt(p, "dve", tB, Bi, cib, ALU.mult, [Bt, sm], [XT])
        _tt(p, "dve", Bbr, tA, tB, ALU.subtract, [XT], [Bb])
        _tt(p, "dve", tA, Br, cib, ALU.mult, [Bt, sm], [XT])
        _tt(p, "dve", tB, Bi, crb, ALU.mult, [Bt, sm], [XT])
        _tt(p, "dve", Bbi, tA, tB, ALU.add, [XT], [Bb])
        for gb in range(8):
            g0 = gb * 8
            def pw(k, ri):
                return PW.ap(((k + 8) * 2 + ri) * 64 + g0, [[1, 8], [0, 16]], np_=64)
            def cplx(dst, ri_stride, slot, Xr, Xi, k, deps_x, neg_im=False):
                dr = dst.ap(slot * 16, [[256, 8], [1, 16]], np_=64); di = dst.ap(128 + slot * 16, [[256, 8], [1, 16]], np_=64)
                _tt(p, "dve", z[0][:], Xr, pw(k, 0), ALU.mult, deps_x + [PW], [z[0]])
                _tt(p, "pool", z[1][:], Xi, pw(k, 1), ALU.mult, deps_x + [PW], [z[1]])
                _tt(p, "dve", dr, z[0][:], z[1][:], ALU.subtract, [z[0], z[1]], [dst])
                _tt(p, "pool", z[2][:], Xr, pw(k, 1), ALU.mult, deps_x + [PW], [z[2]])
                _tt(p, "dve", z[3][:], Xi, pw(k, 0), ALU.mult, deps_x + [PW], [z[3]])
                if neg_im:
                    p.op("dve", lambda e: e.scalar_tensor_tensor(di, z[2][:], -1.0, z[3][:], ALU.mult, ALU.subtract), reads=[z[2], z[3]], writes=[dst])
                else:
                    _tt(p, "dve", di, z[2][:], z[3][:], ALU.add, [z[2], z[3]], [dst])
            bbr = Bb.ap(g0 * 16, [[16, 8], [1, 16]], np_=64); bbi = Bb.ap(1024 + g0 * 16, [[16, 8], [1, 16]], np_=64)
            ccr = Ct.ap(g0 * 16, [[16, 8], [1, 16]], np_=64); cci = Ct.ap(1024 + g0 * 16, [[16, 8], [1, 16]], np_=64)
            for s in range(8):
                kw = (7 - s) if d == 0 else s
                kx = (-s - 1) if d == 0 else (s - 8)
                km = (s + 1) if d == 0 else (8 - s)
                cplx(WT, 0, s, bbr, bbi, kw, [Bb])
                cplx(XT, 0, s, bbr, bbi, kx, [Bb])
                cplx(Mt, 0, s, ccr, cci, km, [Ct], neg_im=True)
            p.op("act", lambda e: e.copy(XTb[:], XT[:]), reads=[XT], writes=[XTb])
            p.op("act", lambda e, g0=g0: e.copy(R.M[:, g0:g0 + 8, :, :], Mt[:]), reads=[Mt], writes=[R.M])
            for gl in range(8):
                g = g0 + gl
                bk = B[gl % 2]
                for ri in range(2):
                    p.op("pe", lambda e, gl=gl, ri=ri, bk=bk: e.transpose(bk[:, ri * 64:(ri + 1) * 64], WT[:, gl, ri, :], cx.ident[0:64, 0:64]),
                         reads=[WT, cx.ident], writes=[bk])
                p.op("act", lambda e, g=g, bk=bk: e.copy(R.Win[:, g, :], bk[:, 0:128]), reads=[bk], writes=[R.Win])
                bt = B[2 + gl % 2]
                for ri in range(2):
                    p.op("pe", lambda e, gl=gl, g=g, ri=ri, bt=bt: e.matmul(bt[:, 0:128], XTb[:, gl, ri, :], R.M[:, g, ri, :], start=(ri == 0), stop=(ri == 1)),
                         reads=[XTb, R.M], writes=[bt])
                p.op("dve", lambda e, g=g, bt=bt: e.tensor_tensor(R.Toep[:, g, :], bt[:, 0:128], tmask[:], ALU.mult), reads=[bt, tmask], writes=[R.Toep])


def ssm_sweep(p, cx, R, jl, d, seqs):
    B = cx.banks
    with Scope(p):
        u8s = [p.sbuf([128, JB, 64], BF16) for _ in range(2)]; Ss = [p.sbuf([64, 2, 64, JB], F32) for _ in range(2)]
        Hs = [p.sbuf([64, 2, 64], F32) for _ in range(2)]; cur = [0]
        t1 = p.sbuf([64, 2, 64], F32); t2 = p.sbuf([64, 2, 64], F32); Hhs = [p.sbuf([64, 2, 64, JB + 1], BF16) for _ in range(2)]
        Y8s = [p.sbuf([128, JB, 64], F32) for _ in range(2)]; Yf = p.sbuf([128, JB, 64], F32); d8 = p.sbuf([128, 64], F32)
        hs = p.sbuf([64, 2, 64], F32)
        p.dma("sp", d8[:], cx.dram["ssm_d8"][jl], writes=[d8])
        udv = cx.Ud.rearrange("(j s) (c g) -> s c j g", s=8, c=16)
        ydv = cx.Yd.rearrange("(j s) (c g) -> s c j g", s=8, c=16)
        items = []
        for sq in seqs:
            nb = sq["n"] // JB
            order = list(range(nb)) if d == 0 else list(range(nb - 1, -1, -1))
            for n_, bi in enumerate(order):
                items.append((sq, sq["j0"] + bi * JB, n_ == 0, n_ == nb - 1))

        def prep(k):
            sq, j0, first, last = items[k]
            u8 = u8s[k % 2]; S = Ss[k % 2]
            tl = (j0 * 8) // 128
            ddeps = [cx.dUd[tl], cx.dUd[tl + 1]]
            for s_ in range(8):
                p.dma("sp", u8[16 * s_:16 * s_ + 16, :, :], udv[s_, :, j0:j0 + JB, :], reads=ddeps, writes=[u8])
            for gq in range(8):
                bk = B[gq % 2]
                for gl in range(8):
                    g = gq * 8 + gl
                    for ri in range(2):
                        p.op("pe", lambda e, g=g, gl=gl, ri=ri, bk=bk: e.matmul(bk[0:64, (gl * 2 + ri) * JB:(gl * 2 + ri + 1) * JB], R.Win[:, g, ri * 64:(ri + 1) * 64],
                                                                               u8.ap(g, [[64, JB]]), start=True, stop=True), reads=[R.Win, u8], writes=[bk])
                p.op("act", lambda e, gq=gq, bk=bk: e.copy(S.ap(gq * 8 * JB, [[JB, 8], [64 * JB, 2], [1, JB]], np_=64),
                                                           bk.ap(0, [[2 * JB, 8], [JB, 2], [1, JB]], np_=64)), reads=[bk], writes=[S])

        def scan(k):
            sq, j0, first, last = items[k]
            S = Ss[k % 2]; Hh = Hhs[k % 2]
            if first:
                H = Hs[cur[0]]
                if sq["init"] is not None:
                    p.dma("sp", H[:], sq["init"], writes=[H])
                else:
                    p.op("dve", lambda e, H=H: e.memset(H[:], 0.0), writes=[H])
            hidx0 = 0 if d == 0 else JB
            p.op("pool", lambda e: e.tensor_copy(Hh.ap(hidx0, [[JB + 1, 128]], np_=64), Hs[cur[0]].ap(0, [[1, 128]], np_=64)), reads=[Hs[cur[0]]], writes=[Hh])
            for jj in (range(JB) if d == 0 else range(JB - 1, -1, -1)):
                H = Hs[cur[0]]; Hn = Hs[1 - cur[0]]
                _tt(p, "dve", t1[:], H[:], R.A2[:], ALU.mult, [H, R.A2], [t1])
                _tt(p, "dve", t2[:, 0, :], H[:, 1, :], R.AI2[:, 0, :], ALU.mult, [H, R.AI2], [t2])
                _tt(p, "dve", t2[:, 1, :], H[:, 0, :], R.AI2[:, 1, :], ALU.mult, [H, R.AI2], [t2])
                _tt(p, "dve", t1[:], t1[:], t2[:], ALU.add, [t1, t2], [t1])
                _tt(p, "dve", Hn[:], t1[:], S.ap(jj, [[64 * JB, 2], [JB, 64]], np_=64), ALU.add, [t1, S], [Hn])
                hi = (jj + 1) if d == 0 else jj
                p.op("pool", lambda e, hi=hi, Hn=Hn: e.tensor_copy(Hh.ap(hi, [[JB + 1, 128]], np_=64), Hn.ap(0, [[1, 128]], np_=64)), reads=[Hn], writes=[Hh])
                cur[0] = 1 - cur[0]
            if last and sq["out"] is not None:
                bk = B[0]
                for ri in range(2):
                    p.op("pe", lambda e, ri=ri: e.transpose(bk[0:64, ri * 64:(ri + 1) * 64], Hs[cur[0]][:, ri, :], cx.ident[0:64, 0:64]), reads=[Hs[cur[0]], cx.ident], writes=[bk])
                p.op("act", lambda e: e.copy(hs[:], bk.ap(0, [[64, 2], [1, 64]], np_=64)), reads=[bk], writes=[hs])
                p.dma("act", sq["out"].rearrange("r g q -> g r q"), hs[:], reads=[hs], writes=[cx.d_out])

        def outp(k):
            sq, j0, first, last = items[k]
            u8 = u8s[k % 2]; Hh = Hhs[k % 2]; Y8 = Y8s[k % 2]
            tl = (j0 * 8) // 128
            hoff = 0 if d == 0 else 1
            for gq in range(4):
                bk = B[2 + gq]
                for gl in range(16):
                    g = gq * 16 + gl
                    o = bk[:, gl * JB:(gl + 1) * JB]
                    p.op("pe", lambda e, g=g, o=o: e.matmul(o, R.Toep[:, g, :], u8.ap(g, [[64, JB]]), start=True, stop=False), reads=[R.Toep, u8], writes=[bk])
                    for ri in range(2):
                        p.op("pe", lambda e, g=g, o=o, ri=ri: e.matmul(o, R.M[:, g, ri, :], Hh.ap((ri * 64 + g) * (JB + 1) + hoff, [[1, JB]], np_=64),
                                                                       start=False, stop=(ri == 1)), reads=[R.M, Hh], writes=[bk])
                p.op("act", lambda e, gq=gq, bk=bk: e.copy(Y8.ap(gq * 16, [[1, 16], [64, JB]]), bk.ap(0, [[JB, 16], [1, JB]])), reads=[bk], writes=[Y8])
            if d == 0:
                p.dma("sp", cx.Yf8[:, j0:j0 + JB, :], Y8[:], reads=[Y8], writes=[cx.dYf[j0 // JB]])
            else:
                p.dma("sp", Yf[:], cx.Yf8[:, j0:j0 + JB, :], reads=[cx.dYf[j0 // JB]], writes=[Yf])
                _tt(p, "pool", Y8[:], Y8[:], Yf[:], ALU.add, [Y8, Yf], [Y8])
                _tt(p, "pool", Yf[:], u8[:], d8.ap(0, [[0, JB], [1, 64]]), ALU.mult, [u8, d8], [Yf])
                _tt(p, "pool", Y8[:], Y8[:], Yf[:], ALU.add, [Y8, Yf], [Y8])
                p.op("act", lambda e: e.activation(Y8[:], Y8[:], AF.Gelu_apprx_tanh), reads=[Y8], writes=[Y8])
                for s_ in range(8):
                    p.dma("sp", ydv[s_, :, j0:j0 + JB, :], Y8[16 * s_:16 * s_ + 16, :, :], reads=[Y8], writes=[cx.dYd[tl], cx.dYd[tl + 1]])

        n = len(items)
        if n:
            prep(0)
        for k in range(n):
            if k + 1 < n:
                prep(k + 1)
            scan(k)
            outp(k)


def ssm_layer(p, cx, jl, mods, lng, lnb, seqs_fn):
    B = cx.banks
    NTt = cx.NT // 128
    nS = cx.n_smp_tiles
    with Scope(p):
        win = p.sbuf([128, 8, 1024], BF16)
        xt = [p.sbuf([128, 1024], F32) for _ in range(2)]; hm = p.sbuf([128, 1024], F32); hT = p.sbuf([128, 8, 128], BF16)
        uo = [p.sbuf([128, 1024], BF16) for _ in range(2)]
        w = cx.dram["ssm_w_in"][jl]
        p.dma("pool", win[:], w.rearrange("(k q) c -> q k c", q=128), writes=[win])
        p.dma("sp", xt[0][:], cx.X[0:128, :], reads=[cx.dX[0]], writes=[xt[0]])
        for i in range(NTt):
            mod = mods["smp"] if i < nS else mods["ctx"]
            x_ = xt[i % 2]
            if i + 1 < NTt:
                p.dma("sp", xt[(i + 1) % 2][:], cx.X[(i + 1) * 128:(i + 2) * 128, :], reads=[cx.dX[i + 1]], writes=[xt[(i + 1) % 2]])
            modulate_transpose(p, cx, x_, hm, lambda half: (hT[:, half * 4:(half + 1) * 4, :], hT), mod, [B[0], B[1]])
            for dh in range(2):
                bk = B[2 + dh]
                for k in range(8):
                    p.op("pe", lambda e, dh=dh, k=k, bk=bk: e.matmul(bk[:], hT[:, k, :], win[:, k, dh * 512:(dh + 1) * 512], start=(k == 0), stop=(k == 7)),
                         reads=[hT, win], writes=[bk])
                p.op("act" if dh == 0 else "dve", (lambda e, dh=dh, bk=bk, i=i: e.copy(uo[i % 2][:, dh * 512:(dh + 1) * 512], bk[:])) if dh == 0 else
                     (lambda e, dh=dh, bk=bk, i=i: e.tensor_copy(uo[i % 2][:, dh * 512:(dh + 1) * 512], bk[:])), reads=[bk], writes=[uo[i % 2]])
            p.dma("act", cx.Ud[i * 128:(i + 1) * 128, :], uo[i % 2][:], reads=[uo[i % 2]], writes=[cx.dUd[i]])
    with Scope(p):
        R = Ctx()
        R.Win = p.sbuf([128, 64, 128], BF16); R.Toep = p.sbuf([128, 64, 128], BF16); R.M = p.sbuf([64, 64, 2, 128], BF16)
        R.A2 = p.sbuf([64, 2, 64], F32); R.AI2 = p.sbuf([64, 2, 64], F32)
        for d in range(2):
            ssm_precompute(p, cx, R, jl, d)
            ssm_sweep(p, cx, R, jl, d, seqs_fn(d))
    with Scope(p):
        wg = p.sbuf([128, 8, 2048], BF16); wo = p.sbuf([128, 8, 1024], BF16)
        xt = [p.sbuf([128, 1024], F32) for _ in range(3)]; yt = [p.sbuf([128, 1024], F32) for _ in range(3)]
        yT = p.sbuf([128, 8, 128], BF16); sg = p.sbuf([128, 1024], F32); vvs = [p.sbuf([128, 1024], F32) for _ in range(2)]; vT = p.sbuf([128, 8, 128], BF16)
        t1 = p.sbuf([128, 1024], F32); y = p.sbuf([128, 1024], F32)
        scr = {"st": p.sbuf([128, 12], F32), "mv": p.sbuf([128, 2], F32), "rstd": p.sbuf([128, 1], F32)}
        p.dma("pool", wg[:], cx.dram["ssm_w_glu"][jl].rearrange("(k q) c -> q k c", q=128), writes=[wg])
        p.dma("pool", wo[:], cx.dram["ssm_w_out"][jl].rearrange("(k q) c -> q k c", q=128), writes=[wo])

        def loads(i):
            p.dma("sp", xt[i % 3][:], cx.X[i * 128:(i + 1) * 128, :], reads=[cx.dX[i]], writes=[xt[i % 3]])
            p.dma("sp", yt[i % 3][:], cx.Yd[i * 128:(i + 1) * 128, :], reads=[cx.dYd[i]], writes=[yt[i % 3]])

        def S1(i):
            y_ = yt[i % 3]; vv = vvs[i % 2]
            for half in range(2):
                bk = B[half]
                for k4 in range(4):
                    k = half * 4 + k4
                    p.op("pe", lambda e, k=k, k4=k4, bk=bk: e.transpose(bk[:, k4 * 128:(k4 + 1) * 128], y_[:, k * 128:(k + 1) * 128], cx.ident[:]), reads=[y_, cx.ident], writes=[bk])
                p.op("act", lambda e, half=half, bk=bk: e.copy(yT[:, half * 4:(half + 1) * 4, :], bk.ap(0, [[128, 4], [1, 128]])), reads=[bk], writes=[yT])
            for q in range(4):
                bk = B[2 + q]
                for k in range(8):
                    p.op("pe", lambda e, q=q, k=k, bk=bk: e.matmul(bk[:], yT[:, k, :], wg[:, k, q * 512:(q + 1) * 512], start=(k == 0), stop=(k == 7)), reads=[yT, wg], writes=[bk])
            for q in range(2):
                p.op("act", lambda e, q=q: e.activation(sg[:, q * 512:(q + 1) * 512], B[4 + q][:], AF.Sigmoid), reads=[B[4 + q]], writes=[sg])
                p.op("dve", lambda e, q=q: e.tensor_tensor(vv[:, q * 512:(q + 1) * 512], B[2 + q][:], sg[:, q * 512:(q + 1) * 512], ALU.mult), reads=[B[2 + q], sg], writes=[vv])

        def S2(i):
            mod = mods["smp"] if i < nS else mods["ctx"]
            vv = vvs[i % 2]
            for half in range(2):
                bk = B[6 + half]
                for k4 in range(4):
                    k = half * 4 + k4
                    p.op("pe", lambda e, k=k, k4=k4, bk=bk: e.transpose(bk[:, k4 * 128:(k4 + 1) * 128], vv[:, k * 128:(k + 1) * 128], cx.ident[:]), reads=[vv, cx.ident], writes=[bk])
                p.op("act", lambda e, half=half, bk=bk: e.copy(vT[:, half * 4:(half + 1) * 4, :], bk.ap(0, [[128, 4], [1, 128]])), reads=[bk], writes=[vT])
            psO = [B[6], B[7]]
            for dh in range(2):
                for k in range(8):
                    p.op("pe", lambda e, dh=dh, k=k: e.matmul(psO[dh][:], vT[:, k, :], wo[:, k, dh * 512:(dh + 1) * 512], start=(k == 0), stop=(k == 7)), reads=[vT, wo], writes=[psO[dh]])
            post_norm_store(p, cx, xt[i % 3], psO, mod, lng, lnb, t1, y, scr, i * 128, q="sp")

        loads(0)
        if NTt > 1:
            loads(1)
        S1(0)
        for i in range(NTt):
            if i + 2 < NTt:
                loads(i + 2)
            if i + 1 < NTt:
                S1(i + 1)
            S2(i)


def host_ssm_layouts(inp, jl):
    o = {}
    pg = np.stack([np.stack([inp["ssm_a_re"][jl, d].T, inp["ssm_a_im"][jl, d].T,
                             np.broadcast_to(inp["ssm_log_dt"][jl, d][None, :], (64, 64))]) for d in range(2)])
    o["ssm_pg"] = np.ascontiguousarray(pg, dtype=np.float32)
    o["ssm_b"] = np.ascontiguousarray(np.stack([np.stack([inp["ssm_b_re"][jl, d].transpose(1, 0, 2).reshape(64, 1024),
                                                          inp["ssm_b_im"][jl, d].transpose(1, 0, 2).reshape(64, 1024)]) for d in range(2)]))
    o["ssm_c"] = np.ascontiguousarray(np.stack([np.stack([inp["ssm_c_re"][jl, d].transpose(2, 0, 1).reshape(64, 1024),
                                                          inp["ssm_c_im"][jl, d].transpose(2, 0, 1).reshape(64, 1024)]) for d in range(2)]))
    dd = inp["ssm_d"][jl].reshape(64, 16).T
    o["ssm_d8"] = np.ascontiguousarray(np.tile(dd, (8, 1)))
    perm = (np.arange(64)[None, :] * 16 + np.arange(16)[:, None]).reshape(-1)
    o["ssm_w_in"] = np.ascontiguousarray(inp["ssm_w_in"][jl][:, perm])
    o["ssm_w_glu"] = np.ascontiguousarray(inp["ssm_w_glu"][jl][perm, :])
    o["ssm_w_out"] = np.ascontiguousarray(inp["ssm_w_out"][jl])
    return o


def host_tmask():
    s = np.arange(128)[:, None] // 16; t = np.arange(128)[None, :] // 16
    return np.stack([(s <= t), (s >= t)]).astype(np.float32)


def compute_mods(p, cx, l, half):
    B = cx.banks
    names = ["sh", "sc1", "g1"]
    with Scope(p):
        wm = [p.sbuf([128, 8, 512], BF16) for _ in range(2)]
        bm = [p.sbuf([128, 512], F32) for _ in range(2)]
        it = 0
        for v in range(3):
            for dh in range(2):
                col = half * 3072 + v * 1024 + dh * 512
                w_ = wm[it % 2]; b_ = bm[it % 2]
                p.dma("pool", w_[:], cx.dram["w_mod"][l][:, col:col + 512].rearrange("(k q) c -> q k c", q=128), writes=[w_])
                p.dma("sp", b_[:], cx.dram["b_mod"][l][col:col + 512].partition_broadcast(128), writes=[b_])
                for ci, cn in enumerate(("smp", "ctx")):
                    bk = B[(it * 2 + ci) % 8]
                    for k in range(8):
                        p.op("pe", lambda e, k=k, ci=ci, bk=bk, w_=w_: e.matmul(bk[:], cx.SCb[ci][:, k, :], w_[:, k, :], start=(k == 0), stop=(k == 7)),
                             reads=[cx.SCb[ci], w_], writes=[bk])
                    dst = cx.mods[cn][names[v]]
                    p.op("dve", lambda e, bk=bk, dst=dst, b_=b_, v=v, dh=dh: e.scalar_tensor_tensor(
                        dst[:, dh * 512:(dh + 1) * 512], bk[:], 1.0 if v >= 1 else 0.0, b_[:], ALU.add, ALU.add), reads=[bk, b_], writes=[dst])
                it += 1


def load_ln(p, cx, l, which):
    p.dma("sp", cx.lng[:], cx.dram["ln_g"][l, which].partition_broadcast(128), writes=[cx.lng])
    p.dma("sp", cx.lnb[:], cx.dram["ln_b"][l, which].partition_broadcast(128), writes=[cx.lnb])


NS_TILES = 32
NP_SEQ = 4
NTOK = NS_TILES * 128 + NP_SEQ * 256


def build_program(nS=NS_TILES, nP=NP_SEQ, depth=DEPTH):
    nc = bass.Bass("TRN2", target_bir_lowering=False)
    p = Prog(nc)
    cx = Ctx()
    NT = nS * 128 + nP * 256
    NTt = NT // 128
    NCH = NT // 8
    cx.NT = NT; cx.n_smp_tiles = nS; cx.n_prm_seq = nP

    def din(name, shape, dt=F32):
        return nc.dram_tensor(name, list(shape), dt, kind="ExternalInput").ap()

    def dout(name, shape):
        return nc.dram_tensor(name, list(shape), F32, kind="ExternalOutput").ap()

    cx.dram = {
        "c_ident": din("c_ident", [128, 128]), "c_e3": din("c_e3", [128, 128]), "c_masks": din("c_masks", [128, 2, 128]),
        "c_rope": din("c_rope", [64, 2, 4096]), "c_tmask": din("c_tmask", [2, 128, 128]),
        "w_mod": din("w_mod", [4, 1024, 6144]), "b_mod": din("b_mod", [4, 6144]), "ln_g": din("ln_g", [4, 2, 1024]), "ln_b": din("ln_b", [4, 2, 1024]),
        "ssm_pg": din("ssm_pg", [2, 2, 3, 64, 64]), "ssm_b": din("ssm_b", [2, 2, 2, 64, 1024]), "ssm_c": din("ssm_c", [2, 2, 2, 64, 1024]),
        "ssm_d8": din("ssm_d8", [2, 128, 64]), "ssm_w_in": din("ssm_w_in", [2, 1024, 1024]), "ssm_w_glu": din("ssm_w_glu", [2, 1024, 2048]),
        "ssm_w_out": din("ssm_w_out", [2, 1024, 1024]),
        "attn_w_qkv": din("attn_w_qkv", [2, 1024, 1536]), "attn_w_qkp": din("attn_w_qkp", [2, 1024, 1280]),
        "attn_w_out": din("attn_w_out", [2, 1024, 1024]), "attn_sink": din("attn_sink", [2, 16]),
        "cache_k": din("cache_k", [2, 512, 256]), "cache_v": din("cache_v", [2, 512, 256]),
        "peer_w_q": din("peer_w_q", [4, 1024, 2048]), "peer_keysT": din("peer_keysT", [4, 128, 16, 128]),
        "peer_ut": din("peer_ut", [4, 128, 128, 1024]), "peer_v": din("peer_v", [4, NEXP, 1024]),
    }
    x_in = din("x_in", [NT, 1024]); cond2 = din("cond2", [2, 128, 8]); st0 = din("st0", [2, 2, 64, 2, 64])
    y_out = dout("y_out", [NT, 1024]); o_state = dout("o_state", [nP, 2, 2, 2, 64, 64])
    o_k = dout("o_k", [nP, 2, 256, 256]); o_v = dout("o_v", [nP, 2, 256, 256])
    cx.X = nc.dram_tensor("Xs", [NT, 1024], F32).ap()
    cx.Ud = nc.dram_tensor("Ud", [NT, 1024], BF16).ap(); cx.Yd = nc.dram_tensor("Yd", [NT, 1024], F32).ap()
    cx.Yf8 = nc.dram_tensor("Yf8", [128, NCH, 64], F32).ap()
    cx.spd = nc.dram_tensor("spd", [2, 3, 8, NT, 128], BF16).ap()
    cx.utb = nc.dram_tensor("utb", [128, 128, 1024], BF16).ap(); cx.vb = nc.dram_tensor("vb", [NEXP, 1024], BF16).ap()
    cx.dX = [Dep() for _ in range(NTt)]; cx.dUd = [Dep() for _ in range(NTt)]; cx.dYd = [Dep() for _ in range(NTt)]
    cx.dYf = [Dep() for _ in range(NCH // JB)]; cx.d_spd = [Dep() for _ in range(NTt)]
    cx.d_utb = [Dep() for _ in range(16)]; cx.d_vb = [Dep() for _ in range(16)]
    cx.d_out = Dep()
    setup_common(p, cx)
    for i in range(NTt):
        p.dma("sp", cx.X[i * 128:(i + 1) * 128, :], x_in[i * 128:(i + 1) * 128, :], writes=[cx.dX[i]])
    cx.mods = {cn: {n: p.sbuf([128, 1024], F32, name="mod_%s_%s" % (cn, n)) for n in ("sh", "sc1", "g1")} for cn in ("smp", "ctx")}
    cx.lng = p.sbuf([128, 1024], F32, name="lng"); cx.lnb = p.sbuf([128, 1024], F32, name="lnb")
    cx.SCb = [p.sbuf([128, 8, 128], BF16, name="SCb%d" % i) for i in range(2)]
    ctile = p.sbuf([128, 2, 8], F32, name="condt")
    p.dma("sp", ctile[:], cond2.rearrange("a q k -> q a k"), writes=[ctile])
    p.op("act", lambda e: e.activation(ctile[:], ctile[:], AF.Silu), reads=[ctile], writes=[ctile])
    for ci in range(2):
        p.op("dve", lambda e, ci=ci: e.tensor_copy(cx.SCb[ci][:], ctile.ap(ci * 8, [[1, 8], [0, 128]])), reads=[ctile], writes=[cx.SCb[ci]])

    for l in range(depth):
        peer_prep_weights(p, cx, l)
        compute_mods(p, cx, l, 0)
        load_ln(p, cx, l, 0)
        if l % 2 == 0:
            jl = l // 2

            def seqs_fn(d, jl=jl):
                L = []
                if nS:
                    L.append(dict(j0=0, n=nS * 16, init=st0[jl, d], out=None))
                for s in range(nP):
                    L.append(dict(j0=nS * 16 + s * 32, n=32, init=None, out=o_state[s, jl, d]))
                return L
            ssm_layer(p, cx, jl, cx.mods, cx.lng, cx.lnb, seqs_fn)
        else:
            j = l // 2
            with Scope(p):
                a = attn_alloc(p, cx)
                attn_load_layer(p, cx, a, j)
                attn_layer(p, cx, a, j, cx.mods, cx.lng, cx.lnb,
                           lambda s, t, j=j: (o_k[s, j, t * 128:(t + 1) * 128, :], o_v[s, j, t * 128:(t + 1) * 128, :]))
        compute_mods(p, cx, l, 1)
        load_ln(p, cx, l, 1)
        with Scope(p):
            a = peer_alloc(p, cx)
            peer_load_layer(p, cx, a, l)
            for blk in range(NT // 256):
                mod = cx.mods["smp"] if blk * 2 < nS else cx.mods["ctx"]
                peer_block(p, cx, a, l, blk, mod, cx.lng, cx.lnb)
    for i in range(NTt):
        p.dma("sp", y_out[i * 128:(i + 1) * 128, :], cx.X[i * 128:(i + 1) * 128, :], reads=[cx.dX[i]], writes=[cx.d_out])
    p.drain_all()
    p.es.close()
    return nc, p


def host_inputs(inp, core, nS=NS_TILES, nP=NP_SEQ):
    f = lambda a: np.ascontiguousarray(a, dtype=np.float32)
    m = dict(_SHARED)
    xs = inp["x_sample"][core][:nS * 128]
    xp = inp["x_prompt"][core * NP_SEQ: core * NP_SEQ + nP].reshape(nP * 256, 1024)
    m["x_in"] = f(np.concatenate([xs, xp], axis=0))
    m["cond2"] = f(np.stack([inp["c"][core].reshape(8, 128).T, inp["c_ctx"].reshape(8, 128).T]))
    st = np.zeros((2, 2, 64, 2, 64), np.float32)
    for jl in range(2):
        for d in range(2):
            st[jl, d, :, 0, :] = inp["state_ssm_re"][core, jl, d].T
            st[jl, d, :, 1, :] = inp["state_ssm_im"][core, jl, d].T
    m["st0"] = st
    m["cache_k"] = f(inp["cache_k"][core].reshape(2, 512, 256))
    m["cache_v"] = f(inp["cache_v"][core].reshape(2, 512, 256))
    return m


_SHARED = {}


def prepare_shared(inp):
    f = lambda a: np.ascontiguousarray(a, dtype=np.float32)
    S = {}
    S.update(host_consts())
    S["c_tmask"] = host_tmask()
    for k in ("w_mod", "b_mod", "ln_g", "ln_b", "attn_w_qkv", "attn_w_out", "attn_sink", "peer_w_q", "peer_v"):
        S[k] = f(inp[k])
    S["attn_w_qkp"] = f(perm_qk(inp["attn_w_qkv"]))
    L = [host_ssm_layouts(inp, jl) for jl in range(2)]
    for k in L[0]:
        S[k] = f(np.stack([L[0][k], L[1][k]]))
    keys = inp["peer_keys"]
    S["peer_keysT"] = f(keys.transpose(0, 4, 2, 1, 3).reshape(4, 128, 16, 128))
    u = inp["peer_u"]
    S["peer_ut"] = f(u.reshape(4, 128, 128, 8, 128).transpose(0, 1, 4, 3, 2).reshape(4, 128, 128, 1024))
    _SHARED.clear()
    _SHARED.update(S)


_PROG = {}


def kernel(**inputs):
    inp = {k: np.asarray(v) for k, v in inputs.items()}
    if "nc" not in _PROG:
        _PROG["nc"], _ = build_program()
    nc = _PROG["nc"]
    prepare_shared(inp)
    in_maps = [host_inputs(inp, c) for c in range(8)]
    res = run_bass_kernel_spmd(nc, in_maps, core_ids=list(range(8)))
    R = res.results
    nS, nP = NS_TILES, NP_SEQ
    y_sample = np.stack([R[c]["y_out"][:nS * 128] for c in range(8)]).astype(np.float32)
    y_prompt = np.concatenate([R[c]["y_out"][nS * 128:].reshape(nP, 256, 1024) for c in range(8)], axis=0).astype(np.float32)
    ost = np.concatenate([R[c]["o_state"] for c in range(8)], axis=0)
    ns_re = np.ascontiguousarray(ost[:, :, :, 0]).astype(np.float32)
    ns_im = np.ascontiguousarray(ost[:, :, :, 1]).astype(np.float32)
    nk = np.concatenate([R[c]["o_k"] for c in range(8)], axis=0).reshape(32, 2, 256, 4, 64).astype(np.float32)
    nv = np.concatenate([R[c]["o_v"] for c in range(8)], axis=0).reshape(32, 2, 256, 4, 64).astype(np.float32)
    return (y_prompt, y_sample, ns_re, ns_im, nk, nv)
```

```python
from contextlib import ExitStack
import numpy as np
import concourse.bass as bass
import concourse.mybir as mybir
from concourse.ap import AP
from concourse.bass_utils import run_bass_kernel_spmd

F32 = mybir.dt.float32
BF16 = mybir.dt.bfloat16
ALU = mybir.AluOpType
AF = mybir.ActivationFunctionType
AX = mybir.AxisListType

D = 1024
DEPTH = 4
DN_ALPHA = (2 * DEPTH) ** 0.25
LN_EPS = 1e-5
NEXP = 16384
NET = 128


class Dep:
    __slots__ = ("w", "r", "excl")

    def __init__(self):
        self.w = None
        self.r = {}
        self.excl = False


class TT:
    __slots__ = ("t", "d", "pstep", "base", "full")

    def __init__(self, t, base=0, shape=None, dep=None):
        self.t = t
        self.d = dep or Dep()
        self.pstep = t[:].ap[0][0]
        self.base = base
        if shape is None:
            self.full = t[:]
        else:
            dims = []
            st = 1
            for n in reversed(shape):
                dims.append([st, n])
                st *= n
            self.full = AP(t, base, [[self.pstep, 128]] + dims[::-1])

    def __getitem__(self, k):
        return self.full[k]

    def ap(self, off, dims, p0=0, np_=128):
        return AP(self.t, self.base + p0 * self.pstep + off, [[self.pstep, np_]] + [list(x) for x in dims])


def fence(src, dst):
    for a in src:
        a = a.d if isinstance(a, TT) else a
        for b in dst:
            b = b.d if isinstance(b, TT) else b
            if a.w is not None:
                b.r[a.w[0]] = max(b.r.get(a.w[0], 0), a.w[1])
            for k, c in a.r.items():
                b.r[k] = max(b.r.get(k, 0), c)


EMBED_WAIT = True
LN_POW = False


class Prog:
    def __init__(self, nc, n_dma_sems=(44, 30, 16)):
        self.nc = nc
        self.es = ExitStack()
        self.engs = {"pe": nc.tensor, "act": nc.scalar, "dve": nc.vector, "pool": nc.gpsimd, "sp": nc.sync}
        self.sem = {}
        self.cnt = {}
        for k in ("pe", "act", "dve", "pool"):
            self.sem[k] = self.es.enter_context(nc.semaphore("s_" + k))
            self.cnt[k] = 0
        self.dpool = {}
        for q, n in zip(("sp", "pool", "act"), n_dma_sems):
            lst = []
            for i in range(n):
                key = "d_%s_%d" % (q, i)
                self.sem[key] = self.es.enter_context(nc.semaphore(key))
                self.cnt[key] = 0
                lst.append(key)
            self.dpool[q] = [lst, 0]
        self.known = {e: {} for e in self.engs}
        self.ninstr = 0
        self.uid = 0

    def sbuf(self, shape, dt, name=None):
        self.uid += 1
        return TT(self.es.enter_context(self.nc.sbuf_tensor("%s_%d" % (name or "sb", self.uid), list(shape), dt)))

    def psum(self, shape, dt, name=None):
        self.uid += 1
        return TT(self.es.enter_context(self.nc.psum_tensor(name or ("ps%d" % self.uid), list(shape), dt)))

    def _need(self, eng, evs, defer_last=False):
        kn = self.known[eng]
        e = self.engs[eng]
        todo = {}
        for (k, c) in evs:
            if c <= 0 or (k == "pe" and eng == "pe") or kn.get(k, 0) >= c:
                continue
            if todo.get(k, 0) < c:
                todo[k] = c
        items = list(todo.items())
        last = None
        if defer_last and items:
            last = items.pop()
        for k, c in items:
            e.wait_ge(self.sem[k], c)
            kn[k] = c
            self.ninstr += 1
        if last is not None:
            kn[last[0]] = last[1]
        return last

    @staticmethod
    def _collect(reads, writes):
        evs = []
        for d in reads:
            if d.w is not None:
                evs.append(d.w)
        for d in writes:
            if d.w is not None:
                evs.append(d.w)
            evs.extend(d.r.items())
        return evs

    @staticmethod
    def _commit(ev, reads, writes):
        for d in reads:
            d.r[ev[0]] = ev[1]
        for d in writes:
            d.w = ev
            d.r = {}

    def op(self, eng, fn, reads=(), writes=()):
        reads = [x.d if isinstance(x, TT) else x for x in reads]
        writes = [x.d if isinstance(x, TT) else x for x in writes]
        ex = [d for d in reads if d.excl]
        if ex:
            reads = [d for d in reads if not d.excl]
            writes = writes + ex
        last = self._need(eng, self._collect(reads, writes), defer_last=EMBED_WAIT)
        ins = fn(self.engs[eng])
        if last is not None:
            ins._wait_ge(self.sem[last[0]], last[1])
        self.cnt[eng] += 1
        ins.then_inc(self.sem[eng], 1)
        self.ninstr += 1
        self._commit((eng, self.cnt[eng]), reads, writes)
        return ins

    def dma(self, q, out, in_, reads=(), writes=(), **kw):
        reads = [x.d if isinstance(x, TT) else x for x in reads]
        writes = [x.d if isinstance(x, TT) else x for x in writes]
        lst, idx = self.dpool[q]
        key = lst[idx % len(lst)]
        self.dpool[q][1] = idx + 1
        evs = self._collect(reads, writes)
        evs.append((key, self.cnt[key]))
        self._need(q, evs)
        ins = self.engs[q].dma_start(out=out, in_=in_, **kw)
        self.cnt[key] += 16
        ins.then_inc(self.sem[key], 16)
        self.ninstr += 1
        self._commit((key, self.cnt[key]), reads, writes)
        return ins

    def drain_all(self):
        self._need("sp", list(self.cnt.items()))


class Ctx:
    pass


def dram_rows(t, r0, nr, c0=0, ncol=None):
    if ncol is None:
        return t[r0:r0 + nr, :]
    return t[r0:r0 + nr, c0:c0 + ncol]


def setup_common(p, cx):
    nc = p.nc
    cx.banks = [p.psum([128, 512], F32, name="bank%d" % i) for i in range(8)]
    for b in cx.banks:
        b.d.excl = True
    cx.ident = p.sbuf([128, 128], F32, name="ident")
    p.dma("sp", cx.ident[:], cx.dram["c_ident"], writes=[cx.ident])
    cx.e3 = p.sbuf([128, 128], BF16, name="e3")
    p.dma("pool", cx.e3[:], cx.dram["c_e3"], writes=[cx.e3])
    cx.eps = p.sbuf([128, 1], F32, name="eps")
    p.op("dve", lambda e: e.memset(cx.eps[:], LN_EPS), writes=[cx.eps])


def layer_norm_rows(p, cx, y, out, g_bc, b_bc, scr):
    st = scr["st"]; mv = scr["mv"]; rstd = scr["rstd"]
    for c in range(2):
        p.op("dve", lambda e, c=c: e.bn_stats(st[:, c * 6:(c + 1) * 6], y[:, c * 512:(c + 1) * 512]), reads=[y], writes=[st])
    p.op("dve", lambda e: e.bn_aggr(mv[:], st[:]), reads=[st], writes=[mv])
    if LN_POW:
        p.op("dve", lambda e: e.tensor_scalar(rstd[:], mv[:, 1:2], LN_EPS, None, ALU.add), reads=[mv], writes=[rstd])
        p.op("dve", lambda e: e.tensor_single_scalar(rstd[:], rstd[:], -0.5, ALU.pow), reads=[rstd], writes=[rstd])
    else:
        p.op("act", lambda e: e.activation(rstd[:], mv[:, 1:2], AF.Sqrt, bias=cx.eps[:], scale=1.0), reads=[mv, cx.eps], writes=[rstd])
        p.op("dve", lambda e: e.reciprocal(rstd[:], rstd[:]), reads=[rstd], writes=[rstd])
    p.op("dve", lambda e: e.tensor_scalar(y[:], y[:], mv[:, 0:1], rstd[:, 0:1], ALU.subtract, ALU.mult),
         reads=[y, mv, rstd], writes=[y])
    p.op("pool", lambda e: e.tensor_tensor(y[:], y[:], g_bc[:], ALU.mult), reads=[y, g_bc], writes=[y])
    p.op("pool", lambda e: e.tensor_tensor(out[:], y[:], b_bc[:], ALU.add), reads=[y, b_bc], writes=[out])


def peer_prep_weights(p, cx, l):
    ut_src = cx.dram["peer_ut"][l]
    v_src = cx.dram["peer_v"][l]
    CH = 8
    for c in range(NET // CH):
        p.dma("pool", cx.utb[c * CH:(c + 1) * CH], ut_src[c * CH:(c + 1) * CH], writes=[cx.d_utb[c]])
        p.dma("pool", cx.vb[c * CH * 128:(c + 1) * CH * 128, :], v_src[c * CH * 128:(c + 1) * CH * 128, :],
              writes=[cx.d_vb[c]])


def peer_alloc(p, cx):
    a = Ctx()
    a.big = p.sbuf([128, 16384], F32, name="p_big")
    hb = a.big.t[:].bitcast(BF16).tensor
    a.Wb = TT(hb, 0, [256, 128])
    a.wq = TT(hb, 0, [8, 2048])
    a.wqk = [TT(hb, k * 2048, [2048]) for k in range(8)]
    a.s_sb = TT(a.big.t, 8192, [16, 128])
    a.cand = TT(a.big.t, 10240, [8, 256])
    a.r1 = TT(a.big.t, 12288, [2, 8, 128])
    a.d1regs = a.wqk + [a.s_sb, a.cand, a.r1]
    a.keys = p.sbuf([128, 16, 128], BF16, name="keysT")
    a.xt = [p.sbuf([128, 1024], F32, name="p_xt%d" % i) for i in range(2)]
    a.hm = p.sbuf([128, 1024], F32, name="p_hm")
    a.hT = p.sbuf([128, 8, 256], BF16, name="p_hT")
    a.qT = p.sbuf([128, 16, 128], BF16, name="p_qT")
    a.V = p.sbuf([128, 16, 16], F32, name="p_V")
    a.tmp128 = p.sbuf([128, 128], F32, name="p_tmp128")
    a.tmpc = [p.sbuf([128, 256], F32, name="p_tmpc%d" % i) for i in range(2)]
    a.C = p.sbuf([128, 8, 24], F32, name="p_C")
    a.sm = p.sbuf([128, 8, 8], F32, name="p_small")
    a.ex = p.sbuf([128, 8, 16], F32, name="p_ex")
    a.v1pp = p.sbuf([128, 128], F32, name="p_v1pp")
    a.thp = p.sbuf([128, 128], F32, name="p_thp")
    a.SPt = p.sbuf([128, 2, 3, 8, 128], BF16, name="p_SPt")
    a.ThT = p.sbuf([128, 256], F32, name="p_ThT")
    a.V1T = p.sbuf([128, 256], F32, name="p_V1T")
    a.E1T = p.sbuf([128, 256], F32, name="p_E1T")
    a.ev1 = p.sbuf([128, 128], F32, name="p_ev1")
    a.SPh = [[p.sbuf([128, 16, 128], BF16, name="p_SPh%d_%d" % (i, s)) for s in range(2)] for i in range(2)]
    for i in range(2):
        for s_ in range(2):
            p.op("pool", lambda e, i=i, s_=s_: e.memset(a.SPh[i][s_][:], 0.0), writes=[a.SPh[i][s_]])
    a.EX = [[p.sbuf([128, 4, 128], BF16, name="p_EX%d_%d" % (i, s)) for s in range(2)] for i in range(3)]
    a.MK = [[p.sbuf([128, 4, 128], BF16, name="p_MK%d_%d" % (i, s)) for s in range(2)] for i in range(3)]
    a.LR = [[p.sbuf([128, 4, 128], BF16, name="p_LR%d_%d" % (i, s)) for s in range(2)] for i in range(3)]
    a.ut = [p.sbuf([128, 1024], BF16, name="p_ut%d" % i) for i in range(4)]
    a.vt = [p.sbuf([128, 1024], BF16, name="p_vt%d" % i) for i in range(4)]
    a.G = [p.sbuf([128, 256], BF16, name="p_G%d" % i) for i in range(3)]
    a.A = [p.sbuf([128, 256], BF16, name="p_A%d" % i) for i in range(3)]
    a.y = p.sbuf([128, 1024], F32, name="p_y")
    a.t1 = a.hm
    a.xo = a.hm
    a.scr = {"st": p.sbuf([128, 12], F32, name="p_st"), "mv": p.sbuf([128, 2], F32, name="p_mv"),
             "rstd": p.sbuf([128, 1], F32, name="p_rstd")}
    return a


def peer_load_layer(p, cx, a, l):
    p.dma("pool", a.keys[:], cx.dram["peer_keysT"][l], writes=[a.keys])


def peer_block(p, cx, a, l, blk, mod, lng, lnb):
    B = cx.banks
    tok0 = blk * 256
    NT = cx.NT
    spd = cx.spd
    fence([a.Wb], a.d1regs)
    wqd = cx.dram["peer_w_q"][l]
    for k in range(8):
        p.dma("pool", a.wqk[k][:], wqd[k * 128:(k + 1) * 128, :], writes=[a.wqk[k]])
    for tl in range(2):
        r0 = tok0 + tl * 128
        xt = a.xt[tl]
        p.dma("sp", xt[:], cx.X[r0:r0 + 128, :], reads=[cx.dX[r0 // 128]], writes=[xt])
        p.op("dve", lambda e: e.tensor_tensor(a.hm[:], xt[:], mod["sc1"][:], ALU.mult), reads=[xt, mod["sc1"]], writes=[a.hm])
        p.op("pool", lambda e: e.tensor_tensor(a.hm[:], a.hm[:], mod["sh"][:], ALU.add), reads=[a.hm, mod["sh"]], writes=[a.hm])
        for half in range(2):
            bk = B[half]
            for k4 in range(4):
                k = half * 4 + k4
                p.op("pe", lambda e, k=k, k4=k4, bk=bk: e.transpose(bk[:, k4 * 128:(k4 + 1) * 128], a.hm[:, k * 128:(k + 1) * 128], cx.ident[:]),
                     reads=[a.hm, cx.ident], writes=[bk])
            p.op("act", lambda e, half=half, bk=bk: e.copy(
                a.hT.ap(half * 4 * 256 + tl * 128, [[256, 4], [1, 128]]), bk.ap(0, [[128, 4], [1, 128]])),
                reads=[bk], writes=[a.hT])
        for m4 in range(4):
            bk = B[2 + (m4 % 2)]
            for mi in range(4):
                m = m4 * 4 + mi
                for k in range(8):
                    p.op("pe", lambda e, m=m, mi=mi, k=k, bk=bk: e.matmul(
                        bk[:, mi * 128:(mi + 1) * 128], a.wqk[k][:, m * 128:(m + 1) * 128],
                        a.hT[:, k, tl * 128:(tl + 1) * 128], start=(k == 0), stop=(k == 7)),
                        reads=[a.wqk[k], a.hT], writes=[bk])
            eng = "act" if m4 % 2 == 0 else "dve"
            if eng == "act":
                p.op("act", lambda e, m4=m4, bk=bk: e.copy(a.qT[:, m4 * 4:(m4 + 1) * 4, :], bk.ap(0, [[128, 4], [1, 128]])),
                     reads=[bk], writes=[a.qT])
            else:
                p.op("dve", lambda e, m4=m4, bk=bk: e.tensor_copy(a.qT[:, m4 * 4:(m4 + 1) * 4, :], bk.ap(0, [[128, 4], [1, 128]])),
                     reads=[bk], writes=[a.qT])
        for m4 in range(4):
            bk = B[4 + m4]
            for mi in range(4):
                m = m4 * 4 + mi
                p.op("pe", lambda e, m=m, mi=mi, bk=bk: e.matmul(bk[:, mi * 128:(mi + 1) * 128], a.qT[:, m, :], a.keys[:, m, :],
                                                                  start=True, stop=True), reads=[a.qT, a.keys], writes=[bk])
            p.op("act", lambda e, m4=m4, bk=bk: e.copy(a.s_sb[:, m4 * 4:(m4 + 1) * 4, :], bk.ap(0, [[128, 4], [1, 128]])),
                 reads=[bk], writes=[a.s_sb])
        for m in range(16):
            p.op("dve", lambda e, m=m: e.max(out=a.V[:, m, 0:8], in_=a.s_sb[:, m, :]), reads=[a.s_sb], writes=[a.V])
            p.op("dve", lambda e, m=m: e.match_replace(out=a.tmp128[:], in_to_replace=a.V[:, m, 0:8], in_values=a.s_sb[:, m, :],
                                                        imm_value=-1e30), reads=[a.s_sb, a.V], writes=[a.tmp128])
            p.op("dve", lambda e, m=m: e.max(out=a.V[:, m, 8:16], in_=a.tmp128[:]), reads=[a.tmp128], writes=[a.V])
        in0 = a.V.ap(0, [[32, 8], [1, 16], [0, 16]])
        in1 = a.V.ap(16, [[32, 8], [0, 16], [1, 16]])
        outc = a.cand.ap(0, [[256, 8], [16, 16], [1, 16]])
        p.op("dve", lambda e: e.tensor_tensor(outc, in0, in1, ALU.add), reads=[a.V], writes=[a.cand])
        for h in range(8):
            p.op("dve", lambda e, h=h: e.max(out=a.C[:, h, 0:8], in_=a.cand[:, h, :]), reads=[a.cand], writes=[a.C])
            p.op("dve", lambda e, h=h: e.match_replace(out=a.tmpc[0][:], in_to_replace=a.C[:, h, 0:8], in_values=a.cand[:, h, :],
                                                        imm_value=-1e30), reads=[a.cand, a.C], writes=[a.tmpc[0]])
            p.op("dve", lambda e, h=h: e.max(out=a.C[:, h, 8:16], in_=a.tmpc[0][:]), reads=[a.tmpc[0]], writes=[a.C])
            p.op("dve", lambda e, h=h: e.match_replace(out=a.tmpc[1][:], in_to_replace=a.C[:, h, 8:16], in_values=a.tmpc[0][:],
                                                        imm_value=-1e30), reads=[a.tmpc[0], a.C], writes=[a.tmpc[1]])
            p.op("dve", lambda e, h=h: e.max(out=a.C[:, h, 16:24], in_=a.tmpc[1][:]), reads=[a.tmpc[1]], writes=[a.C])
        sm = a.sm
        TAU, LNZ, OFF1, C2, ZZ = 0, 1, 2, 3, 4
        c15 = a.C.ap(15, [[24, 8]]); c16 = a.C.ap(16, [[24, 8]]); c0 = a.C.ap(0, [[24, 8]])
        m1v = a.V.ap(0, [[32, 8]]); m2v = a.V.ap(16, [[32, 8]])
        p.op("dve", lambda e: e.tensor_tensor(sm[:, TAU, :], c15, c16, ALU.add), reads=[a.C], writes=[sm])
        p.op("dve", lambda e: e.tensor_scalar(sm[:, TAU, :], sm[:, TAU, :], 0.5, None, ALU.mult), reads=[sm], writes=[sm])
        c016 = a.C.ap(0, [[24, 8], [1, 16]]); c0b = a.C.ap(0, [[24, 8], [0, 16]])
        p.op("dve", lambda e: e.tensor_tensor(a.ex[:], c016, c0b, ALU.subtract), reads=[a.C], writes=[a.ex])
        p.op("act", lambda e: e.activation(a.ex[:], a.ex[:], AF.Exp), reads=[a.ex], writes=[a.ex])
        p.op("dve", lambda e: e.reduce_sum(sm[:, ZZ, :], a.ex[:], axis=AX.X), reads=[a.ex], writes=[sm])
        p.op("act", lambda e: e.activation(sm[:, LNZ, :], sm[:, ZZ, :], AF.Ln), reads=[sm], writes=[sm])
        p.op("dve", lambda e: e.tensor_tensor(sm[:, OFF1, :], m1v, sm[:, LNZ, :], ALU.add), reads=[a.V, sm], writes=[sm])
        p.op("dve", lambda e: e.tensor_tensor(sm[:, C2, :], sm[:, TAU, :], m2v, ALU.subtract), reads=[a.V, sm], writes=[sm])
        s1v = a.s_sb.ap(0, [[256, 8], [1, 128]]); s2v = a.s_sb.ap(128, [[256, 8], [1, 128]])
        off1b = sm.ap(OFF1 * 8, [[1, 8], [0, 128]]); m2b = a.V.ap(16, [[32, 8], [0, 128]])
        sfix = a.cand
        sf4 = sfix.ap(0, [[1024, 2], [128, 8], [1, 128]])
        p.op("dve", lambda e: e.tensor_tensor(sfix.ap(0, [[128, 8], [1, 128]]), s1v, off1b, ALU.subtract), reads=[a.s_sb, sm], writes=[sfix])
        p.op("pool", lambda e: e.tensor_tensor(sfix.ap(1024, [[128, 8], [1, 128]]), s2v, m2b, ALU.subtract), reads=[a.s_sb, a.V], writes=[sfix])
        v1v = a.V.ap(0, [[32, 8], [1, 16]]); off1b16 = sm.ap(OFF1 * 8, [[1, 8], [0, 16]]); c2b16 = sm.ap(C2 * 8, [[1, 8], [0, 16]])
        v1pp3 = a.v1pp.ap(0, [[16, 8], [1, 16]]); thp3 = a.thp.ap(0, [[16, 8], [1, 16]])
        p.op("dve", lambda e: e.tensor_tensor(v1pp3, v1v, off1b16, ALU.subtract), reads=[a.V, sm], writes=[a.v1pp])
        p.op("dve", lambda e: e.tensor_tensor(thp3, c2b16, v1v, ALU.subtract), reads=[a.V, sm], writes=[a.thp])
        SP = a.SPt
        p.op("act", lambda e: e.copy(SP[:, :, 0, :, :], sf4), reads=[sfix], writes=[SP])
        p.op("dve", lambda e: e.tensor_tensor(a.r1[:], sf4, SP[:, :, 0, :, :], ALU.subtract), reads=[sfix, SP], writes=[a.r1])
        p.op("act", lambda e: e.copy(SP[:, :, 1, :, :], a.r1[:]), reads=[a.r1], writes=[SP])
        p.op("dve", lambda e: e.tensor_tensor(a.r1[:], a.r1[:], SP[:, :, 1, :, :], ALU.subtract), reads=[a.r1, SP], writes=[a.r1])
        p.op("act", lambda e: e.copy(SP[:, :, 2, :, :], a.r1[:]), reads=[a.r1], writes=[SP])
        for side in range(2):
            dst = spd[side, :, :, r0:r0 + 128, :].rearrange("a h t i -> t a h i")
            p.dma("sp", dst, SP[:, side, :, :, :], reads=[SP], writes=[cx.d_spd[r0 // 128]])
        bk = B[0]
        p.op("act", lambda e: e.activation(a.ev1[:], a.v1pp[:], AF.Exp), reads=[a.v1pp], writes=[a.ev1])
        p.op("pe", lambda e: e.transpose(bk[:, 0:128], a.thp[:], cx.ident[:]), reads=[a.thp, cx.ident], writes=[bk])
        p.op("pe", lambda e: e.transpose(bk[:, 128:256], a.v1pp[:], cx.ident[:]), reads=[a.v1pp, cx.ident], writes=[bk])
        p.op("pe", lambda e: e.transpose(bk[:, 256:384], a.ev1[:], cx.ident[:]), reads=[a.ev1, cx.ident], writes=[bk])
        p.op("act", lambda e: e.copy(a.ThT[:, tl * 128:(tl + 1) * 128], bk[:, 0:128]), reads=[bk], writes=[a.ThT])
        p.op("act", lambda e: e.copy(a.V1T[:, tl * 128:(tl + 1) * 128], bk[:, 128:256]), reads=[bk], writes=[a.V1T])
        p.op("act", lambda e: e.copy(a.E1T[:, tl * 128:(tl + 1) * 128], bk[:, 256:384]), reads=[bk], writes=[a.E1T])

    if getattr(cx, "stage", 9) < 2:
        return
    fence(a.d1regs, [a.Wb])
    groups = [(sub, g4) for sub in range(16) for g4 in range(4)]
    NG = len(groups)

    def d2_load(sub):
        ta = tok0 + sub * 16
        for side in range(2):
            src = spd[side, :, :, ta:ta + 16, :].rearrange("a h t i -> (a h) t i")
            p.dma("sp", a.SPh[sub % 2][side][0:24, :, :], src, reads=[cx.d_spd[ta // 128]], writes=[a.SPh[sub % 2][side]])

    def stA(gi):
        sub, g4 = groups[gi]
        st = gi % 3
        sph = a.SPh[sub % 2]
        for side in range(2):
            bk = B[st * 2 + side]
            p.op("pe", lambda e, side=side, bk=bk: e.matmul(bk[:], cx.e3[0:24, :], sph[side][0:24, g4 * 4:(g4 + 1) * 4, :], start=True, stop=True),
                 reads=[cx.e3, sph[side]], writes=[bk])
        if g4 == 3 and sub + 2 < 16:
            d2_load(sub + 2)

    def stB(gi):
        sub, g4 = groups[gi]
        st = gi % 3
        tb = sub * 16 + g4 * 4
        EX = a.EX[st]; MK = a.MK[st]
        ops = (ALU.is_equal, ALU.is_ge)
        srcs = (a.V1T, a.ThT)
        for side in range(2):
            bk = B[st * 2 + side]
            if side == 1:
                p.op("act", lambda e, side=side, bk=bk: e.activation(EX[side][:], bk.ap(0, [[128, 4], [1, 128]]), AF.Exp), reads=[bk], writes=[EX[side]])
            cb = srcs[side].ap(tb, [[1, 4], [0, 128]])
            p.op("dve", lambda e, side=side, bk=bk, cb=cb: e.tensor_tensor(MK[side][:], bk.ap(0, [[128, 4], [1, 128]]), cb, ops[side]),
                 reads=[bk, srcs[side]], writes=[MK[side]])

    def stC(gi):
        sub, g4 = groups[gi]
        st = gi % 3
        tb = sub * 16 + g4 * 4
        e1b = a.E1T.ap(tb, [[1, 4], [0, 128]])
        p.op("pool", lambda e: e.tensor_tensor(a.LR[st][0][:], a.MK[st][0][:], e1b, ALU.mult), reads=[a.MK[st][0], a.E1T], writes=[a.LR[st][0]])
        p.op("pool", lambda e: e.tensor_tensor(a.LR[st][1][:], a.MK[st][1][:], a.EX[st][1][:], ALU.mult),
             reads=[a.MK[st][1], a.EX[st][1]], writes=[a.LR[st][1]])

    def stD(gi):
        sub, g4 = groups[gi]
        st = gi % 3
        tb = sub * 16 + g4 * 4
        psW = B[6 + gi % 2]
        LR = a.LR[st]
        for tk in range(4):
            p.op("pe", lambda e, tk=tk: e.matmul(psW[:, tk * 128:(tk + 1) * 128], LR[1][:, tk, :], LR[0][:, tk, :], start=True, stop=True),
                 reads=[LR[0], LR[1]], writes=[psW])
        p.op("act", lambda e: e.copy(a.Wb[:, tb:tb + 4, :], psW.ap(0, [[128, 4], [1, 128]])), reads=[psW], writes=[a.Wb])

    d2_load(0); d2_load(1)
    for i in range(NG + 3):
        if i < NG:
            stA(i)
        if 0 <= i - 1 < NG:
            stB(i - 1)
        if 0 <= i - 2 < NG:
            stC(i - 2)
        if 0 <= i - 3 < NG:
            stD(i - 3)

    if getattr(cx, "stage", 9) < 3:
        return
    psO = [[B[0], B[1]], [B[2], B[3]]]

    def d3_load(et):
        p.dma("sp", a.ut[et % 4][:], cx.utb[et], reads=[cx.d_utb[et // 8]], writes=[a.ut[et % 4]])
        p.dma("sp", a.vt[et % 4][:], cx.vb[et * 128:(et + 1) * 128, :], reads=[cx.d_vb[et // 8]], writes=[a.vt[et % 4]])

    def d3_ux(et):
        ut = a.ut[et % 4]
        psU = B[4 + (et % 3)]
        for k in range(8):
            p.op("pe", lambda e, k=k: e.matmul(psU[:, 0:256], ut[:, k * 128:(k + 1) * 128], a.hT[:, k, :], start=(k == 0), stop=(k == 7)),
                 reads=[ut, a.hT], writes=[psU])
        G = a.G[et % 3]; A = a.A[et % 3]
        p.op("act", lambda e: e.activation(G[:], psU[:, 0:256], AF.Gelu_apprx_tanh), reads=[psU], writes=[G])
        wsl = a.Wb.ap(et, [[128, 256]])
        p.op("dve" if et % 2 == 0 else "pool", lambda e: e.tensor_tensor(A[:], G[:], wsl, ALU.mult), reads=[G, a.Wb], writes=[A])

    def d3_v(et):
        A = a.A[et % 3]; vt = a.vt[et % 4]
        for tl in range(2):
            for dh in range(2):
                p.op("pe", lambda e, tl=tl, dh=dh: e.matmul(psO[tl][dh][:], A[:, tl * 128:(tl + 1) * 128], vt[:, dh * 512:(dh + 1) * 512],
                                                            start=(et == 0), stop=(et == NET - 1)), reads=[A, vt], writes=[psO[tl][dh]])

    d3_load(0); d3_load(1); d3_load(2); d3_ux(0); d3_ux(1)
    for et in range(NET):
        if et + 3 < NET:
            d3_load(et + 3)
        if et + 2 < NET:
            d3_ux(et + 2)
        d3_v(et)
    if getattr(cx, "stage", 9) < 4:
        return
    for tl in range(2):
        r0 = tok0 + tl * 128
        xt = a.xt[tl]
        for dh in range(2):
            p.op("dve", lambda e, dh=dh: e.tensor_tensor(a.t1[:, dh * 512:(dh + 1) * 512], psO[tl][dh][:], mod["g1"][:, dh * 512:(dh + 1) * 512], ALU.mult),
                 reads=[psO[tl][dh], mod["g1"]], writes=[a.t1])
        p.op("dve", lambda e: e.scalar_tensor_tensor(a.y[:], xt[:], DN_ALPHA, a.t1[:], ALU.mult, ALU.add), reads=[xt, a.t1], writes=[a.y])
        layer_norm_rows(p, cx, a.y, a.xo, lng, lnb, a.scr)
        p.dma("act", cx.X[r0:r0 + 128, :], a.xo[:], reads=[a.xo], writes=[cx.dX[r0 // 128]])


def modulate_transpose(p, cx, xt, hm, hT_dst_fn, mod, banks):
    p.op("dve", lambda e: e.tensor_tensor(hm[:], xt[:], mod["sc1"][:], ALU.mult), reads=[xt, mod["sc1"]], writes=[hm])
    p.op("pool", lambda e: e.tensor_tensor(hm[:], hm[:], mod["sh"][:], ALU.add), reads=[hm, mod["sh"]], writes=[hm])
    for half in range(2):
        bk = banks[half]
        for k4 in range(4):
            k = half * 4 + k4
            p.op("pe", lambda e, k=k, k4=k4, bk=bk: e.transpose(bk[:, k4 * 128:(k4 + 1) * 128], hm[:, k * 128:(k + 1) * 128], cx.ident[:]),
                 reads=[hm, cx.ident], writes=[bk])
        dst, dep = hT_dst_fn(half)
        p.op("act", lambda e, bk=bk, dst=dst: e.copy(dst, bk.ap(0, [[128, 4], [1, 128]])), reads=[bk], writes=[dep])


def post_norm_store(p, cx, xt, psO, mod, lng, lnb, t1, y, scr, r0, q="act"):
    for dh in range(2):
        p.op("dve", lambda e, dh=dh: e.tensor_tensor(t1[:, dh * 512:(dh + 1) * 512], psO[dh][:], mod["g1"][:, dh * 512:(dh + 1) * 512], ALU.mult),
             reads=[psO[dh], mod["g1"]], writes=[t1])
    p.op("dve", lambda e: e.scalar_tensor_tensor(y[:], xt[:], DN_ALPHA, t1[:], ALU.mult, ALU.add), reads=[xt, t1], writes=[y])
    layer_norm_rows(p, cx, y, t1, lng, lnb, scr)
    p.dma(q, cx.X[r0:r0 + 128, :], t1[:], reads=[t1], writes=[cx.dX[r0 // 128]])


def attn_alloc(p, cx):
    a = Ctx()
    a.wqkv = p.sbuf([128, 8, 1536], BF16, name="a_wqkv")
    a.wqkp = p.sbuf([128, 8, 1280], BF16, name="a_wqkp")
    a.wout = p.sbuf([128, 8, 1024], BF16, name="a_wout")
    a.esink = p.sbuf([128, 16], F32, name="a_esink")
    a.kcf = p.sbuf([128, 256], F32, name="a_kcf")
    a.KTc = p.sbuf([64, 4, 4, 128], BF16, name="a_KTc")
    a.Vc = p.sbuf([128, 4, 4, 65], BF16, name="a_Vc")
    a.xt = [p.sbuf([128, 1024], F32, name="a_xt%d" % i) for i in range(3)]
    a.hm = p.sbuf([128, 1024], F32, name="a_hm")
    a.hT = p.sbuf([128, 8, 128], BF16, name="a_hT")
    a.QT = [p.sbuf([64, 16, 128], BF16, name="a_QT%d" % i) for i in range(3)]
    a.KT = [p.sbuf([64, 4, 128], BF16, name="a_KT%d" % i) for i in range(3)]
    a.V = [p.sbuf([128, 4, 65], BF16, name="a_V%d" % i) for i in range(3)]
    a.cs = [p.sbuf([64, 2, 128], F32, name="a_cs%d" % i) for i in range(2)]
    a.r1 = p.sbuf([64, 4, 128], F32, name="a_r1")
    a.r2 = p.sbuf([64, 4, 128], F32, name="a_r2")
    a.kvtok = p.sbuf([128, 2, 256], F32, name="a_kvtok")
    a.PT = [p.sbuf([128, 4, 128], BF16, name="a_PT%d" % i) for i in range(7)]
    a.masks = p.sbuf([128, 2, 128], BF16, name="a_masks")
    a.den = p.sbuf([128, 4], F32, name="a_den")
    a.O = p.sbuf([128, 1024], F32, name="a_O")
    a.OT = p.sbuf([128, 8, 128], BF16, name="a_OT")
    a.y = p.sbuf([128, 1024], F32, name="a_y")
    a.scr = {"st": p.sbuf([128, 12], F32, name="a_st"), "mv": p.sbuf([128, 2], F32, name="a_mv"),
             "rstd": p.sbuf([128, 1], F32, name="a_rstd")}
    p.dma("pool", a.masks[:], cx.dram["c_masks"], writes=[a.masks])
    for i in range(3):
        p.op("pool", lambda e, i=i: e.memset(a.V[i][:], 1.0), writes=[a.V[i]])
    p.op("pool", lambda e: e.memset(a.Vc[:], 1.0), writes=[a.Vc])
    return a


def attn_load_layer(p, cx, a, j):
    B = cx.banks
    w = cx.dram["attn_w_qkv"][j]; wp = cx.dram["attn_w_qkp"][j]; wo = cx.dram["attn_w_out"][j]
    p.dma("pool", a.wqkv[:], w.rearrange("(k q) c -> q k c", q=128), writes=[a.wqkv])
    p.dma("pool", a.wqkp[:], wp.rearrange("(k q) c -> q k c", q=128), writes=[a.wqkp])
    p.dma("pool", a.wout[:], wo.rearrange("(k q) c -> q k c", q=128), writes=[a.wout])
    p.dma("sp", a.esink[:], cx.dram["attn_sink"][j].partition_broadcast(128), writes=[a.esink])
    p.op("act", lambda e: e.activation(a.esink[:], a.esink[:], AF.Exp), reads=[a.esink], writes=[a.esink])
    if cx.n_smp_tiles == 0:
        return
    ck = cx.dram["cache_k"][j]
    cv = cx.dram["cache_v"][j]
    for c in range(4):
        p.dma("sp", a.kcf[:], ck[c * 128:(c + 1) * 128, :], writes=[a.kcf])
        bk = B[0]
        for g in range(4):
            p.op("pe", lambda e, g=g: e.transpose(bk[0:64, g * 128:(g + 1) * 128], a.kcf[:, g * 64:(g + 1) * 64], cx.ident[:]),
                 reads=[a.kcf, cx.ident], writes=[bk])
        p.op("act", lambda e, c=c: e.copy(a.KTc[:, :, c, :], bk.ap(0, [[128, 4], [1, 128]], np_=64)), reads=[bk], writes=[a.KTc])
        p.dma("pool", a.Vc[:, c, :, 0:64], cv[c * 128:(c + 1) * 128, :].rearrange("t (g d) -> t g d", g=4), writes=[a.Vc])


def attn_project(p, cx, a, r0, slot, rope_pos, mod, want_kv_out=None):
    B = cx.banks
    xt = a.xt[slot]
    p.dma("sp", xt[:], cx.X[r0:r0 + 128, :], reads=[cx.dX[r0 // 128]], writes=[xt])
    modulate_transpose(p, cx, xt, a.hm, lambda half: (a.hT[:, half * 4:(half + 1) * 4, :], a.hT), mod, [B[0], B[1]])
    QT = a.QT[slot]; KT = a.KT[slot]; V = a.V[slot]
    if rope_pos is not None:
        cs = a.cs[rope_pos % 2]
        p.dma("sp", cs[:], cx.dram["c_rope"][:, :, rope_pos * 128:(rope_pos + 1) * 128], writes=[cs])
    for hg in range(5):
        colbase = hg * 256 if hg < 4 else 1024
        b0 = B[0]; b1 = B[1]
        for r in range(4):
            for k in range(8):
                p.op("pe", lambda e, r=r, k=k: e.matmul(b0[0:64, r * 128:(r + 1) * 128], a.wqkv[:, k, colbase + r * 64: colbase + (r + 1) * 64],
                                                        a.hT[:, k, :], start=(k == 0), stop=(k == 7)), reads=[a.wqkv, a.hT], writes=[b0])
        dst = QT[:, hg * 4:(hg + 1) * 4, :] if hg < 4 else KT[:]
        ddep = QT if hg < 4 else KT
        if rope_pos is None:
            p.op("act", lambda e, dst=dst: e.copy(dst, b0.ap(0, [[128, 4], [1, 128]], np_=64)), reads=[b0], writes=[ddep])
        else:
            for r in range(4):
                for k in range(8):
                    p.op("pe", lambda e, r=r, k=k: e.matmul(b1[0:64, r * 128:(r + 1) * 128], a.wqkp[:, k, colbase + r * 64: colbase + (r + 1) * 64],
                                                            a.hT[:, k, :], start=(k == 0), stop=(k == 7)), reads=[a.wqkp, a.hT], writes=[b1])
            cosb = cs.ap(0, [[0, 4], [1, 128]], np_=64); sinb = cs.ap(128, [[0, 4], [1, 128]], np_=64)
            p.op("dve", lambda e: e.tensor_tensor(a.r1[:], b0.ap(0, [[128, 4], [1, 128]], np_=64), cosb, ALU.mult), reads=[b0, cs], writes=[a.r1])
            p.op("dve", lambda e: e.tensor_tensor(a.r2[:], b1.ap(0, [[128, 4], [1, 128]], np_=64), sinb, ALU.mult), reads=[b1, cs], writes=[a.r2])
            p.op("pool", lambda e, dst=dst: e.tensor_tensor(dst, a.r1[:], a.r2[:], ALU.add), reads=[a.r1, a.r2], writes=[ddep])
    bv = B[2]
    for k in range(8):
        p.op("pe", lambda e, k=k: e.matmul(bv[:, 0:256], a.hT[:, k, :], a.wqkv[:, k, 1280:1536], start=(k == 0), stop=(k == 7)),
             reads=[a.hT, a.wqkv], writes=[bv])
    if want_kv_out is not None:
        for k in range(8):
            p.op("pe", lambda e, k=k: e.matmul(bv[:, 256:512], a.hT[:, k, :], a.wqkv[:, k, 1024:1280], start=(k == 0), stop=(k == 7)),
                 reads=[a.hT, a.wqkv], writes=[bv])
    p.op("act", lambda e: e.copy(V[:, :, 0:64], bv.ap(0, [[64, 4], [1, 64]])), reads=[bv], writes=[V])
    if want_kv_out is not None:
        ko, vo = want_kv_out
        p.op("dve", lambda e: e.tensor_copy(a.kvtok[:, 0, :], bv[:, 256:512]), reads=[bv], writes=[a.kvtok])
        p.op("dve", lambda e: e.tensor_copy(a.kvtok[:, 1, :], bv[:, 0:256]), reads=[bv], writes=[a.kvtok])
        p.dma("act", ko, a.kvtok[:, 0, :], reads=[a.kvtok], writes=[cx.d_out])
        p.dma("act", vo, a.kvtok[:, 1, :], reads=[a.kvtok], writes=[cx.d_out])


def attn_attend(p, cx, a, r0, slot, chunks, mod, lng, lnb):
    B = cx.banks
    QT = a.QT[slot]
    nch = len(chunks)
    for g in range(4):
        for ci, (kfn, vfn, deps, mk) in enumerate(chunks):
            bs = B[2 + (ci % 2)]
            p.op("pe", lambda e, kfn=kfn, bs=bs: e.matmul(bs[:], kfn(g), QT[:, g * 4:(g + 1) * 4, :], start=True, stop=True),
                 reads=[QT] + deps, writes=[bs])
            PT = a.PT[ci]
            p.op("act", lambda e, bs=bs, PT=PT: e.activation(PT[:], bs.ap(0, [[128, 4], [1, 128]]), AF.Exp, scale=0.125), reads=[bs], writes=[PT])
            if mk is not None:
                mb = a.masks.ap(mk * 128, [[0, 4], [1, 128]])
                p.op("pool", lambda e, PT=PT, mb=mb: e.tensor_tensor(PT[:], PT[:], mb, ALU.mult), reads=[PT, a.masks], writes=[PT])
        bo = B[4 + g]
        for r in range(4):
            for ci, (kfn, vfn, deps, mk) in enumerate(chunks):
                p.op("pe", lambda e, r=r, ci=ci, vfn=vfn: e.matmul(bo[:, r * 65:(r + 1) * 65], a.PT[ci][:, r, :], vfn(g),
                                                                   start=(ci == 0), stop=(ci == nch - 1)), reads=[a.PT[ci]] + deps, writes=[bo])
        p.op("dve", lambda e: e.tensor_tensor(a.den[:], bo.ap(64, [[65, 4]]), a.esink[:, g * 4:(g + 1) * 4], ALU.add), reads=[bo, a.esink], writes=[a.den])
        p.op("dve", lambda e: e.reciprocal(a.den[:], a.den[:]), reads=[a.den], writes=[a.den])
        p.op("dve", lambda e: e.tensor_tensor(a.O.ap(g * 256, [[64, 4], [1, 64]]), bo.ap(0, [[65, 4], [1, 64]]), a.den.ap(0, [[1, 4], [0, 64]]), ALU.mult),
             reads=[bo, a.den], writes=[a.O])
    for half in range(2):
        bk = B[half]
        for k4 in range(4):
            k = half * 4 + k4
            p.op("pe", lambda e, k=k, k4=k4, bk=bk: e.transpose(bk[:, k4 * 128:(k4 + 1) * 128], a.O[:, k * 128:(k + 1) * 128], cx.ident[:]),
                 reads=[a.O, cx.ident], writes=[bk])
        p.op("act", lambda e, half=half, bk=bk: e.copy(a.OT[:, half * 4:(half + 1) * 4, :], bk.ap(0, [[128, 4], [1, 128]])), reads=[bk], writes=[a.OT])
    psO = [B[2], B[3]]
    for dh in range(2):
        for k in range(8):
            p.op("pe", lambda e, dh=dh, k=k: e.matmul(psO[dh][:], a.OT[:, k, :], a.wout[:, k, dh * 512:(dh + 1) * 512], start=(k == 0), stop=(k == 7)),
                 reads=[a.OT, a.wout], writes=[psO[dh]])
    post_norm_store(p, cx, a.xt[slot], psO, mod, lng, lnb, a.hm, a.y, a.scr, r0)


def attn_layer(p, cx, a, j, mods, lng, lnb, kv_out):
    nS = cx.n_smp_tiles
    def loc(slot):
        return (lambda g: a.KT[slot][:, g, :]), (lambda g: a.V[slot][:, g, :]), [a.KT[slot], a.V[slot]]
    if nS:
        attn_project(p, cx, a, 0, 0, 0, mods["smp"])
    for i in range(nS):
        if i + 1 < nS:
            attn_project(p, cx, a, (i + 1) * 128, (i + 1) % 3, i + 1, mods["smp"])
        chunks = []
        for c in range(4):
            chunks.append(((lambda g, c=c: a.KTc[:, g, c, :]), (lambda g, c=c: a.Vc[:, c, g, :]), [a.KTc, a.Vc], None))
        if i > 0:
            k_, v_, d_ = loc((i - 1) % 3); chunks.append((k_, v_, d_, 0))
        k_, v_, d_ = loc(i % 3); chunks.append((k_, v_, d_, None))
        if i + 1 < nS:
            k_, v_, d_ = loc((i + 1) % 3); chunks.append((k_, v_, d_, 1))
        attn_attend(p, cx, a, i * 128, i % 3, chunks, mods["smp"], lng, lnb)
    for s in range(cx.n_prm_seq):
        base = nS * 128 + s * 256
        for t in range(2):
            attn_project(p, cx, a, base + t * 128, t, None, mods["ctx"], want_kv_out=kv_out(s, t))
        for t in range(2):
            chunks = []
            for t2 in range(2):
                k_, v_, d_ = loc(t2); chunks.append((k_, v_, d_, None))
            attn_attend(p, cx, a, base + t * 128, t, chunks, mods["ctx"], lng, lnb)


def host_consts():
    C = {}
    C["c_ident"] = np.eye(128, dtype=np.float32)
    e3 = np.zeros((128, 128), np.float32)
    for q in range(4):
        for part in range(3):
            for hh in range(8):
                e3[32 * q + part * 8 + hh, 16 * hh:16 * hh + 16] = 1
    C["c_e3"] = e3
    kk = np.arange(128)[:, None]; qq = np.arange(128)[None, :]
    C["c_masks"] = np.stack([(kk >= qq), (kk <= qq)], axis=1).astype(np.float32)
    pos = np.arange(4096)
    row = (pos // 64).astype(np.float32); col = (pos % 64).astype(np.float32)
    inv = (10000.0 ** (-np.arange(16, dtype=np.float32) / 16)).astype(np.float32)
    rope = np.zeros((64, 2, 4096), np.float32)
    for d in range(64):
        axis = row if d < 32 else col
        ang = (axis * inv[d % 16]).astype(np.float32)
        first = (d % 32) < 16
        rope[d, 0] = np.cos(ang)
        rope[d, 1] = -np.sin(ang) if first else np.sin(ang)
    C["c_rope"] = rope
    return C


def perm_qk(wqkv):
    idx = np.arange(1280)
    d = idx % 64
    partner = np.where((d % 32) < 16, idx + 16, idx - 16)
    return np.ascontiguousarray(wqkv[..., partner])


class Scope:
    def __init__(self, p):
        self.p = p

    def __enter__(self):
        self.saved = self.p.es
        self.p.es = ExitStack()
        return self

    def __exit__(self, *a):
        p = self.p
        allc = list(p.cnt.items())
        for e in ("pe", "act", "dve", "pool", "sp"):
            p._need(e, allc)
        p.es.close()
        p.es = self.saved
        return False


TWO_PI = 2.0 * np.pi
JB = 32


def _tt(p, eng, out, in0, in1, op, reads, writes):
    p.op(eng, lambda e: e.tensor_tensor(out, in0, in1, op), reads=reads, writes=writes)


def ssm_precompute(p, cx, R, jl, d):
    B = cx.banks
    G = 64
    with Scope(p):
        pg = p.sbuf([64, 3, 64], F32); Bt = p.sbuf([64, 2, 1024], F32); Ct = p.sbuf([64, 2, 1024], F32)
        PW = p.sbuf([64, 17, 2, 64], F32); sm = p.sbuf([64, 8, 64], F32); Bb = p.sbuf([64, 2, 1024], F32)
        WT = p.sbuf([64, 8, 2, 128], F32); XT = p.sbuf([64, 8, 2, 128], F32); Mt = p.sbuf([64, 8, 2, 128], F32)
        XTb = p.sbuf([64, 8, 2, 128], BF16); z = [p.sbuf([64, 8, 16], F32) for _ in range(4)]
        tmask = p.sbuf([128, 128], F32)
        qi = p.sbuf([64, 64], mybir.dt.int32)
        p.dma("sp", pg[:], cx.dram["ssm_pg"][jl, d].rearrange("k p g -> p k g"), writes=[pg])
        p.dma("sp", Bt[:], cx.dram["ssm_b"][jl, d].rearrange("k p x -> p k x"), writes=[Bt])
        p.dma("sp", Ct[:], cx.dram["ssm_c"][jl, d].rearrange("k p x -> p k x"), writes=[Ct])
        p.dma("sp", tmask[:], cx.dram["c_tmask"][d], writes=[tmask])
        DT, LR, LI, T0, T1, T2, CR, CI = range(8)
        p.op("act", lambda e: e.activation(sm[:, DT, :], pg[:, 2, :], AF.Exp), reads=[pg], writes=[sm])
        _tt(p, "dve", sm[:, LR, :], pg[:, 0, :], sm[:, DT, :], ALU.mult, [pg, sm], [sm])
        _tt(p, "dve", sm[:, LI, :], pg[:, 1, :], sm[:, DT, :], ALU.mult, [pg, sm], [sm])
        for k in range(-8, 9):
            kk = k + 8
            p.op("act", lambda e, k=k: e.activation(sm[:, T0, :], sm[:, LR, :], AF.Exp, scale=float(k)), reads=[sm], writes=[sm])
            for ri, ph in ((1, 0.0), (0, np.pi / 2)):
                p.op("dve", lambda e, k=k, ph=ph: e.tensor_scalar(sm[:, T1, :], sm[:, LI, :], float(k / TWO_PI), float(ph / TWO_PI + 64.0), ALU.mult, ALU.add),
                     reads=[sm], writes=[sm])
                p.op("dve", lambda e: e.tensor_copy(qi[:], sm[:, T1, :]), reads=[sm], writes=[qi])
                p.op("dve", lambda e: e.tensor_copy(sm[:, T2, :], qi[:]), reads=[qi], writes=[sm])
                _tt(p, "dve", sm[:, T1, :], sm[:, T1, :], sm[:, T2, :], ALU.subtract, [sm], [sm])
                p.op("dve", lambda e: e.tensor_single_scalar(sm[:, T2, :], sm[:, T1, :], 0.5, ALU.is_gt), reads=[sm], writes=[sm])
                _tt(p, "dve", sm[:, T1, :], sm[:, T1, :], sm[:, T2, :], ALU.subtract, [sm], [sm])
                p.op("act", lambda e: e.activation(sm[:, T2, :], sm[:, T1, :], AF.Sin, scale=float(TWO_PI)), reads=[sm], writes=[sm])
                _tt(p, "dve", PW[:, kk, ri, :], sm[:, T0, :], sm[:, T2, :], ALU.mult, [sm], [PW])
        p.op("act", lambda e: e.copy(R.A2[:, 0, :], PW[:, 16, 0, :]), reads=[PW], writes=[R.A2])
        p.op("act", lambda e: e.copy(R.A2[:, 1, :], PW[:, 16, 0, :]), reads=[PW], writes=[R.A2])
        p.op("act", lambda e: e.mul(R.AI2[:, 0, :], PW[:, 16, 1, :], -1.0), reads=[PW], writes=[R.AI2])
        p.op("act", lambda e: e.copy(R.AI2[:, 1, :], PW[:, 16, 1, :]), reads=[PW], writes=[R.AI2])
        are = pg[:, 0, :]; aim = pg[:, 1, :]
        p.op("dve", lambda e: e.tensor_scalar(sm[:, T0, :], PW[:, 9, 0, :], -1.0, None, ALU.add), reads=[PW], writes=[sm])
        _tt(p, "dve", sm[:, T1, :], are, are, ALU.mult, [pg], [sm])
        _tt(p, "dve", sm[:, T2, :], aim, aim, ALU.mult, [pg], [sm])
        _tt(p, "dve", sm[:, T1, :], sm[:, T1, :], sm[:, T2, :], ALU.add, [sm], [sm])
        p.op("dve", lambda e: e.reciprocal(sm[:, T1, :], sm[:, T1, :]), reads=[sm], writes=[sm])
        _tt(p, "dve", sm[:, CR, :], sm[:, T0, :], are, ALU.mult, [sm, pg], [sm])
        _tt(p, "dve", sm[:, T2, :], PW[:, 9, 1, :], aim, ALU.mult, [PW, pg], [sm])
        _tt(p, "dve", sm[:, CR, :], sm[:, CR, :], sm[:, T2, :], ALU.add, [sm], [sm])
        _tt(p, "dve", sm[:, CR, :], sm[:, CR, :], sm[:, T1, :], ALU.mult, [sm], [sm])
        _tt(p, "dve", sm[:, CI, :], PW[:, 9, 1, :], are, ALU.mult, [PW, pg], [sm])
        _tt(p, "dve", sm[:, T2, :], sm[:, T0, :], aim, ALU.mult, [sm, pg], [sm])
        _tt(p, "dve", sm[:, CI, :], sm[:, CI, :], sm[:, T2, :], ALU.subtract, [sm], [sm])
        _tt(p, "dve", sm[:, CI, :], sm[:, CI, :], sm[:, T1, :], ALU.mult, [sm], [sm])
        crb = sm.ap(CR * 64, [[1, 64], [0, 16]], np_=64); cib = sm.ap(CI * 64, [[1, 64], [0, 16]], np_=64)
        Br = Bt.ap(0, [[16, 64], [1, 16]], np_=64); Bi = Bt.ap(1024, [[16, 64], [1, 16]], np_=64)
        Bbr = Bb.ap(0, [[16, 64], [1, 16]], np_=64); Bbi = Bb.ap(1024, [[16, 64], [1, 16]], np_=64)
        big0 = Ct
        tA = XT.ap(0, [[16, 64], [1, 16]], np_=64); tB = XT.ap(1024, [[16, 64], [1, 16]], np_=64)
        _tt(p, "dve", tA, Br, crb, ALU.mult, [Bt, sm], [XT])
        _tt(p, "dve", tB, Bi, cib, ALU.mult, [Bt, sm], [XT])
        _tt(p, "dve", Bbr, tA, tB, ALU.subtract, [XT], [Bb])
        _tt(p, "dve", tA, Br, cib, ALU.mult, [Bt, sm], [XT])
        _tt(p, "dve", tB, Bi, crb, ALU.mult, [Bt, sm], [XT])
        _tt(p, "dve", Bbi, tA, tB, ALU.add, [XT], [Bb])
        for gb in range(8):
            g0 = gb * 8
            def pw(k, ri):
                return PW.ap(((k + 8) * 2 + ri) * 64 + g0, [[1, 8], [0, 16]], np_=64)
            def cplx(dst, ri_stride, slot, Xr, Xi, k, deps_x, neg_im=False):
                dr = dst.ap(slot * 16, [[256, 8], [1, 16]], np_=64); di = dst.ap(128 + slot * 16, [[256, 8], [1, 16]], np_=64)
                _tt(p, "dve", z[0][:], Xr, pw(k, 0), ALU.mult, deps_x + [PW], [z[0]])
                _tt(p, "pool", z[1][:], Xi, pw(k, 1), ALU.mult, deps_x + [PW], [z[1]])
                _tt(p, "dve", dr, z[0][:], z[1][:], ALU.subtract, [z[0], z[1]], [dst])
                _tt(p, "pool", z[2][:], Xr, pw(k, 1), ALU.mult, deps_x + [PW], [z[2]])
                _tt(p, "dve", z[3][:], Xi, pw(k, 0), ALU.mult, deps_x + [PW], [z[3]])
                if neg_im:
                    p.op("dve", lambda e: e.scalar_tensor_tensor(di, z[2][:], -1.0, z[3][:], ALU.mult, ALU.subtract), reads=[z[2], z[3]], writes=[dst])
                else:
                    _tt(p, "dve", di, z[2][:], z[3][:], ALU.add, [z[2], z[3]], [dst])
            bbr = Bb.ap(g0 * 16, [[16, 8], [1, 16]], np_=64); bbi = Bb.ap(1024 + g0 * 16, [[16, 8], [1, 16]], np_=64)
            ccr = Ct.ap(g0 * 16, [[16, 8], [1, 16]], np_=64); cci = Ct.ap(1024 + g0 * 16, [[16, 8], [1, 16]], np_=64)
            for s in range(8):
                kw = (7 - s) if d == 0 else s
                kx = (-s - 1) if d == 0 else (s - 8)
                km = (s + 1) if d == 0 else (8 - s)
                cplx(WT, 0, s, bbr, bbi, kw, [Bb])
                cplx(XT, 0, s, bbr, bbi, kx, [Bb])
                cplx(Mt, 0, s, ccr, cci, km, [Ct], neg_im=True)
            p.op("act", lambda e: e.copy(XTb[:], XT[:]), reads=[XT], writes=[XTb])
            p.op("act", lambda e, g0=g0: e.copy(R.M[:, g0:g0 + 8, :, :], Mt[:]), reads=[Mt], writes=[R.M])
            for gl in range(8):
                g = g0 + gl
                bk = B[gl % 2]
                for ri in range(2):
                    p.op("pe", lambda e, gl=gl, ri=ri, bk=bk: e.transpose(bk[:, ri * 64:(ri + 1) * 64], WT[:, gl, ri, :], cx.ident[0:64, 0:64]),
                         reads=[WT, cx.ident], writes=[bk])
                p.op("act", lambda e, g=g, bk=bk: e.copy(R.Win[:, g, :], bk[:, 0:128]), reads=[bk], writes=[R.Win])
                bt = B[2 + gl % 2]
                for ri in range(2):
                    p.op("pe", lambda e, gl=gl, g=g, ri=ri, bt=bt: e.matmul(bt[:, 0:128], XTb[:, gl, ri, :], R.M[:, g, ri, :], start=(ri == 0), stop=(ri == 1)),
                         reads=[XTb, R.M], writes=[bt])
                p.op("dve", lambda e, g=g, bt=bt: e.tensor_tensor(R.Toep[:, g, :], bt[:, 0:128], tmask[:], ALU.mult), reads=[bt, tmask], writes=[R.Toep])


def ssm_sweep(p, cx, R, jl, d, seqs):
    B = cx.banks
    with Scope(p):
        u8s = [p.sbuf([128, JB, 64], BF16) for _ in range(2)]; Ss = [p.sbuf([64, 2, 64, JB], F32) for _ in range(2)]
        Hs = [p.sbuf([64, 2, 64], F32) for _ in range(2)]; cur = [0]
        t1 = p.sbuf([64, 2, 64], F32); t2 = p.sbuf([64, 2, 64], F32); Hhs = [p.sbuf([64, 2, 64, JB + 1], BF16) for _ in range(2)]
        Y8s = [p.sbuf([128, JB, 64], F32) for _ in range(2)]; Yf = p.sbuf([128, JB, 64], F32); d8 = p.sbuf([128, 64], F32)
        hs = p.sbuf([64, 2, 64], F32)
        p.dma("sp", d8[:], cx.dram["ssm_d8"][jl], writes=[d8])
        udv = cx.Ud.rearrange("(j s) (c g) -> s c j g", s=8, c=16)
        ydv = cx.Yd.rearrange("(j s) (c g) -> s c j g", s=8, c=16)
        items = []
        for sq in seqs:
            nb = sq["n"] // JB
            order = list(range(nb)) if d == 0 else list(range(nb - 1, -1, -1))
            for n_, bi in enumerate(order):
                items.append((sq, sq["j0"] + bi * JB, n_ == 0, n_ == nb - 1))

        def prep(k):
            sq, j0, first, last = items[k]
            u8 = u8s[k % 2]; S = Ss[k % 2]
            tl = (j0 * 8) // 128
            ddeps = [cx.dUd[tl], cx.dUd[tl + 1]]
            for s_ in range(8):
                p.dma("sp", u8[16 * s_:16 * s_ + 16, :, :], udv[s_, :, j0:j0 + JB, :], reads=ddeps, writes=[u8])
            for gq in range(8):
                bk = B[gq % 2]
                for gl in range(8):
                    g = gq * 8 + gl
                    for ri in range(2):
                        p.op("pe", lambda e, g=g, gl=gl, ri=ri, bk=bk: e.matmul(bk[0:64, (gl * 2 + ri) * JB:(gl * 2 + ri + 1) * JB], R.Win[:, g, ri * 64:(ri + 1) * 64],
                                                                               u8.ap(g, [[64, JB]]), start=True, stop=True), reads=[R.Win, u8], writes=[bk])
                p.op("act", lambda e, gq=gq, bk=bk: e.copy(S.ap(gq * 8 * JB, [[JB, 8], [64 * JB, 2], [1, JB]], np_=64),
                                                           bk.ap(0, [[2 * JB, 8], [JB, 2], [1, JB]], np_=64)), reads=[bk], writes=[S])

        def scan(k):
            sq, j0, first, last = items[k]
            S = Ss[k % 2]; Hh = Hhs[k % 2]
            if first:
                H = Hs[cur[0]]
                if sq["init"] is not None:
                    p.dma("sp", H[:], sq["init"], writes=[H])
                else:
                    p.op("dve", lambda e, H=H: e.memset(H[:], 0.0), writes=[H])
            hidx0 = 0 if d == 0 else JB
            p.op("pool", lambda e: e.tensor_copy(Hh.ap(hidx0, [[JB + 1, 128]], np_=64), Hs[cur[0]].ap(0, [[1, 128]], np_=64)), reads=[Hs[cur[0]]], writes=[Hh])
            for jj in (range(JB) if d == 0 else range(JB - 1, -1, -1)):
                H = Hs[cur[0]]; Hn = Hs[1 - cur[0]]
                _tt(p, "dve", t1[:], H[:], R.A2[:], ALU.mult, [H, R.A2], [t1])
                _tt(p, "dve", t2[:, 0, :], H[:, 1, :], R.AI2[:, 0, :], ALU.mult, [H, R.AI2], [t2])
                _tt(p, "dve", t2[:, 1, :], H[:, 0, :], R.AI2[:, 1, :], ALU.mult, [H, R.AI2], [t2])
                _tt(p, "dve", t1[:], t1[:], t2[:], ALU.add, [t1, t2], [t1])
                _tt(p, "dve", Hn[:], t1[:], S.ap(jj, [[64 * JB, 2], [JB, 64]], np_=64), ALU.add, [t1, S], [Hn])
                hi = (jj + 1) if d == 0 else jj
                p.op("pool", lambda e, hi=hi, Hn=Hn: e.tensor_copy(Hh.ap(hi, [[JB + 1, 128]], np_=64), Hn.ap(0, [[1, 128]], np_=64)), reads=[Hn], writes=[Hh])
                cur[0] = 1 - cur[0]
            if last and sq["out"] is not None:
                bk = B[0]
                for ri in range(2):
                    p.op("pe", lambda e, ri=ri: e.transpose(bk[0:64, ri * 64:(ri + 1) * 64], Hs[cur[0]][:, ri, :], cx.ident[0:64, 0:64]), reads=[Hs[cur[0]], cx.ident], writes=[bk])
                p.op("act", lambda e: e.copy(hs[:], bk.ap(0, [[64, 2], [1, 64]], np_=64)), reads=[bk], writes=[hs])
                p.dma("act", sq["out"].rearrange("r g q -> g r q"), hs[:], reads=[hs], writes=[cx.d_out])

        def outp(k):
            sq, j0, first, last = items[k]
            u8 = u8s[k % 2]; Hh = Hhs[k % 2]; Y8 = Y8s[k % 2]
            tl = (j0 * 8) // 128
            hoff = 0 if d == 0 else 1
            for gq in range(4):
                bk = B[2 + gq]
                for gl in range(16):
                    g = gq * 16 + gl
                    o = bk[:, gl * JB:(gl + 1) * JB]
                    p.op("pe", lambda e, g=g, o=o: e.matmul(o, R.Toep[:, g, :], u8.ap(g, [[64, JB]]), start=True, stop=False), reads=[R.Toep, u8], writes=[bk])
                    for ri in range(2):
                        p.op("pe", lambda e, g=g, o=o, ri=ri: e.matmul(o, R.M[:, g, ri, :], Hh.ap((ri * 64 + g) * (JB + 1) + hoff, [[1, JB]], np_=64),
                                                                       start=False, stop=(ri == 1)), reads=[R.M, Hh], writes=[bk])
                p.op("act", lambda e, gq=gq, bk=bk: e.copy(Y8.ap(gq * 16, [[1, 16], [64, JB]]), bk.ap(0, [[JB, 16], [1, JB]])), reads=[bk], writes=[Y8])
            if d == 0:
                p.dma("sp", cx.Yf8[:, j0:j0 + JB, :], Y8[:], reads=[Y8], writes=[cx.dYf[j0 // JB]])
            else:
                p.dma("sp", Yf[:], cx.Yf8[:, j0:j0 + JB, :], reads=[cx.dYf[j0 // JB]], writes=[Yf])
                _tt(p, "pool", Y8[:], Y8[:], Yf[:], ALU.add, [Y8, Yf], [Y8])
                _tt(p, "pool", Yf[:], u8[:], d8.ap(0, [[0, JB], [1, 64]]), ALU.mult, [u8, d8], [Yf])
                _tt(p, "pool", Y8[:], Y8[:], Yf[:], ALU.add, [Y8, Yf], [Y8])
                p.op("act", lambda e: e.activation(Y8[:], Y8[:], AF.Gelu_apprx_tanh), reads=[Y8], writes=[Y8])
                for s_ in range(8):
                    p.dma("sp", ydv[s_, :, j0:j0 + JB, :], Y8[16 * s_:16 * s_ + 16, :, :], reads=[Y8], writes=[cx.dYd[tl], cx.dYd[tl + 1]])

        n = len(items)
        if n:
            prep(0)
        for k in range(n):
            if k + 1 < n:
                prep(k + 1)
            scan(k)
            outp(k)


def ssm_layer(p, cx, jl, mods, lng, lnb, seqs_fn):
    B = cx.banks
    NTt = cx.NT // 128
    nS = cx.n_smp_tiles
    with Scope(p):
        win = p.sbuf([128, 8, 1024], BF16)
        xt = [p.sbuf([128, 1024], F32) for _ in range(2)]; hm = p.sbuf([128, 1024], F32); hT = p.sbuf([128, 8, 128], BF16)
        uo = [p.sbuf([128, 1024], BF16) for _ in range(2)]
        w = cx.dram["ssm_w_in"][jl]
        p.dma("pool", win[:], w.rearrange("(k q) c -> q k c", q=128), writes=[win])
        p.dma("sp", xt[0][:], cx.X[0:128, :], reads=[cx.dX[0]], writes=[xt[0]])
        for i in range(NTt):
            mod = mods["smp"] if i < nS else mods["ctx"]
            x_ = xt[i % 2]
            if i + 1 < NTt:
                p.dma("sp", xt[(i + 1) % 2][:], cx.X[(i + 1) * 128:(i + 2) * 128, :], reads=[cx.dX[i + 1]], writes=[xt[(i + 1) % 2]])
            modulate_transpose(p, cx, x_, hm, lambda half: (hT[:, half * 4:(half + 1) * 4, :], hT), mod, [B[0], B[1]])
            for dh in range(2):
                bk = B[2 + dh]
                for k in range(8):
                    p.op("pe", lambda e, dh=dh, k=k, bk=bk: e.matmul(bk[:], hT[:, k, :], win[:, k, dh * 512:(dh + 1) * 512], start=(k == 0), stop=(k == 7)),
                         reads=[hT, win], writes=[bk])
                p.op("act" if dh == 0 else "dve", (lambda e, dh=dh, bk=bk, i=i: e.copy(uo[i % 2][:, dh * 512:(dh + 1) * 512], bk[:])) if dh == 0 else
                     (lambda e, dh=dh, bk=bk, i=i: e.tensor_copy(uo[i % 2][:, dh * 512:(dh + 1) * 512], bk[:])), reads=[bk], writes=[uo[i % 2]])
            p.dma("act", cx.Ud[i * 128:(i + 1) * 128, :], uo[i % 2][:], reads=[uo[i % 2]], writes=[cx.dUd[i]])
    with Scope(p):
        R = Ctx()
        R.Win = p.sbuf([128, 64, 128], BF16); R.Toep = p.sbuf([128, 64, 128], BF16); R.M = p.sbuf([64, 64, 2, 128], BF16)
        R.A2 = p.sbuf([64, 2, 64], F32); R.AI2 = p.sbuf([64, 2, 64], F32)
        for d in range(2):
            ssm_precompute(p, cx, R, jl, d)
            ssm_sweep(p, cx, R, jl, d, seqs_fn(d))
    with Scope(p):
        wg = p.sbuf([128, 8, 2048], BF16); wo = p.sbuf([128, 8, 1024], BF16)
        xt = [p.sbuf([128, 1024], F32) for _ in range(3)]; yt = [p.sbuf([128, 1024], F32) for _ in range(3)]
        yT = p.sbuf([128, 8, 128], BF16); sg = p.sbuf([128, 1024], F32); vvs = [p.sbuf([128, 1024], F32) for _ in range(2)]; vT = p.sbuf([128, 8, 128], BF16)
        t1 = p.sbuf([128, 1024], F32); y = p.sbuf([128, 1024], F32)
        scr = {"st": p.sbuf([128, 12], F32), "mv": p.sbuf([128, 2], F32), "rstd": p.sbuf([128, 1], F32)}
        p.dma("pool", wg[:], cx.dram["ssm_w_glu"][jl].rearrange("(k q) c -> q k c", q=128), writes=[wg])
        p.dma("pool", wo[:], cx.dram["ssm_w_out"][jl].rearrange("(k q) c -> q k c", q=128), writes=[wo])

        def loads(i):
            p.dma("sp", xt[i % 3][:], cx.X[i * 128:(i + 1) * 128, :], reads=[cx.dX[i]], writes=[xt[i % 3]])
            p.dma("sp", yt[i % 3][:], cx.Yd[i * 128:(i + 1) * 128, :], reads=[cx.dYd[i]], writes=[yt[i % 3]])

        def S1(i):
            y_ = yt[i % 3]; vv = vvs[i % 2]
            for half in range(2):
                bk = B[half]
                for k4 in range(4):
                    k = half * 4 + k4
                    p.op("pe", lambda e, k=k, k4=k4, bk=bk: e.transpose(bk[:, k4 * 128:(k4 + 1) * 128], y_[:, k * 128:(k + 1) * 128], cx.ident[:]), reads=[y_, cx.ident], writes=[bk])
                p.op("act", lambda e, half=half, bk=bk: e.copy(yT[:, half * 4:(half + 1) * 4, :], bk.ap(0, [[128, 4], [1, 128]])), reads=[bk], writes=[yT])
            for q in range(4):
                bk = B[2 + q]
                for k in range(8):
                    p.op("pe", lambda e, q=q, k=k, bk=bk: e.matmul(bk[:], yT[:, k, :], wg[:, k, q * 512:(q + 1) * 512], start=(k == 0), stop=(k == 7)), reads=[yT, wg], writes=[bk])
            for q in range(2):
                p.op("act", lambda e, q=q: e.activation(sg[:, q * 512:(q + 1) * 512], B[4 + q][:], AF.Sigmoid), reads=[B[4 + q]], writes=[sg])
                p.op("dve", lambda e, q=q: e.tensor_tensor(vv[:, q * 512:(q + 1) * 512], B[2 + q][:], sg[:, q * 512:(q + 1) * 512], ALU.mult), reads=[B[2 + q], sg], writes=[vv])

        def S2(i):
            mod = mods["smp"] if i < nS else mods["ctx"]
            vv = vvs[i % 2]
            for half in range(2):
                bk = B[6 + half]
                for k4 in range(4):
                    k = half * 4 + k4
                    p.op("pe", lambda e, k=k, k4=k4, bk=bk: e.transpose(bk[:, k4 * 128:(k4 + 1) * 128], vv[:, k * 128:(k + 1) * 128], cx.ident[:]), reads=[vv, cx.ident], writes=[bk])
                p.op("act", lambda e, half=half, bk=bk: e.copy(vT[:, half * 4:(half + 1) * 4, :], bk.ap(0, [[128, 4], [1, 128]])), reads=[bk], writes=[vT])
            psO = [B[6], B[7]]
            for dh in range(2):
                for k in range(8):
                    p.op("pe", lambda e, dh=dh, k=k: e.matmul(psO[dh][:], vT[:, k, :], wo[:, k, dh * 512:(dh + 1) * 512], start=(k == 0), stop=(k == 7)), reads=[vT, wo], writes=[psO[dh]])
            post_norm_store(p, cx, xt[i % 3], psO, mod, lng, lnb, t1, y, scr, i * 128, q="sp")

        loads(0)
        if NTt > 1:
            loads(1)
        S1(0)
        for i in range(NTt):
            if i + 2 < NTt:
                loads(i + 2)
            if i + 1 < NTt:
                S1(i + 1)
            S2(i)


def host_ssm_layouts(inp, jl):
    o = {}
    pg = np.stack([np.stack([inp["ssm_a_re"][jl, d].T, inp["ssm_a_im"][jl, d].T,
                             np.broadcast_to(inp["ssm_log_dt"][jl, d][None, :], (64, 64))]) for d in range(2)])
    o["ssm_pg"] = np.ascontiguousarray(pg, dtype=np.float32)
    o["ssm_b"] = np.ascontiguousarray(np.stack([np.stack([inp["ssm_b_re"][jl, d].transpose(1, 0, 2).reshape(64, 1024),
                                                          inp["ssm_b_im"][jl, d].transpose(1, 0, 2).reshape(64, 1024)]) for d in range(2)]))
    o["ssm_c"] = np.ascontiguousarray(np.stack([np.stack([inp["ssm_c_re"][jl, d].transpose(2, 0, 1).reshape(64, 1024),
                                                          inp["ssm_c_im"][jl, d].transpose(2, 0, 1).reshape(64, 1024)]) for d in range(2)]))
    dd = inp["ssm_d"][jl].reshape(64, 16).T
    o["ssm_d8"] = np.ascontiguousarray(np.tile(dd, (8, 1)))
    perm = (np.arange(64)[None, :] * 16 + np.arange(16)[:, None]).reshape(-1)
    o["ssm_w_in"] = np.ascontiguousarray(inp["ssm_w_in"][jl][:, perm])
    o["ssm_w_glu"] = np.ascontiguousarray(inp["ssm_w_glu"][jl][perm, :])
    o["ssm_w_out"] = np.ascontiguousarray(inp["ssm_w_out"][jl])
    return o


def host_tmask():
    s = np.arange(128)[:, None] // 16; t = np.arange(128)[None, :] // 16
    return np.stack([(s <= t), (s >= t)]).astype(np.float32)


def compute_mods(p, cx, l, half):
    B = cx.banks
    names = ["sh", "sc1", "g1"]
    with Scope(p):
        wm = [p.sbuf([128, 8, 512], BF16) for _ in range(2)]
        bm = [p.sbuf([128, 512], F32) for _ in range(2)]
        it = 0
        for v in range(3):
            for dh in range(2):
                col = half * 3072 + v * 1024 + dh * 512
                w_ = wm[it % 2]; b_ = bm[it % 2]
                p.dma("pool", w_[:], cx.dram["w_mod"][l][:, col:col + 512].rearrange("(k q) c -> q k c", q=128), writes=[w_])
                p.dma("sp", b_[:], cx.dram["b_mod"][l][col:col + 512].partition_broadcast(128), writes=[b_])
                for ci, cn in enumerate(("smp", "ctx")):
                    bk = B[(it * 2 + ci) % 8]
                    for k in range(8):
                        p.op("pe", lambda e, k=k, ci=ci, bk=bk, w_=w_: e.matmul(bk[:], cx.SCb[ci][:, k, :], w_[:, k, :], start=(k == 0), stop=(k == 7)),
                             reads=[cx.SCb[ci], w_], writes=[bk])
                    dst = cx.mods[cn][names[v]]
                    p.op("dve", lambda e, bk=bk, dst=dst, b_=b_, v=v, dh=dh: e.scalar_tensor_tensor(
                        dst[:, dh * 512:(dh + 1) * 512], bk[:], 1.0 if v >= 1 else 0.0, b_[:], ALU.add, ALU.add), reads=[bk, b_], writes=[dst])
                it += 1


def load_ln(p, cx, l, which):
    p.dma("sp", cx.lng[:], cx.dram["ln_g"][l, which].partition_broadcast(128), writes=[cx.lng])
    p.dma("sp", cx.lnb[:], cx.dram["ln_b"][l, which].partition_broadcast(128), writes=[cx.lnb])


NS_TILES = 32
NP_SEQ = 4
NTOK = NS_TILES * 128 + NP_SEQ * 256


def build_program(nS=NS_TILES, nP=NP_SEQ, depth=DEPTH):
    nc = bass.Bass("TRN2", target_bir_lowering=False)
    p = Prog(nc)
    cx = Ctx()
    NT = nS * 128 + nP * 256
    NTt = NT // 128
    NCH = NT // 8
    cx.NT = NT; cx.n_smp_tiles = nS; cx.n_prm_seq = nP

    def din(name, shape, dt=F32):
        return nc.dram_tensor(name, list(shape), dt, kind="ExternalInput").ap()

    def dout(name, shape):
        return nc.dram_tensor(name, list(shape), F32, kind="ExternalOutput").ap()

    cx.dram = {
        "c_ident": din("c_ident", [128, 128]), "c_e3": din("c_e3", [128, 128]), "c_masks": din("c_masks", [128, 2, 128]),
        "c_rope": din("c_rope", [64, 2, 4096]), "c_tmask": din("c_tmask", [2, 128, 128]),
        "w_mod": din("w_mod", [4, 1024, 6144]), "b_mod": din("b_mod", [4, 6144]), "ln_g": din("ln_g", [4, 2, 1024]), "ln_b": din("ln_b", [4, 2, 1024]),
        "ssm_pg": din("ssm_pg", [2, 2, 3, 64, 64]), "ssm_b": din("ssm_b", [2, 2, 2, 64, 1024]), "ssm_c": din("ssm_c", [2, 2, 2, 64, 1024]),
        "ssm_d8": din("ssm_d8", [2, 128, 64]), "ssm_w_in": din("ssm_w_in", [2, 1024, 1024]), "ssm_w_glu": din("ssm_w_glu", [2, 1024, 2048]),
        "ssm_w_out": din("ssm_w_out", [2, 1024, 1024]),
        "attn_w_qkv": din("attn_w_qkv", [2, 1024, 1536]), "attn_w_qkp": din("attn_w_qkp", [2, 1024, 1280]),
        "attn_w_out": din("attn_w_out", [2, 1024, 1024]), "attn_sink": din("attn_sink", [2, 16]),
        "cache_k": din("cache_k", [2, 512, 256]), "cache_v": din("cache_v", [2, 512, 256]),
        "peer_w_q": din("peer_w_q", [4, 1024, 2048]), "peer_keysT": din("peer_keysT", [4, 128, 16, 128]),
        "peer_ut": din("peer_ut", [4, 128, 128, 1024]), "peer_v": din("peer_v", [4, NEXP, 1024]),
    }
    x_in = din("x_in", [NT, 1024]); cond2 = din("cond2", [2, 128, 8]); st0 = din("st0", [2, 2, 64, 2, 64])
    y_out = dout("y_out", [NT, 1024]); o_state = dout("o_state", [nP, 2, 2, 2, 64, 64])
    o_k = dout("o_k", [nP, 2, 256, 256]); o_v = dout("o_v", [nP, 2, 256, 256])
    cx.X = nc.dram_tensor("Xs", [NT, 1024], F32).ap()
    cx.Ud = nc.dram_tensor("Ud", [NT, 1024], BF16).ap(); cx.Yd = nc.dram_tensor("Yd", [NT, 1024], F32).ap()
    cx.Yf8 = nc.dram_tensor("Yf8", [128, NCH, 64], F32).ap()
    cx.spd = nc.dram_tensor("spd", [2, 3, 8, NT, 128], BF16).ap()
    cx.utb = nc.dram_tensor("utb", [128, 128, 1024], BF16).ap(); cx.vb = nc.dram_tensor("vb", [NEXP, 1024], BF16).ap()
    cx.dX = [Dep() for _ in range(NTt)]; cx.dUd = [Dep() for _ in range(NTt)]; cx.dYd = [Dep() for _ in range(NTt)]
    cx.dYf = [Dep() for _ in range(NCH // JB)]; cx.d_spd = [Dep() for _ in range(NTt)]
    cx.d_utb = [Dep() for _ in range(16)]; cx.d_vb = [Dep() for _ in range(16)]
    cx.d_out = Dep()
    setup_common(p, cx)
    for i in range(NTt):
        p.dma("sp", cx.X[i * 128:(i + 1) * 128, :], x_in[i * 128:(i + 1) * 128, :], writes=[cx.dX[i]])
    cx.mods = {cn: {n: p.sbuf([128, 1024], F32, name="mod_%s_%s" % (cn, n)) for n in ("sh", "sc1", "g1")} for cn in ("smp", "ctx")}
    cx.lng = p.sbuf([128, 1024], F32, name="lng"); cx.lnb = p.sbuf([128, 1024], F32, name="lnb")
    cx.SCb = [p.sbuf([128, 8, 128], BF16, name="SCb%d" % i) for i in range(2)]
    ctile = p.sbuf([128, 2, 8], F32, name="condt")
    p.dma("sp", ctile[:], cond2.rearrange("a q k -> q a k"), writes=[ctile])
    p.op("act", lambda e: e.activation(ctile[:], ctile[:], AF.Silu), reads=[ctile], writes=[ctile])
    for ci in range(2):
        p.op("dve", lambda e, ci=ci: e.tensor_copy(cx.SCb[ci][:], ctile.ap(ci * 8, [[1, 8], [0, 128]])), reads=[ctile], writes=[cx.SCb[ci]])

    for l in range(depth):
        peer_prep_weights(p, cx, l)
        compute_mods(p, cx, l, 0)
        load_ln(p, cx, l, 0)
        if l % 2 == 0:
            jl = l // 2

            def seqs_fn(d, jl=jl):
                L = []
                if nS:
                    L.append(dict(j0=0, n=nS * 16, init=st0[jl, d], out=None))
                for s in range(nP):
                    L.append(dict(j0=nS * 16 + s * 32, n=32, init=None, out=o_state[s, jl, d]))
                return L
            ssm_layer(p, cx, jl, cx.mods, cx.lng, cx.lnb, seqs_fn)
        else:
            j = l // 2
            with Scope(p):
                a = attn_alloc(p, cx)
                attn_load_layer(p, cx, a, j)
                attn_layer(p, cx, a, j, cx.mods, cx.lng, cx.lnb,
                           lambda s, t, j=j: (o_k[s, j, t * 128:(t + 1) * 128, :], o_v[s, j, t * 128:(t + 1) * 128, :]))
        compute_mods(p, cx, l, 1)
        load_ln(p, cx, l, 1)
        with Scope(p):
            a = peer_alloc(p, cx)
            peer_load_layer(p, cx, a, l)
            for blk in range(NT // 256):
                mod = cx.mods["smp"] if blk * 2 < nS else cx.mods["ctx"]
                peer_block(p, cx, a, l, blk, mod, cx.lng, cx.lnb)
    for i in range(NTt):
        p.dma("sp", y_out[i * 128:(i + 1) * 128, :], cx.X[i * 128:(i + 1) * 128, :], reads=[cx.dX[i]], writes=[cx.d_out])
    p.drain_all()
    p.es.close()
    return nc, p


def host_inputs(inp, core, nS=NS_TILES, nP=NP_SEQ):
    f = lambda a: np.ascontiguousarray(a, dtype=np.float32)
    m = dict(_SHARED)
    xs = inp["x_sample"][core][:nS * 128]
    xp = inp["x_prompt"][core * NP_SEQ: core * NP_SEQ + nP].reshape(nP * 256, 1024)
    m["x_in"] = f(np.concatenate([xs, xp], axis=0))
    m["cond2"] = f(np.stack([inp["c"][core].reshape(8, 128).T, inp["c_ctx"].reshape(8, 128).T]))
    st = np.zeros((2, 2, 64, 2, 64), np.float32)
    for jl in range(2):
        for d in range(2):
            st[jl, d, :, 0, :] = inp["state_ssm_re"][core, jl, d].T
            st[jl, d, :, 1, :] = inp["state_ssm_im"][core, jl, d].T
    m["st0"] = st
    m["cache_k"] = f(inp["cache_k"][core].reshape(2, 512, 256))
    m["cache_v"] = f(inp["cache_v"][core].reshape(2, 512, 256))
    return m


_SHARED = {}


def prepare_shared(inp):
    f = lambda a: np.ascontiguousarray(a, dtype=np.float32)
    S = {}
    S.update(host_consts())
    S["c_tmask"] = host_tmask()
    for k in ("w_mod", "b_mod", "ln_g", "ln_b", "attn_w_qkv", "attn_w_out", "attn_sink", "peer_w_q", "peer_v"):
        S[k] = f(inp[k])
    S["attn_w_qkp"] = f(perm_qk(inp["attn_w_qkv"]))
    L = [host_ssm_layouts(inp, jl) for jl in range(2)]
    for k in L[0]:
        S[k] = f(np.stack([L[0][k], L[1][k]]))
    keys = inp["peer_keys"]
    S["peer_keysT"] = f(keys.transpose(0, 4, 2, 1, 3).reshape(4, 128, 16, 128))
    u = inp["peer_u"]
    S["peer_ut"] = f(u.reshape(4, 128, 128, 8, 128).transpose(0, 1, 4, 3, 2).reshape(4, 128, 128, 1024))
    _SHARED.clear()
    _SHARED.update(S)


_PROG = {}


def kernel(**inputs):
    inp = {k: np.asarray(v) for k, v in inputs.items()}
    if "nc" not in _PROG:
        _PROG["nc"], _ = build_program()
    nc = _PROG["nc"]
    prepare_shared(inp)
    in_maps = [host_inputs(inp, c) for c in range(8)]
    res = run_bass_kernel_spmd(nc, in_maps, core_ids=list(range(8)))
    R = res.results
    nS, nP = NS_TILES, NP_SEQ
    y_sample = np.stack([R[c]["y_out"][:nS * 128] for c in range(8)]).astype(np.float32)
    y_prompt = np.concatenate([R[c]["y_out"][nS * 128:].reshape(nP, 256, 1024) for c in range(8)], axis=0).astype(np.float32)
    ost = np.concatenate([R[c]["o_state"] for c in range(8)], axis=0)
    ns_re = np.ascontiguousarray(ost[:, :, :, 0]).astype(np.float32)
    ns_im = np.ascontiguousarray(ost[:, :, :, 1]).astype(np.float32)
    nk = np.concatenate([R[c]["o_k"] for c in range(8)], axis=0).reshape(32, 2, 256, 4, 64).astype(np.float32)
    nv = np.concatenate([R[c]["o_v"] for c in range(8)], axis=0).reshape(32, 2, 256, 4, 64).astype(np.float32)
    return (y_prompt, y_sample, ns_re, ns_im, nk, nv)
```
